# Optimizing a Trainium2 kernel written in Bass

```python
import math
import jax, jax.numpy as jnp
from jax import lax
import numpy as np

D_MODEL = 1024
BATCH = 32
SEQ = 256
DEPTH = 1
DEC_BATCH = 2
DEC_SEQ = 4096
PAST_LEN = 256

GRID_W = 64
N_HEADS = 8
N_KV_HEADS = 2
HEAD_DIM = 128
GROUP = N_HEADS // N_KV_HEADS
ATTN_WIDTH = N_HEADS * HEAD_DIM
KV_WIDTH = N_KV_HEADS * HEAD_DIM
POOL_WINDOWS = (2, 4, 8, 16)
N_POOL_GROUPS = 4
POOL_GROUP_DIM = 128
POOL_WIDTH = N_POOL_GROUPS * POOL_GROUP_DIM
IN_WIDTH = ATTN_WIDTH + 2 * KV_WIDTH + POOL_WIDTH + 2 * D_MODEL
SPLITS = (ATTN_WIDTH, ATTN_WIDTH + KV_WIDTH, ATTN_WIDTH + 2 * KV_WIDTH,
          ATTN_WIDTH + 2 * KV_WIDTH + POOL_WIDTH, ATTN_WIDTH + 2 * KV_WIDTH + POOL_WIDTH + D_MODEL)
D_FF = 2816
N_MOD = 9
ROPE_THETA = 10000.0
Q_BLOCK = 128
LN_EPS = 1e-6
RMS_EPS = 1e-6
ALPHA = (2.0 * DEPTH) ** 0.25
BETA = (8.0 * DEPTH) ** -0.25

kernel_name = "hybrid_flow_prefix_gqa_pool_macaron"


def _layer_norm(x, g, b):
    xf = x.astype(jnp.float32)
    mu = jnp.mean(xf, axis=-1, keepdims=True)
    var = jnp.mean(jnp.square(xf - mu), axis=-1, keepdims=True)
    return ((xf - mu) * lax.rsqrt(var + LN_EPS)).astype(x.dtype) * g + b


def _rms_norm(x, g):
    xf = x.astype(jnp.float32)
    ms = jnp.mean(jnp.square(xf), axis=-1, keepdims=True)
    return (xf * lax.rsqrt(ms + RMS_EPS)).astype(x.dtype) * g


def _modulation(cond, w_mod, b_mod):
    m = jax.nn.silu(cond) @ w_mod + b_mod
    return m.reshape(cond.shape[0], N_MOD, D_MODEL)


def _modulate(x, shift, scale):
    return x * (1.0 + scale) + shift


def _swiglu(h, w1, w2):
    a, b = jnp.split(h @ w1, 2, axis=-1)
    return (jax.nn.silu(a) * b) @ w2


def _rope_tables(n_tokens):
    t = jnp.arange(n_tokens, dtype=jnp.int32)
    row = (t // GRID_W).astype(jnp.float32)
    col = (t % GRID_W).astype(jnp.float32)
    n_freq = HEAD_DIM // 4
    inv_freq = ROPE_THETA ** (-jnp.arange(n_freq, dtype=jnp.float32) / n_freq)
    ang = jnp.concatenate([row[:, None] * inv_freq, col[:, None] * inv_freq], axis=-1)
    return jnp.cos(ang), jnp.sin(ang)


def _apply_rope(x, cos, sin):
    half = HEAD_DIM // 2
    x1, x2 = x[..., :half], x[..., half:]
    c = cos[None, :, None, :].astype(x.dtype)
    s = sin[None, :, None, :].astype(x.dtype)
    return jnp.concatenate([x1 * c - x2 * s, x1 * s + x2 * c], axis=-1)


def _ctx_attention(q, k, v):
    B, S = q.shape[0], q.shape[1]
    qg = q.reshape(B, S, N_KV_HEADS, GROUP, HEAD_DIM)
    s = jnp.einsum('bskgd,btkd->bkgst', qg, k).astype(jnp.float32) * (HEAD_DIM ** -0.5)
    p = jax.nn.softmax(s, axis=-1).astype(v.dtype)
    o = jnp.einsum('bkgst,btkd->bskgd', p, v)
    return o.reshape(B, S, ATTN_WIDTH)


def _latent_attention(q, k_lat, v_lat, k_ctx, v_ctx):
    B, N = q.shape[0], q.shape[1]
    k_all = jnp.concatenate([k_ctx.astype(k_lat.dtype), k_lat], axis=1)
    v_all = jnp.concatenate([v_ctx.astype(v_lat.dtype), v_lat], axis=1)
    n_blk = N // Q_BLOCK
    qb = q.reshape(B, n_blk, Q_BLOCK, N_KV_HEADS, GROUP, HEAD_DIM).transpose(1, 0, 2, 3, 4, 5)

    def block(qblk):
        s = jnp.einsum('bqkgd,btkd->bkgqt', qblk, k_all).astype(jnp.float32) * (HEAD_DIM ** -0.5)
        p = jax.nn.softmax(s, axis=-1).astype(v_all.dtype)
        return jnp.einsum('bkgqt,btkd->bqkgd', p, v_all)

    o = lax.map(block, qb)
    return o.transpose(1, 0, 2, 3, 4, 5).reshape(B, N, ATTN_WIDTH)


def _multiscale_pool(p):
    B, N, _ = p.shape
    cs = jnp.cumsum(p.astype(jnp.float32), axis=1)
    cs = jnp.concatenate([jnp.zeros((B, 1, POOL_WIDTH), jnp.float32), cs], axis=1)
    t = jnp.arange(N, dtype=jnp.int32)
    outs = []
    for g, w in enumerate(POOL_WINDOWS):
        lo = jnp.clip(t - w // 2, 0, N)
        hi = jnp.clip(t - w // 2 + w, 0, N)
        seg = cs[:, :, g * POOL_GROUP_DIM:(g + 1) * POOL_GROUP_DIM]
        cnt = (hi - lo).astype(jnp.float32)[None, :, None]
        outs.append((seg[:, hi] - seg[:, lo]) / cnt)
    pooled = jnp.concatenate(outs, axis=-1).astype(p.dtype)
    return pooled - p


def _mixer(h, lw, rope, kv_ctx):
    B, N, _ = h.shape
    proj = h @ lw['w_in']
    q, k, v, pin, ga, gp = jnp.split(proj, SPLITS, axis=-1)
    q = _rms_norm(q.reshape(B, N, N_HEADS, HEAD_DIM), lw['q_norm_g'])
    k = _rms_norm(k.reshape(B, N, N_KV_HEADS, HEAD_DIM), lw['k_norm_g'])
    v = v.reshape(B, N, N_KV_HEADS, HEAD_DIM)
    if rope is None:
        attn = _ctx_attention(q, k, v)
    else:
        cos, sin = rope
        attn = _latent_attention(_apply_rope(q, cos, sin), _apply_rope(k, cos, sin), v, kv_ctx[0], kv_ctx[1])
    pooled = _multiscale_pool(pin).reshape(B, N, N_POOL_GROUPS, POOL_GROUP_DIM)
    pooled = jnp.einsum('bngc,gcd->bngd', pooled, lw['pool_w']).reshape(B, N, POOL_WIDTH) * lw['pool_scale']
    merged = jax.nn.sigmoid(ga) * (attn @ lw['w_up_attn']) + jax.nn.sigmoid(gp) * (pooled @ lw['w_up_pool'])
    return merged @ lw['w_out'], k, v


def _layer(x, mod, lw, rope, kv_ctx):
    sh1, sc1, g1, sh2, sc2, g2, sh3, sc3, g3 = [mod[:, i][:, None, :] for i in range(N_MOD)]
    f1 = _swiglu(_modulate(x, sh1, sc1), lw['ffn1_w1'], lw['ffn1_w2'])
    x = _layer_norm(ALPHA * x + 0.5 * g1 * f1, lw['ln_g'][0], lw['ln_b'][0])
    mix, k, v = _mixer(_modulate(x, sh2, sc2), lw, rope, kv_ctx)
    x = _layer_norm(ALPHA * x + g2 * mix, lw['ln_g'][1], lw['ln_b'][1])
    f2 = _swiglu(_modulate(x, sh3, sc3), lw['ffn2_w1'], lw['ffn2_w2'])
    x = _layer_norm(ALPHA * x + 0.5 * g3 * f2, lw['ln_g'][2], lw['ln_b'][2])
    return x, k, v


def setup_inputs(seed: int = 0) -> dict:
    key = jax.random.key(seed)
    ks = jax.random.split(key, 24)
    f32 = jnp.float32

    def nrm(k, shape, scale):
        return jax.random.normal(k, shape, f32) * scale

    return {
        'x_prompt': nrm(ks[0], (BATCH, SEQ, D_MODEL), 1.0),
        'x_sample': nrm(ks[1], (DEC_BATCH, DEC_SEQ, D_MODEL), 1.0),
        'cache_k': nrm(ks[2], (DEC_BATCH, DEPTH, PAST_LEN, N_KV_HEADS, HEAD_DIM), 1.0),
        'cache_v': nrm(ks[3], (DEC_BATCH, DEPTH, PAST_LEN, N_KV_HEADS, HEAD_DIM), 1.0),
        'c': nrm(ks[4], (DEC_BATCH, D_MODEL), 1.0),
        'c_ctx': nrm(ks[5], (D_MODEL,), 1.0),
        'w_mod': nrm(ks[6], (DEPTH, D_MODEL, N_MOD * D_MODEL), 0.1 * D_MODEL ** -0.5),
        'b_mod': nrm(ks[7], (DEPTH, N_MOD * D_MODEL), 0.01),
        'ln_g': 1.0 + nrm(ks[8], (DEPTH, 3, D_MODEL), 0.02),
        'ln_b': nrm(ks[9], (DEPTH, 3, D_MODEL), 0.02),
        'ffn1_w1': nrm(ks[10], (DEPTH, D_MODEL, 2 * D_FF), D_MODEL ** -0.5),
        'ffn1_w2': nrm(ks[11], (DEPTH, D_FF, D_MODEL), BETA * D_FF ** -0.5),
        'w_in': nrm(ks[12], (DEPTH, D_MODEL, IN_WIDTH), D_MODEL ** -0.5),
        'q_norm_g': 1.0 + nrm(ks[13], (DEPTH, HEAD_DIM), 0.02),
        'k_norm_g': 1.0 + nrm(ks[14], (DEPTH, HEAD_DIM), 0.02),
        'pool_w': nrm(ks[15], (DEPTH, N_POOL_GROUPS, POOL_GROUP_DIM, POOL_GROUP_DIM), POOL_GROUP_DIM ** -0.5),
        'pool_scale': 1.0 + nrm(ks[16], (DEPTH, POOL_WIDTH), 0.1),
        'w_up_attn': nrm(ks[17], (DEPTH, ATTN_WIDTH, D_MODEL), ATTN_WIDTH ** -0.5),
        'w_up_pool': nrm(ks[18], (DEPTH, POOL_WIDTH, D_MODEL), POOL_WIDTH ** -0.5),
        'w_out': nrm(ks[19], (DEPTH, D_MODEL, D_MODEL), BETA * D_MODEL ** -0.5),
        'ffn2_w1': nrm(ks[20], (DEPTH, D_MODEL, 2 * D_FF), D_MODEL ** -0.5),
        'ffn2_w2': nrm(ks[21], (DEPTH, D_FF, D_MODEL), BETA * D_FF ** -0.5),
    }


def reference(x_prompt, x_sample, cache_k, cache_v, c, c_ctx, w_mod, b_mod, ln_g, ln_b,
              ffn1_w1, ffn1_w2, w_in, q_norm_g, k_norm_g, pool_w, pool_scale,
              w_up_attn, w_up_pool, w_out, ffn2_w1, ffn2_w2):
    n_lat = x_sample.shape[1]
    rows = n_lat // GRID_W
    rope = _rope_tables(rows * GRID_W)
    yp, ys = x_prompt, x_sample
    new_k, new_v = [], []
    for l in range(DEPTH):
        lw = {'ln_g': ln_g[l], 'ln_b': ln_b[l], 'ffn1_w1': ffn1_w1[l], 'ffn1_w2': ffn1_w2[l],
              'w_in': w_in[l], 'q_norm_g': q_norm_g[l], 'k_norm_g': k_norm_g[l],
              'pool_w': pool_w[l], 'pool_scale': pool_scale[l], 'w_up_attn': w_up_attn[l],
              'w_up_pool': w_up_pool[l], 'w_out': w_out[l], 'ffn2_w1': ffn2_w1[l], 'ffn2_w2': ffn2_w2[l]}
        mod_ctx = _modulation(c_ctx[None, :], w_mod[l], b_mod[l])
        mod_lat = _modulation(c, w_mod[l], b_mod[l])
        yp, k_l, v_l = _layer(yp, mod_ctx, lw, None, None)
        ys, _, _ = _layer(ys, mod_lat, lw, rope, (cache_k[:, l], cache_v[:, l]))
        new_k.append(k_l)
        new_v.append(v_l)
    new_cache_k = jnp.stack(new_k, axis=1)
    new_cache_v = jnp.stack(new_v, axis=1)
    return (yp, ys, new_cache_k, new_cache_v)
```

```python
import contextlib
import os
import numpy as np
import concourse.bass as bass
import concourse.mybir as mybir
from concourse.bass_utils import run_bass_kernel_spmd

F32 = mybir.dt.float32
BF16 = mybir.dt.bfloat16
AF = mybir.ActivationFunctionType
ALU = mybir.AluOpType

D = 1024
DFF = 2816
NK = 8
T = 2048
TBS = 512
NTB = 4
ALPHA = 2.0 ** 0.25
LN_EPS = 1e-6
RMS_EPS = 1e-6
SCALE = 128.0 ** -0.5
WINS = (2, 4, 8, 16)
NCORES = 8
AGW = 4096
ARENA_WORDS = 53200
SLOT_ELEMS = 9216


class Prod:
    def __init__(self, sem, step):
        self.sem = sem
        self.step = step
        self.count = 0


class Buf:
    __slots__ = ("w", "r", "name")

    def __init__(self, name=""):
        self.w = None
        self.r = {}
        self.name = name


class Eng:
    def __init__(self, eng, prod):
        self.eng = eng
        self.prod = prod
        self.seen = {}
        self.self_sync = False

    def wait(self, t):
        prod, val = t
        if prod is self.prod and not self.self_sync:
            return
        if self.seen.get(prod, 0) >= val:
            return
        self.eng.wait_ge(prod.sem, val)
        self.seen[prod] = val


class Region:
    def __init__(self, start, end):
        self.start, self.end, self.top = start, end, start

    def reset(self):
        self.top = self.start


class K:
    def __init__(self, nc, st):
        self.nc = nc
        self.st = st
        self.E = {}
        for name, eng in (("pe", nc.tensor), ("act", nc.scalar), ("dve", nc.vector),
                          ("pool", nc.gpsimd), ("sp", nc.sync)):
            self.E[name] = Eng(eng, self.new_prod("e_" + name, 1))
            self.E[name].self_sync = name in ("act", "dve", "pool")
        self.dprods = []
        self.arena_t = st.enter_context(nc.sbuf_tensor("arena", [128, ARENA_WORDS], F32))
        self.arena = self.arena_t[:, :]
        self.ps_t = st.enter_context(nc.psum_tensor("ps", [128, 8, 512], F32))
        self.bank = [Buf("bank%d" % i) for i in range(8)]
        self.jobs = []
        self.job_next = 0
        self.job_cur = 0
        self.job_slot = {}

    def new_prod(self, name, step):
        sem = self.st.enter_context(self.nc.semaphore(name))
        return Prod(sem, step)

    def dprod(self, name):
        p = self.new_prod("d_" + name, 16)
        self.dprods.append(p)
        return p

    def _deps(self, E, reads, writes):
        for b in reads:
            if b.w is not None:
                E.wait(b.w)
        for b in writes:
            if b.w is not None:
                E.wait(b.w)
            for p, v in b.r.items():
                E.wait((p, v))

    def _commit(self, t, reads, writes):
        p, v = t
        for b in reads:
            if b.r.get(p, 0) < v:
                b.r[p] = v
        for b in writes:
            b.w = t
            b.r = {}

    def op(self, en, fn, reads=(), writes=(), pre=()):
        E = self.E[en]
        self._deps(E, reads, writes)
        for f in pre:
            f()
        ins = fn()
        E.prod.count += 1
        ins.then_inc(E.prod.sem, 1)
        t = (E.prod, E.prod.count)
        self._commit(t, reads, writes)
        return t

    def dma(self, qn, out, in_, prod, reads=(), writes=(), nodep=False):
        E = self.E[qn]
        if not nodep:
            self._deps(E, reads, writes)
        ins = E.eng.dma_start(out=out, in_=in_)
        prod.count += 16
        ins.then_inc(prod.sem, 16)
        t = (prod, prod.count)
        self._commit(t, reads, writes)
        return t

    def barrier(self):
        ticks = [(e.prod, e.prod.count) for e in self.E.values() if e.prod.count > 0]
        ticks += [(p, p.count) for p in self.dprods if p.count > 0]
        for e in self.E.values():
            for t in ticks:
                e.wait(t)

    def psb(self, i):
        return self.ps_t[:, i, :]

    def f32v(self, off_bytes, n):
        o = off_bytes // 4
        return self.arena[:, o:o + n]

    def bfv(self, off_bytes, n):
        o = off_bytes // 4
        return self.arena[:, o:o + n // 2].bitcast(BF16)

    def alloc(self, reg, nbytes):
        nbytes = (nbytes + 63) // 64 * 64
        off = reg.top
        reg.top += nbytes
        assert reg.top <= reg.end, ("region overflow", reg.start, reg.end, reg.top)
        return off

    def pe_group(self, out_ap, pairs, reads, writes):
        n = len(pairs)
        nc = self.nc
        pre = [(lambda l=l, r=r, i=i: nc.tensor.matmul(out_ap, lhsT=l, rhs=r, start=(i == 0), stop=False))
               for i, (l, r) in enumerate(pairs[:-1])]
        l, r = pairs[-1]
        return self.op("pe", lambda: nc.tensor.matmul(out_ap, lhsT=l, rhs=r, start=(n == 1), stop=True),
                       reads, writes, pre=pre)

    def pe_mm(self, out_ap, l, r, start, stop, reads, writes):
        nc = self.nc
        return self.op("pe", lambda: nc.tensor.matmul(out_ap, lhsT=l, rhs=r, start=start, stop=stop),
                       reads, writes)

    def act(self, out, in_, func, reads, writes, bias=None, scale=None):
        nc = self.nc
        kw = {}
        if bias is not None:
            kw["bias"] = bias
        if scale is not None:
            kw["scale"] = scale
        return self.op("act", lambda: nc.scalar.activation(out=out, in_=in_, func=func, **kw), reads, writes)

    def tt(self, out, in0, in1, op, reads, writes):
        nc = self.nc
        return self.op("dve", lambda: nc.vector.tensor_tensor(out=out, in0=in0, in1=in1, op=op), reads, writes)

    def ts(self, out, in0, s1, op0, reads, writes, s2=None, op1=None):
        nc = self.nc
        if op1 is None:
            return self.op("dve", lambda: nc.vector.tensor_scalar(out=out, in0=in0, scalar1=s1, scalar2=None,
                                                                  op0=op0), reads, writes)
        return self.op("dve", lambda: nc.vector.tensor_scalar(out=out, in0=in0, scalar1=s1, scalar2=s2,
                                                              op0=op0, op1=op1), reads, writes)

    def stt(self, out, in0, scalar, in1, op0, op1, reads, writes):
        nc = self.nc
        return self.op("dve", lambda: nc.vector.scalar_tensor_tensor(out=out, in0=in0, scalar=scalar, in1=in1,
                                                                     op0=op0, op1=op1), reads, writes)

    def recip(self, out, in_, reads, writes):
        nc = self.nc
        return self.op("dve", lambda: nc.vector.reciprocal(out=out, in_=in_), reads, writes)

    def vcopy(self, out, in_, reads, writes):
        nc = self.nc
        return self.op("dve", lambda: nc.vector.tensor_copy(out=out, in_=in_), reads, writes)

    def memset(self, out, val, writes):
        nc = self.nc
        return self.op("dve", lambda: nc.vector.memset(out, val), (), writes)

    def add_job(self, fn):
        self.jobs.append(fn)

    def prefetch(self, si):
        if self.job_next < len(self.jobs):
            self.jobs[self.job_next](si)
            self.job_slot[self.job_next] = si
            self.job_next += 1

    def take(self):
        assert self.job_cur < self.job_next, "job not prefetched"
        si = self.job_slot[self.job_cur]
        self.job_cur += 1
        return si


class _Stop(Exception):
    pass


def build_program(stop=None):
    nc = bass.Bass("TRN2", target_bir_lowering=False)
    dr = {}

    def din(name, shape, dt=F32):
        dr[name] = nc.dram_tensor(name, list(shape), dt, kind="ExternalInput").ap()
        return dr[name]

    def dout(name, shape, dt=F32):
        dr[name] = nc.dram_tensor(name, list(shape), dt, kind="ExternalOutput").ap()
        return dr[name]

    xT = din("xT", [128, NK, T])
    condT = din("condT", [128, 16])
    w_mod = din("w_mod", [D, 9 * D])
    bmodT = din("bmodT", [128, 72])
    lngT = din("lngT", [128, 24])
    lnbT = din("lnbT", [128, 24])
    ffn1_w1 = din("ffn1_w1", [128, 2, 8 * DFF])
    ffn1_w2 = din("ffn1_w2", [DFF, D])
    ffn2_w1 = din("ffn2_w1", [128, 2, 8 * DFF])
    ffn2_w2 = din("ffn2_w2", [DFF, D])
    w_in = din("w_in", [D, 4096])
    qkg = din("qkg", [128, 2])
    pool_w = din("pool_w", [128, 512])
    pscaleT = din("pscaleT", [128, 4])
    w_up_attn = din("w_up_attn", [D, D])
    w_up_pool = din("w_up_pool", [512, D])
    w_out = din("w_out", [D, D])
    kctx = din("kctx", [128, 512])
    vctx = din("vctx", [128, 512])
    ropeCS = din("ropeCS", [128, 2048])
    pmat = din("pmat", [128, 128])
    masks = din("masks", [128, 8])
    corr = din("corr", [128, 128])
    esel = din("esel", [128, 128])
    yT = dout("yT", [128, NK, T])
    nk = dout("nk", [128, 2, 1024])
    nv = dout("nv", [128, 8, 256])
    ag_in = nc.dram_tensor("ag_in", [128, AGW], BF16)
    ag_out = nc.dram_tensor("ag_out", [512, AGW], BF16)
    ag_in_ap = ag_in.ap()
    ag_out_ap = ag_out.ap()
    dr["ag2_in"] = nc.dram_tensor("ag2_in", [128, 128], BF16).ap()
    dr["ag2_out"] = nc.dram_tensor("ag2_out", [512, 128], BF16).ap()

    with contextlib.ExitStack() as st:
        k = K(nc, st)
        k.stop = stop
        try:
            _emit(k, nc, dr, ag_in, ag_out, ag_in_ap, ag_out_ap)
        except _Stop:
            pass
    return nc


def _emit(k, nc, dr, ag_in, ag_out, ag_in_ap, ag_out_ap):
    ps = k.psb
    bank = k.bank
    SKIP = set(os.environ.get("KSKIP", "").split(","))
    G = Region(0, ARENA_WORDS * 4)
    oR = k.alloc(G, NK * T * 4)
    oSlot = [k.alloc(G, SLOT_ELEMS * 2), k.alloc(G, SLOT_ELEMS * 2)]
    oOnesLn = k.alloc(G, 256)
    oOnesH = k.alloc(G, 256)
    oOnes1 = k.alloc(G, 256)
    oPm = k.alloc(G, 256)
    oModp = k.alloc(G, 144 * 4)
    oBmod = k.alloc(G, 72 * 4)
    oLng = k.alloc(G, 24 * 4)
    oLnb = k.alloc(G, 24 * 4)
    oCond = k.alloc(G, 16 * 4)
    oScT = k.alloc(G, 16 * 2)
    oQkg = k.alloc(G, 2 * 4)
    oPsc = k.alloc(G, 4 * 4)
    oMask = k.alloc(G, 8 * 4)
    oCorr = k.alloc(G, 128 * 4)
    oPoolW = k.alloc(G, 512 * 2)
    oDer = k.alloc(G, 32 * 8 * 4)
    gend = G.top
    RH = Region(gend, gend + 32768)
    RQ = Region(RH.end, RH.end + 32768)
    RP2 = Region(RQ.end, RQ.end + 16384)
    RX = Region(RP2.end, ARENA_WORDS * 4)
    RHQ = Region(RH.start, RQ.end)
    RQX = Region(RQ.start, RX.end)
    assert RX.end - RX.start >= 20000, (RX.start, RX.end)

    R3 = k.f32v(oR, NK * T).rearrange("p (k t) -> p k t", k=NK)
    Rbk = [[Buf("R%d_%d" % (i, j)) for j in range(NK)] for i in range(NTB)]
    Rprod = [k.dprod("R%d" % i) for i in range(NTB)]
    slot = [k.bfv(o, SLOT_ELEMS) for o in oSlot]
    slotb = [Buf("slot0"), Buf("slot1")]
    slotb2 = [Buf("slot0w2"), Buf("slot1w2")]
    k.stage = None
    slotp = [k.dprod("slot0"), k.dprod("slot1")]
    slotp2 = [k.dprod("slot0w2"), k.dprod("slot1w2")]
    ones_ln = k.bfv(oOnesLn, 128)
    ones_h = k.bfv(oOnesH, 128)
    ones1 = k.bfv(oOnes1, 128)
    Pm = k.bfv(oPm, 128)
    modp = k.f32v(oModp, 144)
    modp3 = modp.rearrange("p (j g) -> p j g", g=2)
    modp4 = modp.rearrange("p (i k g) -> p i k g", i=9, k=8)
    bmod = k.f32v(oBmod, 72)
    lng = k.f32v(oLng, 24)
    lnb = k.f32v(oLnb, 24)
    cond = k.f32v(oCond, 16)
    scT = k.bfv(oScT, 16)
    qkg = k.f32v(oQkg, 2)
    psc = k.f32v(oPsc, 4)
    mask = k.f32v(oMask, 8)
    corr = k.f32v(oCorr, 128)
    poolw = k.bfv(oPoolW, 512)
    der = k.f32v(oDer, 256)
    constb = Buf("const")
    cprod = k.dprod("const")
    cprod2 = k.dprod("const2")
    derb = Buf("der")

    dcol = {}

    def DER(name, g=None):
        key = (name, g)
        if key not in dcol:
            dcol[key] = len(dcol)
            assert len(dcol) <= 32
        c = dcol[key] * 8
        return der[:, c:c + 8]

    def DERk(name, g, kk):
        c = dcol[(name, g)] * 8 + kk
        return der[:, c:c + 1]

    def finalize():
        sp = k.E["sp"]
        for p in k.dprods:
            if p.count > 0:
                sp.wait((p, p.count))
        for e in k.E.values():
            if e.prod.count > 0:
                sp.wait((e.prod, e.prod.count))

    dbgp = k.dprod("dbg")

    def cut(stage):
        if k.stop is not None and k.stop == stage:
            k.barrier()
            for tb in range(NTB):
                k.dma("sp", dr["yT"][:, :, tb * TBS:(tb + 1) * TBS], R3[:, :, tb * TBS:(tb + 1) * TBS], dbgp, Rbk[tb], [])
            finalize()
            raise _Stop()

    def job_wmod(i):
        def f(si):
            k.dma("pool", slot[si][:, 0:8192].rearrange("p (k c) -> p k c", k=8),
                  dr["w_mod"][:, i * 1024:(i + 1) * 1024].rearrange("(k p) c -> p k c", p=128),
                  slotp[si], (), [slotb[si], slotb2[si]])
        return f

    PIECES = [(0, 2), (2, 2)] + [(c0, 3) for c0 in range(4, 22, 3)]

    def job_ffn(w1, w2, c0, n):
        def f(si):
            s = slot[si]
            o0 = 8 * 128 * c0
            k.dma("pool", s[:, 0:3072].rearrange("p (k c) -> p k c", k=8)[:, :, 0:n * 128],
                  w1[:, 0, o0:o0 + 8 * n * 128].rearrange("p (k c) -> p k c", k=8),
                  slotp[si], (), [slotb[si]])
            k.dma("pool", s[:, 3072:6144].rearrange("p (k c) -> p k c", k=8)[:, :, 0:n * 128],
                  w1[:, 1, o0:o0 + 8 * n * 128].rearrange("p (k c) -> p k c", k=8),
                  slotp[si], (), [slotb[si]], nodep=True)
            if k.stage is None:
                k.dma("pool", s[:, 6144:6144 + n * 1024].rearrange("p (j c) -> p j c", j=n),
                      w2[c0 * 128:(c0 + n) * 128, :].rearrange("(j p) c -> p j c", p=128),
                      slotp2[si], (), [slotb2[si]])
            else:
                stg, stgb, stgp = k.stage[si]
                k.dma("sp", stg[:, 0:n * 1024].rearrange("p (j c) -> p j c", j=n),
                      w2[c0 * 128:(c0 + n) * 128, :].rearrange("(j p) c -> p j c", p=128),
                      stgp, (), [stgb])
                for j in range(n):
                    k.act(s[:, 6144 + j * 1024:6144 + (j + 1) * 1024], stg[:, j * 1024:(j + 1) * 1024], AF.Copy,
                          [stgb], [slotb2[si]])
        return f

    def job_win(col0):
        def f(si):
            k.dma("pool", slot[si][:, 0:8192].rearrange("p (k c) -> p k c", k=8),
                  dr["w_in"][:, col0:col0 + 1024].rearrange("(k p) c -> p k c", p=128),
                  slotp[si], (), [slotb[si], slotb2[si]])
        return f

    def job_g1(ocp):
        def f(si):
            s = slot[si]
            c0 = ocp * 256
            k.dma("pool", s[:, 0:2048].rearrange("p (k c) -> p k c", k=8),
                  dr["w_in"][:, 2048 + c0:2048 + c0 + 256].rearrange("(k p) c -> p k c", p=128),
                  slotp[si], (), [slotb[si], slotb2[si]])
            k.dma("pool", s[:, 2048:4096].rearrange("p (k c) -> p k c", k=8),
                  dr["w_in"][:, 3072 + c0:3072 + c0 + 256].rearrange("(k p) c -> p k c", p=128),
                  slotp[si], (), [slotb[si]], nodep=True)
            k.dma("pool", s[:, 4096:6144].rearrange("p (k c) -> p k c", k=8),
                  dr["w_up_attn"][:, c0:c0 + 256].rearrange("(k p) c -> p k c", p=128),
                  slotp[si], (), [slotb[si]], nodep=True)
            k.dma("pool", s[:, 6144:7168].rearrange("p (k c) -> p k c", k=4),
                  dr["w_up_pool"][:, c0:c0 + 256].rearrange("(k p) c -> p k c", p=128),
                  slotp[si], (), [slotb[si]], nodep=True)
        return f

    def job_wout():
        def f(si):
            k.dma("pool", slot[si][:, 0:8192].rearrange("p (k c) -> p k c", k=8),
                  dr["w_out"][:, :].rearrange("(k p) c -> p k c", p=128),
                  slotp[si], (), [slotb[si], slotb2[si]])
        return f

    for i in range(2):
        k.add_job(job_wmod(i))
    for pi, (c0, n) in enumerate(PIECES):
        k.add_job(job_ffn(dr["ffn1_w1"], dr["ffn1_w2"], c0, n))
        if pi == 0:
            k.add_job(job_wmod(2))
        if pi == 3:
            k.add_job(job_wmod(3))
        if pi == 5:
            k.add_job(job_wmod(4))
    k.add_job(job_win(1024))
    k.add_job(job_win(0))
    for i in range(5, 9):
        k.add_job(job_wmod(i))
    for ocp in range(4):
        k.add_job(job_g1(ocp))
    k.add_job(job_wout())
    for (c0, n) in PIECES:
        k.add_job(job_ffn(dr["ffn2_w1"], dr["ffn2_w2"], c0, n))

    for (dst, src) in ((cond, "condT"), (bmod, "bmodT"), (lng, "lngT"), (lnb, "lnbT"), (qkg, "qkg"),
                       (psc, "pscaleT"), (mask, "masks"), (corr, "corr")):
        k.dma("sp", dst, dr[src][:, :], cprod, (), [constb])
    k.dma("pool", Pm, dr["pmat"][:, :], cprod2, (), [constb])
    k.dma("pool", poolw, dr["pool_w"][:, :], cprod2, (), [constb])
    k.prefetch(0)
    k.prefetch(1)
    for tb in range(NTB):
        k.dma("sp", R3[:, :, tb * TBS:(tb + 1) * TBS], dr["xT"][:, :, tb * TBS:(tb + 1) * TBS],
              Rprod[tb], (), Rbk[tb])
    onesb = Buf("ones")
    k.memset(ones_ln, 1.0 / 1024.0, [onesb])
    k.memset(ones_h, 1.0 / 128.0, [onesb])
    k.memset(ones1, 1.0, [onesb])
    scb = Buf("scT")
    k.act(scT, cond, AF.Silu, [constb], [scb])
    def mod_compute(i, bk):
        si = k.take()
        for j in range(8):
            col = (i * 8 + j) * 2
            for kk in range(8):
                nc_l = slot[si][:, kk * 1024 + j * 128: kk * 1024 + (j + 1) * 128]
                if kk == 0 and j == 0:
                    k._deps(k.E["pe"], [slotb[si], scb], [bank[bk]])
                ins = nc.tensor.matmul(ps(bk)[:, col:col + 2], lhsT=nc_l, rhs=scT[:, kk * 2:(kk + 1) * 2],
                                       start=(kk == 0), stop=(kk == 7))
        E = k.E["pe"]
        E.prod.count += 1
        ins.then_inc(E.prod.sem, 1)
        tk = (E.prod, E.prod.count)
        k._commit(tk, [slotb[si], scb], [bank[bk]])
        k.prefetch(si)

    def mod_finish(i0, i1, bk):
        psm3 = ps(bk)[:, 0:144].rearrange("p (j g) -> p j g", g=2)
        for g in range(2):
            k.tt(modp3[:, i0 * 8:i1 * 8, g], psm3[:, i0 * 8:i1 * 8, g], bmod[:, i0 * 8:i1 * 8], ALU.add,
                 [bank[bk], constb], [derb])

    for i in range(2):
        mod_compute(i, 0)
    mod_finish(0, 2, 0)

    def M(i, g):
        return modp4[:, i, :, g]

    for g in range(2):
        k.ts(DER("A1", g), M(1, g), 1.0, ALU.add, [], [derb])
        k.vcopy(DER("SH1", g), M(0, g), [], [derb])
        DER("HG1", g)
    k.ts(DER("RS1"), lng[:, 0:8], ALPHA, ALU.mult, [constb], [derb])
    k.ts(DER("RB1"), lnb[:, 0:8], ALPHA, ALU.mult, [constb], [derb])
    k.ts(DER("RS2"), lng[:, 8:16], ALPHA, ALU.mult, [constb], [derb])
    k.ts(DER("RB2"), lnb[:, 8:16], ALPHA, ALU.mult, [constb], [derb])
    k.vcopy(DER("RS3"), lng[:, 16:24], [constb], [derb])
    k.vcopy(DER("RB3"), lnb[:, 16:24], [constb], [derb])
    for g in range(2):
        for nm in ("H2S", "H2B", "G2", "T3", "H3S", "H3B", "HG3"):
            DER(nm, g)

    def derive_mix():
      for g in range(2):
        k.ts(DER("H2S", g), M(4, g), 1.0, ALU.add, [], [derb], s2=1.0 / ALPHA, op1=ALU.mult)
        k.vcopy(DER("H2B", g), M(3, g), [], [derb])

    def derive_late():
      for g in range(2):
        k.vcopy(DER("G2", g), M(5, g), [], [derb])
        k.ts(DER("T3", g), M(7, g), 1.0, ALU.add, [], [derb])
        k.tt(DER("H3S", g), DER("T3", g), lng[:, 8:16], ALU.mult, [constb], [derb])
        k.tt(DER("H3B", g), DER("T3", g), lnb[:, 8:16], ALU.mult, [constb], [derb])
        k.tt(DER("H3B", g), DER("H3B", g), M(6, g), ALU.add, [], [derb])
        k.ts(DER("HG3", g), M(8, g), 0.5, ALU.mult, [], [derb])

    oH = k.alloc(RH, NK * T * 2)
    H3 = k.bfv(oH, NK * T).rearrange("p (k t) -> p k t", k=NK)
    Hb = [Buf("H%d" % i) for i in range(NTB)]
    yprod = [k.dprod("y%d" % i) for i in range(NTB)]

    def ln_alloc(reg):
        o = {}
        o["yb"] = [k.bfv(k.alloc(reg, 1024), 512) for _ in range(NK)]
        o["ysq"] = [k.bfv(k.alloc(reg, 1024), 512) for _ in range(NK)]
        o["ybb"] = [Buf() for _ in range(NK)]
        o["ysqb"] = [Buf() for _ in range(NK)]
        o["mean"] = [k.f32v(k.alloc(reg, 2048), 512) for _ in range(2)]
        o["tmp"] = k.f32v(k.alloc(reg, 2048), 512)
        o["rstd"] = [k.f32v(k.alloc(reg, 2048), 512) for _ in range(2)]
        o["meanb"], o["tmpb"], o["rstdb"] = [Buf(), Buf()], Buf(), [Buf(), Buf()]
        return o

    def ln_stats_chunk(tb, L, kk):
        tsl = slice(tb * TBS, (tb + 1) * TBS)
        k.act(L["yb"][kk], R3[:, kk, tsl], AF.Copy, [Rbk[tb][kk]], [L["ybb"][kk]])
        k.act(L["ysq"][kk], R3[:, kk, tsl], AF.Square, [Rbk[tb][kk]], [L["ysqb"][kk]])

    def ln_pe_stats(tb, L, b6=6, b7=7):
        for kk in range(NK):
            k.pe_mm(ps(b6), ones_ln, L["yb"][kk], kk == 0, kk == NK - 1, [L["ybb"][kk], onesb], [bank[b6]])
        for kk in range(NK):
            k.pe_mm(ps(b7), ones_ln, L["ysq"][kk], kk == 0, kk == NK - 1, [L["ysqb"][kk], onesb], [bank[b7]])

    def ln_fin_a(tb, L, b6=6, b7=7):
        par = tb % 2
        k.vcopy(L["mean"][par], ps(b6), [bank[b6]], [L["meanb"][par]])
        k.tt(L["tmp"], L["mean"][par], L["mean"][par], ALU.mult, [L["meanb"][par]], [L["tmpb"]])
        k.tt(L["tmp"], ps(b7), L["tmp"], ALU.subtract, [bank[b7]], [L["tmpb"]])
        k.act(L["rstd"][par], L["tmp"], AF.Ln, [L["tmpb"]], [L["rstdb"][par]], bias=LN_EPS, scale=1.0)
        k.act(L["rstd"][par], L["rstd"][par], AF.Exp, [], [L["rstdb"][par]], scale=-0.5)

    def ln_fin_b1(tb, L):
        par = tb % 2
        tsl = slice(tb * TBS, (tb + 1) * TBS)
        Rt = R3[:, :, tsl]
        k.tt(Rt, Rt, L["mean"][par].unsqueeze(1).to_broadcast([128, NK, TBS]), ALU.subtract,
             [L["meanb"][par]], Rbk[tb])

    def ln_fin_b2(tb, L, rs, rb, hs=None, hb=None, final=False):
        par = tb % 2
        g = tb // 2
        tsl = slice(tb * TBS, (tb + 1) * TBS)
        Rt = R3[:, :, tsl]
        k.tt(Rt, Rt, L["rstd"][par].unsqueeze(1).to_broadcast([128, NK, TBS]), ALU.mult,
             [L["rstdb"][par]], Rbk[tb])
        for kk in range(NK):
            if hs is not None:
                k.ts(H3[:, kk, tsl], R3[:, kk, tsl], DERk(hs, g, kk), ALU.mult, [Rbk[tb][kk], derb], [Hb[tb]],
                     s2=DERk(hb, g, kk), op1=ALU.add)
            if hs is None and kk % 2 == 1:
                k.ts(R3[:, kk, tsl], R3[:, kk, tsl], DERk(rs, None, kk), ALU.mult, [derb], [Rbk[tb][kk]],
                     s2=DERk(rb, None, kk), op1=ALU.add)
            else:
                k.act(R3[:, kk, tsl], R3[:, kk, tsl], AF.Identity, [derb], [Rbk[tb][kk]],
                      bias=DERk(rb, None, kk), scale=DERk(rs, None, kk))
        if final:
            k.dma("sp", dr["yT"][:, :, tsl], R3[:, :, tsl], yprod[tb], Rbk[tb], [])

    def ln_finish(tb, L, rs, rb, hs=None, hb=None, final=False, b6=6, b7=7):
        ln_fin_a(tb, L, b6, b7)
        ln_fin_b1(tb, L)
        ln_fin_b2(tb, L, rs, rb, hs=hs, hb=hb, final=final)

    stgprods = [yprod[0], yprod[1]]

    def ffn_alloc():
        reg = RQX
        reg.reset()
        o = {}
        o["gbuf"] = [[k.bfv(k.alloc(reg, 1024), 512) for _ in range(3)] for _ in range(2)]
        o["gb"] = [[Buf() for _ in range(3)] for _ in range(2)]
        o["sa"] = [k.f32v(k.alloc(reg, 2048), 512) for _ in range(2)]
        o["sab"] = [Buf(), Buf()]
        o["Ln"] = ln_alloc(reg)
        o["stage"] = [(k.f32v(k.alloc(reg, 12288), 3072), Buf("stg%d" % i), stgprods[i]) for i in range(2)]
        return o

    def ffn_phase(hgname, L, ln_args, between=None, pre_stt=None):
        fb = ffn_alloc() if L is None else L
        gbuf, gb, sa, sab, Ln = fb["gbuf"], fb["gb"], fb["sa"], fb["sab"], fb["Ln"]
        k.stage = fb["stage"]
        cnt = 0
        cntf = 0
        gi = 0
        for pi, (c0, n) in enumerate(PIECES):
            si = k.take()
            s = slot[si]
            last = (pi == len(PIECES) - 1)
            for tb in range(NTB):
                g = tb // 2
                tsl = slice(tb * TBS, (tb + 1) * TBS)
                gs = gi % 2
                gi += 1
                for j in range(n):
                    a, b2 = cnt % 2, 2 + cnt % 2
                    cnt += 1
                    k.pe_group(ps(a), [(s[:, kk * 384 + j * 128: kk * 384 + (j + 1) * 128], H3[:, kk, tsl])
                                       for kk in range(NK)], [slotb[si], Hb[tb]], [bank[a]])
                    k.pe_group(ps(b2), [(s[:, 3072 + kk * 384 + j * 128: 3072 + kk * 384 + (j + 1) * 128],
                                         H3[:, kk, tsl]) for kk in range(NK)], [slotb[si], Hb[tb]], [bank[b2]])
                    k.act(sa[a], ps(a), AF.Silu, [bank[a]], [sab[a]])
                    k.tt(gbuf[gs][j], sa[a], ps(b2), ALU.mult, [sab[a], bank[b2]], [gb[gs][j]])
                if last and tb >= 1:
                    ln_fin_b1(tb - 1, Ln)
                if pre_stt is not None and pi == 0 and tb == 0:
                    pre_stt()
                for oc in range(NK):
                    f = 4 + cntf % 2
                    cntf += 1
                    k.pe_group(ps(f), [(s[:, 6144 + j * 1024 + oc * 128: 6144 + j * 1024 + (oc + 1) * 128],
                                        gbuf[gs][j]) for j in range(n)],
                               [slotb2[si]] + [gb[gs][j] for j in range(n)], [bank[f]])
                    k.stt(R3[:, oc, tsl], ps(f), DERk(hgname, g, oc), R3[:, oc, tsl], ALU.mult, ALU.add,
                          [bank[f], derb], [Rbk[tb][oc]])
                    if last:
                        ln_stats_chunk(tb, Ln, oc)
                if last:
                    if tb >= 1:
                        ln_fin_b2(tb - 1, Ln, *ln_args[0], **ln_args[1])
                    ln_pe_stats(tb, Ln)
                    ln_fin_a(tb, Ln)
                    if tb == NTB - 1:
                        ln_fin_b1(tb, Ln)
                        ln_fin_b2(tb, Ln, *ln_args[0], **ln_args[1])
            if pi + 2 >= len(PIECES):
                k.stage = None
            k.prefetch(si)
            if between is not None:
                between(pi)

    for tb in range(NTB):
        g = tb // 2
        tsl = slice(tb * TBS, (tb + 1) * TBS)
        for kk in range(NK):
            k.act(H3[:, kk, tsl], R3[:, kk, tsl], AF.Identity, [Rbk[tb][kk], derb], [Hb[tb]],
                  bias=DERk("SH1", g, kk), scale=DERk("A1", g, kk))
        k.act(R3[:, :, tsl], R3[:, :, tsl], AF.Identity, [], Rbk[tb], scale=ALPHA)

    cut(0)
    def ffn1_between(pi):
        if pi == 3:
            mod_compute(3, 6)
        if pi == 5:
            mod_compute(4, 6)
            mod_finish(3, 5, 6)
            derive_mix()

    def ffn1_pre_stt():
        mod_compute(2, 6)
        mod_finish(2, 3, 6)
        for g in range(2):
            k.ts(DER("HG1", g), M(2, g), 0.5, ALU.mult, [], [derb])

    ffn_phase("HG1", None, (("RS1", "RB1"), {}), between=ffn1_between, pre_stt=ffn1_pre_stt)
    k.barrier()
    cut(1)

    RH.reset(); RQ.reset(); RP2.reset(); RX.reset(); RHQ.reset()
    pooled2 = k.bfv(k.alloc(RP2, 4 * T * 2), 4 * T).rearrange("p (g t) -> p g t", g=4)
    p2b = [Buf("p2_%d" % i) for i in range(NTB)]
    oKp = k.alloc(RX, 2 * 1024 * 2)
    oVp = k.alloc(RX, 8 * 256 * 2)
    Kp = k.bfv(oKp, 2048).rearrange("p (h t) -> p h t", h=2)
    Vp = k.bfv(oVp, 2048).rearrange("p (j c) -> p j c", j=8)
    Kpb = [Buf(), Buf()]
    Vpb = [Buf(), Buf()]
    xmark = RX.top

    A = RHQ
    h2t = k.bfv(k.alloc(A, 8192), 4096).rearrange("p (k t) -> p k t", k=NK)
    h2b = [Buf("h2t%d" % i) for i in range(NK)]
    sq = [k.bfv(k.alloc(A, 1024), 512) for _ in range(2)]
    sqb = [Buf(), Buf()]
    sd = [k.f32v(k.alloc(A, 2048), 512) for _ in range(2)]
    sdb = [Buf(), Buf()]
    kn = [k.f32v(k.alloc(A, 2048), 512) for _ in range(2)]
    knb_ = [Buf(), Buf()]
    knbf = k.bfv(k.alloc(A, 1024), 512)
    knbfb = Buf()
    t1 = k.f32v(k.alloc(A, 2048), 512)
    t2 = k.f32v(k.alloc(A, 2048), 512)
    t1b, t2b = Buf(), Buf()
    kst = [k.bfv(k.alloc(A, 1024), 512) for _ in range(2)]
    kstb = [Buf(), Buf()]
    kstp = [k.dprod("kst0"), k.dprod("kst1")]
    knp = [k.dprod("kn0"), k.dprod("kn1")]
    vst = [k.f32v(k.alloc(A, 1024), 256) for _ in range(2)]
    vstb = [Buf(), Buf()]
    vstp = [k.dprod("vst0"), k.dprod("vst1")]
    vsb = [k.bfv(k.alloc(A, 512), 256) for _ in range(2)]
    vsbb = [Buf(), Buf()]
    vsbp = [k.dprod("vsb0"), k.dprod("vsb1")]
    ropeC = k.f32v(k.alloc(RX, 4096), 1024)
    ropeS = k.f32v(k.alloc(RX, 4096), 1024)
    ropeb = Buf("rope")
    ropep = k.dprod("rope")
    pp = [k.f32v(k.alloc(A, 2176), 544).rearrange("p (b t) -> p b t", b=2) for _ in range(2)]
    ppb = [Buf(), Buf()]
    pin_s = k.f32v(k.alloc(A, 4 * 1040 * 4), 4 * 1040).rearrange("p (g t) -> p g t", g=4)
    pinb = [Buf("pin_s%d" % i) for i in range(4)]
    ta = k.f32v(k.alloc(A, 4160), 1040)
    tb_ = k.f32v(k.alloc(A, 4160), 1040)
    tab, tbb = Buf(), Buf()
    plb = [k.bfv(k.alloc(A, 1024), 512) for _ in range(2)]
    plbb = [Buf(), Buf()]
    e32 = k.f32v(k.alloc(A, 256), 64).rearrange("p (g i) -> p g i", g=4)
    est = k.bfv(k.alloc(A, 256), 128)
    estb = Buf()
    estp = k.dprod("est")
    eg = k.bfv(k.alloc(A, 1024), 512).rearrange("p (r c) -> p r c", r=4)
    egb = Buf()
    egp = k.dprod("eg")
    ef = k.f32v(k.alloc(A, 1024), 256).rearrange("p (r c) -> p r c", r=4)
    efb = Buf()
    agb = Buf("ag_in")
    agob = Buf("ag_out")
    ag2b = Buf("ag2_in")
    ag2ob = Buf("ag2_out")
    ccprod = k.new_prod("cc", 1)
    k.dprods.append(ccprod)

    k.dma("sp", ropeC, dr["ropeCS"][:, 0:1024], ropep, (), [ropeb])
    k.dma("sp", ropeS, dr["ropeCS"][:, 1024:2048], ropep, (), [ropeb])
    for i in range(2):
        k.memset(pp[i], 0.0, [ppb[i]])
    k.memset(pin_s, 0.0, pinb)

    def mod2(tb, h2t_=None, h2b_=None, on_dve=False):
        g = tb // 2
        tsl = slice(tb * TBS, (tb + 1) * TBS)
        h2t_ = h2t if h2t_ is None else h2t_
        h2b_ = h2b if h2b_ is None else h2b_
        for kk in range(NK):
            if on_dve:
                k.ts(h2t_[:, kk, :], R3[:, kk, tsl], DERk("H2S", g, kk), ALU.mult, [Rbk[tb][kk], derb],
                     [h2b_[kk]], s2=DERk("H2B", g, kk), op1=ALU.add)
                continue
            k.act(h2t_[:, kk, :], R3[:, kk, tsl], AF.Identity, [Rbk[tb][kk], derb], [h2b_[kk]],
                  bias=DERk("H2B", g, kk), scale=DERk("H2S", g, kk))

    cnts = {"p": 0, "m": 0, "r": 0, "x": 0}

    def rms_head(si, col0, gcol, tb, sample, out_bf, out_bufs, kout=None):
        s = slot[si]
        pb = cnts["p"] % 3
        cnts["p"] += 1
        mb = 3 + cnts["m"] % 2
        cnts["m"] += 1
        x = cnts["x"] % 2
        cnts["x"] += 1
        k.pe_group(ps(pb), [(s[:, kk * 1024 + col0: kk * 1024 + col0 + 128], h2t[:, kk, :]) for kk in range(NK)],
                   [slotb[si]] + h2b, [bank[pb]])
        k.act(sq[x], ps(pb), AF.Square, [bank[pb]], [sqb[x]])
        k.pe_mm(ps(mb), ones_h, sq[x], True, True, [sqb[x], onesb], [bank[mb]])
        k.act(sd[x], ps(mb), AF.Ln, [bank[mb]], [sdb[x]], bias=RMS_EPS, scale=1.0)
        k.act(sd[x], sd[x], AF.Exp, [], [sdb[x]], scale=-0.5)
        if not sample:
            if kout is None:
                k.stt(out_bf, ps(pb), qkg[:, gcol:gcol + 1], sd[x], ALU.mult, ALU.mult,
                      [bank[pb], sdb[x], constb], out_bufs)
            else:
                k.stt(kn[x], ps(pb), qkg[:, gcol:gcol + 1], sd[x], ALU.mult, ALU.mult,
                      [bank[pb], sdb[x], constb], [knb_[x]])
                if "kvout" not in SKIP:
                    k.dma("sp", kout, kn[x], knp[x], [knb_[x]], [])
                k.act(out_bf, kn[x], AF.Copy, [knb_[x]], out_bufs)
        else:
            rbk = 5 + cnts["r"] % 2
            cnts["r"] += 1
            tq = slice((tb - 2) * TBS, (tb - 1) * TBS)
            k.stt(kn[x], ps(pb), qkg[:, gcol:gcol + 1], sd[x], ALU.mult, ALU.mult,
                  [bank[pb], sdb[x], constb], [knb_[x]])
            k.act(knbf, kn[x], AF.Copy, [knb_[x]], [knbfb])
            k.pe_mm(ps(rbk), Pm, knbf, True, True, [knbfb, constb], [bank[rbk]])
            k.tt(t1, kn[x], ropeC[:, tq], ALU.mult, [knb_[x], ropeb], [t1b])
            k.tt(t2, ps(rbk), ropeS[:, tq], ALU.mult, [bank[rbk], ropeb], [t2b])
            k.tt(out_bf, t1, t2, ALU.add, [t1b, t2b], out_bufs)

    def pool_segment(src3, nseg, n, gq, corr_off, tbo, outcol0):
        w = WINS[gq]
        half = w // 2
        L = n + 16
        cur = src3
        curb = None
        bufs = [(ta, tab), (tb_, tbb)]
        bi = 0
        d = 1
        Lc = L
        while d < w:
            dst, dstb = bufs[bi]
            bi ^= 1
            dv = dst[:, 0:nseg * L].rearrange("p (b t) -> p b t", b=nseg)
            Ln_ = Lc - d
            rd = [] if curb is None else [curb]
            k.tt(dv[:, :, 0:Ln_], cur[:, :, 0:Ln_], cur[:, :, d:d + Ln_], ALU.add, rd + list(tbo["src"]), [dstb])
            cur, curb, Lc = dv, dstb, Ln_
            d *= 2
        o0 = 8 - half
        dst, dstb = bufs[bi]
        dv = dst[:, 0:nseg * n].rearrange("p (b t) -> p b t", b=nseg)
        k.ts(dv, cur[:, :, o0:o0 + n], 1.0 / w, ALU.mult, [curb], [dstb])
        cl = corr[:, corr_off + gq * 16: corr_off + gq * 16 + 8].unsqueeze(1).to_broadcast([128, nseg, 8])
        cr = corr[:, corr_off + gq * 16 + 8: corr_off + gq * 16 + 16].unsqueeze(1).to_broadcast([128, nseg, 8])
        k.tt(dv[:, :, 0:8], dv[:, :, 0:8], cl, ALU.mult, [constb], [dstb])
        k.tt(dv[:, :, n - 8:n], dv[:, :, n - 8:n], cr, ALU.mult, [constb], [dstb])
        tot = nseg * n
        segs_per = 512 // n if n < 512 else 1
        for c in range(tot // 512):
            x = cnts["x"] % 2
            cnts["x"] += 1
            if n < 512:
                k.tt(plb[x].rearrange("p (b t) -> p b t", b=nseg), dv, src3[:, :, 8:8 + n], ALU.subtract,
                     [dstb] + list(tbo["src"]), [plbb[x]])
            else:
                k.tt(plb[x], dst[:, c * 512:(c + 1) * 512], src3[:, 0, 8 + c * 512: 8 + (c + 1) * 512],
                     ALU.subtract, [dstb] + list(tbo["src"]), [plbb[x]])
            k.pe_mm(ps(7), poolw[:, gq * 128:(gq + 1) * 128], plb[x], True, True, [plbb[x], constb], [bank[7]])
            k.act(pooled2[:, gq, outcol0 + c * 512: outcol0 + (c + 1) * 512], ps(7), AF.Identity,
                  [bank[7], constb], tbo["dst"][c], scale=psc[:, gq:gq + 1])

    si1 = k.take()
    s1 = slot[si1]
    vcnt = [0]
    OVERLAP_CC = os.environ.get("KNOOVERLAP") is None

    def part_kv(tb):
        sample = tb >= 2
        tsl = slice(tb * TBS, (tb + 1) * TBS)
        for hd in range(2):
            if sample:
                x = cnts["x"] % 2
                rms_head(si1, hd * 128, 1, tb, True, kst[x], [kstb[x]])
                c0 = hd * 1024 + (tb - 2) * TBS
                k.dma("sp", ag_in_ap[:, c0:c0 + TBS], kst[x], kstp[x], [kstb[x], agb], [])
            else:
                rms_head(si1, hd * 128, 1, tb, False, Kp[:, hd, tsl], [Kpb[tb]],
                         kout=dr["nk"][:, hd, tsl])
        for tt_ in range(4):
            pb = cnts["p"] % 3
            cnts["p"] += 1
            tile = (tb % 2) * 4 + tt_
            k.pe_group(ps(pb)[:, 0:256], [(h2t[:, kk, tt_ * 128:(tt_ + 1) * 128], s1[:, kk * 1024 + 256: kk * 1024 + 512])
                                          for kk in range(NK)], [slotb[si1]] + h2b, [bank[pb]])
            v = vcnt[0] % 2
            vcnt[0] += 1
            if sample:
                k.act(vsb[v], ps(pb)[:, 0:256], AF.Copy, [bank[pb]], [vsbb[v]])
                c0 = 2048 + tile * 256
                k.dma("sp", ag_in_ap[:, c0:c0 + 256], vsb[v], vsbp[v], [vsbb[v], agb], [])
            else:
                k.vcopy(vst[v], ps(pb)[:, 0:256], [bank[pb]], [vstb[v]])
                k.dma("sp", dr["nv"][:, tile, :], vst[v], vstp[v], [vstb[v]], [])
                k.act(Vp[:, tile, :], vst[v], AF.Copy, [vstb[v]], [Vpb[tb]])

    def part_pin(tb):
        sample = tb >= 2
        for gq in range(4):
            pb = cnts["p"] % 3
            cnts["p"] += 1
            k.pe_group(ps(pb), [(s1[:, kk * 1024 + 512 + gq * 128: kk * 1024 + 512 + (gq + 1) * 128], h2t[:, kk, :])
                                for kk in range(NK)], [slotb[si1]] + h2b, [bank[pb]])
            if sample:
                c0 = 8 + (tb - 2) * TBS
                k.act(pin_s[:, gq, c0:c0 + TBS], ps(pb), AF.Copy, [bank[pb]], [pinb[gq]])
            else:
                x = gq % 2
                k.act(pp[x][:, :, 8:264], ps(pb).rearrange("p (b t) -> p b t", b=2), AF.Copy, [bank[pb]], [ppb[x]])
                pool_segment(pp[x], 2, 256, gq, 64, {"src": [ppb[x]], "dst": [[p2b[tb]]]}, tb * TBS)

    def collectives():
        k.vcopy(e32[:, :, 0:8], pin_s[:, :, 8:16], pinb, [efb])
        k.vcopy(e32[:, :, 8:16], pin_s[:, :, 1024:1032], pinb, [efb])
        e32f = e32.rearrange("p g i -> p (g i)")
        k.vcopy(est[:, 0:64], e32f, [efb], [estb])
        k.tt(est[:, 64:128], e32f, est[:, 0:64], ALU.subtract, [efb], [estb])
        k.dma("sp", dr["ag2_in"][:, :], est, estp, [estb, ag2b], [])
        k.barrier()
        E = k.E["pool"]
        for (src, dst, sb, db) in ((dr["ag2_in"], dr["ag2_out"], ag2b, ag2ob), (ag_in_ap, ag_out_ap, agb, agob)):
            k._deps(E, [], [sb, db])
            ins = nc.gpsimd.collective_compute("AllGather", ALU.bypass,
                                               replica_groups=[[0, 1, 2, 3], [4, 5, 6, 7]],
                                               ins=[src.opt()], outs=[dst.opt()])
            ccprod.count += 1
            ins.then_inc(ccprod.sem, 1)
            k._commit((ccprod, ccprod.count), [], [sb, db])
        if OVERLAP_CC:
            for qn in ("sp", "pool"):
                k.E[qn].wait((ccprod, ccprod.count))
        else:
            k.barrier()

    if OVERLAP_CC:
        for tb in (2, 3):
            mod2(tb, on_dve=True)
            part_kv(tb)
            part_pin(tb)
        for tb in (0, 1):
            mod2(tb, on_dve=True)
            part_kv(tb)
        collectives()
        for tb in (0, 1):
            mod2(tb)
            part_pin(tb)
        k.barrier()
    else:
        for tb in (2, 3, 0, 1):
            mod2(tb)
            part_kv(tb)
            part_pin(tb)
            if tb == 3:
                collectives()
    if "cc" in SKIP or "cutpost" in SKIP:
        cut(2)
    k.dma("sp", eg, dr["ag2_out"][:, :].rearrange("(r p) c -> p r c", p=128), egp, [ag2ob], [egb])
    k.tt(ef, eg[:, :, 0:64], eg[:, :, 64:128], ALU.add, [egb], [efb])
    ef4 = ef.rearrange("p r (g i) -> p r g i", g=4)
    for side, (dst, lo) in enumerate(((pin_s[:, :, 0:8], 8), (pin_s[:, :, 1032:1040], 0))):
        for r in range(4):
            m = mask[:, side * 4 + r: side * 4 + r + 1]
            src = ef4[:, r, :, lo:lo + 8]
            if r == 0:
                k.ts(dst, src, m, ALU.mult, [efb, constb], pinb)
            else:
                k.stt(dst, src, m, dst, ALU.mult, ALU.add, [efb, constb], pinb)
    for gq in range(4):
        pool_segment(pin_s[:, gq:gq + 1, :], 1, 1024, gq, 0,
                     {"src": [pinb[gq]], "dst": [[p2b[2]], [p2b[3]]]}, 1024)
    k.prefetch(si1)
    k.barrier()
    cut(2)

    RH.reset(); RQ.reset()
    RX.top = xmark
    Q3 = k.bfv(k.alloc(RQ, NK * T * 2), NK * T).rearrange("p (h t) -> p h t", h=NK)
    Qb = [[Buf("Q%d_%d" % (h, i)) for i in range(NTB)] for h in range(NK)]
    B1 = RH
    h2t2 = [k.bfv(k.alloc(B1, 8192), 4096).rearrange("p (k t) -> p k t", k=NK) for _ in range(2)]
    h2b2 = [[Buf() for _ in range(NK)] for _ in range(2)]
    sq = [k.bfv(k.alloc(B1, 1024), 512) for _ in range(2)]
    sqb = [Buf(), Buf()]
    sd = [k.f32v(k.alloc(B1, 2048), 512) for _ in range(3)]
    sdb = [Buf() for _ in range(3)]
    kn = [k.f32v(k.alloc(B1, 2048), 512) for _ in range(3)]
    knb_ = [Buf() for _ in range(3)]
    knbf2 = [k.bfv(k.alloc(B1, 1024), 512) for _ in range(2)]
    knbfb2 = [Buf(), Buf()]
    ropeC = k.f32v(k.alloc(RX, 4096), 1024)
    ropeS = k.f32v(k.alloc(RX, 4096), 1024)
    t1 = k.f32v(k.alloc(RX, 2048), 512)
    t2 = k.f32v(k.alloc(RX, 2048), 512)
    t1b, t2b = Buf(), Buf()
    ropeb = Buf("rope2")
    k.dma("sp", ropeC, dr["ropeCS"][:, 0:1024], ropep, (), [ropeb])
    k.dma("sp", ropeS, dr["ropeCS"][:, 1024:2048], ropep, (), [ropeb])
    si2 = k.take()
    s2 = slot[si2]
    items = [(tb, hd) for tb in range(NTB) for hd in range(NK)]
    NI = len(items)

    def mod2c(tb, kk):
        g = tb // 2
        tsl = slice(tb * TBS, (tb + 1) * TBS)
        k.ts(h2t2[tb % 2][:, kk, :], R3[:, kk, tsl], DERk("H2S", g, kk), ALU.mult, [Rbk[tb][kk], derb],
             [h2b2[tb % 2][kk]], s2=DERk("H2B", g, kk), op1=ALU.add)

    def P1(i):
        tb, hd = items[i]
        pb, x = i % 3, i % 2
        k.pe_group(ps(pb), [(s2[:, kk * 1024 + hd * 128: kk * 1024 + (hd + 1) * 128], h2t2[tb % 2][:, kk, :])
                            for kk in range(NK)], [slotb[si2]] + h2b2[tb % 2], [bank[pb]])
        k.act(sq[x], ps(pb), AF.Square, [bank[pb]], [sqb[x]])
        if tb + 1 < NTB:
            mod2c(tb + 1, hd)

    def P2(i):
        tb, hd = items[i]
        pb, x, mb, y = i % 3, i % 2, 3 + i % 2, i % 3
        tsl = slice(tb * TBS, (tb + 1) * TBS)
        k.pe_mm(ps(mb), ones_h, sq[x], True, True, [sqb[x], onesb], [bank[mb]])
        k.act(sd[y], ps(mb), AF.Ln, [bank[mb]], [sdb[y]], bias=RMS_EPS, scale=1.0)
        k.act(sd[y], sd[y], AF.Exp, [], [sdb[y]], scale=-0.5)
        if tb < 2:
            k.stt(Q3[:, hd, tsl], ps(pb), qkg[:, 0:1], sd[y], ALU.mult, ALU.mult,
                  [bank[pb], sdb[y], constb], [Qb[hd][tb]])
        else:
            k.stt(kn[y], ps(pb), qkg[:, 0:1], sd[y], ALU.mult, ALU.mult,
                  [bank[pb], sdb[y], constb], [knb_[y]])
            k.act(knbf2[x], kn[y], AF.Copy, [knb_[y]], [knbfb2[x]])

    def P3(i):
        tb, hd = items[i]
        if tb < 2:
            return
        x, y, rbk = i % 2, i % 3, 5 + i % 2
        tsl = slice(tb * TBS, (tb + 1) * TBS)
        tq = slice((tb - 2) * TBS, (tb - 1) * TBS)
        k.pe_mm(ps(rbk), Pm, knbf2[x], True, True, [knbfb2[x], constb], [bank[rbk]])
        k.tt(t1, kn[y], ropeC[:, tq], ALU.mult, [knb_[y], ropeb], [t1b])
        k.tt(t2, ps(rbk), ropeS[:, tq], ALU.mult, [bank[rbk], ropeb], [t2b])
        k.tt(Q3[:, hd, tsl], t1, t2, ALU.add, [t1b, t2b], [Qb[hd][tb]])

    for kk in range(NK):
        mod2c(0, kk)
    for step in range(NI + 2):
        if step < NI:
            P1(step)
        if 0 <= step - 1 < NI:
            P2(step - 1)
        if 0 <= step - 2 < NI:
            P3(step - 2)
    k.prefetch(si2)
    k.barrier()
    cut(3)

    RH.reset()
    B2 = RH
    KC = k.bfv(k.alloc(B2, 1024), 512).rearrange("p (h t) -> p h t", h=2)
    VC = k.bfv(k.alloc(B2, 1024), 512).rearrange("p (j c) -> p j c", j=2)
    ctxb = Buf("ctx")
    ctxp = k.dprod("ctx")
    KA0 = k.bfv(k.alloc(B2, 8192), 4096).rearrange("p (r t) -> p r t", r=4)
    VA0 = k.bfv(k.alloc(B2, 8192), 4096).rearrange("p (j c) -> p j c", j=32)
    kab0, vab0 = Buf("KA"), Buf("VA")
    kap, vap = k.dprod("KA"), k.dprod("VA")
    SB = [0, 1, 2, 7]
    PT8 = [k.bfv(k.alloc(B2, 1024), 512) for _ in range(8)]
    PT8b = [Buf() for _ in range(8)]
    PT, PTb = PT8[:4], PT8b[:4]
    RX.top = xmark
    xs = k.f32v(k.alloc(RX, 2048), 512)
    xsb = Buf("xs")
    Esel = k.f32v(k.alloc(RX, 512), 128)
    eselb = Buf("esel")
    eselp = k.dprod("esel")
    k.dma("sp", Esel, dr["esel"][:, :], eselp, (), [eselb])
    rec = [k.f32v(k.alloc(B2, 2048), 512) for _ in range(2)]
    recb = [Buf(), Buf()]
    k.dma("pool", KC, dr["kctx"][:, :].rearrange("p (h t) -> p h t", h=2), ctxp, (), [ctxb])
    k.dma("pool", VC, dr["vctx"][:, :].rearrange("p (j c) -> p j c", j=2), ctxp, (), [ctxb])

    it = 0
    for b in range(4):
        tb = b // 2
        bsl = slice(b * 256, (b + 1) * 256)
        for pr in range(4):
            kvh = pr // 2
            ob, sb_ = 3 + it % 2, 5 + it % 2
            r = it % 2
            it += 1
            qv = Q3[:, 2 * pr:2 * pr + 2, bsl]
            qbufs = [Qb[2 * pr][tb], Qb[2 * pr + 1][tb]]
            pts = []
            for kc in range(2):
                pi_ = cnts["p"] % 4
                sbk = SB[pi_]
                cnts["p"] += 1
                k.pe_mm(ps(sbk).rearrange("p (a t) -> p a t", a=2),
                        Kp[:, kvh, b * 256 + kc * 128: b * 256 + (kc + 1) * 128], qv, True, True,
                        [Kpb[tb]] + qbufs, [bank[sbk]])
                k.act(PT[pi_], ps(sbk), AF.Exp, [bank[sbk]], [PTb[pi_]], scale=SCALE)
                pts.append(pi_)
            for kc in range(2):
                x = pts[kc]
                k.pe_mm(ps(ob), Vp[:, b * 2 + kc, kvh * 128:(kvh + 1) * 128], PT[x], kc == 0, kc == 1,
                        [Vpb[tb], PTb[x]], [bank[ob]])
                k.pe_mm(ps(sb_), ones1, PT[x], kc == 0, kc == 1, [PTb[x], onesb], [bank[sb_]])
            k.act(rec[r], ps(sb_), AF.Ln, [bank[sb_]], [recb[r]])
            k.act(rec[r], rec[r], AF.Exp, [], [recb[r]], scale=-1.0)
            k.tt(qv, ps(ob).rearrange("p (a t) -> p a t", a=2), rec[r].rearrange("p (a t) -> p a t", a=2),
                 ALU.mult, [bank[ob], recb[r]], qbufs)

    KA1 = k.bfv(k.alloc(RX, 8192), 4096).rearrange("p (r t) -> p r t", r=4)
    VA1 = k.bfv(oKp, 4096).rearrange("p (j c) -> p j c", j=32)
    kab1, vab1 = Buf("KA1"), Buf("VA1")
    kvt = [(KA0, VA0, kab0, vab0, []), (KA1, VA1, kab1, vab1, Kpb + Vpb)]
    kvp_ = [(kap, vap), (k.dprod("KA1"), k.dprod("VA1"))]
    for kvh in range(2):
        KA, VA, kab, vab, extra = kvt[kvh]
        k.dma("sp", KA, ag_out_ap[:, kvh * 1024:(kvh + 1) * 1024].rearrange("(r p) c -> p r c", p=128),
              kvp_[kvh][0], [agob], [kab])
        for r_ in range(4):
            k.dma("sp", VA[:, r_ * 8:(r_ + 1) * 8, :],
                  ag_out_ap[r_ * 128:(r_ + 1) * 128, 2048:4096].rearrange("p (j c) -> p j c", j=8)[:, :, kvh * 128:(kvh + 1) * 128],
                  kvp_[kvh][1], [agob], [vab] + extra)
    for kvh in range(2):
        KA, VA, kab, vab, extra = kvt[kvh]
        chunks = []
        for c in range(2):
            chunks.append((KC[:, kvh, c * 128:(c + 1) * 128], VC[:, c, kvh * 128:(kvh + 1) * 128], [ctxb]))
        for r_ in range(4):
            for j in range(8):
                chunks.append((KA[:, r_, j * 128:(j + 1) * 128], VA[:, r_ * 8 + j, :], [kab, vab]))
        NCH = len(chunks)
        for hd in range(4 * kvh, 4 * kvh + 4):
            for tb in (2, 3):
                tsl = slice(tb * TBS, (tb + 1) * TBS)
                ob, sb_ = 3 + it % 2, 5 + it % 2
                r = it % 2
                it += 1
                qv = Q3[:, hd, tsl]
                sbks = {}

                def qk(c):
                    pi_ = cnts["p"] % 4
                    sbk = SB[pi_]
                    cnts["p"] += 1
                    sbks[c] = c % 8
                    k.pe_mm(ps(sbk), chunks[c][0], qv, True, True, chunks[c][2] + [Qb[hd][tb]], [bank[sbk]])
                    k.act(PT8[c % 8], ps(sbk), AF.Exp, [bank[sbk]], [PT8b[c % 8]], scale=SCALE)

                def sum_mm(cc):
                    j = cc % 4
                    x = sbks[cc]
                    k.op("pe", lambda: nc.tensor.matmul(ps(sb_)[32 * j:32 * j + 32, :], lhsT=ones1[:, 0:32],
                                                        rhs=PT8[x], start=(cc == j), stop=(cc + 4 >= NCH),
                                                        tile_position=(0, 32 * j)),
                         [PT8b[x], onesb], [bank[sb_]])

                qk(0)
                qk(1)
                qk(2)
                for c in range(NCH):
                    x = sbks[c]
                    k.pe_mm(ps(ob), chunks[c][1], PT8[x], c == 0, c == NCH - 1, chunks[c][2] + [PT8b[x]], [bank[ob]])
                    if c % 4 == 3 or c == NCH - 1:
                        for cc in range((c // 4) * 4, c + 1):
                            sum_mm(cc)
                    if c + 3 < NCH:
                        qk(c + 3)
                k.vcopy(xs, ps(sb_), [bank[sb_]], [xsb])
                k.pe_mm(ps(sb_), Esel, xs, True, True, [xsb, eselb], [bank[sb_]])
                k.recip(rec[r], ps(sb_), [bank[sb_]], [recb[r]])
                k.tt(qv, ps(ob), rec[r], ALU.mult, [bank[ob], recb[r]], [Qb[hd][tb]])
                if kvh == 0 and tb == 3:
                    mi = 5 + (hd - 4 * kvh)
                    mod_compute(mi, 7)
                    mod_finish(mi, mi + 1, 7)
                    if mi == 8:
                        derive_late()
    k.barrier()
    cut(4)

    RH.reset(); RX.reset()
    oH2 = k.alloc(RH, NK * T * 2)
    assert oH2 == oH
    GX = RX
    h2tg = [k.bfv(k.alloc(GX, 8192), 4096).rearrange("p (k t) -> p k t", k=NK) for _ in range(2)]
    h2bg = [[Buf() for _ in range(NK)] for _ in range(2)]
    sg = [k.f32v(k.alloc(GX, 2048), 512) for _ in range(2)]
    sgb = [Buf(), Buf()]
    m12 = sg
    m12b = sgb
    hi_ = 0
    mod2(0, h2tg[0], h2bg[0])
    for ocp in range(4):
        si = k.take()
        s = slot[si]
        for tb in range(NTB):
            h2t, h2b = h2tg[hi_ % 2], h2bg[hi_ % 2]
            hi_ += 1
            if not (ocp == 3 and tb == NTB - 1):
                mod2((tb + 1) % NTB, h2tg[hi_ % 2], h2bg[hi_ % 2])
            tsl = slice(tb * TBS, (tb + 1) * TBS)
            for o in range(2):
                oc = 2 * ocp + o
                base = 4 * o
                k.pe_group(ps(base + 0), [(s[:, kk * 256 + o * 128: kk * 256 + (o + 1) * 128], h2t[:, kk, :])
                                          for kk in range(NK)], [slotb[si]] + h2b, [bank[base + 0]])
                k.pe_group(ps(base + 1), [(s[:, 2048 + kk * 256 + o * 128: 2048 + kk * 256 + (o + 1) * 128], h2t[:, kk, :])
                                          for kk in range(NK)], [slotb[si]] + h2b, [bank[base + 1]])
                k.pe_group(ps(base + 2), [(s[:, 4096 + kk * 256 + o * 128: 4096 + kk * 256 + (o + 1) * 128], Q3[:, kk, tsl])
                                          for kk in range(NK)], [slotb[si]] + [Qb[kk][tb] for kk in range(NK)],
                           [bank[base + 2]])
                k.pe_group(ps(base + 3), [(s[:, 6144 + kk * 256 + o * 128: 6144 + kk * 256 + (o + 1) * 128], pooled2[:, kk, tsl])
                                          for kk in range(4)], [slotb[si], p2b[tb]], [bank[base + 3]])
                k.act(sg[0], ps(base + 0), AF.Sigmoid, [bank[base + 0]], [sgb[0]])
                k.act(sg[1], ps(base + 1), AF.Sigmoid, [bank[base + 1]], [sgb[1]])
                k.tt(m12[0], sg[0], ps(base + 2), ALU.mult, [sgb[0], bank[base + 2]], [m12b[0]])
                k.tt(m12[1], sg[1], ps(base + 3), ALU.mult, [sgb[1], bank[base + 3]], [m12b[1]])
                k.tt(H3[:, oc, tsl], m12[0], m12[1], ALU.add, [m12b[0], m12b[1]], [Hb[tb]])
        k.prefetch(si)
    k.barrier()
    cut(5)

    FB2 = ffn_alloc()
    L2 = FB2["Ln"]
    si = k.take()
    s = slot[si]
    cntf = 0
    for tb in range(NTB):
        g = tb // 2
        tsl = slice(tb * TBS, (tb + 1) * TBS)
        for oc in range(NK):
            f = 4 + cntf % 2
            cntf += 1
            k.pe_group(ps(f), [(s[:, kk * 1024 + oc * 128: kk * 1024 + (oc + 1) * 128], H3[:, kk, tsl])
                               for kk in range(NK)], [slotb[si], Hb[tb]], [bank[f]])
            k.stt(R3[:, oc, tsl], ps(f), DERk("G2", g, oc), R3[:, oc, tsl], ALU.mult, ALU.add,
                  [bank[f], derb], [Rbk[tb][oc]])
            ln_stats_chunk(tb, L2, oc)
        sbk6, sbk7 = ((6, 7), (2, 3))[tb % 2]
        ln_pe_stats(tb, L2, sbk6, sbk7)
        if tb >= 1:
            pb6, pb7 = ((6, 7), (2, 3))[(tb - 1) % 2]
            ln_finish(tb - 1, L2, "RS2", "RB2", hs="H3S", hb="H3B", b6=pb6, b7=pb7)
    pb6, pb7 = ((6, 7), (2, 3))[(NTB - 1) % 2]
    ln_finish(NTB - 1, L2, "RS2", "RB2", hs="H3S", hb="H3B", b6=pb6, b7=pb7)
    k.prefetch(si)
    cut(6)

    ffn_phase("HG3", FB2, (("RS3", "RB3"), {"final": True}))

    finalize()


_PROGRAM = None


def _host_consts():
    n_freq = 32
    inv_freq = (10000.0 ** (-np.arange(n_freq, dtype=np.float32) / n_freq)).astype(np.float32)
    pm = np.zeros((128, 128), np.float32)
    for d2 in range(64):
        pm[d2 + 64, d2] = -1.0
        pm[d2, d2 + 64] = 1.0

    def corr_tab(n, left, right):
        out = np.ones((4, 16), np.float32)
        for g, w in enumerate(WINS):
            for i in range(8):
                if left:
                    t = i
                    lo = min(max(t - w // 2, 0), n); hi = min(max(t - w // 2 + w, 0), n)
                    out[g, i] = w / float(hi - lo)
                if right:
                    t = n - 8 + i
                    lo = min(max(t - w // 2, 0), n); hi = min(max(t - w // 2 + w, 0), n)
                    out[g, 8 + i] = w / float(hi - lo)
        return out.reshape(64)
    return inv_freq, pm, corr_tab


def kernel(x_prompt, x_sample, cache_k, cache_v, c, c_ctx, w_mod, b_mod, ln_g, ln_b,
           ffn1_w1, ffn1_w2, w_in, q_norm_g, k_norm_g, pool_w, pool_scale,
           w_up_attn, w_up_pool, w_out, ffn2_w1, ffn2_w2):
    global _PROGRAM
    f32 = np.float32
    A = lambda a: np.ascontiguousarray(np.asarray(a, dtype=f32))
    x_prompt, x_sample, cache_k, cache_v = A(x_prompt), A(x_sample), A(cache_k), A(cache_v)
    c, c_ctx = A(c), A(c_ctx)
    inv_freq, pm, corr_tab = _host_consts()

    def fm(v):
        return np.ascontiguousarray(v.reshape(8, 128).T)

    def w1_pm(w):
        out = np.empty((128, 2, 8 * DFF), f32)
        pieces = [(0, 2), (2, 2)] + [(c0, 3) for c0 in range(4, 22, 3)]
        for half in range(2):
            wh = w[:, half * DFF:(half + 1) * DFF].reshape(8, 128, DFF)
            for (c0, n) in pieces:
                blk = wh[:, :, c0 * 128:(c0 + n) * 128].transpose(1, 0, 2)
                out[:, half, 8 * 128 * c0: 8 * 128 * (c0 + n)] = blk.reshape(128, 8 * n * 128)
        return out

    shared = {
        "w_mod": A(w_mod)[0], "ffn1_w1": w1_pm(A(ffn1_w1)[0]), "ffn1_w2": A(ffn1_w2)[0],
        "ffn2_w1": w1_pm(A(ffn2_w1)[0]), "ffn2_w2": A(ffn2_w2)[0], "w_in": A(w_in)[0],
        "w_up_attn": A(w_up_attn)[0], "w_up_pool": A(w_up_pool)[0], "w_out": A(w_out)[0],
        "bmodT": np.ascontiguousarray(A(b_mod)[0].reshape(72, 128).T),
        "lngT": np.ascontiguousarray(A(ln_g)[0].reshape(24, 128).T),
        "lnbT": np.ascontiguousarray(A(ln_b)[0].reshape(24, 128).T),
        "qkg": np.ascontiguousarray(np.stack([A(q_norm_g)[0], A(k_norm_g)[0]], axis=1)),
        "pool_w": np.ascontiguousarray(A(pool_w)[0].transpose(1, 0, 2).reshape(128, 512)),
        "pscaleT": np.ascontiguousarray(A(pool_scale)[0].reshape(4, 128).T),
        "pmat": pm,
        "esel": np.ascontiguousarray(np.broadcast_to((np.arange(128) % 32 == 0).astype(f32)[:, None], (128, 128))),
    }
    in_maps = []
    for core in range(NCORES):
        b, q = core // 4, core % 4
        xp = x_prompt[4 * core:4 * core + 4].reshape(1024, D)
        xs = x_sample[b, 1024 * q:1024 * (q + 1)]
        xall = np.concatenate([xp, xs], axis=0)
        xT = np.ascontiguousarray(xall.reshape(T, 8, 128).transpose(2, 1, 0))
        condT = np.stack([fm(c_ctx), fm(c[b])], axis=2).reshape(128, 16)
        kctx = np.ascontiguousarray(cache_k[b, 0].transpose(2, 1, 0).reshape(128, 512))
        vctx = np.ascontiguousarray(cache_v[b, 0].reshape(2, 128, 256).transpose(1, 0, 2).reshape(128, 512))
        pos = np.arange(1024 * q, 1024 * (q + 1))
        row = (pos // 64).astype(f32)
        col = (pos % 64).astype(f32)
        ang = np.concatenate([row[:, None] * inv_freq, col[:, None] * inv_freq], axis=-1).astype(f32)
        cs, sn = np.cos(ang).astype(f32).T, np.sin(ang).astype(f32).T
        ropeCS = np.concatenate([np.concatenate([cs, cs], 0), np.concatenate([sn, sn], 0)], axis=1)
        m = np.zeros(8, f32)
        if q - 1 >= 0:
            m[q - 1] = 1.0
        if q + 1 <= 3:
            m[4 + q + 1] = 1.0
        corr = np.concatenate([corr_tab(4096, q == 0, q == 3), corr_tab(256, True, True)])
        im = dict(shared)
        im.update({
            "xT": xT, "condT": np.ascontiguousarray(condT), "kctx": kctx, "vctx": vctx,
            "ropeCS": np.ascontiguousarray(ropeCS.astype(f32)),
            "masks": np.ascontiguousarray(np.broadcast_to(m, (128, 8))),
            "corr": np.ascontiguousarray(np.broadcast_to(corr.astype(f32), (128, 128))),
        })
        in_maps.append(im)

    if _PROGRAM is None:
        _st = os.environ.get("KSTOP")
        _PROGRAM = build_program(None if _st is None else int(_st))
    res = run_bass_kernel_spmd(_PROGRAM, in_maps, core_ids=list(range(NCORES)))

    y_prompt = np.empty((32, 256, D), f32)
    y_sample = np.empty((2, 4096, D), f32)
    new_k = np.empty((32, 1, 256, 2, 128), f32)
    new_v = np.empty((32, 1, 256, 2, 128), f32)
    for core in range(NCORES):
        r = res.results[core]
        b, q = core // 4, core % 4
        yT = np.asarray(r["yT"], dtype=f32)
        yall = yT.transpose(2, 1, 0).reshape(T, D)
        y_prompt[4 * core:4 * core + 4] = yall[:1024].reshape(4, 256, D)
        y_sample[b, 1024 * q:1024 * (q + 1)] = yall[1024:]
        nk_ = np.asarray(r["nk"], dtype=f32)
        new_k[4 * core:4 * core + 4, 0] = nk_.transpose(2, 1, 0).reshape(4, 256, 2, 128)
        nv_ = np.asarray(r["nv"], dtype=f32)
        new_v[4 * core:4 * core + 4, 0] = nv_.transpose(1, 0, 2).reshape(4, 256, 2, 128)
    return (y_prompt, y_sample, new_k, new_v)
```

```python
import contextlib
import os
import numpy as np
import concourse.bass as bass
import concourse.mybir as mybir
from concourse.bass_utils import run_bass_kernel_spmd

F32 = mybir.dt.float32
BF16 = mybir.dt.bfloat16
AF = mybir.ActivationFunctionType
ALU = mybir.AluOpType

D = 1024
DFF = 2816
NK = 8
T = 2048
TBS = 512
NTB = 4
ALPHA = 2.0 ** 0.25
LN_EPS = 1e-6
RMS_EPS = 1e-6
SCALE = 128.0 ** -0.5
WINS = (2, 4, 8, 16)
NCORES = 8
AGW = 4096
ARENA_WORDS = 53200
SLOT_ELEMS = 9216


class Prod:
    def __init__(self, sem, step):
        self.sem = sem
        self.step = step
        self.count = 0


class Buf:
    __slots__ = ("w", "r", "name")

    def __init__(self, name=""):
        self.w = None
        self.r = {}
        self.name = name


class Eng:
    def __init__(self, eng, prod):
        self.eng = eng
        self.prod = prod
        self.seen = {}
        self.self_sync = False

    def wait(self, t):
        prod, val = t
        if prod is self.prod and not self.self_sync:
            return
        if self.seen.get(prod, 0) >= val:
            return
        self.eng.wait_ge(prod.sem, val)
        self.seen[prod] = val


class Region:
    def __init__(self, start, end):
        self.start, self.end, self.top = start, end, start

    def reset(self):
        self.top = self.start


class K:
    def __init__(self, nc, st):
        self.nc = nc
        self.st = st
        self.E = {}
        for name, eng in (("pe", nc.tensor), ("act", nc.scalar), ("dve", nc.vector),
                          ("pool", nc.gpsimd), ("sp", nc.sync)):
            self.E[name] = Eng(eng, self.new_prod("e_" + name, 1))
            self.E[name].self_sync = name in ("act", "dve", "pool")
        self.dprods = []
        self.arena_t = st.enter_context(nc.sbuf_tensor("arena", [128, ARENA_WORDS], F32))
        self.arena = self.arena_t[:, :]
        self.ps_t = st.enter_context(nc.psum_tensor("ps", [128, 8, 512], F32))
        self.bank = [Buf("bank%d" % i) for i in range(8)]
        self.jobs = []
        self.job_next = 0
        self.job_cur = 0
        self.job_slot = {}

    def new_prod(self, name, step):
        sem = self.st.enter_context(self.nc.semaphore(name))
        return Prod(sem, step)

    def dprod(self, name):
        p = self.new_prod("d_" + name, 16)
        self.dprods.append(p)
        return p

    def _deps(self, E, reads, writes):
        for b in reads:
            if b.w is not None:
                E.wait(b.w)
        for b in writes:
            if b.w is not None:
                E.wait(b.w)
            for p, v in b.r.items():
                E.wait((p, v))

    def _commit(self, t, reads, writes):
        p, v = t
        for b in reads:
            if b.r.get(p, 0) < v:
                b.r[p] = v
        for b in writes:
            b.w = t
            b.r = {}

    def op(self, en, fn, reads=(), writes=(), pre=()):
        E = self.E[en]
        self._deps(E, reads, writes)
        for f in pre:
            f()
        ins = fn()
        E.prod.count += 1
        ins.then_inc(E.prod.sem, 1)
        t = (E.prod, E.prod.count)
        self._commit(t, reads, writes)
        return t

    def dma(self, qn, out, in_, prod, reads=(), writes=(), nodep=False):
        E = self.E[qn]
        if not nodep:
            self._deps(E, reads, writes)
        ins = E.eng.dma_start(out=out, in_=in_)
        prod.count += 16
        ins.then_inc(prod.sem, 16)
        t = (prod, prod.count)
        self._commit(t, reads, writes)
        return t

    def barrier(self):
        ticks = [(e.prod, e.prod.count) for e in self.E.values() if e.prod.count > 0]
        ticks += [(p, p.count) for p in self.dprods if p.count > 0]
        for e in self.E.values():
            for t in ticks:
                e.wait(t)

    def psb(self, i):
        return self.ps_t[:, i, :]

    def f32v(self, off_bytes, n):
        o = off_bytes // 4
        return self.arena[:, o:o + n]

    def bfv(self, off_bytes, n):
        o = off_bytes // 4
        return self.arena[:, o:o + n // 2].bitcast(BF16)

    def alloc(self, reg, nbytes):
        nbytes = (nbytes + 63) // 64 * 64
        off = reg.top
        reg.top += nbytes
        assert reg.top <= reg.end, ("region overflow", reg.start, reg.end, reg.top)
        return off

    def pe_group(self, out_ap, pairs, reads, writes):
        n = len(pairs)
        nc = self.nc
        pre = [(lambda l=l, r=r, i=i: nc.tensor.matmul(out_ap, lhsT=l, rhs=r, start=(i == 0), stop=False))
               for i, (l, r) in enumerate(pairs[:-1])]
        l, r = pairs[-1]
        return self.op("pe", lambda: nc.tensor.matmul(out_ap, lhsT=l, rhs=r, start=(n == 1), stop=True),
                       reads, writes, pre=pre)

    def pe_mm(self, out_ap, l, r, start, stop, reads, writes):
        nc = self.nc
        return self.op("pe", lambda: nc.tensor.matmul(out_ap, lhsT=l, rhs=r, start=start, stop=stop),
                       reads, writes)

    def act(self, out, in_, func, reads, writes, bias=None, scale=None):
        nc = self.nc
        kw = {}
        if bias is not None:
            kw["bias"] = bias
        if scale is not None:
            kw["scale"] = scale
        return self.op("act", lambda: nc.scalar.activation(out=out, in_=in_, func=func, **kw), reads, writes)

    def tt(self, out, in0, in1, op, reads, writes):
        nc = self.nc
        return self.op("dve", lambda: nc.vector.tensor_tensor(out=out, in0=in0, in1=in1, op=op), reads, writes)

    def ts(self, out, in0, s1, op0, reads, writes, s2=None, op1=None):
        nc = self.nc
        if op1 is None:
            return self.op("dve", lambda: nc.vector.tensor_scalar(out=out, in0=in0, scalar1=s1, scalar2=None,
                                                                  op0=op0), reads, writes)
        return self.op("dve", lambda: nc.vector.tensor_scalar(out=out, in0=in0, scalar1=s1, scalar2=s2,
                                                              op0=op0, op1=op1), reads, writes)

    def stt(self, out, in0, scalar, in1, op0, op1, reads, writes):
        nc = self.nc
        return self.op("dve", lambda: nc.vector.scalar_tensor_tensor(out=out, in0=in0, scalar=scalar, in1=in1,
                                                                     op0=op0, op1=op1), reads, writes)

    def recip(self, out, in_, reads, writes):
        nc = self.nc
        return self.op("dve", lambda: nc.vector.reciprocal(out=out, in_=in_), reads, writes)

    def vcopy(self, out, in_, reads, writes):
        nc = self.nc
        return self.op("dve", lambda: nc.vector.tensor_copy(out=out, in_=in_), reads, writes)

    def memset(self, out, val, writes):
        nc = self.nc
        return self.op("dve", lambda: nc.vector.memset(out, val), (), writes)

    def add_job(self, fn):
        self.jobs.append(fn)

    def prefetch(self, si):
        if self.job_next < len(self.jobs):
            self.jobs[self.job_next](si)
            self.job_slot[self.job_next] = si
            self.job_next += 1

    def take(self):
        assert self.job_cur < self.job_next, "job not prefetched"
        si = self.job_slot[self.job_cur]
        self.job_cur += 1
        return si


class _Stop(Exception):
    pass


def build_program(stop=None):
    nc = bass.Bass("TRN2", target_bir_lowering=False)
    dr = {}

    def din(name, shape, dt=F32):
        dr[name] = nc.dram_tensor(name, list(shape), dt, kind="ExternalInput").ap()
        return dr[name]

    def dout(name, shape, dt=F32):
        dr[name] = nc.dram_tensor(name, list(shape), dt, kind="ExternalOutput").ap()
        return dr[name]

    xT = din("xT", [128, NK, T])
    condT = din("condT", [128, 16])
    w_mod = din("w_mod", [D, 9 * D])
    bmodT = din("bmodT", [128, 72])
    lngT = din("lngT", [128, 24])
    lnbT = din("lnbT", [128, 24])
    ffn1_w1 = din("ffn1_w1", [128, 2, 8 * DFF])
    ffn1_w2 = din("ffn1_w2", [DFF, D])
    ffn2_w1 = din("ffn2_w1", [128, 2, 8 * DFF])
    ffn2_w2 = din("ffn2_w2", [DFF, D])
    w_in = din("w_in", [D, 4096])
    qkg = din("qkg", [128, 2])
    pool_w = din("pool_w", [128, 512])
    pscaleT = din("pscaleT", [128, 4])
    w_up_attn = din("w_up_attn", [D, D])
    w_up_pool = din("w_up_pool", [512, D])
    w_out = din("w_out", [D, D])
    kctx = din("kctx", [128, 512])
    vctx = din("vctx", [128, 512])
    ropeCS = din("ropeCS", [128, 2048])
    pmat = din("pmat", [128, 128])
    masks = din("masks", [128, 8])
    corr = din("corr", [128, 128])
    esel = din("esel", [128, 128])
    yT = dout("yT", [128, NK, T])
    nk = dout("nk", [128, 2, 1024])
    nv = dout("nv", [128, 8, 256])
    ag_in = nc.dram_tensor("ag_in", [128, AGW], BF16)
    ag_out = nc.dram_tensor("ag_out", [512, AGW], BF16)
    ag_in_ap = ag_in.ap()
    ag_out_ap = ag_out.ap()
    dr["ag2_in"] = nc.dram_tensor("ag2_in", [128, 128], BF16).ap()
    dr["ag2_out"] = nc.dram_tensor("ag2_out", [512, 128], BF16).ap()

    with contextlib.ExitStack() as st:
        k = K(nc, st)
        k.stop = stop
        try:
            _emit(k, nc, dr, ag_in, ag_out, ag_in_ap, ag_out_ap)
        except _Stop:
            pass
    return nc


def _emit(k, nc, dr, ag_in, ag_out, ag_in_ap, ag_out_ap):
    ps = k.psb
    bank = k.bank
    SKIP = set(os.environ.get("KSKIP", "").split(","))
    G = Region(0, ARENA_WORDS * 4)
    oR = k.alloc(G, NK * T * 4)
    oSlot = [k.alloc(G, SLOT_ELEMS * 2), k.alloc(G, SLOT_ELEMS * 2)]
    oOnesLn = k.alloc(G, 256)
    oOnesH = k.alloc(G, 256)
    oOnes1 = k.alloc(G, 256)
    oPm = k.alloc(G, 256)
    oModp = k.alloc(G, 144 * 4)
    oBmod = k.alloc(G, 72 * 4)
    oLng = k.alloc(G, 24 * 4)
    oLnb = k.alloc(G, 24 * 4)
    oCond = k.alloc(G, 16 * 4)
    oScT = k.alloc(G, 16 * 2)
    oQkg = k.alloc(G, 2 * 4)
    oPsc = k.alloc(G, 4 * 4)
    oMask = k.alloc(G, 8 * 4)
    oCorr = k.alloc(G, 128 * 4)
    oPoolW = k.alloc(G, 512 * 2)
    oDer = k.alloc(G, 32 * 8 * 4)
    gend = G.top
    RH = Region(gend, gend + 32768)
    RQ = Region(RH.end, RH.end + 32768)
    RP2 = Region(RQ.end, RQ.end + 16384)
    RX = Region(RP2.end, ARENA_WORDS * 4)
    RHQ = Region(RH.start, RQ.end)
    RQX = Region(RQ.start, RX.end)
    assert RX.end - RX.start >= 20000, (RX.start, RX.end)

    R3 = k.f32v(oR, NK * T).rearrange("p (k t) -> p k t", k=NK)
    Rbk = [[Buf("R%d_%d" % (i, j)) for j in range(NK)] for i in range(NTB)]
    Rprod = [k.dprod("R%d" % i) for i in range(NTB)]
    slot = [k.bfv(o, SLOT_ELEMS) for o in oSlot]
    slotb = [Buf("slot0"), Buf("slot1")]
    slotb2 = [Buf("slot0w2"), Buf("slot1w2")]
    k.stage = None
    slotp = [k.dprod("slot0"), k.dprod("slot1")]
    slotp2 = [k.dprod("slot0w2"), k.dprod("slot1w2")]
    ones_ln = k.bfv(oOnesLn, 128)
    ones_h = k.bfv(oOnesH, 128)
    ones1 = k.bfv(oOnes1, 128)
    Pm = k.bfv(oPm, 128)
    modp = k.f32v(oModp, 144)
    modp3 = modp.rearrange("p (j g) -> p j g", g=2)
    modp4 = modp.rearrange("p (i k g) -> p i k g", i=9, k=8)
    bmod = k.f32v(oBmod, 72)
    lng = k.f32v(oLng, 24)
    lnb = k.f32v(oLnb, 24)
    cond = k.f32v(oCond, 16)
    scT = k.bfv(oScT, 16)
    qkg = k.f32v(oQkg, 2)
    psc = k.f32v(oPsc, 4)
    mask = k.f32v(oMask, 8)
    corr = k.f32v(oCorr, 128)
    poolw = k.bfv(oPoolW, 512)
    der = k.f32v(oDer, 256)
    constb = Buf("const")
    cprod = k.dprod("const")
    cprod2 = k.dprod("const2")
    derb = Buf("der")

    dcol = {}

    def DER(name, g=None):
        key = (name, g)
        if key not in dcol:
            dcol[key] = len(dcol)
            assert len(dcol) <= 32
        c = dcol[key] * 8
        return der[:, c:c + 8]

    def DERk(name, g, kk):
        c = dcol[(name, g)] * 8 + kk
        return der[:, c:c + 1]

    def finalize():
        sp = k.E["sp"]
        for p in k.dprods:
            if p.count > 0:
                sp.wait((p, p.count))
        for e in k.E.values():
            if e.prod.count > 0:
                sp.wait((e.prod, e.prod.count))

    dbgp = k.dprod("dbg")

    def cut(stage):
        if k.stop is not None and k.stop == stage:
            k.barrier()
            for tb in range(NTB):
                k.dma("sp", dr["yT"][:, :, tb * TBS:(tb + 1) * TBS], R3[:, :, tb * TBS:(tb + 1) * TBS], dbgp, Rbk[tb], [])
            finalize()
            raise _Stop()

    def job_wmod(i):
        def f(si):
            k.dma("pool", slot[si][:, 0:8192].rearrange("p (k c) -> p k c", k=8),
                  dr["w_mod"][:, i * 1024:(i + 1) * 1024].rearrange("(k p) c -> p k c", p=128),
                  slotp[si], (), [slotb[si], slotb2[si]])
        return f

    PIECES = [(0, 2)] + [(c0, 3) for c0 in range(2, 20, 3)] + [(20, 2)]

    def job_ffn(w1, w2, c0, n):
        def f(si):
            s = slot[si]
            o0 = 8 * 128 * c0
            k.dma("pool", s[:, 0:3072].rearrange("p (k c) -> p k c", k=8)[:, :, 0:n * 128],
                  w1[:, 0, o0:o0 + 8 * n * 128].rearrange("p (k c) -> p k c", k=8),
                  slotp[si], (), [slotb[si]])
            k.dma("pool", s[:, 3072:6144].rearrange("p (k c) -> p k c", k=8)[:, :, 0:n * 128],
                  w1[:, 1, o0:o0 + 8 * n * 128].rearrange("p (k c) -> p k c", k=8),
                  slotp[si], (), [slotb[si]], nodep=True)
            if k.stage is None:
                k.dma("pool", s[:, 6144:6144 + n * 1024].rearrange("p (j c) -> p j c", j=n),
                      w2[c0 * 128:(c0 + n) * 128, :].rearrange("(j p) c -> p j c", p=128),
                      slotp2[si], (), [slotb2[si]])
            else:
                stg, stgb, stgp = k.stage[si]
                k.dma("sp", stg[:, 0:n * 1024].rearrange("p (j c) -> p j c", j=n),
                      w2[c0 * 128:(c0 + n) * 128, :].rearrange("(j p) c -> p j c", p=128),
                      stgp, (), [stgb])
                for j in range(n):
                    k.act(s[:, 6144 + j * 1024:6144 + (j + 1) * 1024], stg[:, j * 1024:(j + 1) * 1024], AF.Copy,
                          [stgb], [slotb2[si]])
        return f

    def job_win(col0):
        def f(si):
            k.dma("pool", slot[si][:, 0:8192].rearrange("p (k c) -> p k c", k=8),
                  dr["w_in"][:, col0:col0 + 1024].rearrange("(k p) c -> p k c", p=128),
                  slotp[si], (), [slotb[si], slotb2[si]])
        return f

    def job_g1(ocp):
        def f(si):
            s = slot[si]
            c0 = ocp * 256
            k.dma("pool", s[:, 0:2048].rearrange("p (k c) -> p k c", k=8),
                  dr["w_in"][:, 2048 + c0:2048 + c0 + 256].rearrange("(k p) c -> p k c", p=128),
                  slotp[si], (), [slotb[si], slotb2[si]])
            k.dma("pool", s[:, 2048:4096].rearrange("p (k c) -> p k c", k=8),
                  dr["w_in"][:, 3072 + c0:3072 + c0 + 256].rearrange("(k p) c -> p k c", p=128),
                  slotp[si], (), [slotb[si]], nodep=True)
            k.dma("pool", s[:, 4096:6144].rearrange("p (k c) -> p k c", k=8),
                  dr["w_up_attn"][:, c0:c0 + 256].rearrange("(k p) c -> p k c", p=128),
                  slotp[si], (), [slotb[si]], nodep=True)
            k.dma("pool", s[:, 6144:7168].rearrange("p (k c) -> p k c", k=4),
                  dr["w_up_pool"][:, c0:c0 + 256].rearrange("(k p) c -> p k c", p=128),
                  slotp[si], (), [slotb[si]], nodep=True)
        return f

    def job_wout():
        def f(si):
            k.dma("pool", slot[si][:, 0:8192].rearrange("p (k c) -> p k c", k=8),
                  dr["w_out"][:, :].rearrange("(k p) c -> p k c", p=128),
                  slotp[si], (), [slotb[si], slotb2[si]])
        return f

    for i in range(2):
        k.add_job(job_wmod(i))
    for pi, (c0, n) in enumerate(PIECES):
        k.add_job(job_ffn(dr["ffn1_w1"], dr["ffn1_w2"], c0, n))
        if pi == 0:
            k.add_job(job_wmod(2))
        if pi == 3:
            k.add_job(job_wmod(3))
        if pi == 5:
            k.add_job(job_wmod(4))
    k.add_job(job_win(1024))
    k.add_job(job_win(0))
    for i in range(5, 9):
        k.add_job(job_wmod(i))
    for ocp in range(4):
        k.add_job(job_g1(ocp))
    k.add_job(job_wout())
    for (c0, n) in PIECES:
        k.add_job(job_ffn(dr["ffn2_w1"], dr["ffn2_w2"], c0, n))

    for (dst, src) in ((cond, "condT"), (bmod, "bmodT"), (lng, "lngT"), (lnb, "lnbT"), (qkg, "qkg"),
                       (psc, "pscaleT"), (mask, "masks"), (corr, "corr")):
        k.dma("sp", dst, dr[src][:, :], cprod, (), [constb])
    k.dma("pool", Pm, dr["pmat"][:, :], cprod2, (), [constb])
    k.dma("pool", poolw, dr["pool_w"][:, :], cprod2, (), [constb])
    k.prefetch(0)
    k.prefetch(1)
    for tb in range(NTB):
        k.dma("sp", R3[:, :, tb * TBS:(tb + 1) * TBS], dr["xT"][:, :, tb * TBS:(tb + 1) * TBS],
              Rprod[tb], (), Rbk[tb])
    onesb = Buf("ones")
    k.memset(ones_ln, 1.0 / 1024.0, [onesb])
    k.memset(ones_h, 1.0 / 128.0, [onesb])
    k.memset(ones1, 1.0, [onesb])
    scb = Buf("scT")
    k.act(scT, cond, AF.Silu, [constb], [scb])
    def mod_compute(i, bk):
        si = k.take()
        for j in range(8):
            col = (i * 8 + j) * 2
            for kk in range(8):
                nc_l = slot[si][:, kk * 1024 + j * 128: kk * 1024 + (j + 1) * 128]
                if kk == 0 and j == 0:
                    k._deps(k.E["pe"], [slotb[si], scb], [bank[bk]])
                ins = nc.tensor.matmul(ps(bk)[:, col:col + 2], lhsT=nc_l, rhs=scT[:, kk * 2:(kk + 1) * 2],
                                       start=(kk == 0), stop=(kk == 7))
        E = k.E["pe"]
        E.prod.count += 1
        ins.then_inc(E.prod.sem, 1)
        tk = (E.prod, E.prod.count)
        k._commit(tk, [slotb[si], scb], [bank[bk]])
        k.prefetch(si)

    def mod_finish(i0, i1, bk):
        psm3 = ps(bk)[:, 0:144].rearrange("p (j g) -> p j g", g=2)
        for g in range(2):
            k.tt(modp3[:, i0 * 8:i1 * 8, g], psm3[:, i0 * 8:i1 * 8, g], bmod[:, i0 * 8:i1 * 8], ALU.add,
                 [bank[bk], constb], [derb])

    for i in range(2):
        mod_compute(i, 0)
    mod_finish(0, 2, 0)

    def M(i, g):
        return modp4[:, i, :, g]

    for g in range(2):
        k.ts(DER("A1", g), M(1, g), 1.0, ALU.add, [], [derb])
        k.vcopy(DER("SH1", g), M(0, g), [], [derb])
        DER("HG1", g)
    k.ts(DER("RS1"), lng[:, 0:8], ALPHA, ALU.mult, [constb], [derb])
    k.ts(DER("RB1"), lnb[:, 0:8], ALPHA, ALU.mult, [constb], [derb])
    k.ts(DER("RS2"), lng[:, 8:16], ALPHA, ALU.mult, [constb], [derb])
    k.ts(DER("RB2"), lnb[:, 8:16], ALPHA, ALU.mult, [constb], [derb])
    k.vcopy(DER("RS3"), lng[:, 16:24], [constb], [derb])
    k.vcopy(DER("RB3"), lnb[:, 16:24], [constb], [derb])
    for g in range(2):
        for nm in ("H2S", "H2B", "G2", "T3", "H3S", "H3B", "HG3"):
            DER(nm, g)

    def derive_mix():
      for g in range(2):
        k.ts(DER("H2S", g), M(4, g), 1.0, ALU.add, [], [derb], s2=1.0 / ALPHA, op1=ALU.mult)
        k.vcopy(DER("H2B", g), M(3, g), [], [derb])

    def derive_late():
      for g in range(2):
        k.vcopy(DER("G2", g), M(5, g), [], [derb])
        k.ts(DER("T3", g), M(7, g), 1.0, ALU.add, [], [derb])
        k.tt(DER("H3S", g), DER("T3", g), lng[:, 8:16], ALU.mult, [constb], [derb])
        k.tt(DER("H3B", g), DER("T3", g), lnb[:, 8:16], ALU.mult, [constb], [derb])
        k.tt(DER("H3B", g), DER("H3B", g), M(6, g), ALU.add, [], [derb])
        k.ts(DER("HG3", g), M(8, g), 0.5, ALU.mult, [], [derb])

    oH = k.alloc(RH, NK * T * 2)
    H3 = k.bfv(oH, NK * T).rearrange("p (k t) -> p k t", k=NK)
    Hb = [Buf("H%d" % i) for i in range(NTB)]
    yprod = [k.dprod("y%d" % i) for i in range(NTB)]

    def ln_alloc(reg):
        o = {}
        o["yb"] = [k.bfv(k.alloc(reg, 1024), 512) for _ in range(NK)]
        o["ysq"] = [k.bfv(k.alloc(reg, 1024), 512) for _ in range(NK)]
        o["ybb"] = [Buf() for _ in range(NK)]
        o["ysqb"] = [Buf() for _ in range(NK)]
        o["mean"] = [k.f32v(k.alloc(reg, 2048), 512) for _ in range(2)]
        o["tmp"] = k.f32v(k.alloc(reg, 2048), 512)
        o["rstd"] = [k.f32v(k.alloc(reg, 2048), 512) for _ in range(2)]
        o["meanb"], o["tmpb"], o["rstdb"] = [Buf(), Buf()], Buf(), [Buf(), Buf()]
        return o

    def ln_stats_chunk(tb, L, kk):
        tsl = slice(tb * TBS, (tb + 1) * TBS)
        k.act(L["yb"][kk], R3[:, kk, tsl], AF.Copy, [Rbk[tb][kk]], [L["ybb"][kk]])
        k.act(L["ysq"][kk], R3[:, kk, tsl], AF.Square, [Rbk[tb][kk]], [L["ysqb"][kk]])

    def ln_pe_stats(tb, L, b6=6, b7=7):
        for kk in range(NK):
            k.pe_mm(ps(b6), ones_ln, L["yb"][kk], kk == 0, kk == NK - 1, [L["ybb"][kk], onesb], [bank[b6]])
        for kk in range(NK):
            k.pe_mm(ps(b7), ones_ln, L["ysq"][kk], kk == 0, kk == NK - 1, [L["ysqb"][kk], onesb], [bank[b7]])

    def ln_fin_a(tb, L, b6=6, b7=7):
        par = tb % 2
        k.vcopy(L["mean"][par], ps(b6), [bank[b6]], [L["meanb"][par]])
        k.tt(L["tmp"], L["mean"][par], L["mean"][par], ALU.mult, [L["meanb"][par]], [L["tmpb"]])
        k.tt(L["tmp"], ps(b7), L["tmp"], ALU.subtract, [bank[b7]], [L["tmpb"]])
        k.act(L["rstd"][par], L["tmp"], AF.Ln, [L["tmpb"]], [L["rstdb"][par]], bias=LN_EPS, scale=1.0)
        k.act(L["rstd"][par], L["rstd"][par], AF.Exp, [], [L["rstdb"][par]], scale=-0.5)

    def ln_fin_b1(tb, L):
        par = tb % 2
        tsl = slice(tb * TBS, (tb + 1) * TBS)
        Rt = R3[:, :, tsl]
        k.tt(Rt, Rt, L["mean"][par].unsqueeze(1).to_broadcast([128, NK, TBS]), ALU.subtract,
             [L["meanb"][par]], Rbk[tb])

    def ln_fin_b2(tb, L, rs, rb, hs=None, hb=None, final=False):
        par = tb % 2
        g = tb // 2
        tsl = slice(tb * TBS, (tb + 1) * TBS)
        Rt = R3[:, :, tsl]
        k.tt(Rt, Rt, L["rstd"][par].unsqueeze(1).to_broadcast([128, NK, TBS]), ALU.mult,
             [L["rstdb"][par]], Rbk[tb])
        for kk in range(NK):
            if hs is not None:
                k.ts(H3[:, kk, tsl], R3[:, kk, tsl], DERk(hs, g, kk), ALU.mult, [Rbk[tb][kk], derb], [Hb[tb]],
                     s2=DERk(hb, g, kk), op1=ALU.add)
            if hs is None and kk % 2 == 1:
                k.ts(R3[:, kk, tsl], R3[:, kk, tsl], DERk(rs, None, kk), ALU.mult, [derb], [Rbk[tb][kk]],
                     s2=DERk(rb, None, kk), op1=ALU.add)
            else:
                k.act(R3[:, kk, tsl], R3[:, kk, tsl], AF.Identity, [derb], [Rbk[tb][kk]],
                      bias=DERk(rb, None, kk), scale=DERk(rs, None, kk))
        if final:
            k.dma("sp", dr["yT"][:, :, tsl], R3[:, :, tsl], yprod[tb], Rbk[tb], [])

    def ln_finish(tb, L, rs, rb, hs=None, hb=None, final=False, b6=6, b7=7):
        ln_fin_a(tb, L, b6, b7)
        ln_fin_b1(tb, L)
        ln_fin_b2(tb, L, rs, rb, hs=hs, hb=hb, final=final)

    stgprods = [yprod[0], yprod[1]]

    def ffn_alloc():
        reg = RQX
        reg.reset()
        o = {}
        o["gbuf"] = [[k.bfv(k.alloc(reg, 1024), 512) for _ in range(3)] for _ in range(2)]
        o["gb"] = [[Buf() for _ in range(3)] for _ in range(2)]
        o["sa"] = [k.f32v(k.alloc(reg, 2048), 512) for _ in range(2)]
        o["sab"] = [Buf(), Buf()]
        o["Ln"] = ln_alloc(reg)
        o["stage"] = [(k.f32v(k.alloc(reg, 12288), 3072), Buf("stg%d" % i), stgprods[i]) for i in range(2)]
        return o

    def ffn_phase(hgname, L, ln_args, between=None, pre_stt=None):
        fb = ffn_alloc() if L is None else L
        gbuf, gb, sa, sab, Ln = fb["gbuf"], fb["gb"], fb["sa"], fb["sab"], fb["Ln"]
        k.stage = fb["stage"]
        cnt = 0
        cntf = 0
        gi = 0
        for pi, (c0, n) in enumerate(PIECES):
            si = k.take()
            s = slot[si]
            last = (pi == len(PIECES) - 1)
            for tb in range(NTB):
                g = tb // 2
                tsl = slice(tb * TBS, (tb + 1) * TBS)
                gs = gi % 2
                gi += 1
                for j in range(n):
                    a, b2 = cnt % 2, 2 + cnt % 2
                    cnt += 1
                    k.pe_group(ps(a), [(s[:, kk * 384 + j * 128: kk * 384 + (j + 1) * 128], H3[:, kk, tsl])
                                       for kk in range(NK)], [slotb[si], Hb[tb]], [bank[a]])
                    k.pe_group(ps(b2), [(s[:, 3072 + kk * 384 + j * 128: 3072 + kk * 384 + (j + 1) * 128],
                                         H3[:, kk, tsl]) for kk in range(NK)], [slotb[si], Hb[tb]], [bank[b2]])
                    k.act(sa[a], ps(a), AF.Silu, [bank[a]], [sab[a]])
                    k.tt(gbuf[gs][j], sa[a], ps(b2), ALU.mult, [sab[a], bank[b2]], [gb[gs][j]])
                if last and tb >= 1:
                    ln_fin_b1(tb - 1, Ln)
                if pre_stt is not None and pi == 0 and tb == 0:
                    pre_stt()
                for oc in range(NK):
                    f = 4 + cntf % 2
                    cntf += 1
                    k.pe_group(ps(f), [(s[:, 6144 + j * 1024 + oc * 128: 6144 + j * 1024 + (oc + 1) * 128],
                                        gbuf[gs][j]) for j in range(n)],
                               [slotb2[si]] + [gb[gs][j] for j in range(n)], [bank[f]])
                    k.stt(R3[:, oc, tsl], ps(f), DERk(hgname, g, oc), R3[:, oc, tsl], ALU.mult, ALU.add,
                          [bank[f], derb], [Rbk[tb][oc]])
                    if last:
                        ln_stats_chunk(tb, Ln, oc)
                if last:
                    if tb >= 1:
                        ln_fin_b2(tb - 1, Ln, *ln_args[0], **ln_args[1])
                    ln_pe_stats(tb, Ln)
                    ln_fin_a(tb, Ln)
                    if tb == NTB - 1:
                        ln_fin_b1(tb, Ln)
                        ln_fin_b2(tb, Ln, *ln_args[0], **ln_args[1])
            if pi + 2 >= len(PIECES):
                k.stage = None
            k.prefetch(si)
            if between is not None:
                between(pi)

    for tb in range(NTB):
        g = tb // 2
        tsl = slice(tb * TBS, (tb + 1) * TBS)
        for kk in range(NK):
            k.act(H3[:, kk, tsl], R3[:, kk, tsl], AF.Identity, [Rbk[tb][kk], derb], [Hb[tb]],
                  bias=DERk("SH1", g, kk), scale=DERk("A1", g, kk))
        k.act(R3[:, :, tsl], R3[:, :, tsl], AF.Identity, [], Rbk[tb], scale=ALPHA)

    cut(0)
    def ffn1_between(pi):
        if pi == 3:
            mod_compute(3, 6)
        if pi == 5:
            mod_compute(4, 6)
            mod_finish(3, 5, 6)
            derive_mix()

    def ffn1_pre_stt():
        mod_compute(2, 6)
        mod_finish(2, 3, 6)
        for g in range(2):
            k.ts(DER("HG1", g), M(2, g), 0.5, ALU.mult, [], [derb])

    ffn_phase("HG1", None, (("RS1", "RB1"), {}), between=ffn1_between, pre_stt=ffn1_pre_stt)
    k.barrier()
    cut(1)

    RH.reset(); RQ.reset(); RP2.reset(); RX.reset(); RHQ.reset()
    pooled2 = k.bfv(k.alloc(RP2, 4 * T * 2), 4 * T).rearrange("p (g t) -> p g t", g=4)
    p2b = [Buf("p2_%d" % i) for i in range(NTB)]
    oKp = k.alloc(RX, 2 * 1024 * 2)
    oVp = k.alloc(RX, 8 * 256 * 2)
    Kp = k.bfv(oKp, 2048).rearrange("p (h t) -> p h t", h=2)
    Vp = k.bfv(oVp, 2048).rearrange("p (j c) -> p j c", j=8)
    Kpb = [Buf(), Buf()]
    Vpb = [Buf(), Buf()]
    xmark = RX.top

    A = RHQ
    h2t = k.bfv(k.alloc(A, 8192), 4096).rearrange("p (k t) -> p k t", k=NK)
    h2b = [Buf("h2t%d" % i) for i in range(NK)]
    sq = [k.bfv(k.alloc(A, 1024), 512) for _ in range(2)]
    sqb = [Buf(), Buf()]
    sd = [k.f32v(k.alloc(A, 2048), 512) for _ in range(2)]
    sdb = [Buf(), Buf()]
    kn = [k.f32v(k.alloc(A, 2048), 512) for _ in range(2)]
    knb_ = [Buf(), Buf()]
    knbf = k.bfv(k.alloc(A, 1024), 512)
    knbfb = Buf()
    t1 = k.f32v(k.alloc(A, 2048), 512)
    t2 = k.f32v(k.alloc(A, 2048), 512)
    t1b, t2b = Buf(), Buf()
    kst = [k.bfv(k.alloc(A, 1024), 512) for _ in range(2)]
    kstb = [Buf(), Buf()]
    kstp = [k.dprod("kst0"), k.dprod("kst1")]
    knp = [k.dprod("kn0"), k.dprod("kn1")]
    vst = [k.f32v(k.alloc(A, 1024), 256) for _ in range(2)]
    vstb = [Buf(), Buf()]
    vstp = [k.dprod("vst0"), k.dprod("vst1")]
    vsb = [k.bfv(k.alloc(A, 512), 256) for _ in range(2)]
    vsbb = [Buf(), Buf()]
    vsbp = [k.dprod("vsb0"), k.dprod("vsb1")]
    ropeC = k.f32v(k.alloc(RX, 4096), 1024)
    ropeS = k.f32v(k.alloc(RX, 4096), 1024)
    ropeb = Buf("rope")
    ropep = k.dprod("rope")
    pp = [k.f32v(k.alloc(A, 2176), 544).rearrange("p (b t) -> p b t", b=2) for _ in range(2)]
    ppb = [Buf(), Buf()]
    pin_s = k.f32v(k.alloc(A, 4 * 1040 * 4), 4 * 1040).rearrange("p (g t) -> p g t", g=4)
    pinb = [Buf("pin_s%d" % i) for i in range(4)]
    ta = k.f32v(k.alloc(A, 4160), 1040)
    tb_ = k.f32v(k.alloc(A, 4160), 1040)
    tab, tbb = Buf(), Buf()
    plb = [k.bfv(k.alloc(A, 1024), 512) for _ in range(2)]
    plbb = [Buf(), Buf()]
    e32 = k.f32v(k.alloc(A, 256), 64).rearrange("p (g i) -> p g i", g=4)
    est = k.bfv(k.alloc(A, 256), 128)
    estb = Buf()
    estp = k.dprod("est")
    eg = k.bfv(k.alloc(A, 1024), 512).rearrange("p (r c) -> p r c", r=4)
    egb = Buf()
    egp = k.dprod("eg")
    ef = k.f32v(k.alloc(A, 1024), 256).rearrange("p (r c) -> p r c", r=4)
    efb = Buf()
    agb = Buf("ag_in")
    agob = Buf("ag_out")
    ag2b = Buf("ag2_in")
    ag2ob = Buf("ag2_out")
    ccprod = k.new_prod("cc", 1)
    k.dprods.append(ccprod)

    k.dma("sp", ropeC, dr["ropeCS"][:, 0:1024], ropep, (), [ropeb])
    k.dma("sp", ropeS, dr["ropeCS"][:, 1024:2048], ropep, (), [ropeb])
    for i in range(2):
        k.memset(pp[i], 0.0, [ppb[i]])
    k.memset(pin_s, 0.0, pinb)

    def mod2(tb, h2t_=None, h2b_=None, on_dve=False):
        g = tb // 2
        tsl = slice(tb * TBS, (tb + 1) * TBS)
        h2t_ = h2t if h2t_ is None else h2t_
        h2b_ = h2b if h2b_ is None else h2b_
        for kk in range(NK):
            if on_dve:
                k.ts(h2t_[:, kk, :], R3[:, kk, tsl], DERk("H2S", g, kk), ALU.mult, [Rbk[tb][kk], derb],
                     [h2b_[kk]], s2=DERk("H2B", g, kk), op1=ALU.add)
                continue
            k.act(h2t_[:, kk, :], R3[:, kk, tsl], AF.Identity, [Rbk[tb][kk], derb], [h2b_[kk]],
                  bias=DERk("H2B", g, kk), scale=DERk("H2S", g, kk))

    cnts = {"p": 0, "m": 0, "r": 0, "x": 0}

    def rms_head(si, col0, gcol, tb, sample, out_bf, out_bufs, kout=None):
        s = slot[si]
        pb = cnts["p"] % 3
        cnts["p"] += 1
        mb = 3 + cnts["m"] % 2
        cnts["m"] += 1
        x = cnts["x"] % 2
        cnts["x"] += 1
        k.pe_group(ps(pb), [(s[:, kk * 1024 + col0: kk * 1024 + col0 + 128], h2t[:, kk, :]) for kk in range(NK)],
                   [slotb[si]] + h2b, [bank[pb]])
        k.act(sq[x], ps(pb), AF.Square, [bank[pb]], [sqb[x]])
        k.pe_mm(ps(mb), ones_h, sq[x], True, True, [sqb[x], onesb], [bank[mb]])
        k.act(sd[x], ps(mb), AF.Ln, [bank[mb]], [sdb[x]], bias=RMS_EPS, scale=1.0)
        k.act(sd[x], sd[x], AF.Exp, [], [sdb[x]], scale=-0.5)
        if not sample:
            if kout is None:
                k.stt(out_bf, ps(pb), qkg[:, gcol:gcol + 1], sd[x], ALU.mult, ALU.mult,
                      [bank[pb], sdb[x], constb], out_bufs)
            else:
                k.stt(kn[x], ps(pb), qkg[:, gcol:gcol + 1], sd[x], ALU.mult, ALU.mult,
                      [bank[pb], sdb[x], constb], [knb_[x]])
                if "kvout" not in SKIP:
                    k.dma("sp", kout, kn[x], knp[x], [knb_[x]], [])
                k.act(out_bf, kn[x], AF.Copy, [knb_[x]], out_bufs)
        else:
            rbk = 5 + cnts["r"] % 2
            cnts["r"] += 1
            tq = slice((tb - 2) * TBS, (tb - 1) * TBS)
            k.stt(kn[x], ps(pb), qkg[:, gcol:gcol + 1], sd[x], ALU.mult, ALU.mult,
                  [bank[pb], sdb[x], constb], [knb_[x]])
            k.act(knbf, kn[x], AF.Copy, [knb_[x]], [knbfb])
            k.pe_mm(ps(rbk), Pm, knbf, True, True, [knbfb, constb], [bank[rbk]])
            k.tt(t1, kn[x], ropeC[:, tq], ALU.mult, [knb_[x], ropeb], [t1b])
            k.tt(t2, ps(rbk), ropeS[:, tq], ALU.mult, [bank[rbk], ropeb], [t2b])
            k.tt(out_bf, t1, t2, ALU.add, [t1b, t2b], out_bufs)

    def pool_segment(src3, nseg, n, gq, corr_off, tbo, outcol0):
        w = WINS[gq]
        half = w // 2
        L = n + 16
        cur = src3
        curb = None
        bufs = [(ta, tab), (tb_, tbb)]
        bi = 0
        d = 1
        Lc = L
        while d < w:
            dst, dstb = bufs[bi]
            bi ^= 1
            dv = dst[:, 0:nseg * L].rearrange("p (b t) -> p b t", b=nseg)
            Ln_ = Lc - d
            rd = [] if curb is None else [curb]
            k.tt(dv[:, :, 0:Ln_], cur[:, :, 0:Ln_], cur[:, :, d:d + Ln_], ALU.add, rd + list(tbo["src"]), [dstb])
            cur, curb, Lc = dv, dstb, Ln_
            d *= 2
        o0 = 8 - half
        dst, dstb = bufs[bi]
        dv = dst[:, 0:nseg * n].rearrange("p (b t) -> p b t", b=nseg)
        k.act(dv, cur[:, :, o0:o0 + n], AF.Identity, [curb], [dstb], scale=1.0 / w)
        cl = corr[:, corr_off + gq * 16: corr_off + gq * 16 + 8].unsqueeze(1).to_broadcast([128, nseg, 8])
        cr = corr[:, corr_off + gq * 16 + 8: corr_off + gq * 16 + 16].unsqueeze(1).to_broadcast([128, nseg, 8])
        k.tt(dv[:, :, 0:8], dv[:, :, 0:8], cl, ALU.mult, [constb], [dstb])
        k.tt(dv[:, :, n - 8:n], dv[:, :, n - 8:n], cr, ALU.mult, [constb], [dstb])
        tot = nseg * n
        segs_per = 512 // n if n < 512 else 1
        for c in range(tot // 512):
            x = cnts["x"] % 2
            cnts["x"] += 1
            if n < 512:
                k.tt(plb[x].rearrange("p (b t) -> p b t", b=nseg), dv, src3[:, :, 8:8 + n], ALU.subtract,
                     [dstb] + list(tbo["src"]), [plbb[x]])
            else:
                k.tt(plb[x], dst[:, c * 512:(c + 1) * 512], src3[:, 0, 8 + c * 512: 8 + (c + 1) * 512],
                     ALU.subtract, [dstb] + list(tbo["src"]), [plbb[x]])
            k.pe_mm(ps(7), poolw[:, gq * 128:(gq + 1) * 128], plb[x], True, True, [plbb[x], constb], [bank[7]])
            k.act(pooled2[:, gq, outcol0 + c * 512: outcol0 + (c + 1) * 512], ps(7), AF.Identity,
                  [bank[7], constb], tbo["dst"][c], scale=psc[:, gq:gq + 1])

    si1 = k.take()
    s1 = slot[si1]
    vcnt = [0]
    OVERLAP_CC = os.environ.get("KNOOVERLAP") is None

    def part_kv(tb):
        sample = tb >= 2
        tsl = slice(tb * TBS, (tb + 1) * TBS)
        for hd in range(2):
            if sample:
                x = cnts["x"] % 2
                rms_head(si1, hd * 128, 1, tb, True, kst[x], [kstb[x]])
                c0 = hd * 1024 + (tb - 2) * TBS
                k.dma("sp", ag_in_ap[:, c0:c0 + TBS], kst[x], kstp[x], [kstb[x], agb], [])
            else:
                rms_head(si1, hd * 128, 1, tb, False, Kp[:, hd, tsl], [Kpb[tb]],
                         kout=dr["nk"][:, hd, tsl])
        for tt_ in range(4):
            pb = cnts["p"] % 3
            cnts["p"] += 1
            tile = (tb % 2) * 4 + tt_
            k.pe_group(ps(pb)[:, 0:256], [(h2t[:, kk, tt_ * 128:(tt_ + 1) * 128], s1[:, kk * 1024 + 256: kk * 1024 + 512])
                                          for kk in range(NK)], [slotb[si1]] + h2b, [bank[pb]])
            v = vcnt[0] % 2
            vcnt[0] += 1
            if sample:
                k.act(vsb[v], ps(pb)[:, 0:256], AF.Copy, [bank[pb]], [vsbb[v]])
                c0 = 2048 + tile * 256
                k.dma("sp", ag_in_ap[:, c0:c0 + 256], vsb[v], vsbp[v], [vsbb[v], agb], [])
            else:
                k.vcopy(vst[v], ps(pb)[:, 0:256], [bank[pb]], [vstb[v]])
                k.dma("sp", dr["nv"][:, tile, :], vst[v], vstp[v], [vstb[v]], [])
                k.act(Vp[:, tile, :], vst[v], AF.Copy, [vstb[v]], [Vpb[tb]])

    def part_pin(tb):
        sample = tb >= 2
        for gq in range(4):
            pb = cnts["p"] % 3
            cnts["p"] += 1
            k.pe_group(ps(pb), [(s1[:, kk * 1024 + 512 + gq * 128: kk * 1024 + 512 + (gq + 1) * 128], h2t[:, kk, :])
                                for kk in range(NK)], [slotb[si1]] + h2b, [bank[pb]])
            if sample:
                c0 = 8 + (tb - 2) * TBS
                k.act(pin_s[:, gq, c0:c0 + TBS], ps(pb), AF.Copy, [bank[pb]], [pinb[gq]])
            else:
                x = gq % 2
                k.act(pp[x][:, :, 8:264], ps(pb).rearrange("p (b t) -> p b t", b=2), AF.Copy, [bank[pb]], [ppb[x]])
                pool_segment(pp[x], 2, 256, gq, 64, {"src": [ppb[x]], "dst": [[p2b[tb]]]}, tb * TBS)

    def collectives():
        k.vcopy(e32[:, :, 0:8], pin_s[:, :, 8:16], pinb, [efb])
        k.vcopy(e32[:, :, 8:16], pin_s[:, :, 1024:1032], pinb, [efb])
        e32f = e32.rearrange("p g i -> p (g i)")
        k.vcopy(est[:, 0:64], e32f, [efb], [estb])
        k.tt(est[:, 64:128], e32f, est[:, 0:64], ALU.subtract, [efb], [estb])
        k.dma("sp", dr["ag2_in"][:, :], est, estp, [estb, ag2b], [])
        k.barrier()
        E = k.E["pool"]
        for (src, dst, sb, db) in ((dr["ag2_in"], dr["ag2_out"], ag2b, ag2ob), (ag_in_ap, ag_out_ap, agb, agob)):
            k._deps(E, [], [sb, db])
            ins = nc.gpsimd.collective_compute("AllGather", ALU.bypass,
                                               replica_groups=[[0, 1, 2, 3], [4, 5, 6, 7]],
                                               ins=[src.opt()], outs=[dst.opt()])
            ccprod.count += 1
            ins.then_inc(ccprod.sem, 1)
            k._commit((ccprod, ccprod.count), [], [sb, db])
        if OVERLAP_CC:
            for qn in ("sp", "pool"):
                k.E[qn].wait((ccprod, ccprod.count))
        else:
            k.barrier()

    if OVERLAP_CC:
        for tb in (2, 3):
            mod2(tb, on_dve=True)
            part_kv(tb)
            part_pin(tb)
        for tb in (0, 1):
            mod2(tb, on_dve=True)
            part_kv(tb)
        collectives()
        for tb in (0, 1):
            mod2(tb)
            part_pin(tb)
        k.barrier()
    else:
        for tb in (2, 3, 0, 1):
            mod2(tb)
            part_kv(tb)
            part_pin(tb)
            if tb == 3:
                collectives()
    if "cc" in SKIP or "cutpost" in SKIP:
        cut(2)
    k.dma("sp", eg, dr["ag2_out"][:, :].rearrange("(r p) c -> p r c", p=128), egp, [ag2ob], [egb])
    k.tt(ef, eg[:, :, 0:64], eg[:, :, 64:128], ALU.add, [egb], [efb])
    ef4 = ef.rearrange("p r (g i) -> p r g i", g=4)
    for side, (dst, lo) in enumerate(((pin_s[:, :, 0:8], 8), (pin_s[:, :, 1032:1040], 0))):
        for r in range(4):
            m = mask[:, side * 4 + r: side * 4 + r + 1]
            src = ef4[:, r, :, lo:lo + 8]
            if r == 0:
                k.ts(dst, src, m, ALU.mult, [efb, constb], pinb)
            else:
                k.stt(dst, src, m, dst, ALU.mult, ALU.add, [efb, constb], pinb)
    for gq in range(4):
        pool_segment(pin_s[:, gq:gq + 1, :], 1, 1024, gq, 0,
                     {"src": [pinb[gq]], "dst": [[p2b[2]], [p2b[3]]]}, 1024)
    k.prefetch(si1)
    k.barrier()
    cut(2)

    RH.reset(); RQ.reset()
    RX.top = xmark
    Q3 = k.bfv(k.alloc(RQ, NK * T * 2), NK * T).rearrange("p (h t) -> p h t", h=NK)
    Qb = [[Buf("Q%d_%d" % (h, i)) for i in range(NTB)] for h in range(NK)]
    B1 = RH
    h2t2 = [k.bfv(k.alloc(B1, 8192), 4096).rearrange("p (k t) -> p k t", k=NK) for _ in range(2)]
    h2b2 = [[Buf() for _ in range(NK)] for _ in range(2)]
    sq = [k.bfv(k.alloc(B1, 1024), 512) for _ in range(2)]
    sqb = [Buf(), Buf()]
    sd = [k.f32v(k.alloc(B1, 2048), 512) for _ in range(3)]
    sdb = [Buf() for _ in range(3)]
    kn = [k.f32v(k.alloc(B1, 2048), 512) for _ in range(3)]
    knb_ = [Buf() for _ in range(3)]
    knbf2 = [k.bfv(k.alloc(B1, 1024), 512) for _ in range(2)]
    knbfb2 = [Buf(), Buf()]
    ropeC = k.f32v(k.alloc(RX, 4096), 1024)
    ropeS = k.f32v(k.alloc(RX, 4096), 1024)
    t1 = k.f32v(k.alloc(RX, 2048), 512)
    t2 = k.f32v(k.alloc(RX, 2048), 512)
    t1b, t2b = Buf(), Buf()
    ropeb = Buf("rope2")
    k.dma("sp", ropeC, dr["ropeCS"][:, 0:1024], ropep, (), [ropeb])
    k.dma("sp", ropeS, dr["ropeCS"][:, 1024:2048], ropep, (), [ropeb])
    si2 = k.take()
    s2 = slot[si2]
    items = [(tb, hd) for tb in range(NTB) for hd in range(NK)]
    NI = len(items)

    def mod2c(tb, kk):
        g = tb // 2
        tsl = slice(tb * TBS, (tb + 1) * TBS)
        k.ts(h2t2[tb % 2][:, kk, :], R3[:, kk, tsl], DERk("H2S", g, kk), ALU.mult, [Rbk[tb][kk], derb],
             [h2b2[tb % 2][kk]], s2=DERk("H2B", g, kk), op1=ALU.add)

    def P1(i):
        tb, hd = items[i]
        pb, x = i % 3, i % 2
        k.pe_group(ps(pb), [(s2[:, kk * 1024 + hd * 128: kk * 1024 + (hd + 1) * 128], h2t2[tb % 2][:, kk, :])
                            for kk in range(NK)], [slotb[si2]] + h2b2[tb % 2], [bank[pb]])
        k.act(sq[x], ps(pb), AF.Square, [bank[pb]], [sqb[x]])
        if tb + 1 < NTB:
            mod2c(tb + 1, hd)

    def P2(i):
        tb, hd = items[i]
        pb, x, mb, y = i % 3, i % 2, 3 + i % 2, i % 3
        tsl = slice(tb * TBS, (tb + 1) * TBS)
        k.pe_mm(ps(mb), ones_h, sq[x], True, True, [sqb[x], onesb], [bank[mb]])
        k.act(sd[y], ps(mb), AF.Ln, [bank[mb]], [sdb[y]], bias=RMS_EPS, scale=1.0)
        k.act(sd[y], sd[y], AF.Exp, [], [sdb[y]], scale=-0.5)
        if tb < 2:
            k.stt(Q3[:, hd, tsl], ps(pb), qkg[:, 0:1], sd[y], ALU.mult, ALU.mult,
                  [bank[pb], sdb[y], constb], [Qb[hd][tb]])
        else:
            k.stt(kn[y], ps(pb), qkg[:, 0:1], sd[y], ALU.mult, ALU.mult,
                  [bank[pb], sdb[y], constb], [knb_[y]])
            k.act(knbf2[x], kn[y], AF.Copy, [knb_[y]], [knbfb2[x]])

    def P3(i):
        tb, hd = items[i]
        if tb < 2:
            return
        x, y, rbk = i % 2, i % 3, 5 + i % 2
        tsl = slice(tb * TBS, (tb + 1) * TBS)
        tq = slice((tb - 2) * TBS, (tb - 1) * TBS)
        k.pe_mm(ps(rbk), Pm, knbf2[x], True, True, [knbfb2[x], constb], [bank[rbk]])
        k.tt(t1, kn[y], ropeC[:, tq], ALU.mult, [knb_[y], ropeb], [t1b])
        k.tt(t2, ps(rbk), ropeS[:, tq], ALU.mult, [bank[rbk], ropeb], [t2b])
        k.tt(Q3[:, hd, tsl], t1, t2, ALU.add, [t1b, t2b], [Qb[hd][tb]])

    for kk in range(NK):
        mod2c(0, kk)
    for step in range(NI + 2):
        if step < NI:
            P1(step)
        if 0 <= step - 1 < NI:
            P2(step - 1)
        if 0 <= step - 2 < NI:
            P3(step - 2)
    k.prefetch(si2)
    k.barrier()
    cut(3)

    RH.reset()
    B2 = RH
    KC = k.bfv(k.alloc(B2, 1024), 512).rearrange("p (h t) -> p h t", h=2)
    VC = k.bfv(k.alloc(B2, 1024), 512).rearrange("p (j c) -> p j c", j=2)
    ctxb = Buf("ctx")
    ctxp = k.dprod("ctx")
    KA0 = k.bfv(k.alloc(B2, 8192), 4096).rearrange("p (r t) -> p r t", r=4)
    VA0 = k.bfv(k.alloc(B2, 8192), 4096).rearrange("p (j c) -> p j c", j=32)
    kab0, vab0 = Buf("KA"), Buf("VA")
    kap, vap = k.dprod("KA"), k.dprod("VA")
    SB = [0, 1, 2, 7]
    PT8 = [k.bfv(k.alloc(B2, 1024), 512) for _ in range(8)]
    PT8b = [Buf() for _ in range(8)]
    PT, PTb = PT8[:4], PT8b[:4]
    RX.top = xmark
    xs = k.f32v(k.alloc(RX, 2048), 512)
    xsb = Buf("xs")
    Esel = k.f32v(k.alloc(RX, 512), 128)
    eselb = Buf("esel")
    eselp = k.dprod("esel")
    k.dma("sp", Esel, dr["esel"][:, :], eselp, (), [eselb])
    rec = [k.f32v(k.alloc(B2, 2048), 512) for _ in range(2)]
    recb = [Buf(), Buf()]
    k.dma("pool", KC, dr["kctx"][:, :].rearrange("p (h t) -> p h t", h=2), ctxp, (), [ctxb])
    k.dma("pool", VC, dr["vctx"][:, :].rearrange("p (j c) -> p j c", j=2), ctxp, (), [ctxb])

    it = 0
    for b in range(4):
        tb = b // 2
        bsl = slice(b * 256, (b + 1) * 256)
        for pr in range(4):
            kvh = pr // 2
            ob, sb_ = 3 + it % 2, 5 + it % 2
            r = it % 2
            it += 1
            qv = Q3[:, 2 * pr:2 * pr + 2, bsl]
            qbufs = [Qb[2 * pr][tb], Qb[2 * pr + 1][tb]]
            pts = []
            for kc in range(2):
                pi_ = cnts["p"] % 4
                sbk = SB[pi_]
                cnts["p"] += 1
                k.pe_mm(ps(sbk).rearrange("p (a t) -> p a t", a=2),
                        Kp[:, kvh, b * 256 + kc * 128: b * 256 + (kc + 1) * 128], qv, True, True,
                        [Kpb[tb]] + qbufs, [bank[sbk]])
                k.act(PT[pi_], ps(sbk), AF.Exp, [bank[sbk]], [PTb[pi_]], scale=SCALE)
                pts.append(pi_)
            for kc in range(2):
                x = pts[kc]
                k.pe_mm(ps(ob), Vp[:, b * 2 + kc, kvh * 128:(kvh + 1) * 128], PT[x], kc == 0, kc == 1,
                        [Vpb[tb], PTb[x]], [bank[ob]])
                k.pe_mm(ps(sb_), ones1, PT[x], kc == 0, kc == 1, [PTb[x], onesb], [bank[sb_]])
            k.act(rec[r], ps(sb_), AF.Ln, [bank[sb_]], [recb[r]])
            k.act(rec[r], rec[r], AF.Exp, [], [recb[r]], scale=-1.0)
            k.tt(qv, ps(ob).rearrange("p (a t) -> p a t", a=2), rec[r].rearrange("p (a t) -> p a t", a=2),
                 ALU.mult, [bank[ob], recb[r]], qbufs)

    KA1 = k.bfv(k.alloc(RX, 8192), 4096).rearrange("p (r t) -> p r t", r=4)
    VA1 = k.bfv(oKp, 4096).rearrange("p (j c) -> p j c", j=32)
    kab1, vab1 = Buf("KA1"), Buf("VA1")
    kvt = [(KA0, VA0, kab0, vab0, []), (KA1, VA1, kab1, vab1, Kpb + Vpb)]
    kvp_ = [(kap, vap), (k.dprod("KA1"), k.dprod("VA1"))]
    for kvh in range(2):
        KA, VA, kab, vab, extra = kvt[kvh]
        k.dma("sp", KA, ag_out_ap[:, kvh * 1024:(kvh + 1) * 1024].rearrange("(r p) c -> p r c", p=128),
              kvp_[kvh][0], [agob], [kab])
        for r_ in range(4):
            k.dma("sp", VA[:, r_ * 8:(r_ + 1) * 8, :],
                  ag_out_ap[r_ * 128:(r_ + 1) * 128, 2048:4096].rearrange("p (j c) -> p j c", j=8)[:, :, kvh * 128:(kvh + 1) * 128],
                  kvp_[kvh][1], [agob], [vab] + extra, nodep=(r_ > 0))
    for kvh in range(2):
        KA, VA, kab, vab, extra = kvt[kvh]
        chunks = []
        for c in range(2):
            chunks.append((KC[:, kvh, c * 128:(c + 1) * 128], VC[:, c, kvh * 128:(kvh + 1) * 128], [ctxb]))
        for r_ in range(4):
            for j in range(8):
                chunks.append((KA[:, r_, j * 128:(j + 1) * 128], VA[:, r_ * 8 + j, :], [kab, vab]))
        NCH = len(chunks)
        for hd in range(4 * kvh, 4 * kvh + 4):
            for tb in (2, 3):
                tsl = slice(tb * TBS, (tb + 1) * TBS)
                ob, sb_ = 3 + it % 2, 5 + it % 2
                r = it % 2
                it += 1
                qv = Q3[:, hd, tsl]
                sbks = {}

                def qk(c):
                    pi_ = cnts["p"] % 4
                    sbk = SB[pi_]
                    cnts["p"] += 1
                    sbks[c] = c % 8
                    k.pe_mm(ps(sbk), chunks[c][0], qv, True, True, chunks[c][2] + [Qb[hd][tb]], [bank[sbk]])
                    k.act(PT8[c % 8], ps(sbk), AF.Exp, [bank[sbk]], [PT8b[c % 8]], scale=SCALE)

                def sum_mm(cc):
                    j = cc % 4
                    x = sbks[cc]
                    k.op("pe", lambda: nc.tensor.matmul(ps(sb_)[32 * j:32 * j + 32, :], lhsT=ones1[:, 0:32],
                                                        rhs=PT8[x], start=(cc == j), stop=(cc + 4 >= NCH),
                                                        tile_position=(0, 32 * j)),
                         [PT8b[x], onesb], [bank[sb_]])

                qk(0)
                qk(1)
                qk(2)
                for c in range(NCH):
                    x = sbks[c]
                    k.pe_mm(ps(ob), chunks[c][1], PT8[x], c == 0, c == NCH - 1, chunks[c][2] + [PT8b[x]], [bank[ob]])
                    if c % 4 == 3 or c == NCH - 1:
                        for cc in range((c // 4) * 4, c + 1):
                            sum_mm(cc)
                    if c + 3 < NCH:
                        qk(c + 3)
                k.vcopy(xs, ps(sb_), [bank[sb_]], [xsb])
                k.pe_mm(ps(sb_), Esel, xs, True, True, [xsb, eselb], [bank[sb_]])
                k.recip(rec[r], ps(sb_), [bank[sb_]], [recb[r]])
                k.tt(qv, ps(ob), rec[r], ALU.mult, [bank[ob], recb[r]], [Qb[hd][tb]])
                if kvh == 0 and tb == 3:
                    mi = 5 + (hd - 4 * kvh)
                    mod_compute(mi, 7)
                    mod_finish(mi, mi + 1, 7)
                    if mi == 8:
                        derive_late()
    k.barrier()
    cut(4)

    RH.reset(); RX.reset()
    oH2 = k.alloc(RH, NK * T * 2)
    assert oH2 == oH
    GX = RX
    h2tg = [k.bfv(k.alloc(GX, 8192), 4096).rearrange("p (k t) -> p k t", k=NK) for _ in range(2)]
    h2bg = [[Buf() for _ in range(NK)] for _ in range(2)]
    sg = [k.f32v(k.alloc(GX, 2048), 512) for _ in range(2)]
    sgb = [Buf(), Buf()]
    m12 = sg
    m12b = sgb
    hi_ = 0
    mod2(0, h2tg[0], h2bg[0])
    for ocp in range(4):
        si = k.take()
        s = slot[si]
        for tb in range(NTB):
            h2t, h2b = h2tg[hi_ % 2], h2bg[hi_ % 2]
            hi_ += 1
            if not (ocp == 3 and tb == NTB - 1):
                mod2((tb + 1) % NTB, h2tg[hi_ % 2], h2bg[hi_ % 2])
            tsl = slice(tb * TBS, (tb + 1) * TBS)
            for o in range(2):
                oc = 2 * ocp + o
                base = 4 * o
                k.pe_group(ps(base + 0), [(s[:, kk * 256 + o * 128: kk * 256 + (o + 1) * 128], h2t[:, kk, :])
                                          for kk in range(NK)], [slotb[si]] + h2b, [bank[base + 0]])
                k.pe_group(ps(base + 1), [(s[:, 2048 + kk * 256 + o * 128: 2048 + kk * 256 + (o + 1) * 128], h2t[:, kk, :])
                                          for kk in range(NK)], [slotb[si]] + h2b, [bank[base + 1]])
                k.pe_group(ps(base + 2), [(s[:, 4096 + kk * 256 + o * 128: 4096 + kk * 256 + (o + 1) * 128], Q3[:, kk, tsl])
                                          for kk in range(NK)], [slotb[si]] + [Qb[kk][tb] for kk in range(NK)],
                           [bank[base + 2]])
                k.pe_group(ps(base + 3), [(s[:, 6144 + kk * 256 + o * 128: 6144 + kk * 256 + (o + 1) * 128], pooled2[:, kk, tsl])
                                          for kk in range(4)], [slotb[si], p2b[tb]], [bank[base + 3]])
                k.act(sg[0], ps(base + 0), AF.Sigmoid, [bank[base + 0]], [sgb[0]])
                k.act(sg[1], ps(base + 1), AF.Sigmoid, [bank[base + 1]], [sgb[1]])
                k.tt(m12[0], sg[0], ps(base + 2), ALU.mult, [sgb[0], bank[base + 2]], [m12b[0]])
                k.tt(m12[1], sg[1], ps(base + 3), ALU.mult, [sgb[1], bank[base + 3]], [m12b[1]])
                k.tt(H3[:, oc, tsl], m12[0], m12[1], ALU.add, [m12b[0], m12b[1]], [Hb[tb]])
        k.prefetch(si)
    k.barrier()
    cut(5)

    FB2 = ffn_alloc()
    L2 = FB2["Ln"]
    si = k.take()
    s = slot[si]
    cntf = 0
    for tb in range(NTB):
        g = tb // 2
        tsl = slice(tb * TBS, (tb + 1) * TBS)
        for oc in range(NK):
            f = 4 + cntf % 2
            cntf += 1
            k.pe_group(ps(f), [(s[:, kk * 1024 + oc * 128: kk * 1024 + (oc + 1) * 128], H3[:, kk, tsl])
                               for kk in range(NK)], [slotb[si], Hb[tb]], [bank[f]])
            k.stt(R3[:, oc, tsl], ps(f), DERk("G2", g, oc), R3[:, oc, tsl], ALU.mult, ALU.add,
                  [bank[f], derb], [Rbk[tb][oc]])
            ln_stats_chunk(tb, L2, oc)
        sbk6, sbk7 = ((6, 7), (2, 3))[tb % 2]
        ln_pe_stats(tb, L2, sbk6, sbk7)
        if tb >= 1:
            pb6, pb7 = ((6, 7), (2, 3))[(tb - 1) % 2]
            ln_finish(tb - 1, L2, "RS2", "RB2", hs="H3S", hb="H3B", b6=pb6, b7=pb7)
    pb6, pb7 = ((6, 7), (2, 3))[(NTB - 1) % 2]
    ln_finish(NTB - 1, L2, "RS2", "RB2", hs="H3S", hb="H3B", b6=pb6, b7=pb7)
    k.prefetch(si)
    cut(6)

    ffn_phase("HG3", FB2, (("RS3", "RB3"), {"final": True}))

    finalize()


_PROGRAM = None


def _host_consts():
    n_freq = 32
    inv_freq = (10000.0 ** (-np.arange(n_freq, dtype=np.float32) / n_freq)).astype(np.float32)
    pm = np.zeros((128, 128), np.float32)
    for d2 in range(64):
        pm[d2 + 64, d2] = -1.0
        pm[d2, d2 + 64] = 1.0

    def corr_tab(n, left, right):
        out = np.ones((4, 16), np.float32)
        for g, w in enumerate(WINS):
            for i in range(8):
                if left:
                    t = i
                    lo = min(max(t - w // 2, 0), n); hi = min(max(t - w // 2 + w, 0), n)
                    out[g, i] = w / float(hi - lo)
                if right:
                    t = n - 8 + i
                    lo = min(max(t - w // 2, 0), n); hi = min(max(t - w // 2 + w, 0), n)
                    out[g, 8 + i] = w / float(hi - lo)
        return out.reshape(64)
    return inv_freq, pm, corr_tab


def kernel(x_prompt, x_sample, cache_k, cache_v, c, c_ctx, w_mod, b_mod, ln_g, ln_b,
           ffn1_w1, ffn1_w2, w_in, q_norm_g, k_norm_g, pool_w, pool_scale,
           w_up_attn, w_up_pool, w_out, ffn2_w1, ffn2_w2):
    global _PROGRAM
    f32 = np.float32
    A = lambda a: np.ascontiguousarray(np.asarray(a, dtype=f32))
    x_prompt, x_sample, cache_k, cache_v = A(x_prompt), A(x_sample), A(cache_k), A(cache_v)
    c, c_ctx = A(c), A(c_ctx)
    inv_freq, pm, corr_tab = _host_consts()

    def fm(v):
        return np.ascontiguousarray(v.reshape(8, 128).T)

    def w1_pm(w):
        out = np.empty((128, 2, 8 * DFF), f32)
        pieces = [(0, 2)] + [(c0, 3) for c0 in range(2, 20, 3)] + [(20, 2)]
        for half in range(2):
            wh = w[:, half * DFF:(half + 1) * DFF].reshape(8, 128, DFF)
            for (c0, n) in pieces:
                blk = wh[:, :, c0 * 128:(c0 + n) * 128].transpose(1, 0, 2)
                out[:, half, 8 * 128 * c0: 8 * 128 * (c0 + n)] = blk.reshape(128, 8 * n * 128)
        return out

    shared = {
        "w_mod": A(w_mod)[0], "ffn1_w1": w1_pm(A(ffn1_w1)[0]), "ffn1_w2": A(ffn1_w2)[0],
        "ffn2_w1": w1_pm(A(ffn2_w1)[0]), "ffn2_w2": A(ffn2_w2)[0], "w_in": A(w_in)[0],
        "w_up_attn": A(w_up_attn)[0], "w_up_pool": A(w_up_pool)[0], "w_out": A(w_out)[0],
        "bmodT": np.ascontiguousarray(A(b_mod)[0].reshape(72, 128).T),
        "lngT": np.ascontiguousarray(A(ln_g)[0].reshape(24, 128).T),
        "lnbT": np.ascontiguousarray(A(ln_b)[0].reshape(24, 128).T),
        "qkg": np.ascontiguousarray(np.stack([A(q_norm_g)[0], A(k_norm_g)[0]], axis=1)),
        "pool_w": np.ascontiguousarray(A(pool_w)[0].transpose(1, 0, 2).reshape(128, 512)),
        "pscaleT": np.ascontiguousarray(A(pool_scale)[0].reshape(4, 128).T),
        "pmat": pm,
        "esel": np.ascontiguousarray(np.broadcast_to((np.arange(128) % 32 == 0).astype(f32)[:, None], (128, 128))),
    }
    in_maps = []
    for core in range(NCORES):
        b, q = core // 4, core % 4
        xp = x_prompt[4 * core:4 * core + 4].reshape(1024, D)
        xs = x_sample[b, 1024 * q:1024 * (q + 1)]
        xall = np.concatenate([xp, xs], axis=0)
        xT = np.ascontiguousarray(xall.reshape(T, 8, 128).transpose(2, 1, 0))
        condT = np.stack([fm(c_ctx), fm(c[b])], axis=2).reshape(128, 16)
        kctx = np.ascontiguousarray(cache_k[b, 0].transpose(2, 1, 0).reshape(128, 512))
        vctx = np.ascontiguousarray(cache_v[b, 0].reshape(2, 128, 256).transpose(1, 0, 2).reshape(128, 512))
        pos = np.arange(1024 * q, 1024 * (q + 1))
        row = (pos // 64).astype(f32)
        col = (pos % 64).astype(f32)
        ang = np.concatenate([row[:, None] * inv_freq, col[:, None] * inv_freq], axis=-1).astype(f32)
        cs, sn = np.cos(ang).astype(f32).T, np.sin(ang).astype(f32).T
        ropeCS = np.concatenate([np.concatenate([cs, cs], 0), np.concatenate([sn, sn], 0)], axis=1)
        m = np.zeros(8, f32)
        if q - 1 >= 0:
            m[q - 1] = 1.0
        if q + 1 <= 3:
            m[4 + q + 1] = 1.0
        corr = np.concatenate([corr_tab(4096, q == 0, q == 3), corr_tab(256, True, True)])
        im = dict(shared)
        im.update({
            "xT": xT, "condT": np.ascontiguousarray(condT), "kctx": kctx, "vctx": vctx,
            "ropeCS": np.ascontiguousarray(ropeCS.astype(f32)),
            "masks": np.ascontiguousarray(np.broadcast_to(m, (128, 8))),
            "corr": np.ascontiguousarray(np.broadcast_to(corr.astype(f32), (128, 128))),
        })
        in_maps.append(im)

    if _PROGRAM is None:
        _st = os.environ.get("KSTOP")
        _PROGRAM = build_program(None if _st is None else int(_st))
    res = run_bass_kernel_spmd(_PROGRAM, in_maps, core_ids=list(range(NCORES)))

    y_prompt = np.empty((32, 256, D), f32)
    y_sample = np.empty((2, 4096, D), f32)
    new_k = np.empty((32, 1, 256, 2, 128), f32)
    new_v = np.empty((32, 1, 256, 2, 128), f32)
    for core in range(NCORES):
        r = res.results[core]
        b, q = core // 4, core % 4
        yT = np.asarray(r["yT"], dtype=f32)
        yall = yT.transpose(2, 1, 0).reshape(T, D)
        y_prompt[4 * core:4 * core + 4] = yall[:1024].reshape(4, 256, D)
        y_sample[b, 1024 * q:1024 * (q + 1)] = yall[1024:]
        nk_ = np.asarray(r["nk"], dtype=f32)
        new_k[4 * core:4 * core + 4, 0] = nk_.transpose(2, 1, 0).reshape(4, 256, 2, 128)
        nv_ = np.asarray(r["nv"], dtype=f32)
        new_v[4 * core:4 * core + 4, 0] = nv_.transpose(1, 0, 2).reshape(4, 256, 2, 128)
    return (y_prompt, y_sample, new_k, new_v)
```

```python
import contextlib
import os
import numpy as np
import concourse.bass as bass
import concourse.mybir as mybir
from concourse.bass_utils import run_bass_kernel_spmd

F32 = mybir.dt.float32
BF16 = mybir.dt.bfloat16
AF = mybir.ActivationFunctionType
ALU = mybir.AluOpType

D = 1024
DFF = 2816
NK = 8
T = 2048
TBS = 512
NTB = 4
ALPHA = 2.0 ** 0.25
LN_EPS = 1e-6
RMS_EPS = 1e-6
SCALE = 128.0 ** -0.5
WINS = (2, 4, 8, 16)
NCORES = 8
AGW = 4096
ARENA_WORDS = 53200
SLOT_ELEMS = 9216


class Prod:
    def __init__(self, sem, step):
        self.sem = sem
        self.step = step
        self.count = 0


class Buf:
    __slots__ = ("w", "r", "name")

    def __init__(self, name=""):
        self.w = None
        self.r = {}
        self.name = name


class Eng:
    def __init__(self, eng, prod):
        self.eng = eng
        self.prod = prod
        self.seen = {}
        self.self_sync = False

    def wait(self, t):
        prod, val = t
        if prod is self.prod and not self.self_sync:
            return
        if self.seen.get(prod, 0) >= val:
            return
        self.eng.wait_ge(prod.sem, val)
        self.seen[prod] = val


class Region:
    def __init__(self, start, end):
        self.start, self.end, self.top = start, end, start

    def reset(self):
        self.top = self.start


class K:
    def __init__(self, nc, st):
        self.nc = nc
        self.st = st
        self.E = {}
        for name, eng in (("pe", nc.tensor), ("act", nc.scalar), ("dve", nc.vector),
                          ("pool", nc.gpsimd), ("sp", nc.sync)):
            self.E[name] = Eng(eng, self.new_prod("e_" + name, 1))
            self.E[name].self_sync = name in ("act", "dve", "pool")
        self.dprods = []
        self.arena_t = st.enter_context(nc.sbuf_tensor("arena", [128, ARENA_WORDS], F32))
        self.arena = self.arena_t[:, :]
        self.ps_t = st.enter_context(nc.psum_tensor("ps", [128, 8, 512], F32))
        self.bank = [Buf("bank%d" % i) for i in range(8)]
        self.jobs = []
        self.job_next = 0
        self.job_cur = 0
        self.job_slot = {}

    def new_prod(self, name, step):
        sem = self.st.enter_context(self.nc.semaphore(name))
        return Prod(sem, step)

    def dprod(self, name):
        p = self.new_prod("d_" + name, 16)
        self.dprods.append(p)
        return p

    def _deps(self, E, reads, writes):
        for b in reads:
            if b.w is not None:
                E.wait(b.w)
        for b in writes:
            if b.w is not None:
                E.wait(b.w)
            for p, v in b.r.items():
                E.wait((p, v))

    def _commit(self, t, reads, writes):
        p, v = t
        for b in reads:
            if b.r.get(p, 0) < v:
                b.r[p] = v
        for b in writes:
            b.w = t
            b.r = {}

    def op(self, en, fn, reads=(), writes=(), pre=()):
        E = self.E[en]
        self._deps(E, reads, writes)
        for f in pre:
            f()
        ins = fn()
        E.prod.count += 1
        ins.then_inc(E.prod.sem, 1)
        t = (E.prod, E.prod.count)
        self._commit(t, reads, writes)
        return t

    def dma(self, qn, out, in_, prod, reads=(), writes=(), nodep=False):
        E = self.E[qn]
        if not nodep:
            self._deps(E, reads, writes)
        ins = E.eng.dma_start(out=out, in_=in_)
        prod.count += 16
        ins.then_inc(prod.sem, 16)
        t = (prod, prod.count)
        self._commit(t, reads, writes)
        return t

    def barrier(self):
        ticks = [(e.prod, e.prod.count) for e in self.E.values() if e.prod.count > 0]
        ticks += [(p, p.count) for p in self.dprods if p.count > 0]
        for e in self.E.values():
            for t in ticks:
                e.wait(t)

    def psb(self, i):
        return self.ps_t[:, i, :]

    def f32v(self, off_bytes, n):
        o = off_bytes // 4
        return self.arena[:, o:o + n]

    def bfv(self, off_bytes, n):
        o = off_bytes // 4
        return self.arena[:, o:o + n // 2].bitcast(BF16)

    def alloc(self, reg, nbytes):
        nbytes = (nbytes + 63) // 64 * 64
        off = reg.top
        reg.top += nbytes
        assert reg.top <= reg.end, ("region overflow", reg.start, reg.end, reg.top)
        return off

    def pe_group(self, out_ap, pairs, reads, writes):
        n = len(pairs)
        nc = self.nc
        pre = [(lambda l=l, r=r, i=i: nc.tensor.matmul(out_ap, lhsT=l, rhs=r, start=(i == 0), stop=False))
               for i, (l, r) in enumerate(pairs[:-1])]
        l, r = pairs[-1]
        return self.op("pe", lambda: nc.tensor.matmul(out_ap, lhsT=l, rhs=r, start=(n == 1), stop=True),
                       reads, writes, pre=pre)

    def pe_mm(self, out_ap, l, r, start, stop, reads, writes):
        nc = self.nc
        return self.op("pe", lambda: nc.tensor.matmul(out_ap, lhsT=l, rhs=r, start=start, stop=stop),
                       reads, writes)

    def act(self, out, in_, func, reads, writes, bias=None, scale=None):
        nc = self.nc
        kw = {}
        if bias is not None:
            kw["bias"] = bias
        if scale is not None:
            kw["scale"] = scale
        return self.op("act", lambda: nc.scalar.activation(out=out, in_=in_, func=func, **kw), reads, writes)

    def tt(self, out, in0, in1, op, reads, writes):
        nc = self.nc
        return self.op("dve", lambda: nc.vector.tensor_tensor(out=out, in0=in0, in1=in1, op=op), reads, writes)

    def ts(self, out, in0, s1, op0, reads, writes, s2=None, op1=None):
        nc = self.nc
        if op1 is None:
            return self.op("dve", lambda: nc.vector.tensor_scalar(out=out, in0=in0, scalar1=s1, scalar2=None,
                                                                  op0=op0), reads, writes)
        return self.op("dve", lambda: nc.vector.tensor_scalar(out=out, in0=in0, scalar1=s1, scalar2=s2,
                                                              op0=op0, op1=op1), reads, writes)

    def stt(self, out, in0, scalar, in1, op0, op1, reads, writes):
        nc = self.nc
        return self.op("dve", lambda: nc.vector.scalar_tensor_tensor(out=out, in0=in0, scalar=scalar, in1=in1,
                                                                     op0=op0, op1=op1), reads, writes)

    def recip(self, out, in_, reads, writes):
        nc = self.nc
        return self.op("dve", lambda: nc.vector.reciprocal(out=out, in_=in_), reads, writes)

    def vcopy(self, out, in_, reads, writes):
        nc = self.nc
        return self.op("dve", lambda: nc.vector.tensor_copy(out=out, in_=in_), reads, writes)

    def memset(self, out, val, writes):
        nc = self.nc
        return self.op("dve", lambda: nc.vector.memset(out, val), (), writes)

    def add_job(self, fn):
        self.jobs.append(fn)

    def prefetch(self, si):
        if self.job_next < len(self.jobs):
            self.jobs[self.job_next](si)
            self.job_slot[self.job_next] = si
            self.job_next += 1

    def take(self):
        assert self.job_cur < self.job_next, "job not prefetched"
        si = self.job_slot[self.job_cur]
        self.job_cur += 1
        return si


class _Stop(Exception):
    pass


def build_program(stop=None):
    nc = bass.Bass("TRN2", target_bir_lowering=False)
    dr = {}

    def din(name, shape, dt=F32):
        dr[name] = nc.dram_tensor(name, list(shape), dt, kind="ExternalInput").ap()
        return dr[name]

    def dout(name, shape, dt=F32):
        dr[name] = nc.dram_tensor(name, list(shape), dt, kind="ExternalOutput").ap()
        return dr[name]

    xT = din("xT", [128, NK, T])
    condT = din("condT", [128, 16])
    w_mod = din("w_mod", [D, 9 * D])
    bmodT = din("bmodT", [128, 72])
    lngT = din("lngT", [128, 24])
    lnbT = din("lnbT", [128, 24])
    ffn1_w1 = din("ffn1_w1", [128, 2, 8 * DFF])
    ffn1_w2 = din("ffn1_w2", [DFF, D])
    ffn2_w1 = din("ffn2_w1", [128, 2, 8 * DFF])
    ffn2_w2 = din("ffn2_w2", [DFF, D])
    w_in = din("w_in", [D, 4096])
    qkg = din("qkg", [128, 2])
    pool_w = din("pool_w", [128, 512])
    pscaleT = din("pscaleT", [128, 4])
    w_up_attn = din("w_up_attn", [D, D])
    w_up_pool = din("w_up_pool", [512, D])
    w_out = din("w_out", [D, D])
    kctx = din("kctx", [128, 512])
    vctx = din("vctx", [128, 512])
    ropeCS = din("ropeCS", [128, 2048])
    pmat = din("pmat", [128, 128])
    masks = din("masks", [128, 8])
    corr = din("corr", [128, 128])
    esel = din("esel", [128, 128])
    yT = dout("yT", [128, NK, T])
    nk = dout("nk", [128, 2, 1024])
    nv = dout("nv", [128, 8, 256])
    ag_in = nc.dram_tensor("ag_in", [128, AGW], BF16)
    ag_out = nc.dram_tensor("ag_out", [512, AGW], BF16)
    ag_in_ap = ag_in.ap()
    ag_out_ap = ag_out.ap()
    dr["ag2_in"] = nc.dram_tensor("ag2_in", [128, 128], BF16).ap()
    dr["ag2_out"] = nc.dram_tensor("ag2_out", [512, 128], BF16).ap()

    with contextlib.ExitStack() as st:
        k = K(nc, st)
        k.stop = stop
        try:
            _emit(k, nc, dr, ag_in, ag_out, ag_in_ap, ag_out_ap)
        except _Stop:
            pass
    return nc


def _emit(k, nc, dr, ag_in, ag_out, ag_in_ap, ag_out_ap):
    ps = k.psb
    bank = k.bank
    SKIP = set(os.environ.get("KSKIP", "").split(","))
    G = Region(0, ARENA_WORDS * 4)
    oR = k.alloc(G, NK * T * 4)
    oSlot = [k.alloc(G, SLOT_ELEMS * 2), k.alloc(G, SLOT_ELEMS * 2)]
    oOnesLn = k.alloc(G, 256)
    oOnesH = k.alloc(G, 256)
    oOnes1 = k.alloc(G, 256)
    oPm = k.alloc(G, 256)
    oModp = k.alloc(G, 144 * 4)
    oBmod = k.alloc(G, 72 * 4)
    oLng = k.alloc(G, 24 * 4)
    oLnb = k.alloc(G, 24 * 4)
    oCond = k.alloc(G, 16 * 4)
    oScT = k.alloc(G, 16 * 2)
    oQkg = k.alloc(G, 2 * 4)
    oPsc = k.alloc(G, 4 * 4)
    oMask = k.alloc(G, 8 * 4)
    oCorr = k.alloc(G, 128 * 4)
    oPoolW = k.alloc(G, 512 * 2)
    oDer = k.alloc(G, 32 * 8 * 4)
    gend = G.top
    RH = Region(gend, gend + 32768)
    RQ = Region(RH.end, RH.end + 32768)
    RP2 = Region(RQ.end, RQ.end + 16384)
    RX = Region(RP2.end, ARENA_WORDS * 4)
    RHQ = Region(RH.start, RQ.end)
    RQX = Region(RQ.start, RX.end)
    assert RX.end - RX.start >= 20000, (RX.start, RX.end)

    R3 = k.f32v(oR, NK * T).rearrange("p (k t) -> p k t", k=NK)
    Rbk = [[Buf("R%d_%d" % (i, j)) for j in range(NK)] for i in range(NTB)]
    Rprod = [k.dprod("R%d" % i) for i in range(NTB)]
    slot = [k.bfv(o, SLOT_ELEMS) for o in oSlot]
    slotb = [Buf("slot0"), Buf("slot1")]
    slotb2 = [Buf("slot0w2"), Buf("slot1w2")]
    k.stage = None
    slotp = [k.dprod("slot0"), k.dprod("slot1")]
    slotp2 = [k.dprod("slot0w2"), k.dprod("slot1w2")]
    ones_ln = k.bfv(oOnesLn, 128)
    ones_h = k.bfv(oOnesH, 128)
    ones1 = k.bfv(oOnes1, 128)
    Pm = k.bfv(oPm, 128)
    modp = k.f32v(oModp, 144)
    modp3 = modp.rearrange("p (j g) -> p j g", g=2)
    modp4 = modp.rearrange("p (i k g) -> p i k g", i=9, k=8)
    bmod = k.f32v(oBmod, 72)
    lng = k.f32v(oLng, 24)
    lnb = k.f32v(oLnb, 24)
    cond = k.f32v(oCond, 16)
    scT = k.bfv(oScT, 16)
    qkg = k.f32v(oQkg, 2)
    psc = k.f32v(oPsc, 4)
    mask = k.f32v(oMask, 8)
    corr = k.f32v(oCorr, 128)
    poolw = k.bfv(oPoolW, 512)
    der = k.f32v(oDer, 256)
    constb = Buf("const")
    cprod = k.dprod("const")
    cprod2 = k.dprod("const2")
    derb = Buf("der")

    dcol = {}

    def DER(name, g=None):
        key = (name, g)
        if key not in dcol:
            dcol[key] = len(dcol)
            assert len(dcol) <= 32
        c = dcol[key] * 8
        return der[:, c:c + 8]

    def DERk(name, g, kk):
        c = dcol[(name, g)] * 8 + kk
        return der[:, c:c + 1]

    def finalize():
        sp = k.E["sp"]
        for p in k.dprods:
            if p.count > 0:
                sp.wait((p, p.count))
        for e in k.E.values():
            if e.prod.count > 0:
                sp.wait((e.prod, e.prod.count))

    dbgp = k.dprod("dbg")

    def cut(stage):
        if k.stop is not None and k.stop == stage:
            k.barrier()
            for tb in range(NTB):
                k.dma("sp", dr["yT"][:, :, tb * TBS:(tb + 1) * TBS], R3[:, :, tb * TBS:(tb + 1) * TBS], dbgp, Rbk[tb], [])
            finalize()
            raise _Stop()

    def job_wmod(i):
        def f(si):
            k.dma("pool", slot[si][:, 0:8192].rearrange("p (k c) -> p k c", k=8),
                  dr["w_mod"][:, i * 1024:(i + 1) * 1024].rearrange("(k p) c -> p k c", p=128),
                  slotp[si], (), [slotb[si], slotb2[si]])
        return f

    PIECES = [(0, 2)] + [(c0, 3) for c0 in range(2, 20, 3)] + [(20, 2)]

    def job_ffn(w1, w2, c0, n):
        def f(si):
            s = slot[si]
            o0 = 8 * 128 * c0
            k.dma("pool", s[:, 0:3072].rearrange("p (k c) -> p k c", k=8)[:, :, 0:n * 128],
                  w1[:, 0, o0:o0 + 8 * n * 128].rearrange("p (k c) -> p k c", k=8),
                  slotp[si], (), [slotb[si]])
            k.dma("pool", s[:, 3072:6144].rearrange("p (k c) -> p k c", k=8)[:, :, 0:n * 128],
                  w1[:, 1, o0:o0 + 8 * n * 128].rearrange("p (k c) -> p k c", k=8),
                  slotp[si], (), [slotb[si]], nodep=True)
            if k.stage is None:
                k.dma("pool", s[:, 6144:6144 + n * 1024].rearrange("p (j c) -> p j c", j=n),
                      w2[c0 * 128:(c0 + n) * 128, :].rearrange("(j p) c -> p j c", p=128),
                      slotp2[si], (), [slotb2[si]])
            else:
                stg, stgb, stgp = k.stage[si]
                k.dma("sp", stg[:, 0:n * 1024].rearrange("p (j c) -> p j c", j=n),
                      w2[c0 * 128:(c0 + n) * 128, :].rearrange("(j p) c -> p j c", p=128),
                      stgp, (), [stgb])
                for j in range(n):
                    k.act(s[:, 6144 + j * 1024:6144 + (j + 1) * 1024], stg[:, j * 1024:(j + 1) * 1024], AF.Copy,
                          [stgb], [slotb2[si]])
        return f

    def job_win(col0):
        def f(si):
            k.dma("pool", slot[si][:, 0:8192].rearrange("p (k c) -> p k c", k=8),
                  dr["w_in"][:, col0:col0 + 1024].rearrange("(k p) c -> p k c", p=128),
                  slotp[si], (), [slotb[si], slotb2[si]])
        return f

    def job_g1(ocp):
        def f(si):
            s = slot[si]
            c0 = ocp * 256
            k.dma("pool", s[:, 0:2048].rearrange("p (k c) -> p k c", k=8),
                  dr["w_in"][:, 2048 + c0:2048 + c0 + 256].rearrange("(k p) c -> p k c", p=128),
                  slotp[si], (), [slotb[si], slotb2[si]])
            k.dma("pool", s[:, 2048:4096].rearrange("p (k c) -> p k c", k=8),
                  dr["w_in"][:, 3072 + c0:3072 + c0 + 256].rearrange("(k p) c -> p k c", p=128),
                  slotp[si], (), [slotb[si]], nodep=True)
            k.dma("pool", s[:, 4096:6144].rearrange("p (k c) -> p k c", k=8),
                  dr["w_up_attn"][:, c0:c0 + 256].rearrange("(k p) c -> p k c", p=128),
                  slotp[si], (), [slotb[si]], nodep=True)
            k.dma("pool", s[:, 6144:7168].rearrange("p (k c) -> p k c", k=4),
                  dr["w_up_pool"][:, c0:c0 + 256].rearrange("(k p) c -> p k c", p=128),
                  slotp[si], (), [slotb[si]], nodep=True)
        return f

    def job_wout():
        def f(si):
            k.dma("pool", slot[si][:, 0:8192].rearrange("p (k c) -> p k c", k=8),
                  dr["w_out"][:, :].rearrange("(k p) c -> p k c", p=128),
                  slotp[si], (), [slotb[si], slotb2[si]])
        return f

    for i in range(2):
        k.add_job(job_wmod(i))
    for pi, (c0, n) in enumerate(PIECES):
        k.add_job(job_ffn(dr["ffn1_w1"], dr["ffn1_w2"], c0, n))
        if pi == 0:
            k.add_job(job_wmod(2))
        if pi == 3:
            k.add_job(job_wmod(3))
        if pi == 5:
            k.add_job(job_wmod(4))
    k.add_job(job_win(1024))
    k.add_job(job_win(0))
    for i in range(5, 9):
        k.add_job(job_wmod(i))
    for ocp in range(4):
        k.add_job(job_g1(ocp))
    k.add_job(job_wout())
    for (c0, n) in PIECES:
        k.add_job(job_ffn(dr["ffn2_w1"], dr["ffn2_w2"], c0, n))

    for (dst, src) in ((cond, "condT"), (bmod, "bmodT"), (lng, "lngT"), (lnb, "lnbT"), (qkg, "qkg"),
                       (psc, "pscaleT"), (mask, "masks"), (corr, "corr")):
        k.dma("sp", dst, dr[src][:, :], cprod, (), [constb])
    k.dma("pool", Pm, dr["pmat"][:, :], cprod2, (), [constb])
    k.dma("pool", poolw, dr["pool_w"][:, :], cprod2, (), [constb])
    k.prefetch(0)
    k.prefetch(1)
    for tb in range(NTB):
        k.dma("sp", R3[:, :, tb * TBS:(tb + 1) * TBS], dr["xT"][:, :, tb * TBS:(tb + 1) * TBS],
              Rprod[tb], (), Rbk[tb])
    onesb = Buf("ones")
    k.memset(ones_ln, 1.0 / 1024.0, [onesb])
    k.memset(ones_h, 1.0 / 128.0, [onesb])
    k.memset(ones1, 1.0, [onesb])
    scb = Buf("scT")
    k.act(scT, cond, AF.Silu, [constb], [scb])
    def mod_compute(i, bk):
        si = k.take()
        for j in range(8):
            col = (i * 8 + j) * 2
            for kk in range(8):
                nc_l = slot[si][:, kk * 1024 + j * 128: kk * 1024 + (j + 1) * 128]
                if kk == 0 and j == 0:
                    k._deps(k.E["pe"], [slotb[si], scb], [bank[bk]])
                ins = nc.tensor.matmul(ps(bk)[:, col:col + 2], lhsT=nc_l, rhs=scT[:, kk * 2:(kk + 1) * 2],
                                       start=(kk == 0), stop=(kk == 7))
        E = k.E["pe"]
        E.prod.count += 1
        ins.then_inc(E.prod.sem, 1)
        tk = (E.prod, E.prod.count)
        k._commit(tk, [slotb[si], scb], [bank[bk]])
        k.prefetch(si)

    def mod_finish(i0, i1, bk):
        psm3 = ps(bk)[:, 0:144].rearrange("p (j g) -> p j g", g=2)
        for g in range(2):
            k.tt(modp3[:, i0 * 8:i1 * 8, g], psm3[:, i0 * 8:i1 * 8, g], bmod[:, i0 * 8:i1 * 8], ALU.add,
                 [bank[bk], constb], [derb])

    for i in range(2):
        mod_compute(i, 0)
    mod_finish(0, 2, 0)

    def M(i, g):
        return modp4[:, i, :, g]

    for g in range(2):
        k.ts(DER("A1", g), M(1, g), 1.0, ALU.add, [], [derb])
        k.vcopy(DER("SH1", g), M(0, g), [], [derb])
        DER("HG1", g)
    k.ts(DER("RS1"), lng[:, 0:8], ALPHA, ALU.mult, [constb], [derb])
    k.ts(DER("RB1"), lnb[:, 0:8], ALPHA, ALU.mult, [constb], [derb])
    k.ts(DER("RS2"), lng[:, 8:16], ALPHA, ALU.mult, [constb], [derb])
    k.ts(DER("RB2"), lnb[:, 8:16], ALPHA, ALU.mult, [constb], [derb])
    k.vcopy(DER("RS3"), lng[:, 16:24], [constb], [derb])
    k.vcopy(DER("RB3"), lnb[:, 16:24], [constb], [derb])
    for g in range(2):
        for nm in ("H2S", "H2B", "G2", "T3", "H3S", "H3B", "HG3"):
            DER(nm, g)

    def derive_mix():
      for g in range(2):
        k.ts(DER("H2S", g), M(4, g), 1.0, ALU.add, [], [derb], s2=1.0 / ALPHA, op1=ALU.mult)
        k.vcopy(DER("H2B", g), M(3, g), [], [derb])

    def derive_late():
      for g in range(2):
        k.vcopy(DER("G2", g), M(5, g), [], [derb])
        k.ts(DER("T3", g), M(7, g), 1.0, ALU.add, [], [derb])
        k.tt(DER("H3S", g), DER("T3", g), lng[:, 8:16], ALU.mult, [constb], [derb])
        k.tt(DER("H3B", g), DER("T3", g), lnb[:, 8:16], ALU.mult, [constb], [derb])
        k.tt(DER("H3B", g), DER("H3B", g), M(6, g), ALU.add, [], [derb])
        k.ts(DER("HG3", g), M(8, g), 0.5, ALU.mult, [], [derb])

    oH = k.alloc(RH, NK * T * 2)
    H3 = k.bfv(oH, NK * T).rearrange("p (k t) -> p k t", k=NK)
    Hb = [Buf("H%d" % i) for i in range(NTB)]
    yprod = [k.dprod("y%d" % i) for i in range(NTB)]

    def ln_alloc(reg):
        o = {}
        o["yb"] = [k.bfv(k.alloc(reg, 1024), 512) for _ in range(NK)]
        o["ysq"] = [k.bfv(k.alloc(reg, 1024), 512) for _ in range(NK)]
        o["ybb"] = [Buf() for _ in range(NK)]
        o["ysqb"] = [Buf() for _ in range(NK)]
        o["mean"] = [k.f32v(k.alloc(reg, 2048), 512) for _ in range(2)]
        o["tmp"] = k.f32v(k.alloc(reg, 2048), 512)
        o["rstd"] = [k.f32v(k.alloc(reg, 2048), 512) for _ in range(2)]
        o["meanb"], o["tmpb"], o["rstdb"] = [Buf(), Buf()], Buf(), [Buf(), Buf()]
        return o

    def ln_stats_chunk(tb, L, kk):
        tsl = slice(tb * TBS, (tb + 1) * TBS)
        k.act(L["yb"][kk], R3[:, kk, tsl], AF.Copy, [Rbk[tb][kk]], [L["ybb"][kk]])
        k.act(L["ysq"][kk], R3[:, kk, tsl], AF.Square, [Rbk[tb][kk]], [L["ysqb"][kk]])

    def ln_pe_stats(tb, L, b6=6, b7=7):
        for kk in range(NK):
            k.pe_mm(ps(b6), ones_ln, L["yb"][kk], kk == 0, kk == NK - 1, [L["ybb"][kk], onesb], [bank[b6]])
        for kk in range(NK):
            k.pe_mm(ps(b7), ones_ln, L["ysq"][kk], kk == 0, kk == NK - 1, [L["ysqb"][kk], onesb], [bank[b7]])

    def ln_fin_a(tb, L, b6=6, b7=7):
        par = tb % 2
        k.vcopy(L["mean"][par], ps(b6), [bank[b6]], [L["meanb"][par]])
        k.tt(L["tmp"], L["mean"][par], L["mean"][par], ALU.mult, [L["meanb"][par]], [L["tmpb"]])
        k.tt(L["tmp"], ps(b7), L["tmp"], ALU.subtract, [bank[b7]], [L["tmpb"]])
        k.act(L["rstd"][par], L["tmp"], AF.Ln, [L["tmpb"]], [L["rstdb"][par]], bias=LN_EPS, scale=1.0)
        k.act(L["rstd"][par], L["rstd"][par], AF.Exp, [], [L["rstdb"][par]], scale=-0.5)

    def ln_fin_b1(tb, L):
        par = tb % 2
        tsl = slice(tb * TBS, (tb + 1) * TBS)
        Rt = R3[:, :, tsl]
        k.tt(Rt, Rt, L["mean"][par].unsqueeze(1).to_broadcast([128, NK, TBS]), ALU.subtract,
             [L["meanb"][par]], Rbk[tb])

    def ln_fin_b2(tb, L, rs, rb, hs=None, hb=None, final=False):
        par = tb % 2
        g = tb // 2
        tsl = slice(tb * TBS, (tb + 1) * TBS)
        Rt = R3[:, :, tsl]
        k.tt(Rt, Rt, L["rstd"][par].unsqueeze(1).to_broadcast([128, NK, TBS]), ALU.mult,
             [L["rstdb"][par]], Rbk[tb])
        for kk in range(NK):
            if hs is not None:
                k.ts(H3[:, kk, tsl], R3[:, kk, tsl], DERk(hs, g, kk), ALU.mult, [Rbk[tb][kk], derb], [Hb[tb]],
                     s2=DERk(hb, g, kk), op1=ALU.add)
            if hs is None and kk % 2 == 1:
                k.ts(R3[:, kk, tsl], R3[:, kk, tsl], DERk(rs, None, kk), ALU.mult, [derb], [Rbk[tb][kk]],
                     s2=DERk(rb, None, kk), op1=ALU.add)
            else:
                k.act(R3[:, kk, tsl], R3[:, kk, tsl], AF.Identity, [derb], [Rbk[tb][kk]],
                      bias=DERk(rb, None, kk), scale=DERk(rs, None, kk))
        if final:
            k.dma("sp", dr["yT"][:, :, tsl], R3[:, :, tsl], yprod[tb], Rbk[tb], [])

    def ln_finish(tb, L, rs, rb, hs=None, hb=None, final=False, b6=6, b7=7):
        ln_fin_a(tb, L, b6, b7)
        ln_fin_b1(tb, L)
        ln_fin_b2(tb, L, rs, rb, hs=hs, hb=hb, final=final)

    stgprods = [yprod[0], yprod[1]]

    def ffn_alloc():
        reg = RQX
        reg.reset()
        o = {}
        o["gbuf"] = [[k.bfv(k.alloc(reg, 1024), 512) for _ in range(3)] for _ in range(2)]
        o["gb"] = [[Buf() for _ in range(3)] for _ in range(2)]
        o["sa"] = [k.f32v(k.alloc(reg, 2048), 512) for _ in range(2)]
        o["sab"] = [Buf(), Buf()]
        o["Ln"] = ln_alloc(reg)
        o["stage"] = [(k.f32v(k.alloc(reg, 12288), 3072), Buf("stg%d" % i), stgprods[i]) for i in range(2)]
        return o

    def ffn_phase(hgname, L, ln_args, between=None, pre_stt=None):
        fb = ffn_alloc() if L is None else L
        gbuf, gb, sa, sab, Ln = fb["gbuf"], fb["gb"], fb["sa"], fb["sab"], fb["Ln"]
        k.stage = fb["stage"]
        cnt = 0
        cntf = 0
        gi = 0
        for pi, (c0, n) in enumerate(PIECES):
            si = k.take()
            s = slot[si]
            last = (pi == len(PIECES) - 1)
            for tb in range(NTB):
                g = tb // 2
                tsl = slice(tb * TBS, (tb + 1) * TBS)
                gs = gi % 2
                gi += 1
                for j in range(n):
                    a, b2 = cnt % 2, 2 + cnt % 2
                    cnt += 1
                    k.pe_group(ps(a), [(s[:, kk * 384 + j * 128: kk * 384 + (j + 1) * 128], H3[:, kk, tsl])
                                       for kk in range(NK)], [slotb[si], Hb[tb]], [bank[a]])
                    k.pe_group(ps(b2), [(s[:, 3072 + kk * 384 + j * 128: 3072 + kk * 384 + (j + 1) * 128],
                                         H3[:, kk, tsl]) for kk in range(NK)], [slotb[si], Hb[tb]], [bank[b2]])
                    k.act(sa[a], ps(a), AF.Silu, [bank[a]], [sab[a]])
                    k.tt(gbuf[gs][j], sa[a], ps(b2), ALU.mult, [sab[a], bank[b2]], [gb[gs][j]])
                if last and tb >= 1:
                    ln_fin_b1(tb - 1, Ln)
                if pre_stt is not None and pi == 0 and tb == 0:
                    pre_stt()
                for oc in range(NK):
                    f = 4 + cntf % 2
                    cntf += 1
                    k.pe_group(ps(f), [(s[:, 6144 + j * 1024 + oc * 128: 6144 + j * 1024 + (oc + 1) * 128],
                                        gbuf[gs][j]) for j in range(n)],
                               [slotb2[si]] + [gb[gs][j] for j in range(n)], [bank[f]])
                    k.stt(R3[:, oc, tsl], ps(f), DERk(hgname, g, oc), R3[:, oc, tsl], ALU.mult, ALU.add,
                          [bank[f], derb], [Rbk[tb][oc]])
                    if last:
                        ln_stats_chunk(tb, Ln, oc)
                if last:
                    if tb >= 1:
                        ln_fin_b2(tb - 1, Ln, *ln_args[0], **ln_args[1])
                    ln_pe_stats(tb, Ln)
                    ln_fin_a(tb, Ln)
                    if tb == NTB - 1:
                        ln_fin_b1(tb, Ln)
                        ln_fin_b2(tb, Ln, *ln_args[0], **ln_args[1])
            if pi + 2 >= len(PIECES):
                k.stage = None
            k.prefetch(si)
            if between is not None:
                between(pi)

    for tb in range(NTB):
        g = tb // 2
        tsl = slice(tb * TBS, (tb + 1) * TBS)
        for kk in range(NK):
            k.act(H3[:, kk, tsl], R3[:, kk, tsl], AF.Identity, [Rbk[tb][kk], derb], [Hb[tb]],
                  bias=DERk("SH1", g, kk), scale=DERk("A1", g, kk))
        k.act(R3[:, :, tsl], R3[:, :, tsl], AF.Identity, [], Rbk[tb], scale=ALPHA)

    cut(0)
    def ffn1_between(pi):
        if pi == 3:
            mod_compute(3, 6)
        if pi == 5:
            mod_compute(4, 6)
            mod_finish(3, 5, 6)
            derive_mix()

    def ffn1_pre_stt():
        mod_compute(2, 6)
        mod_finish(2, 3, 6)
        for g in range(2):
            k.ts(DER("HG1", g), M(2, g), 0.5, ALU.mult, [], [derb])

    ffn_phase("HG1", None, (("RS1", "RB1"), {}), between=ffn1_between, pre_stt=ffn1_pre_stt)
    k.barrier()
    cut(1)

    RH.reset(); RQ.reset(); RP2.reset(); RX.reset(); RHQ.reset()
    pooled2 = k.bfv(k.alloc(RP2, 4 * T * 2), 4 * T).rearrange("p (g t) -> p g t", g=4)
    p2b = [Buf("p2_%d" % i) for i in range(NTB)]
    oKp = k.alloc(RX, 2 * 1024 * 2)
    oVp = k.alloc(RX, 8 * 256 * 2)
    Kp = k.bfv(oKp, 2048).rearrange("p (h t) -> p h t", h=2)
    Vp = k.bfv(oVp, 2048).rearrange("p (j c) -> p j c", j=8)
    Kpb = [Buf(), Buf()]
    Vpb = [Buf(), Buf()]
    xmark = RX.top

    A = RHQ
    h2t = k.bfv(k.alloc(A, 8192), 4096).rearrange("p (k t) -> p k t", k=NK)
    h2b = [Buf("h2t%d" % i) for i in range(NK)]
    sq = [k.bfv(k.alloc(A, 1024), 512) for _ in range(2)]
    sqb = [Buf(), Buf()]
    sd = [k.f32v(k.alloc(A, 2048), 512) for _ in range(2)]
    sdb = [Buf(), Buf()]
    kn = [k.f32v(k.alloc(A, 2048), 512) for _ in range(2)]
    knb_ = [Buf(), Buf()]
    knbf = k.bfv(k.alloc(A, 1024), 512)
    knbfb = Buf()
    t1 = k.f32v(k.alloc(A, 2048), 512)
    t2 = k.f32v(k.alloc(A, 2048), 512)
    t1b, t2b = Buf(), Buf()
    kst = [k.bfv(k.alloc(A, 1024), 512) for _ in range(2)]
    kstb = [Buf(), Buf()]
    kstp = [k.dprod("kst0"), k.dprod("kst1")]
    knp = [k.dprod("kn0"), k.dprod("kn1")]
    vst = [k.f32v(k.alloc(A, 1024), 256) for _ in range(2)]
    vstb = [Buf(), Buf()]
    vstp = [k.dprod("vst0"), k.dprod("vst1")]
    vsb = [k.bfv(k.alloc(A, 512), 256) for _ in range(2)]
    vsbb = [Buf(), Buf()]
    vsbp = [k.dprod("vsb0"), k.dprod("vsb1")]
    ropeC = k.f32v(k.alloc(RX, 4096), 1024)
    ropeS = k.f32v(k.alloc(RX, 4096), 1024)
    ropeb = Buf("rope")
    ropep = k.dprod("rope")
    pp = [k.f32v(k.alloc(A, 2176), 544).rearrange("p (b t) -> p b t", b=2) for _ in range(2)]
    ppb = [Buf(), Buf()]
    pin_s = k.f32v(k.alloc(A, 4 * 1040 * 4), 4 * 1040).rearrange("p (g t) -> p g t", g=4)
    pinb = [Buf("pin_s%d" % i) for i in range(4)]
    ta = k.f32v(k.alloc(A, 4160), 1040)
    tb_ = k.f32v(k.alloc(A, 4160), 1040)
    tab, tbb = Buf(), Buf()
    plb = [k.bfv(k.alloc(A, 1024), 512) for _ in range(2)]
    plbb = [Buf(), Buf()]
    e32 = k.f32v(k.alloc(A, 256), 64).rearrange("p (g i) -> p g i", g=4)
    est = k.bfv(k.alloc(A, 256), 128)
    estb = Buf()
    estp = k.dprod("est")
    eg = k.bfv(k.alloc(A, 1024), 512).rearrange("p (r c) -> p r c", r=4)
    egb = Buf()
    egp = k.dprod("eg")
    ef = k.f32v(k.alloc(A, 1024), 256).rearrange("p (r c) -> p r c", r=4)
    efb = Buf()
    agb = Buf("ag_in")
    agob = Buf("ag_out")
    ag2b = Buf("ag2_in")
    ag2ob = Buf("ag2_out")
    ccprod = k.new_prod("cc", 1)
    k.dprods.append(ccprod)

    k.dma("sp", ropeC, dr["ropeCS"][:, 0:1024], ropep, (), [ropeb])
    k.dma("sp", ropeS, dr["ropeCS"][:, 1024:2048], ropep, (), [ropeb])
    for i in range(2):
        k.memset(pp[i], 0.0, [ppb[i]])
    k.memset(pin_s, 0.0, pinb)

    def mod2(tb, h2t_=None, h2b_=None, on_dve=False):
        g = tb // 2
        tsl = slice(tb * TBS, (tb + 1) * TBS)
        h2t_ = h2t if h2t_ is None else h2t_
        h2b_ = h2b if h2b_ is None else h2b_
        for kk in range(NK):
            if on_dve:
                k.ts(h2t_[:, kk, :], R3[:, kk, tsl], DERk("H2S", g, kk), ALU.mult, [Rbk[tb][kk], derb],
                     [h2b_[kk]], s2=DERk("H2B", g, kk), op1=ALU.add)
                continue
            k.act(h2t_[:, kk, :], R3[:, kk, tsl], AF.Identity, [Rbk[tb][kk], derb], [h2b_[kk]],
                  bias=DERk("H2B", g, kk), scale=DERk("H2S", g, kk))

    cnts = {"p": 0, "m": 0, "r": 0, "x": 0}

    def rms_head(si, col0, gcol, tb, sample, out_bf, out_bufs, kout=None):
        s = slot[si]
        pb = cnts["p"] % 3
        cnts["p"] += 1
        mb = 3 + cnts["m"] % 2
        cnts["m"] += 1
        x = cnts["x"] % 2
        cnts["x"] += 1
        k.pe_group(ps(pb), [(s[:, kk * 1024 + col0: kk * 1024 + col0 + 128], h2t[:, kk, :]) for kk in range(NK)],
                   [slotb[si]] + h2b, [bank[pb]])
        k.act(sq[x], ps(pb), AF.Square, [bank[pb]], [sqb[x]])
        k.pe_mm(ps(mb), ones_h, sq[x], True, True, [sqb[x], onesb], [bank[mb]])
        k.act(sd[x], ps(mb), AF.Ln, [bank[mb]], [sdb[x]], bias=RMS_EPS, scale=1.0)
        k.act(sd[x], sd[x], AF.Exp, [], [sdb[x]], scale=-0.5)
        if not sample:
            if kout is None:
                k.stt(out_bf, ps(pb), qkg[:, gcol:gcol + 1], sd[x], ALU.mult, ALU.mult,
                      [bank[pb], sdb[x], constb], out_bufs)
            else:
                k.stt(kn[x], ps(pb), qkg[:, gcol:gcol + 1], sd[x], ALU.mult, ALU.mult,
                      [bank[pb], sdb[x], constb], [knb_[x]])
                if "kvout" not in SKIP:
                    k.dma("sp", kout, kn[x], knp[x], [knb_[x]], [])
                k.act(out_bf, kn[x], AF.Copy, [knb_[x]], out_bufs)
        else:
            rbk = 5 + cnts["r"] % 2
            cnts["r"] += 1
            tq = slice((tb - 2) * TBS, (tb - 1) * TBS)
            k.stt(kn[x], ps(pb), qkg[:, gcol:gcol + 1], sd[x], ALU.mult, ALU.mult,
                  [bank[pb], sdb[x], constb], [knb_[x]])
            k.act(knbf, kn[x], AF.Copy, [knb_[x]], [knbfb])
            k.pe_mm(ps(rbk), Pm, knbf, True, True, [knbfb, constb], [bank[rbk]])
            k.tt(t1, kn[x], ropeC[:, tq], ALU.mult, [knb_[x], ropeb], [t1b])
            k.tt(t2, ps(rbk), ropeS[:, tq], ALU.mult, [bank[rbk], ropeb], [t2b])
            k.tt(out_bf, t1, t2, ALU.add, [t1b, t2b], out_bufs)

    def pool_segment(src3, nseg, n, gq, corr_off, tbo, outcol0):
        w = WINS[gq]
        half = w // 2
        L = n + 16
        cur = src3
        curb = None
        bufs = [(ta, tab), (tb_, tbb)]
        bi = 0
        d = 1
        Lc = L
        while d < w:
            dst, dstb = bufs[bi]
            bi ^= 1
            dv = dst[:, 0:nseg * L].rearrange("p (b t) -> p b t", b=nseg)
            Ln_ = Lc - d
            rd = [] if curb is None else [curb]
            k.tt(dv[:, :, 0:Ln_], cur[:, :, 0:Ln_], cur[:, :, d:d + Ln_], ALU.add, rd + list(tbo["src"]), [dstb])
            cur, curb, Lc = dv, dstb, Ln_
            d *= 2
        o0 = 8 - half
        dst, dstb = bufs[bi]
        dv = dst[:, 0:nseg * n].rearrange("p (b t) -> p b t", b=nseg)
        k.act(dv, cur[:, :, o0:o0 + n], AF.Identity, [curb], [dstb], scale=1.0 / w)
        cl = corr[:, corr_off + gq * 16: corr_off + gq * 16 + 8].unsqueeze(1).to_broadcast([128, nseg, 8])
        cr = corr[:, corr_off + gq * 16 + 8: corr_off + gq * 16 + 16].unsqueeze(1).to_broadcast([128, nseg, 8])
        k.tt(dv[:, :, 0:8], dv[:, :, 0:8], cl, ALU.mult, [constb], [dstb])
        k.tt(dv[:, :, n - 8:n], dv[:, :, n - 8:n], cr, ALU.mult, [constb], [dstb])
        tot = nseg * n
        segs_per = 512 // n if n < 512 else 1
        for c in range(tot // 512):
            x = cnts["x"] % 2
            cnts["x"] += 1
            if n < 512:
                k.tt(plb[x].rearrange("p (b t) -> p b t", b=nseg), dv, src3[:, :, 8:8 + n], ALU.subtract,
                     [dstb] + list(tbo["src"]), [plbb[x]])
            else:
                k.tt(plb[x], dst[:, c * 512:(c + 1) * 512], src3[:, 0, 8 + c * 512: 8 + (c + 1) * 512],
                     ALU.subtract, [dstb] + list(tbo["src"]), [plbb[x]])
            k.pe_mm(ps(7), poolw[:, gq * 128:(gq + 1) * 128], plb[x], True, True, [plbb[x], constb], [bank[7]])
            k.act(pooled2[:, gq, outcol0 + c * 512: outcol0 + (c + 1) * 512], ps(7), AF.Identity,
                  [bank[7], constb], tbo["dst"][c], scale=psc[:, gq:gq + 1])

    si1 = k.take()
    s1 = slot[si1]
    vcnt = [0]
    OVERLAP_CC = os.environ.get("KNOOVERLAP") is None

    def part_kv(tb):
        sample = tb >= 2
        tsl = slice(tb * TBS, (tb + 1) * TBS)
        for hd in range(2):
            if sample:
                x = cnts["x"] % 2
                rms_head(si1, hd * 128, 1, tb, True, kst[x], [kstb[x]])
                c0 = hd * 1024 + (tb - 2) * TBS
                k.dma("sp", ag_in_ap[:, c0:c0 + TBS], kst[x], kstp[x], [kstb[x], agb], [])
            else:
                rms_head(si1, hd * 128, 1, tb, False, Kp[:, hd, tsl], [Kpb[tb]],
                         kout=dr["nk"][:, hd, tsl])
        for tt_ in range(4):
            pb = cnts["p"] % 3
            cnts["p"] += 1
            tile = (tb % 2) * 4 + tt_
            k.pe_group(ps(pb)[:, 0:256], [(h2t[:, kk, tt_ * 128:(tt_ + 1) * 128], s1[:, kk * 1024 + 256: kk * 1024 + 512])
                                          for kk in range(NK)], [slotb[si1]] + h2b, [bank[pb]])
            v = vcnt[0] % 2
            vcnt[0] += 1
            if sample:
                k.act(vsb[v], ps(pb)[:, 0:256], AF.Copy, [bank[pb]], [vsbb[v]])
                c0 = 2048 + tile * 256
                k.dma("sp", ag_in_ap[:, c0:c0 + 256], vsb[v], vsbp[v], [vsbb[v], agb], [])
            else:
                k.vcopy(vst[v], ps(pb)[:, 0:256], [bank[pb]], [vstb[v]])
                k.dma("sp", dr["nv"][:, tile, :], vst[v], vstp[v], [vstb[v]], [])
                k.act(Vp[:, tile, :], vst[v], AF.Copy, [vstb[v]], [Vpb[tb]])

    def part_pin(tb):
        sample = tb >= 2
        for gq in range(4):
            pb = cnts["p"] % 3
            cnts["p"] += 1
            k.pe_group(ps(pb), [(s1[:, kk * 1024 + 512 + gq * 128: kk * 1024 + 512 + (gq + 1) * 128], h2t[:, kk, :])
                                for kk in range(NK)], [slotb[si1]] + h2b, [bank[pb]])
            if sample:
                c0 = 8 + (tb - 2) * TBS
                k.act(pin_s[:, gq, c0:c0 + TBS], ps(pb), AF.Copy, [bank[pb]], [pinb[gq]])
            else:
                x = gq % 2
                k.act(pp[x][:, :, 8:264], ps(pb).rearrange("p (b t) -> p b t", b=2), AF.Copy, [bank[pb]], [ppb[x]])
                pool_segment(pp[x], 2, 256, gq, 64, {"src": [ppb[x]], "dst": [[p2b[tb]]]}, tb * TBS)

    def collectives():
        k.vcopy(e32[:, :, 0:8], pin_s[:, :, 8:16], pinb, [efb])
        k.vcopy(e32[:, :, 8:16], pin_s[:, :, 1024:1032], pinb, [efb])
        e32f = e32.rearrange("p g i -> p (g i)")
        k.vcopy(est[:, 0:64], e32f, [efb], [estb])
        k.tt(est[:, 64:128], e32f, est[:, 0:64], ALU.subtract, [efb], [estb])
        k.dma("sp", dr["ag2_in"][:, :], est, estp, [estb, ag2b], [])
        k.barrier()
        E = k.E["pool"]
        for (src, dst, sb, db) in ((dr["ag2_in"], dr["ag2_out"], ag2b, ag2ob), (ag_in_ap, ag_out_ap, agb, agob)):
            k._deps(E, [], [sb, db])
            ins = nc.gpsimd.collective_compute("AllGather", ALU.bypass,
                                               replica_groups=[[0, 1, 2, 3], [4, 5, 6, 7]],
                                               ins=[src.opt()], outs=[dst.opt()])
            ccprod.count += 1
            ins.then_inc(ccprod.sem, 1)
            k._commit((ccprod, ccprod.count), [], [sb, db])
        if OVERLAP_CC:
            for qn in ("sp", "pool"):
                k.E[qn].wait((ccprod, ccprod.count))
        else:
            k.barrier()

    if OVERLAP_CC:
        for tb in (2, 3):
            mod2(tb, on_dve=True)
            part_kv(tb)
            part_pin(tb)
        for tb in (0, 1):
            mod2(tb, on_dve=True)
            part_kv(tb)
        collectives()
        for tb in (0, 1):
            mod2(tb)
            part_pin(tb)
        k.barrier()
    else:
        for tb in (2, 3, 0, 1):
            mod2(tb)
            part_kv(tb)
            part_pin(tb)
            if tb == 3:
                collectives()
    if "cc" in SKIP or "cutpost" in SKIP:
        cut(2)
    k.dma("sp", eg, dr["ag2_out"][:, :].rearrange("(r p) c -> p r c", p=128), egp, [ag2ob], [egb])
    k.tt(ef, eg[:, :, 0:64], eg[:, :, 64:128], ALU.add, [egb], [efb])
    ef4 = ef.rearrange("p r (g i) -> p r g i", g=4)
    for side, (dst, lo) in enumerate(((pin_s[:, :, 0:8], 8), (pin_s[:, :, 1032:1040], 0))):
        for r in range(4):
            m = mask[:, side * 4 + r: side * 4 + r + 1]
            src = ef4[:, r, :, lo:lo + 8]
            if r == 0:
                k.ts(dst, src, m, ALU.mult, [efb, constb], pinb)
            else:
                k.stt(dst, src, m, dst, ALU.mult, ALU.add, [efb, constb], pinb)
    for gq in range(4):
        pool_segment(pin_s[:, gq:gq + 1, :], 1, 1024, gq, 0,
                     {"src": [pinb[gq]], "dst": [[p2b[2]], [p2b[3]]]}, 1024)
    k.prefetch(si1)
    k.barrier()
    cut(2)

    RH.reset(); RQ.reset()
    RX.top = xmark
    Q3 = k.bfv(k.alloc(RQ, NK * T * 2), NK * T).rearrange("p (h t) -> p h t", h=NK)
    Qb = [[Buf("Q%d_%d" % (h, i)) for i in range(NTB)] for h in range(NK)]
    B1 = RH
    h2t2 = [k.bfv(k.alloc(B1, 8192), 4096).rearrange("p (k t) -> p k t", k=NK) for _ in range(2)]
    h2b2 = [[Buf() for _ in range(NK)] for _ in range(2)]
    sq = [k.bfv(k.alloc(B1, 1024), 512) for _ in range(2)]
    sqb = [Buf(), Buf()]
    sd = [k.f32v(k.alloc(B1, 2048), 512) for _ in range(3)]
    sdb = [Buf() for _ in range(3)]
    kn = [k.f32v(k.alloc(B1, 2048), 512) for _ in range(3)]
    knb_ = [Buf() for _ in range(3)]
    knbf2 = [k.bfv(k.alloc(B1, 1024), 512) for _ in range(2)]
    knbfb2 = [Buf(), Buf()]
    ropeC = k.f32v(k.alloc(RX, 4096), 1024)
    ropeS = k.f32v(k.alloc(RX, 4096), 1024)
    t1 = k.f32v(k.alloc(RX, 2048), 512)
    t2 = k.f32v(k.alloc(RX, 2048), 512)
    t1b, t2b = Buf(), Buf()
    ropeb = Buf("rope2")
    k.dma("sp", ropeC, dr["ropeCS"][:, 0:1024], ropep, (), [ropeb])
    k.dma("sp", ropeS, dr["ropeCS"][:, 1024:2048], ropep, (), [ropeb])
    si2 = k.take()
    s2 = slot[si2]
    items = [(tb, hd) for tb in range(NTB) for hd in range(NK)]
    NI = len(items)

    def mod2c(tb, kk):
        g = tb // 2
        tsl = slice(tb * TBS, (tb + 1) * TBS)
        k.ts(h2t2[tb % 2][:, kk, :], R3[:, kk, tsl], DERk("H2S", g, kk), ALU.mult, [Rbk[tb][kk], derb],
             [h2b2[tb % 2][kk]], s2=DERk("H2B", g, kk), op1=ALU.add)

    def P1(i):
        tb, hd = items[i]
        pb, x = i % 3, i % 2
        k.pe_group(ps(pb), [(s2[:, kk * 1024 + hd * 128: kk * 1024 + (hd + 1) * 128], h2t2[tb % 2][:, kk, :])
                            for kk in range(NK)], [slotb[si2]] + h2b2[tb % 2], [bank[pb]])
        k.act(sq[x], ps(pb), AF.Square, [bank[pb]], [sqb[x]])
        if tb + 1 < NTB:
            mod2c(tb + 1, hd)

    def P2(i):
        tb, hd = items[i]
        pb, x, mb, y = i % 3, i % 2, 3 + i % 2, i % 3
        tsl = slice(tb * TBS, (tb + 1) * TBS)
        k.pe_mm(ps(mb), ones_h, sq[x], True, True, [sqb[x], onesb], [bank[mb]])
        k.act(sd[y], ps(mb), AF.Ln, [bank[mb]], [sdb[y]], bias=RMS_EPS, scale=1.0)
        k.act(sd[y], sd[y], AF.Exp, [], [sdb[y]], scale=-0.5)
        if tb < 2:
            k.stt(Q3[:, hd, tsl], ps(pb), qkg[:, 0:1], sd[y], ALU.mult, ALU.mult,
                  [bank[pb], sdb[y], constb], [Qb[hd][tb]])
        else:
            k.stt(kn[y], ps(pb), qkg[:, 0:1], sd[y], ALU.mult, ALU.mult,
                  [bank[pb], sdb[y], constb], [knb_[y]])
            k.act(knbf2[x], kn[y], AF.Copy, [knb_[y]], [knbfb2[x]])

    def P3(i):
        tb, hd = items[i]
        if tb < 2:
            return
        x, y, rbk = i % 2, i % 3, 5 + i % 2
        tsl = slice(tb * TBS, (tb + 1) * TBS)
        tq = slice((tb - 2) * TBS, (tb - 1) * TBS)
        k.pe_mm(ps(rbk), Pm, knbf2[x], True, True, [knbfb2[x], constb], [bank[rbk]])
        k.tt(t1, kn[y], ropeC[:, tq], ALU.mult, [knb_[y], ropeb], [t1b])
        k.tt(t2, ps(rbk), ropeS[:, tq], ALU.mult, [bank[rbk], ropeb], [t2b])
        k.tt(Q3[:, hd, tsl], t1, t2, ALU.add, [t1b, t2b], [Qb[hd][tb]])

    for kk in range(NK):
        mod2c(0, kk)
    for step in range(NI + 2):
        if step < NI:
            P1(step)
        if 0 <= step - 1 < NI:
            P2(step - 1)
        if 0 <= step - 2 < NI:
            P3(step - 2)
    k.prefetch(si2)
    k.barrier()
    cut(3)

    RH.reset()
    B2 = RH
    KC = k.bfv(k.alloc(B2, 1024), 512).rearrange("p (h t) -> p h t", h=2)
    VC = k.bfv(k.alloc(B2, 1024), 512).rearrange("p (j c) -> p j c", j=2)
    ctxb = Buf("ctx")
    ctxp = k.dprod("ctx")
    KA0 = k.bfv(k.alloc(B2, 8192), 4096).rearrange("p (r t) -> p r t", r=4)
    VA0 = k.bfv(k.alloc(B2, 8192), 4096).rearrange("p (j c) -> p j c", j=32)
    kab0, vab0 = Buf("KA"), Buf("VA")
    kap, vap = k.dprod("KA"), k.dprod("VA")
    SB = [0, 1, 2, 7]
    PT8 = [k.bfv(k.alloc(B2, 1024), 512) for _ in range(8)]
    PT8b = [Buf() for _ in range(8)]
    PT, PTb = PT8[:4], PT8b[:4]
    RX.top = xmark
    xs = k.f32v(k.alloc(RX, 2048), 512)
    xsb = Buf("xs")
    Esel = k.f32v(k.alloc(RX, 512), 128)
    eselb = Buf("esel")
    eselp = k.dprod("esel")
    k.dma("sp", Esel, dr["esel"][:, :], eselp, (), [eselb])
    rec = [k.f32v(k.alloc(B2, 2048), 512) for _ in range(2)]
    recb = [Buf(), Buf()]
    k.dma("pool", KC, dr["kctx"][:, :].rearrange("p (h t) -> p h t", h=2), ctxp, (), [ctxb])
    k.dma("pool", VC, dr["vctx"][:, :].rearrange("p (j c) -> p j c", j=2), ctxp, (), [ctxb])

    it = 0
    for b in range(4):
        tb = b // 2
        bsl = slice(b * 256, (b + 1) * 256)
        for pr in range(4):
            kvh = pr // 2
            ob, sb_ = 3 + it % 2, 5 + it % 2
            r = it % 2
            it += 1
            qv = Q3[:, 2 * pr:2 * pr + 2, bsl]
            qbufs = [Qb[2 * pr][tb], Qb[2 * pr + 1][tb]]
            pts = []
            for kc in range(2):
                pi_ = cnts["p"] % 4
                sbk = SB[pi_]
                cnts["p"] += 1
                k.pe_mm(ps(sbk).rearrange("p (a t) -> p a t", a=2),
                        Kp[:, kvh, b * 256 + kc * 128: b * 256 + (kc + 1) * 128], qv, True, True,
                        [Kpb[tb]] + qbufs, [bank[sbk]])
                k.act(PT[pi_], ps(sbk), AF.Exp, [bank[sbk]], [PTb[pi_]], scale=SCALE)
                pts.append(pi_)
            for kc in range(2):
                x = pts[kc]
                k.pe_mm(ps(ob), Vp[:, b * 2 + kc, kvh * 128:(kvh + 1) * 128], PT[x], kc == 0, kc == 1,
                        [Vpb[tb], PTb[x]], [bank[ob]])
                k.pe_mm(ps(sb_), ones1, PT[x], kc == 0, kc == 1, [PTb[x], onesb], [bank[sb_]])
            k.act(rec[r], ps(sb_), AF.Ln, [bank[sb_]], [recb[r]])
            k.act(rec[r], rec[r], AF.Exp, [], [recb[r]], scale=-1.0)
            k.tt(qv, ps(ob).rearrange("p (a t) -> p a t", a=2), rec[r].rearrange("p (a t) -> p a t", a=2),
                 ALU.mult, [bank[ob], recb[r]], qbufs)

    KA1 = k.bfv(k.alloc(RX, 8192), 4096).rearrange("p (r t) -> p r t", r=4)
    VA1 = k.bfv(oKp, 4096).rearrange("p (j c) -> p j c", j=32)
    kab1, vab1 = Buf("KA1"), Buf("VA1")
    kvt = [(KA0, VA0, kab0, vab0, []), (KA1, VA1, kab1, vab1, Kpb + Vpb)]
    kvp_ = [(kap, vap), (k.dprod("KA1"), k.dprod("VA1"))]
    for kvh in range(2):
        KA, VA, kab, vab, extra = kvt[kvh]
        k.dma("sp", KA, ag_out_ap[:, kvh * 1024:(kvh + 1) * 1024].rearrange("(r p) c -> p r c", p=128),
              kvp_[kvh][0], [agob], [kab])
        for r_ in range(4):
            k.dma("sp", VA[:, r_ * 8:(r_ + 1) * 8, :],
                  ag_out_ap[r_ * 128:(r_ + 1) * 128, 2048:4096].rearrange("p (j c) -> p j c", j=8)[:, :, kvh * 128:(kvh + 1) * 128],
                  kvp_[kvh][1], [agob], [vab] + extra, nodep=(r_ > 0))
    for kvh in range(2):
        KA, VA, kab, vab, extra = kvt[kvh]
        chunks = []
        for c in range(2):
            chunks.append((KC[:, kvh, c * 128:(c + 1) * 128], VC[:, c, kvh * 128:(kvh + 1) * 128], [ctxb]))
        for r_ in range(4):
            for j in range(8):
                chunks.append((KA[:, r_, j * 128:(j + 1) * 128], VA[:, r_ * 8 + j, :], [kab, vab]))
        NCH = len(chunks)
        for hd in range(4 * kvh, 4 * kvh + 4):
            for tb in (2, 3):
                tsl = slice(tb * TBS, (tb + 1) * TBS)
                ob, sb_ = 3 + it % 2, 5 + it % 2
                r = it % 2
                it += 1
                qv = Q3[:, hd, tsl]
                sbks = {}

                def qk(c):
                    pi_ = cnts["p"] % 4
                    sbk = SB[pi_]
                    cnts["p"] += 1
                    sbks[c] = c % 8
                    k.pe_mm(ps(sbk), chunks[c][0], qv, True, True, chunks[c][2] + [Qb[hd][tb]], [bank[sbk]])
                    k.act(PT8[c % 8], ps(sbk), AF.Exp, [bank[sbk]], [PT8b[c % 8]], scale=SCALE)

                def sum_mm(cc):
                    j = cc % 4
                    x = sbks[cc]
                    k.op("pe", lambda: nc.tensor.matmul(ps(sb_)[32 * j:32 * j + 32, :], lhsT=ones1[:, 0:32],
                                                        rhs=PT8[x], start=(cc == j), stop=(cc + 4 >= NCH),
                                                        tile_position=(0, 32 * j)),
                         [PT8b[x], onesb], [bank[sb_]])

                qk(0)
                qk(1)
                qk(2)
                for c in range(NCH):
                    x = sbks[c]
                    k.pe_mm(ps(ob), chunks[c][1], PT8[x], c == 0, c == NCH - 1, chunks[c][2] + [PT8b[x]], [bank[ob]])
                    if c + 3 < NCH:
                        qk(c + 3)
                    if c % 4 == 3 or c == NCH - 1:
                        for cc in range((c // 4) * 4, c + 1):
                            sum_mm(cc)
                k.vcopy(xs, ps(sb_), [bank[sb_]], [xsb])
                k.pe_mm(ps(sb_), Esel, xs, True, True, [xsb, eselb], [bank[sb_]])
                k.recip(rec[r], ps(sb_), [bank[sb_]], [recb[r]])
                k.tt(qv, ps(ob), rec[r], ALU.mult, [bank[ob], recb[r]], [Qb[hd][tb]])
                if kvh == 0 and tb == 3:
                    mi = 5 + (hd - 4 * kvh)
                    mod_compute(mi, 7)
                    mod_finish(mi, mi + 1, 7)
                    if mi == 8:
                        derive_late()
    k.barrier()
    cut(4)

    RH.reset(); RX.reset()
    oH2 = k.alloc(RH, NK * T * 2)
    assert oH2 == oH
    GX = RX
    h2tg = [k.bfv(k.alloc(GX, 8192), 4096).rearrange("p (k t) -> p k t", k=NK) for _ in range(2)]
    h2bg = [[Buf() for _ in range(NK)] for _ in range(2)]
    sg = [k.f32v(k.alloc(GX, 2048), 512) for _ in range(2)]
    sgb = [Buf(), Buf()]
    m12 = sg
    m12b = sgb
    hi_ = 0
    mod2(0, h2tg[0], h2bg[0])
    for ocp in range(4):
        si = k.take()
        s = slot[si]
        for tb in range(NTB):
            h2t, h2b = h2tg[hi_ % 2], h2bg[hi_ % 2]
            hi_ += 1
            if not (ocp == 3 and tb == NTB - 1):
                mod2((tb + 1) % NTB, h2tg[hi_ % 2], h2bg[hi_ % 2])
            tsl = slice(tb * TBS, (tb + 1) * TBS)
            for o in range(2):
                oc = 2 * ocp + o
                base = 4 * o
                k.pe_group(ps(base + 0), [(s[:, kk * 256 + o * 128: kk * 256 + (o + 1) * 128], h2t[:, kk, :])
                                          for kk in range(NK)], [slotb[si]] + h2b, [bank[base + 0]])
                k.pe_group(ps(base + 1), [(s[:, 2048 + kk * 256 + o * 128: 2048 + kk * 256 + (o + 1) * 128], h2t[:, kk, :])
                                          for kk in range(NK)], [slotb[si]] + h2b, [bank[base + 1]])
                k.pe_group(ps(base + 2), [(s[:, 4096 + kk * 256 + o * 128: 4096 + kk * 256 + (o + 1) * 128], Q3[:, kk, tsl])
                                          for kk in range(NK)], [slotb[si]] + [Qb[kk][tb] for kk in range(NK)],
                           [bank[base + 2]])
                k.pe_group(ps(base + 3), [(s[:, 6144 + kk * 256 + o * 128: 6144 + kk * 256 + (o + 1) * 128], pooled2[:, kk, tsl])
                                          for kk in range(4)], [slotb[si], p2b[tb]], [bank[base + 3]])
                k.act(sg[0], ps(base + 0), AF.Sigmoid, [bank[base + 0]], [sgb[0]])
                k.act(sg[1], ps(base + 1), AF.Sigmoid, [bank[base + 1]], [sgb[1]])
                k.tt(m12[0], sg[0], ps(base + 2), ALU.mult, [sgb[0], bank[base + 2]], [m12b[0]])
                k.tt(m12[1], sg[1], ps(base + 3), ALU.mult, [sgb[1], bank[base + 3]], [m12b[1]])
                k.tt(H3[:, oc, tsl], m12[0], m12[1], ALU.add, [m12b[0], m12b[1]], [Hb[tb]])
        k.prefetch(si)
    k.barrier()
    cut(5)

    FB2 = ffn_alloc()
    L2 = FB2["Ln"]
    si = k.take()
    s = slot[si]
    cntf = 0
    for tb in range(NTB):
        g = tb // 2
        tsl = slice(tb * TBS, (tb + 1) * TBS)
        for oc in range(NK):
            f = 4 + cntf % 2
            cntf += 1
            k.pe_group(ps(f), [(s[:, kk * 1024 + oc * 128: kk * 1024 + (oc + 1) * 128], H3[:, kk, tsl])
                               for kk in range(NK)], [slotb[si], Hb[tb]], [bank[f]])
            k.stt(R3[:, oc, tsl], ps(f), DERk("G2", g, oc), R3[:, oc, tsl], ALU.mult, ALU.add,
                  [bank[f], derb], [Rbk[tb][oc]])
            ln_stats_chunk(tb, L2, oc)
        sbk6, sbk7 = ((6, 7), (2, 3))[tb % 2]
        ln_pe_stats(tb, L2, sbk6, sbk7)
        if tb >= 1:
            pb6, pb7 = ((6, 7), (2, 3))[(tb - 1) % 2]
            ln_finish(tb - 1, L2, "RS2", "RB2", hs="H3S", hb="H3B", b6=pb6, b7=pb7)
    pb6, pb7 = ((6, 7), (2, 3))[(NTB - 1) % 2]
    ln_finish(NTB - 1, L2, "RS2", "RB2", hs="H3S", hb="H3B", b6=pb6, b7=pb7)
    k.prefetch(si)
    cut(6)

    ffn_phase("HG3", FB2, (("RS3", "RB3"), {"final": True}))

    finalize()


_PROGRAM = None


def _host_consts():
    n_freq = 32
    inv_freq = (10000.0 ** (-np.arange(n_freq, dtype=np.float32) / n_freq)).astype(np.float32)
    pm = np.zeros((128, 128), np.float32)
    for d2 in range(64):
        pm[d2 + 64, d2] = -1.0
        pm[d2, d2 + 64] = 1.0

    def corr_tab(n, left, right):
        out = np.ones((4, 16), np.float32)
        for g, w in enumerate(WINS):
            for i in range(8):
                if left:
                    t = i
                    lo = min(max(t - w // 2, 0), n); hi = min(max(t - w // 2 + w, 0), n)
                    out[g, i] = w / float(hi - lo)
                if right:
                    t = n - 8 + i
                    lo = min(max(t - w // 2, 0), n); hi = min(max(t - w // 2 + w, 0), n)
                    out[g, 8 + i] = w / float(hi - lo)
        return out.reshape(64)
    return inv_freq, pm, corr_tab


def kernel(x_prompt, x_sample, cache_k, cache_v, c, c_ctx, w_mod, b_mod, ln_g, ln_b,
           ffn1_w1, ffn1_w2, w_in, q_norm_g, k_norm_g, pool_w, pool_scale,
           w_up_attn, w_up_pool, w_out, ffn2_w1, ffn2_w2):
    global _PROGRAM
    f32 = np.float32
    A = lambda a: np.ascontiguousarray(np.asarray(a, dtype=f32))
    x_prompt, x_sample, cache_k, cache_v = A(x_prompt), A(x_sample), A(cache_k), A(cache_v)
    c, c_ctx = A(c), A(c_ctx)
    inv_freq, pm, corr_tab = _host_consts()

    def fm(v):
        return np.ascontiguousarray(v.reshape(8, 128).T)

    def w1_pm(w):
        out = np.empty((128, 2, 8 * DFF), f32)
        pieces = [(0, 2)] + [(c0, 3) for c0 in range(2, 20, 3)] + [(20, 2)]
        for half in range(2):
            wh = w[:, half * DFF:(half + 1) * DFF].reshape(8, 128, DFF)
            for (c0, n) in pieces:
                blk = wh[:, :, c0 * 128:(c0 + n) * 128].transpose(1, 0, 2)
                out[:, half, 8 * 128 * c0: 8 * 128 * (c0 + n)] = blk.reshape(128, 8 * n * 128)
        return out

    shared = {
        "w_mod": A(w_mod)[0], "ffn1_w1": w1_pm(A(ffn1_w1)[0]), "ffn1_w2": A(ffn1_w2)[0],
        "ffn2_w1": w1_pm(A(ffn2_w1)[0]), "ffn2_w2": A(ffn2_w2)[0], "w_in": A(w_in)[0],
        "w_up_attn": A(w_up_attn)[0], "w_up_pool": A(w_up_pool)[0], "w_out": A(w_out)[0],
        "bmodT": np.ascontiguousarray(A(b_mod)[0].reshape(72, 128).T),
        "lngT": np.ascontiguousarray(A(ln_g)[0].reshape(24, 128).T),
        "lnbT": np.ascontiguousarray(A(ln_b)[0].reshape(24, 128).T),
        "qkg": np.ascontiguousarray(np.stack([A(q_norm_g)[0], A(k_norm_g)[0]], axis=1)),
        "pool_w": np.ascontiguousarray(A(pool_w)[0].transpose(1, 0, 2).reshape(128, 512)),
        "pscaleT": np.ascontiguousarray(A(pool_scale)[0].reshape(4, 128).T),
        "pmat": pm,
        "esel": np.ascontiguousarray(np.broadcast_to((np.arange(128) % 32 == 0).astype(f32)[:, None], (128, 128))),
    }
    in_maps = []
    for core in range(NCORES):
        b, q = core // 4, core % 4
        xp = x_prompt[4 * core:4 * core + 4].reshape(1024, D)
        xs = x_sample[b, 1024 * q:1024 * (q + 1)]
        xall = np.concatenate([xp, xs], axis=0)
        xT = np.ascontiguousarray(xall.reshape(T, 8, 128).transpose(2, 1, 0))
        condT = np.stack([fm(c_ctx), fm(c[b])], axis=2).reshape(128, 16)
        kctx = np.ascontiguousarray(cache_k[b, 0].transpose(2, 1, 0).reshape(128, 512))
        vctx = np.ascontiguousarray(cache_v[b, 0].reshape(2, 128, 256).transpose(1, 0, 2).reshape(128, 512))
        pos = np.arange(1024 * q, 1024 * (q + 1))
        row = (pos // 64).astype(f32)
        col = (pos % 64).astype(f32)
        ang = np.concatenate([row[:, None] * inv_freq, col[:, None] * inv_freq], axis=-1).astype(f32)
        cs, sn = np.cos(ang).astype(f32).T, np.sin(ang).astype(f32).T
        ropeCS = np.concatenate([np.concatenate([cs, cs], 0), np.concatenate([sn, sn], 0)], axis=1)
        m = np.zeros(8, f32)
        if q - 1 >= 0:
            m[q - 1] = 1.0
        if q + 1 <= 3:
            m[4 + q + 1] = 1.0
        corr = np.concatenate([corr_tab(4096, q == 0, q == 3), corr_tab(256, True, True)])
        im = dict(shared)
        im.update({
            "xT": xT, "condT": np.ascontiguousarray(condT), "kctx": kctx, "vctx": vctx,
            "ropeCS": np.ascontiguousarray(ropeCS.astype(f32)),
            "masks": np.ascontiguousarray(np.broadcast_to(m, (128, 8))),
            "corr": np.ascontiguousarray(np.broadcast_to(corr.astype(f32), (128, 128))),
        })
        in_maps.append(im)

    if _PROGRAM is None:
        _st = os.environ.get("KSTOP")
        _PROGRAM = build_program(None if _st is None else int(_st))
    res = run_bass_kernel_spmd(_PROGRAM, in_maps, core_ids=list(range(NCORES)))

    y_prompt = np.empty((32, 256, D), f32)
    y_sample = np.empty((2, 4096, D), f32)
    new_k = np.empty((32, 1, 256, 2, 128), f32)
    new_v = np.empty((32, 1, 256, 2, 128), f32)
    for core in range(NCORES):
        r = res.results[core]
        b, q = core // 4, core % 4
        yT = np.asarray(r["yT"], dtype=f32)
        yall = yT.transpose(2, 1, 0).reshape(T, D)
        y_prompt[4 * core:4 * core + 4] = yall[:1024].reshape(4, 256, D)
        y_sample[b, 1024 * q:1024 * (q + 1)] = yall[1024:]
        nk_ = np.asarray(r["nk"], dtype=f32)
        new_k[4 * core:4 * core + 4, 0] = nk_.transpose(2, 1, 0).reshape(4, 256, 2, 128)
        nv_ = np.asarray(r["nv"], dtype=f32)
        new_v[4 * core:4 * core + 4, 0] = nv_.transpose(1, 0, 2).reshape(4, 256, 2, 128)
    return (y_prompt, y_sample, new_k, new_v)
```

```python
import contextlib
import os
import numpy as np
import concourse.bass as bass
import concourse.mybir as mybir
from concourse.bass_utils import run_bass_kernel_spmd

F32 = mybir.dt.float32
BF16 = mybir.dt.bfloat16
AF = mybir.ActivationFunctionType
ALU = mybir.AluOpType

D = 1024
DFF = 2816
NK = 8
T = 2048
TBS = 512
NTB = 4
ALPHA = 2.0 ** 0.25
LN_EPS = 1e-6
RMS_EPS = 1e-6
SCALE = 128.0 ** -0.5
WINS = (2, 4, 8, 16)
NCORES = 8
AGW = 4096
ARENA_WORDS = 53200
SLOT_ELEMS = 9216


class Prod:
    def __init__(self, sem, step):
        self.sem = sem
        self.step = step
        self.count = 0


class Buf:
    __slots__ = ("w", "r", "name")

    def __init__(self, name=""):
        self.w = None
        self.r = {}
        self.name = name


class Eng:
    def __init__(self, eng, prod):
        self.eng = eng
        self.prod = prod
        self.seen = {}
        self.self_sync = False

    def wait(self, t):
        prod, val = t
        if prod is self.prod and not self.self_sync:
            return
        if self.seen.get(prod, 0) >= val:
            return
        self.eng.wait_ge(prod.sem, val)
        self.seen[prod] = val


class Region:
    def __init__(self, start, end):
        self.start, self.end, self.top = start, end, start

    def reset(self):
        self.top = self.start


class K:
    def __init__(self, nc, st):
        self.nc = nc
        self.st = st
        self.E = {}
        for name, eng in (("pe", nc.tensor), ("act", nc.scalar), ("dve", nc.vector),
                          ("pool", nc.gpsimd), ("sp", nc.sync)):
            self.E[name] = Eng(eng, self.new_prod("e_" + name, 1))
            self.E[name].self_sync = name in ("act", "dve", "pool")
        self.dprods = []
        self.arena_t = st.enter_context(nc.sbuf_tensor("arena", [128, ARENA_WORDS], F32))
        self.arena = self.arena_t[:, :]
        self.ps_t = st.enter_context(nc.psum_tensor("ps", [128, 8, 512], F32))
        self.bank = [Buf("bank%d" % i) for i in range(8)]
        self.jobs = []
        self.job_next = 0
        self.job_cur = 0
        self.job_slot = {}

    def new_prod(self, name, step):
        sem = self.st.enter_context(self.nc.semaphore(name))
        return Prod(sem, step)

    def dprod(self, name):
        p = self.new_prod("d_" + name, 16)
        self.dprods.append(p)
        return p

    def _deps(self, E, reads, writes):
        for b in reads:
            if b.w is not None:
                E.wait(b.w)
        for b in writes:
            if b.w is not None:
                E.wait(b.w)
            for p, v in b.r.items():
                E.wait((p, v))

    def _commit(self, t, reads, writes):
        p, v = t
        for b in reads:
            if b.r.get(p, 0) < v:
                b.r[p] = v
        for b in writes:
            b.w = t
            b.r = {}

    def op(self, en, fn, reads=(), writes=(), pre=()):
        E = self.E[en]
        self._deps(E, reads, writes)
        for f in pre:
            f()
        ins = fn()
        E.prod.count += 1
        ins.then_inc(E.prod.sem, 1)
        t = (E.prod, E.prod.count)
        self._commit(t, reads, writes)
        return t

    def dma(self, qn, out, in_, prod, reads=(), writes=(), nodep=False):
        E = self.E[qn]
        if not nodep:
            self._deps(E, reads, writes)
        ins = E.eng.dma_start(out=out, in_=in_)
        prod.count += 16
        ins.then_inc(prod.sem, 16)
        t = (prod, prod.count)
        self._commit(t, reads, writes)
        return t

    def barrier(self):
        ticks = [(e.prod, e.prod.count) for e in self.E.values() if e.prod.count > 0]
        ticks += [(p, p.count) for p in self.dprods if p.count > 0]
        for e in self.E.values():
            for t in ticks:
                e.wait(t)

    def psb(self, i):
        return self.ps_t[:, i, :]

    def f32v(self, off_bytes, n):
        o = off_bytes // 4
        return self.arena[:, o:o + n]

    def bfv(self, off_bytes, n):
        o = off_bytes // 4
        return self.arena[:, o:o + n // 2].bitcast(BF16)

    def alloc(self, reg, nbytes):
        nbytes = (nbytes + 63) // 64 * 64
        off = reg.top
        reg.top += nbytes
        assert reg.top <= reg.end, ("region overflow", reg.start, reg.end, reg.top)
        return off

    def pe_group(self, out_ap, pairs, reads, writes):
        n = len(pairs)
        nc = self.nc
        pre = [(lambda l=l, r=r, i=i: nc.tensor.matmul(out_ap, lhsT=l, rhs=r, start=(i == 0), stop=False))
               for i, (l, r) in enumerate(pairs[:-1])]
        l, r = pairs[-1]
        return self.op("pe", lambda: nc.tensor.matmul(out_ap, lhsT=l, rhs=r, start=(n == 1), stop=True),
                       reads, writes, pre=pre)

    def pe_mm(self, out_ap, l, r, start, stop, reads, writes):
        nc = self.nc
        return self.op("pe", lambda: nc.tensor.matmul(out_ap, lhsT=l, rhs=r, start=start, stop=stop),
                       reads, writes)

    def act(self, out, in_, func, reads, writes, bias=None, scale=None):
        nc = self.nc
        kw = {}
        if bias is not None:
            kw["bias"] = bias
        if scale is not None:
            kw["scale"] = scale
        return self.op("act", lambda: nc.scalar.activation(out=out, in_=in_, func=func, **kw), reads, writes)

    def tt(self, out, in0, in1, op, reads, writes):
        nc = self.nc
        return self.op("dve", lambda: nc.vector.tensor_tensor(out=out, in0=in0, in1=in1, op=op), reads, writes)

    def ts(self, out, in0, s1, op0, reads, writes, s2=None, op1=None):
        nc = self.nc
        if op1 is None:
            return self.op("dve", lambda: nc.vector.tensor_scalar(out=out, in0=in0, scalar1=s1, scalar2=None,
                                                                  op0=op0), reads, writes)
        return self.op("dve", lambda: nc.vector.tensor_scalar(out=out, in0=in0, scalar1=s1, scalar2=s2,
                                                              op0=op0, op1=op1), reads, writes)

    def stt(self, out, in0, scalar, in1, op0, op1, reads, writes):
        nc = self.nc
        return self.op("dve", lambda: nc.vector.scalar_tensor_tensor(out=out, in0=in0, scalar=scalar, in1=in1,
                                                                     op0=op0, op1=op1), reads, writes)

    def recip(self, out, in_, reads, writes):
        nc = self.nc
        return self.op("dve", lambda: nc.vector.reciprocal(out=out, in_=in_), reads, writes)

    def vcopy(self, out, in_, reads, writes):
        nc = self.nc
        return self.op("dve", lambda: nc.vector.tensor_copy(out=out, in_=in_), reads, writes)

    def memset(self, out, val, writes):
        nc = self.nc
        return self.op("dve", lambda: nc.vector.memset(out, val), (), writes)

    def add_job(self, fn):
        self.jobs.append(fn)

    def prefetch(self, si):
        if self.job_next < len(self.jobs):
            self.jobs[self.job_next](si)
            self.job_slot[self.job_next] = si
            self.job_next += 1

    def take(self):
        assert self.job_cur < self.job_next, "job not prefetched"
        si = self.job_slot[self.job_cur]
        self.job_cur += 1
        return si


class _Stop(Exception):
    pass


def build_program(stop=None):
    nc = bass.Bass("TRN2", target_bir_lowering=False)
    dr = {}

    def din(name, shape, dt=F32):
        dr[name] = nc.dram_tensor(name, list(shape), dt, kind="ExternalInput").ap()
        return dr[name]

    def dout(name, shape, dt=F32):
        dr[name] = nc.dram_tensor(name, list(shape), dt, kind="ExternalOutput").ap()
        return dr[name]

    xT = din("xT", [128, NK, T])
    condT = din("condT", [128, 16])
    w_mod = din("w_mod", [D, 9 * D])
    bmodT = din("bmodT", [128, 72])
    lngT = din("lngT", [128, 24])
    lnbT = din("lnbT", [128, 24])
    ffn1_w1 = din("ffn1_w1", [128, 2, 8 * DFF])
    ffn1_w2 = din("ffn1_w2", [DFF, D])
    ffn2_w1 = din("ffn2_w1", [128, 2, 8 * DFF])
    ffn2_w2 = din("ffn2_w2", [DFF, D])
    w_in = din("w_in", [D, 4096])
    qkg = din("qkg", [128, 2])
    pool_w = din("pool_w", [128, 512])
    pscaleT = din("pscaleT", [128, 4])
    w_up_attn = din("w_up_attn", [D, D])
    w_up_pool = din("w_up_pool", [512, D])
    w_out = din("w_out", [D, D])
    kctx = din("kctx", [128, 512])
    vctx = din("vctx", [128, 512])
    ropeCS = din("ropeCS", [128, 2048])
    pmat = din("pmat", [128, 128])
    masks = din("masks", [128, 8])
    corr = din("corr", [128, 128])
    esel = din("esel", [128, 128])
    yT = dout("yT", [128, NK, T])
    nk = dout("nk", [128, 2, 1024])
    nv = dout("nv", [128, 8, 256])
    ag_in = nc.dram_tensor("ag_in", [128, AGW], BF16)
    ag_out = nc.dram_tensor("ag_out", [512, AGW], BF16)
    ag_in_ap = ag_in.ap()
    ag_out_ap = ag_out.ap()
    dr["ag2_in"] = nc.dram_tensor("ag2_in", [128, 128], BF16).ap()
    dr["ag2_out"] = nc.dram_tensor("ag2_out", [512, 128], BF16).ap()

    with contextlib.ExitStack() as st:
        k = K(nc, st)
        k.stop = stop
        try:
            _emit(k, nc, dr, ag_in, ag_out, ag_in_ap, ag_out_ap)
        except _Stop:
            pass
    return nc


def _emit(k, nc, dr, ag_in, ag_out, ag_in_ap, ag_out_ap):
    ps = k.psb
    bank = k.bank
    SKIP = set(os.environ.get("KSKIP", "").split(","))
    G = Region(0, ARENA_WORDS * 4)
    oR = k.alloc(G, NK * T * 4)
    oSlot = [k.alloc(G, SLOT_ELEMS * 2), k.alloc(G, SLOT_ELEMS * 2)]
    oOnesLn = k.alloc(G, 256)
    oOnesH = k.alloc(G, 256)
    oOnes1 = k.alloc(G, 256)
    oPm = k.alloc(G, 256)
    oModp = k.alloc(G, 144 * 4)
    oBmod = k.alloc(G, 72 * 4)
    oLng = k.alloc(G, 24 * 4)
    oLnb = k.alloc(G, 24 * 4)
    oCond = k.alloc(G, 16 * 4)
    oScT = k.alloc(G, 16 * 2)
    oQkg = k.alloc(G, 2 * 4)
    oPsc = k.alloc(G, 4 * 4)
    oMask = k.alloc(G, 8 * 4)
    oCorr = k.alloc(G, 128 * 4)
    oPoolW = k.alloc(G, 512 * 2)
    oDer = k.alloc(G, 32 * 8 * 4)
    gend = G.top
    RH = Region(gend, gend + 32768)
    RQ = Region(RH.end, RH.end + 32768)
    RP2 = Region(RQ.end, RQ.end + 16384)
    RX = Region(RP2.end, ARENA_WORDS * 4)
    RHQ = Region(RH.start, RQ.end)
    RQX = Region(RQ.start, RX.end)
    assert RX.end - RX.start >= 20000, (RX.start, RX.end)

    R3 = k.f32v(oR, NK * T).rearrange("p (k t) -> p k t", k=NK)
    Rbk = [[Buf("R%d_%d" % (i, j)) for j in range(NK)] for i in range(NTB)]
    Rprod = [k.dprod("R%d" % i) for i in range(NTB)]
    slot = [k.bfv(o, SLOT_ELEMS) for o in oSlot]
    slotb = [Buf("slot0"), Buf("slot1")]
    slotb2 = [Buf("slot0w2"), Buf("slot1w2")]
    k.stage = None
    slotp = [k.dprod("slot0"), k.dprod("slot1")]
    slotp2 = [k.dprod("slot0w2"), k.dprod("slot1w2")]
    ones_ln = k.bfv(oOnesLn, 128)
    ones_h = k.bfv(oOnesH, 128)
    ones1 = k.bfv(oOnes1, 128)
    Pm = k.bfv(oPm, 128)
    modp = k.f32v(oModp, 144)
    modp3 = modp.rearrange("p (j g) -> p j g", g=2)
    modp4 = modp.rearrange("p (i k g) -> p i k g", i=9, k=8)
    bmod = k.f32v(oBmod, 72)
    lng = k.f32v(oLng, 24)
    lnb = k.f32v(oLnb, 24)
    cond = k.f32v(oCond, 16)
    scT = k.bfv(oScT, 16)
    qkg = k.f32v(oQkg, 2)
    psc = k.f32v(oPsc, 4)
    mask = k.f32v(oMask, 8)
    corr = k.f32v(oCorr, 128)
    poolw = k.bfv(oPoolW, 512)
    der = k.f32v(oDer, 256)
    constb = Buf("const")
    cprod = k.dprod("const")
    cprod2 = k.dprod("const2")
    derb = Buf("der")

    dcol = {}

    def DER(name, g=None):
        key = (name, g)
        if key not in dcol:
            dcol[key] = len(dcol)
            assert len(dcol) <= 32
        c = dcol[key] * 8
        return der[:, c:c + 8]

    def DERk(name, g, kk):
        c = dcol[(name, g)] * 8 + kk
        return der[:, c:c + 1]

    def finalize():
        sp = k.E["sp"]
        for p in k.dprods:
            if p.count > 0:
                sp.wait((p, p.count))
        for e in k.E.values():
            if e.prod.count > 0:
                sp.wait((e.prod, e.prod.count))

    dbgp = k.dprod("dbg")

    def cut(stage):
        if k.stop is not None and k.stop == stage:
            k.barrier()
            for tb in range(NTB):
                k.dma("sp", dr["yT"][:, :, tb * TBS:(tb + 1) * TBS], R3[:, :, tb * TBS:(tb + 1) * TBS], dbgp, Rbk[tb], [])
            finalize()
            raise _Stop()

    def job_wmod(i):
        def f(si):
            k.dma("pool", slot[si][:, 0:8192].rearrange("p (k c) -> p k c", k=8),
                  dr["w_mod"][:, i * 1024:(i + 1) * 1024].rearrange("(k p) c -> p k c", p=128),
                  slotp[si], (), [slotb[si], slotb2[si]])
        return f

    PIECES = [(0, 2)] + [(c0, 3) for c0 in range(2, 20, 3)] + [(20, 2)]

    def job_ffn(w1, w2, c0, n):
        def f(si):
            s = slot[si]
            o0 = 8 * 128 * c0
            k.dma("pool", s[:, 0:3072].rearrange("p (k c) -> p k c", k=8)[:, :, 0:n * 128],
                  w1[:, 0, o0:o0 + 8 * n * 128].rearrange("p (k c) -> p k c", k=8),
                  slotp[si], (), [slotb[si]])
            k.dma("pool", s[:, 3072:6144].rearrange("p (k c) -> p k c", k=8)[:, :, 0:n * 128],
                  w1[:, 1, o0:o0 + 8 * n * 128].rearrange("p (k c) -> p k c", k=8),
                  slotp[si], (), [slotb[si]], nodep=True)
            if k.stage is None:
                k.dma("pool", s[:, 6144:6144 + n * 1024].rearrange("p (j c) -> p j c", j=n),
                      w2[c0 * 128:(c0 + n) * 128, :].rearrange("(j p) c -> p j c", p=128),
                      slotp2[si], (), [slotb2[si]])
            else:
                stg, stgb, stgp = k.stage[si]
                k.dma("sp", stg[:, 0:n * 1024].rearrange("p (j c) -> p j c", j=n),
                      w2[c0 * 128:(c0 + n) * 128, :].rearrange("(j p) c -> p j c", p=128),
                      stgp, (), [stgb])
                for j in range(n):
                    k.act(s[:, 6144 + j * 1024:6144 + (j + 1) * 1024], stg[:, j * 1024:(j + 1) * 1024], AF.Copy,
                          [stgb], [slotb2[si]])
        return f

    def job_win(col0):
        def f(si):
            k.dma("pool", slot[si][:, 0:8192].rearrange("p (k c) -> p k c", k=8),
                  dr["w_in"][:, col0:col0 + 1024].rearrange("(k p) c -> p k c", p=128),
                  slotp[si], (), [slotb[si], slotb2[si]])
        return f

    def job_g1(ocp):
        def f(si):
            s = slot[si]
            c0 = ocp * 256
            k.dma("pool", s[:, 0:2048].rearrange("p (k c) -> p k c", k=8),
                  dr["w_in"][:, 2048 + c0:2048 + c0 + 256].rearrange("(k p) c -> p k c", p=128),
                  slotp[si], (), [slotb[si], slotb2[si]])
            k.dma("pool", s[:, 2048:4096].rearrange("p (k c) -> p k c", k=8),
                  dr["w_in"][:, 3072 + c0:3072 + c0 + 256].rearrange("(k p) c -> p k c", p=128),
                  slotp[si], (), [slotb[si]], nodep=True)
            k.dma("pool", s[:, 4096:6144].rearrange("p (k c) -> p k c", k=8),
                  dr["w_up_attn"][:, c0:c0 + 256].rearrange("(k p) c -> p k c", p=128),
                  slotp[si], (), [slotb[si]], nodep=True)
            k.dma("pool", s[:, 6144:7168].rearrange("p (k c) -> p k c", k=4),
                  dr["w_up_pool"][:, c0:c0 + 256].rearrange("(k p) c -> p k c", p=128),
                  slotp[si], (), [slotb[si]], nodep=True)
        return f

    def job_wout():
        def f(si):
            k.dma("pool", slot[si][:, 0:8192].rearrange("p (k c) -> p k c", k=8),
                  dr["w_out"][:, :].rearrange("(k p) c -> p k c", p=128),
                  slotp[si], (), [slotb[si], slotb2[si]])
        return f

    for i in range(2):
        k.add_job(job_wmod(i))
    for pi, (c0, n) in enumerate(PIECES):
        k.add_job(job_ffn(dr["ffn1_w1"], dr["ffn1_w2"], c0, n))
        if pi == 0:
            k.add_job(job_wmod(2))
        if pi == 3:
            k.add_job(job_wmod(3))
        if pi == 5:
            k.add_job(job_wmod(4))
    k.add_job(job_win(1024))
    k.add_job(job_win(0))
    for i in range(5, 9):
        k.add_job(job_wmod(i))
    for ocp in range(4):
        k.add_job(job_g1(ocp))
    k.add_job(job_wout())
    for (c0, n) in PIECES:
        k.add_job(job_ffn(dr["ffn2_w1"], dr["ffn2_w2"], c0, n))

    for (dst, src) in ((cond, "condT"), (bmod, "bmodT"), (lng, "lngT"), (lnb, "lnbT"), (qkg, "qkg"),
                       (psc, "pscaleT"), (mask, "masks"), (corr, "corr")):
        k.dma("sp", dst, dr[src][:, :], cprod, (), [constb])
    k.dma("pool", Pm, dr["pmat"][:, :], cprod2, (), [constb])
    k.dma("pool", poolw, dr["pool_w"][:, :], cprod2, (), [constb])
    k.prefetch(0)
    k.prefetch(1)
    for tb in range(NTB):
        k.dma("sp", R3[:, :, tb * TBS:(tb + 1) * TBS], dr["xT"][:, :, tb * TBS:(tb + 1) * TBS],
              Rprod[tb], (), Rbk[tb])
    onesb = Buf("ones")
    k.memset(ones_ln, 1.0 / 1024.0, [onesb])
    k.memset(ones_h, 1.0 / 128.0, [onesb])
    k.memset(ones1, 1.0, [onesb])
    scb = Buf("scT")
    k.act(scT, cond, AF.Silu, [constb], [scb])
    def mod_compute(i, bk):
        si = k.take()
        for j in range(8):
            col = (i * 8 + j) * 2
            for kk in range(8):
                nc_l = slot[si][:, kk * 1024 + j * 128: kk * 1024 + (j + 1) * 128]
                if kk == 0 and j == 0:
                    k._deps(k.E["pe"], [slotb[si], scb], [bank[bk]])
                ins = nc.tensor.matmul(ps(bk)[:, col:col + 2], lhsT=nc_l, rhs=scT[:, kk * 2:(kk + 1) * 2],
                                       start=(kk == 0), stop=(kk == 7))
        E = k.E["pe"]
        E.prod.count += 1
        ins.then_inc(E.prod.sem, 1)
        tk = (E.prod, E.prod.count)
        k._commit(tk, [slotb[si], scb], [bank[bk]])
        k.prefetch(si)

    def mod_finish(i0, i1, bk):
        psm3 = ps(bk)[:, 0:144].rearrange("p (j g) -> p j g", g=2)
        for g in range(2):
            k.tt(modp3[:, i0 * 8:i1 * 8, g], psm3[:, i0 * 8:i1 * 8, g], bmod[:, i0 * 8:i1 * 8], ALU.add,
                 [bank[bk], constb], [derb])

    for i in range(2):
        mod_compute(i, 0)
    mod_finish(0, 2, 0)

    def M(i, g):
        return modp4[:, i, :, g]

    for g in range(2):
        k.ts(DER("A1", g), M(1, g), 1.0, ALU.add, [], [derb])
        k.vcopy(DER("SH1", g), M(0, g), [], [derb])
        DER("HG1", g)
    k.ts(DER("RS1"), lng[:, 0:8], ALPHA, ALU.mult, [constb], [derb])
    k.ts(DER("RB1"), lnb[:, 0:8], ALPHA, ALU.mult, [constb], [derb])
    k.ts(DER("RS2"), lng[:, 8:16], ALPHA, ALU.mult, [constb], [derb])
    k.ts(DER("RB2"), lnb[:, 8:16], ALPHA, ALU.mult, [constb], [derb])
    k.vcopy(DER("RS3"), lng[:, 16:24], [constb], [derb])
    k.vcopy(DER("RB3"), lnb[:, 16:24], [constb], [derb])
    for g in range(2):
        for nm in ("H2S", "H2B", "G2", "T3", "H3S", "H3B", "HG3"):
            DER(nm, g)

    def derive_mix():
      for g in range(2):
        k.ts(DER("H2S", g), M(4, g), 1.0, ALU.add, [], [derb], s2=1.0 / ALPHA, op1=ALU.mult)
        k.vcopy(DER("H2B", g), M(3, g), [], [derb])

    def derive_late():
      for g in range(2):
        k.vcopy(DER("G2", g), M(5, g), [], [derb])
        k.ts(DER("T3", g), M(7, g), 1.0, ALU.add, [], [derb])
        k.tt(DER("H3S", g), DER("T3", g), lng[:, 8:16], ALU.mult, [constb], [derb])
        k.tt(DER("H3B", g), DER("T3", g), lnb[:, 8:16], ALU.mult, [constb], [derb])
        k.tt(DER("H3B", g), DER("H3B", g), M(6, g), ALU.add, [], [derb])
        k.ts(DER("HG3", g), M(8, g), 0.5, ALU.mult, [], [derb])

    oH = k.alloc(RH, NK * T * 2)
    H3 = k.bfv(oH, NK * T).rearrange("p (k t) -> p k t", k=NK)
    Hb = [Buf("H%d" % i) for i in range(NTB)]
    yprod = [k.dprod("y%d" % i) for i in range(NTB)]

    def ln_alloc(reg):
        o = {}
        o["yb"] = [k.bfv(k.alloc(reg, 1024), 512) for _ in range(NK)]
        o["ysq"] = [k.bfv(k.alloc(reg, 1024), 512) for _ in range(NK)]
        o["ybb"] = [Buf() for _ in range(NK)]
        o["ysqb"] = [Buf() for _ in range(NK)]
        o["mean"] = [k.f32v(k.alloc(reg, 2048), 512) for _ in range(2)]
        o["tmp"] = k.f32v(k.alloc(reg, 2048), 512)
        o["rstd"] = [k.f32v(k.alloc(reg, 2048), 512) for _ in range(2)]
        o["meanb"], o["tmpb"], o["rstdb"] = [Buf(), Buf()], Buf(), [Buf(), Buf()]
        return o

    def ln_stats_chunk(tb, L, kk):
        tsl = slice(tb * TBS, (tb + 1) * TBS)
        k.act(L["yb"][kk], R3[:, kk, tsl], AF.Copy, [Rbk[tb][kk]], [L["ybb"][kk]])
        k.act(L["ysq"][kk], R3[:, kk, tsl], AF.Square, [Rbk[tb][kk]], [L["ysqb"][kk]])

    def ln_pe_stats(tb, L, b6=6, b7=7):
        for kk in range(NK):
            k.pe_mm(ps(b6), ones_ln, L["yb"][kk], kk == 0, kk == NK - 1, [L["ybb"][kk], onesb], [bank[b6]])
        for kk in range(NK):
            k.pe_mm(ps(b7), ones_ln, L["ysq"][kk], kk == 0, kk == NK - 1, [L["ysqb"][kk], onesb], [bank[b7]])

    def ln_fin_a(tb, L, b6=6, b7=7):
        par = tb % 2
        k.vcopy(L["mean"][par], ps(b6), [bank[b6]], [L["meanb"][par]])
        k.tt(L["tmp"], L["mean"][par], L["mean"][par], ALU.mult, [L["meanb"][par]], [L["tmpb"]])
        k.tt(L["tmp"], ps(b7), L["tmp"], ALU.subtract, [bank[b7]], [L["tmpb"]])
        k.act(L["rstd"][par], L["tmp"], AF.Ln, [L["tmpb"]], [L["rstdb"][par]], bias=LN_EPS, scale=1.0)
        k.act(L["rstd"][par], L["rstd"][par], AF.Exp, [], [L["rstdb"][par]], scale=-0.5)

    def ln_fin_b1(tb, L):
        par = tb % 2
        tsl = slice(tb * TBS, (tb + 1) * TBS)
        Rt = R3[:, :, tsl]
        k.tt(Rt, Rt, L["mean"][par].unsqueeze(1).to_broadcast([128, NK, TBS]), ALU.subtract,
             [L["meanb"][par]], Rbk[tb])

    def ln_fin_b2(tb, L, rs, rb, hs=None, hb=None, final=False):
        par = tb % 2
        g = tb // 2
        tsl = slice(tb * TBS, (tb + 1) * TBS)
        Rt = R3[:, :, tsl]
        k.tt(Rt, Rt, L["rstd"][par].unsqueeze(1).to_broadcast([128, NK, TBS]), ALU.mult,
             [L["rstdb"][par]], Rbk[tb])
        for kk in range(NK):
            if hs is not None:
                k.ts(H3[:, kk, tsl], R3[:, kk, tsl], DERk(hs, g, kk), ALU.mult, [Rbk[tb][kk], derb], [Hb[tb]],
                     s2=DERk(hb, g, kk), op1=ALU.add)
            if hs is None and kk % 2 == 1:
                k.ts(R3[:, kk, tsl], R3[:, kk, tsl], DERk(rs, None, kk), ALU.mult, [derb], [Rbk[tb][kk]],
                     s2=DERk(rb, None, kk), op1=ALU.add)
            else:
                k.act(R3[:, kk, tsl], R3[:, kk, tsl], AF.Identity, [derb], [Rbk[tb][kk]],
                      bias=DERk(rb, None, kk), scale=DERk(rs, None, kk))
        if final:
            k.dma("sp", dr["yT"][:, :, tsl], R3[:, :, tsl], yprod[tb], Rbk[tb], [])

    def ln_finish(tb, L, rs, rb, hs=None, hb=None, final=False, b6=6, b7=7):
        ln_fin_a(tb, L, b6, b7)
        ln_fin_b1(tb, L)
        ln_fin_b2(tb, L, rs, rb, hs=hs, hb=hb, final=final)

    stgprods = [yprod[0], yprod[1]]

    def ffn_alloc():
        reg = RQX
        reg.reset()
        o = {}
        o["gbuf"] = [[k.bfv(k.alloc(reg, 1024), 512) for _ in range(3)] for _ in range(2)]
        o["gb"] = [[Buf() for _ in range(3)] for _ in range(2)]
        o["sa"] = [k.f32v(k.alloc(reg, 2048), 512) for _ in range(2)]
        o["sab"] = [Buf(), Buf()]
        o["Ln"] = ln_alloc(reg)
        o["stage"] = [(k.f32v(k.alloc(reg, 12288), 3072), Buf("stg%d" % i), stgprods[i]) for i in range(2)]
        return o

    def ffn_phase(hgname, L, ln_args, between=None, pre_stt=None):
        fb = ffn_alloc() if L is None else L
        gbuf, gb, sa, sab, Ln = fb["gbuf"], fb["gb"], fb["sa"], fb["sab"], fb["Ln"]
        k.stage = fb["stage"]
        cnt = 0
        cntf = 0
        gi = 0
        for pi, (c0, n) in enumerate(PIECES):
            si = k.take()
            s = slot[si]
            last = (pi == len(PIECES) - 1)
            for tb in range(NTB):
                g = tb // 2
                tsl = slice(tb * TBS, (tb + 1) * TBS)
                gs = gi % 2
                gi += 1
                for j in range(n):
                    a, b2 = cnt % 2, 2 + cnt % 2
                    cnt += 1
                    k.pe_group(ps(a), [(s[:, kk * 384 + j * 128: kk * 384 + (j + 1) * 128], H3[:, kk, tsl])
                                       for kk in range(NK)], [slotb[si], Hb[tb]], [bank[a]])
                    k.pe_group(ps(b2), [(s[:, 3072 + kk * 384 + j * 128: 3072 + kk * 384 + (j + 1) * 128],
                                         H3[:, kk, tsl]) for kk in range(NK)], [slotb[si], Hb[tb]], [bank[b2]])
                    k.act(sa[a], ps(a), AF.Silu, [bank[a]], [sab[a]])
                    k.tt(gbuf[gs][j], sa[a], ps(b2), ALU.mult, [sab[a], bank[b2]], [gb[gs][j]])
                if last and tb >= 1:
                    ln_fin_b1(tb - 1, Ln)
                if pre_stt is not None and pi == 0 and tb == 0:
                    pre_stt()
                for oc in range(NK):
                    f = 4 + cntf % 2
                    cntf += 1
                    k.pe_group(ps(f), [(s[:, 6144 + j * 1024 + oc * 128: 6144 + j * 1024 + (oc + 1) * 128],
                                        gbuf[gs][j]) for j in range(n)],
                               [slotb2[si]] + [gb[gs][j] for j in range(n)], [bank[f]])
                    k.stt(R3[:, oc, tsl], ps(f), DERk(hgname, g, oc), R3[:, oc, tsl], ALU.mult, ALU.add,
                          [bank[f], derb], [Rbk[tb][oc]])
                    if last:
                        ln_stats_chunk(tb, Ln, oc)
                if last:
                    if tb >= 1:
                        ln_fin_b2(tb - 1, Ln, *ln_args[0], **ln_args[1])
                    ln_pe_stats(tb, Ln)
                    ln_fin_a(tb, Ln)
                    if tb == NTB - 1:
                        ln_fin_b1(tb, Ln)
                        ln_fin_b2(tb, Ln, *ln_args[0], **ln_args[1])
            if pi + 2 >= len(PIECES):
                k.stage = None
            k.prefetch(si)
            if between is not None:
                between(pi)

    for tb in range(NTB):
        g = tb // 2
        tsl = slice(tb * TBS, (tb + 1) * TBS)
        for kk in range(NK):
            if kk % 2 == 1:
                k.ts(H3[:, kk, tsl], R3[:, kk, tsl], DERk("A1", g, kk), ALU.mult, [Rbk[tb][kk], derb], [Hb[tb]],
                     s2=DERk("SH1", g, kk), op1=ALU.add)
            else:
                k.act(H3[:, kk, tsl], R3[:, kk, tsl], AF.Identity, [Rbk[tb][kk], derb], [Hb[tb]],
                      bias=DERk("SH1", g, kk), scale=DERk("A1", g, kk))
        k.ts(R3[:, :, tsl], R3[:, :, tsl], ALPHA, ALU.mult, [], Rbk[tb])

    cut(0)
    def ffn1_between(pi):
        if pi == 3:
            mod_compute(3, 6)
        if pi == 5:
            mod_compute(4, 6)
            mod_finish(3, 5, 6)
            derive_mix()

    def ffn1_pre_stt():
        mod_compute(2, 6)
        mod_finish(2, 3, 6)
        for g in range(2):
            k.ts(DER("HG1", g), M(2, g), 0.5, ALU.mult, [], [derb])

    ffn_phase("HG1", None, (("RS1", "RB1"), {}), between=ffn1_between, pre_stt=ffn1_pre_stt)
    k.barrier()
    cut(1)

    RH.reset(); RQ.reset(); RP2.reset(); RX.reset(); RHQ.reset()
    pooled2 = k.bfv(k.alloc(RP2, 4 * T * 2), 4 * T).rearrange("p (g t) -> p g t", g=4)
    p2b = [Buf("p2_%d" % i) for i in range(NTB)]
    oKp = k.alloc(RX, 2 * 1024 * 2)
    oVp = k.alloc(RX, 8 * 256 * 2)
    Kp = k.bfv(oKp, 2048).rearrange("p (h t) -> p h t", h=2)
    Vp = k.bfv(oVp, 2048).rearrange("p (j c) -> p j c", j=8)
    Kpb = [Buf(), Buf()]
    Vpb = [Buf(), Buf()]
    xmark = RX.top

    A = RHQ
    h2t = k.bfv(k.alloc(A, 8192), 4096).rearrange("p (k t) -> p k t", k=NK)
    h2b = [Buf("h2t%d" % i) for i in range(NK)]
    sq = [k.bfv(k.alloc(A, 1024), 512) for _ in range(2)]
    sqb = [Buf(), Buf()]
    sd = [k.f32v(k.alloc(A, 2048), 512) for _ in range(2)]
    sdb = [Buf(), Buf()]
    kn = [k.f32v(k.alloc(A, 2048), 512) for _ in range(2)]
    knb_ = [Buf(), Buf()]
    knbf = k.bfv(k.alloc(A, 1024), 512)
    knbfb = Buf()
    t1 = k.f32v(k.alloc(A, 2048), 512)
    t2 = k.f32v(k.alloc(A, 2048), 512)
    t1b, t2b = Buf(), Buf()
    kst = [k.bfv(k.alloc(A, 1024), 512) for _ in range(2)]
    kstb = [Buf(), Buf()]
    kstp = [k.dprod("kst0"), k.dprod("kst1")]
    knp = [k.dprod("kn0"), k.dprod("kn1")]
    vst = [k.f32v(k.alloc(A, 1024), 256) for _ in range(2)]
    vstb = [Buf(), Buf()]
    vstp = [k.dprod("vst0"), k.dprod("vst1")]
    vsb = [k.bfv(k.alloc(A, 512), 256) for _ in range(2)]
    vsbb = [Buf(), Buf()]
    vsbp = [k.dprod("vsb0"), k.dprod("vsb1")]
    ropeC = k.f32v(k.alloc(RX, 4096), 1024)
    ropeS = k.f32v(k.alloc(RX, 4096), 1024)
    ropeb = Buf("rope")
    ropep = k.dprod("rope")
    pp = [k.f32v(k.alloc(A, 2176), 544).rearrange("p (b t) -> p b t", b=2) for _ in range(2)]
    ppb = [Buf(), Buf()]
    pin_s = k.f32v(k.alloc(A, 4 * 1040 * 4), 4 * 1040).rearrange("p (g t) -> p g t", g=4)
    pinb = [Buf("pin_s%d" % i) for i in range(4)]
    ta = k.f32v(k.alloc(A, 4160), 1040)
    tb_ = k.f32v(k.alloc(A, 4160), 1040)
    tab, tbb = Buf(), Buf()
    plb = [k.bfv(k.alloc(A, 1024), 512) for _ in range(2)]
    plbb = [Buf(), Buf()]
    e32 = k.f32v(k.alloc(A, 256), 64).rearrange("p (g i) -> p g i", g=4)
    est = k.bfv(k.alloc(A, 256), 128)
    estb = Buf()
    estp = k.dprod("est")
    eg = k.bfv(k.alloc(A, 1024), 512).rearrange("p (r c) -> p r c", r=4)
    egb = Buf()
    egp = k.dprod("eg")
    ef = k.f32v(k.alloc(A, 1024), 256).rearrange("p (r c) -> p r c", r=4)
    efb = Buf()
    agb = Buf("ag_in")
    agob = Buf("ag_out")
    ag2b = Buf("ag2_in")
    ag2ob = Buf("ag2_out")
    ccprod = k.new_prod("cc", 1)
    k.dprods.append(ccprod)

    k.dma("sp", ropeC, dr["ropeCS"][:, 0:1024], ropep, (), [ropeb])
    k.dma("sp", ropeS, dr["ropeCS"][:, 1024:2048], ropep, (), [ropeb])
    for i in range(2):
        k.memset(pp[i], 0.0, [ppb[i]])
    k.memset(pin_s, 0.0, pinb)

    def mod2(tb, h2t_=None, h2b_=None, on_dve=False):
        g = tb // 2
        tsl = slice(tb * TBS, (tb + 1) * TBS)
        h2t_ = h2t if h2t_ is None else h2t_
        h2b_ = h2b if h2b_ is None else h2b_
        for kk in range(NK):
            if on_dve:
                k.ts(h2t_[:, kk, :], R3[:, kk, tsl], DERk("H2S", g, kk), ALU.mult, [Rbk[tb][kk], derb],
                     [h2b_[kk]], s2=DERk("H2B", g, kk), op1=ALU.add)
                continue
            k.act(h2t_[:, kk, :], R3[:, kk, tsl], AF.Identity, [Rbk[tb][kk], derb], [h2b_[kk]],
                  bias=DERk("H2B", g, kk), scale=DERk("H2S", g, kk))

    cnts = {"p": 0, "m": 0, "r": 0, "x": 0}

    def rms_head(si, col0, gcol, tb, sample, out_bf, out_bufs, kout=None):
        s = slot[si]
        pb = cnts["p"] % 3
        cnts["p"] += 1
        mb = 3 + cnts["m"] % 2
        cnts["m"] += 1
        x = cnts["x"] % 2
        cnts["x"] += 1
        k.pe_group(ps(pb), [(s[:, kk * 1024 + col0: kk * 1024 + col0 + 128], h2t[:, kk, :]) for kk in range(NK)],
                   [slotb[si]] + h2b, [bank[pb]])
        k.act(sq[x], ps(pb), AF.Square, [bank[pb]], [sqb[x]])
        k.pe_mm(ps(mb), ones_h, sq[x], True, True, [sqb[x], onesb], [bank[mb]])
        k.act(sd[x], ps(mb), AF.Ln, [bank[mb]], [sdb[x]], bias=RMS_EPS, scale=1.0)
        k.act(sd[x], sd[x], AF.Exp, [], [sdb[x]], scale=-0.5)
        if not sample:
            if kout is None:
                k.stt(out_bf, ps(pb), qkg[:, gcol:gcol + 1], sd[x], ALU.mult, ALU.mult,
                      [bank[pb], sdb[x], constb], out_bufs)
            else:
                k.stt(kn[x], ps(pb), qkg[:, gcol:gcol + 1], sd[x], ALU.mult, ALU.mult,
                      [bank[pb], sdb[x], constb], [knb_[x]])
                if "kvout" not in SKIP:
                    k.dma("sp", kout, kn[x], knp[x], [knb_[x]], [])
                k.act(out_bf, kn[x], AF.Copy, [knb_[x]], out_bufs)
        else:
            rbk = 5 + cnts["r"] % 2
            cnts["r"] += 1
            tq = slice((tb - 2) * TBS, (tb - 1) * TBS)
            k.stt(kn[x], ps(pb), qkg[:, gcol:gcol + 1], sd[x], ALU.mult, ALU.mult,
                  [bank[pb], sdb[x], constb], [knb_[x]])
            k.act(knbf, kn[x], AF.Copy, [knb_[x]], [knbfb])
            k.pe_mm(ps(rbk), Pm, knbf, True, True, [knbfb, constb], [bank[rbk]])
            k.tt(t1, kn[x], ropeC[:, tq], ALU.mult, [knb_[x], ropeb], [t1b])
            k.tt(t2, ps(rbk), ropeS[:, tq], ALU.mult, [bank[rbk], ropeb], [t2b])
            k.tt(out_bf, t1, t2, ALU.add, [t1b, t2b], out_bufs)

    def pool_segment(src3, nseg, n, gq, corr_off, tbo, outcol0):
        w = WINS[gq]
        half = w // 2
        L = n + 16
        cur = src3
        curb = None
        bufs = [(ta, tab), (tb_, tbb)]
        bi = 0
        d = 1
        Lc = L
        while d < w:
            dst, dstb = bufs[bi]
            bi ^= 1
            dv = dst[:, 0:nseg * L].rearrange("p (b t) -> p b t", b=nseg)
            Ln_ = Lc - d
            rd = [] if curb is None else [curb]
            k.tt(dv[:, :, 0:Ln_], cur[:, :, 0:Ln_], cur[:, :, d:d + Ln_], ALU.add, rd + list(tbo["src"]), [dstb])
            cur, curb, Lc = dv, dstb, Ln_
            d *= 2
        o0 = 8 - half
        dst, dstb = bufs[bi]
        dv = dst[:, 0:nseg * n].rearrange("p (b t) -> p b t", b=nseg)
        k.ts(dv, cur[:, :, o0:o0 + n], 1.0 / w, ALU.mult, [curb], [dstb])
        cl = corr[:, corr_off + gq * 16: corr_off + gq * 16 + 8].unsqueeze(1).to_broadcast([128, nseg, 8])
        cr = corr[:, corr_off + gq * 16 + 8: corr_off + gq * 16 + 16].unsqueeze(1).to_broadcast([128, nseg, 8])
        k.tt(dv[:, :, 0:8], dv[:, :, 0:8], cl, ALU.mult, [constb], [dstb])
        k.tt(dv[:, :, n - 8:n], dv[:, :, n - 8:n], cr, ALU.mult, [constb], [dstb])
        tot = nseg * n
        segs_per = 512 // n if n < 512 else 1
        for c in range(tot // 512):
            x = cnts["x"] % 2
            cnts["x"] += 1
            if n < 512:
                k.tt(plb[x].rearrange("p (b t) -> p b t", b=nseg), dv, src3[:, :, 8:8 + n], ALU.subtract,
                     [dstb] + list(tbo["src"]), [plbb[x]])
            else:
                k.tt(plb[x], dst[:, c * 512:(c + 1) * 512], src3[:, 0, 8 + c * 512: 8 + (c + 1) * 512],
                     ALU.subtract, [dstb] + list(tbo["src"]), [plbb[x]])
            k.pe_mm(ps(7), poolw[:, gq * 128:(gq + 1) * 128], plb[x], True, True, [plbb[x], constb], [bank[7]])
            k.act(pooled2[:, gq, outcol0 + c * 512: outcol0 + (c + 1) * 512], ps(7), AF.Identity,
                  [bank[7], constb], tbo["dst"][c], scale=psc[:, gq:gq + 1])

    si1 = k.take()
    s1 = slot[si1]
    vcnt = [0]
    OVERLAP_CC = os.environ.get("KNOOVERLAP") is None

    def part_kv(tb):
        sample = tb >= 2
        tsl = slice(tb * TBS, (tb + 1) * TBS)
        for hd in range(2):
            if sample:
                x = cnts["x"] % 2
                rms_head(si1, hd * 128, 1, tb, True, kst[x], [kstb[x]])
                c0 = hd * 1024 + (tb - 2) * TBS
                k.dma("sp", ag_in_ap[:, c0:c0 + TBS], kst[x], kstp[x], [kstb[x], agb], [])
            else:
                rms_head(si1, hd * 128, 1, tb, False, Kp[:, hd, tsl], [Kpb[tb]],
                         kout=dr["nk"][:, hd, tsl])
        for tt_ in range(4):
            pb = cnts["p"] % 3
            cnts["p"] += 1
            tile = (tb % 2) * 4 + tt_
            k.pe_group(ps(pb)[:, 0:256], [(h2t[:, kk, tt_ * 128:(tt_ + 1) * 128], s1[:, kk * 1024 + 256: kk * 1024 + 512])
                                          for kk in range(NK)], [slotb[si1]] + h2b, [bank[pb]])
            v = vcnt[0] % 2
            vcnt[0] += 1
            if sample:
                k.act(vsb[v], ps(pb)[:, 0:256], AF.Copy, [bank[pb]], [vsbb[v]])
                c0 = 2048 + tile * 256
                k.dma("sp", ag_in_ap[:, c0:c0 + 256], vsb[v], vsbp[v], [vsbb[v], agb], [])
            else:
                k.vcopy(vst[v], ps(pb)[:, 0:256], [bank[pb]], [vstb[v]])
                k.dma("sp", dr["nv"][:, tile, :], vst[v], vstp[v], [vstb[v]], [])
                k.act(Vp[:, tile, :], vst[v], AF.Copy, [vstb[v]], [Vpb[tb]])

    def part_pin(tb):
        sample = tb >= 2
        for gq in range(4):
            pb = cnts["p"] % 3
            cnts["p"] += 1
            k.pe_group(ps(pb), [(s1[:, kk * 1024 + 512 + gq * 128: kk * 1024 + 512 + (gq + 1) * 128], h2t[:, kk, :])
                                for kk in range(NK)], [slotb[si1]] + h2b, [bank[pb]])
            if sample:
                c0 = 8 + (tb - 2) * TBS
                k.act(pin_s[:, gq, c0:c0 + TBS], ps(pb), AF.Copy, [bank[pb]], [pinb[gq]])
            else:
                x = gq % 2
                k.act(pp[x][:, :, 8:264], ps(pb).rearrange("p (b t) -> p b t", b=2), AF.Copy, [bank[pb]], [ppb[x]])
                pool_segment(pp[x], 2, 256, gq, 64, {"src": [ppb[x]], "dst": [[p2b[tb]]]}, tb * TBS)

    def collectives():
        k.vcopy(e32[:, :, 0:8], pin_s[:, :, 8:16], pinb, [efb])
        k.vcopy(e32[:, :, 8:16], pin_s[:, :, 1024:1032], pinb, [efb])
        e32f = e32.rearrange("p g i -> p (g i)")
        k.vcopy(est[:, 0:64], e32f, [efb], [estb])
        k.tt(est[:, 64:128], e32f, est[:, 0:64], ALU.subtract, [efb], [estb])
        k.dma("sp", dr["ag2_in"][:, :], est, estp, [estb, ag2b], [])
        k.barrier()
        E = k.E["pool"]
        for (src, dst, sb, db) in ((dr["ag2_in"], dr["ag2_out"], ag2b, ag2ob), (ag_in_ap, ag_out_ap, agb, agob)):
            k._deps(E, [], [sb, db])
            ins = nc.gpsimd.collective_compute("AllGather", ALU.bypass,
                                               replica_groups=[[0, 1, 2, 3], [4, 5, 6, 7]],
                                               ins=[src.opt()], outs=[dst.opt()])
            ccprod.count += 1
            ins.then_inc(ccprod.sem, 1)
            k._commit((ccprod, ccprod.count), [], [sb, db])
        if OVERLAP_CC:
            for qn in ("sp", "pool"):
                k.E[qn].wait((ccprod, ccprod.count))
        else:
            k.barrier()

    if OVERLAP_CC:
        for tb in (2, 3):
            mod2(tb, on_dve=True)
            part_kv(tb)
            part_pin(tb)
        for tb in (0, 1):
            mod2(tb, on_dve=True)
            part_kv(tb)
        collectives()
        for tb in (0, 1):
            mod2(tb)
            part_pin(tb)
        k.barrier()
    else:
        for tb in (2, 3, 0, 1):
            mod2(tb)
            part_kv(tb)
            part_pin(tb)
            if tb == 3:
                collectives()
    if "cc" in SKIP or "cutpost" in SKIP:
        cut(2)
    k.dma("sp", eg, dr["ag2_out"][:, :].rearrange("(r p) c -> p r c", p=128), egp, [ag2ob], [egb])
    k.tt(ef, eg[:, :, 0:64], eg[:, :, 64:128], ALU.add, [egb], [efb])
    ef4 = ef.rearrange("p r (g i) -> p r g i", g=4)
    for side, (dst, lo) in enumerate(((pin_s[:, :, 0:8], 8), (pin_s[:, :, 1032:1040], 0))):
        for r in range(4):
            m = mask[:, side * 4 + r: side * 4 + r + 1]
            src = ef4[:, r, :, lo:lo + 8]
            if r == 0:
                k.ts(dst, src, m, ALU.mult, [efb, constb], pinb)
            else:
                k.stt(dst, src, m, dst, ALU.mult, ALU.add, [efb, constb], pinb)
    for gq in range(4):
        pool_segment(pin_s[:, gq:gq + 1, :], 1, 1024, gq, 0,
                     {"src": [pinb[gq]], "dst": [[p2b[2]], [p2b[3]]]}, 1024)
    k.prefetch(si1)
    k.barrier()
    cut(2)

    RH.reset(); RQ.reset()
    RX.top = xmark
    Q3 = k.bfv(k.alloc(RQ, NK * T * 2), NK * T).rearrange("p (h t) -> p h t", h=NK)
    Qb = [[Buf("Q%d_%d" % (h, i)) for i in range(NTB)] for h in range(NK)]
    B1 = RH
    h2t2 = [k.bfv(k.alloc(B1, 8192), 4096).rearrange("p (k t) -> p k t", k=NK) for _ in range(2)]
    h2b2 = [[Buf() for _ in range(NK)] for _ in range(2)]
    sq = [k.bfv(k.alloc(B1, 1024), 512) for _ in range(2)]
    sqb = [Buf(), Buf()]
    sd = [k.f32v(k.alloc(B1, 2048), 512) for _ in range(3)]
    sdb = [Buf() for _ in range(3)]
    kn = [k.f32v(k.alloc(B1, 2048), 512) for _ in range(3)]
    knb_ = [Buf() for _ in range(3)]
    knbf2 = [k.bfv(k.alloc(B1, 1024), 512) for _ in range(2)]
    knbfb2 = [Buf(), Buf()]
    ropeC = k.f32v(k.alloc(RX, 4096), 1024)
    ropeS = k.f32v(k.alloc(RX, 4096), 1024)
    t1 = k.f32v(k.alloc(RX, 2048), 512)
    t2 = k.f32v(k.alloc(RX, 2048), 512)
    t1b, t2b = Buf(), Buf()
    ropeb = Buf("rope2")
    k.dma("sp", ropeC, dr["ropeCS"][:, 0:1024], ropep, (), [ropeb])
    k.dma("sp", ropeS, dr["ropeCS"][:, 1024:2048], ropep, (), [ropeb])
    si2 = k.take()
    s2 = slot[si2]
    items = [(tb, hd) for tb in range(NTB) for hd in range(NK)]
    NI = len(items)

    def mod2c(tb, kk):
        g = tb // 2
        tsl = slice(tb * TBS, (tb + 1) * TBS)
        k.ts(h2t2[tb % 2][:, kk, :], R3[:, kk, tsl], DERk("H2S", g, kk), ALU.mult, [Rbk[tb][kk], derb],
             [h2b2[tb % 2][kk]], s2=DERk("H2B", g, kk), op1=ALU.add)

    def P1(i):
        tb, hd = items[i]
        pb, x = i % 3, i % 2
        k.pe_group(ps(pb), [(s2[:, kk * 1024 + hd * 128: kk * 1024 + (hd + 1) * 128], h2t2[tb % 2][:, kk, :])
                            for kk in range(NK)], [slotb[si2]] + h2b2[tb % 2], [bank[pb]])
        k.act(sq[x], ps(pb), AF.Square, [bank[pb]], [sqb[x]])
        if tb + 1 < NTB:
            mod2c(tb + 1, hd)

    def P2(i):
        tb, hd = items[i]
        pb, x, mb, y = i % 3, i % 2, 3 + i % 2, i % 3
        tsl = slice(tb * TBS, (tb + 1) * TBS)
        k.pe_mm(ps(mb), ones_h, sq[x], True, True, [sqb[x], onesb], [bank[mb]])
        k.act(sd[y], ps(mb), AF.Ln, [bank[mb]], [sdb[y]], bias=RMS_EPS, scale=1.0)
        k.act(sd[y], sd[y], AF.Exp, [], [sdb[y]], scale=-0.5)
        if tb < 2:
            k.stt(Q3[:, hd, tsl], ps(pb), qkg[:, 0:1], sd[y], ALU.mult, ALU.mult,
                  [bank[pb], sdb[y], constb], [Qb[hd][tb]])
        else:
            k.stt(kn[y], ps(pb), qkg[:, 0:1], sd[y], ALU.mult, ALU.mult,
                  [bank[pb], sdb[y], constb], [knb_[y]])
            k.act(knbf2[x], kn[y], AF.Copy, [knb_[y]], [knbfb2[x]])

    def P3(i):
        tb, hd = items[i]
        if tb < 2:
            return
        x, y, rbk = i % 2, i % 3, 5 + i % 2
        tsl = slice(tb * TBS, (tb + 1) * TBS)
        tq = slice((tb - 2) * TBS, (tb - 1) * TBS)
        k.pe_mm(ps(rbk), Pm, knbf2[x], True, True, [knbfb2[x], constb], [bank[rbk]])
        k.tt(t1, kn[y], ropeC[:, tq], ALU.mult, [knb_[y], ropeb], [t1b])
        k.tt(t2, ps(rbk), ropeS[:, tq], ALU.mult, [bank[rbk], ropeb], [t2b])
        k.tt(Q3[:, hd, tsl], t1, t2, ALU.add, [t1b, t2b], [Qb[hd][tb]])

    for kk in range(NK):
        mod2c(0, kk)
    for step in range(NI + 2):
        if step < NI:
            P1(step)
        if 0 <= step - 1 < NI:
            P2(step - 1)
        if 0 <= step - 2 < NI:
            P3(step - 2)
    k.prefetch(si2)
    k.barrier()
    cut(3)

    RH.reset()
    B2 = RH
    KC = k.bfv(k.alloc(B2, 1024), 512).rearrange("p (h t) -> p h t", h=2)
    VC = k.bfv(k.alloc(B2, 1024), 512).rearrange("p (j c) -> p j c", j=2)
    ctxb = Buf("ctx")
    ctxp = k.dprod("ctx")
    KA0 = k.bfv(k.alloc(B2, 8192), 4096).rearrange("p (r t) -> p r t", r=4)
    VA0 = k.bfv(k.alloc(B2, 8192), 4096).rearrange("p (j c) -> p j c", j=32)
    kab0, vab0 = Buf("KA"), Buf("VA")
    kap, vap = k.dprod("KA"), k.dprod("VA")
    SB = [0, 1, 2, 7]
    PT8 = [k.bfv(k.alloc(B2, 1024), 512) for _ in range(8)]
    PT8b = [Buf() for _ in range(8)]
    PT, PTb = PT8[:4], PT8b[:4]
    RX.top = xmark
    xs = k.f32v(k.alloc(RX, 2048), 512)
    xsb = Buf("xs")
    Esel = k.f32v(k.alloc(RX, 512), 128)
    eselb = Buf("esel")
    eselp = k.dprod("esel")
    k.dma("sp", Esel, dr["esel"][:, :], eselp, (), [eselb])
    rec = [k.f32v(k.alloc(B2, 2048), 512) for _ in range(2)]
    recb = [Buf(), Buf()]
    k.dma("pool", KC, dr["kctx"][:, :].rearrange("p (h t) -> p h t", h=2), ctxp, (), [ctxb])
    k.dma("pool", VC, dr["vctx"][:, :].rearrange("p (j c) -> p j c", j=2), ctxp, (), [ctxb])

    it = 0
    for b in range(4):
        tb = b // 2
        bsl = slice(b * 256, (b + 1) * 256)
        for pr in range(4):
            kvh = pr // 2
            ob, sb_ = 3 + it % 2, 5 + it % 2
            r = it % 2
            it += 1
            qv = Q3[:, 2 * pr:2 * pr + 2, bsl]
            qbufs = [Qb[2 * pr][tb], Qb[2 * pr + 1][tb]]
            pts = []
            for kc in range(2):
                pi_ = cnts["p"] % 4
                sbk = SB[pi_]
                cnts["p"] += 1
                k.pe_mm(ps(sbk).rearrange("p (a t) -> p a t", a=2),
                        Kp[:, kvh, b * 256 + kc * 128: b * 256 + (kc + 1) * 128], qv, True, True,
                        [Kpb[tb]] + qbufs, [bank[sbk]])
                k.act(PT[pi_], ps(sbk), AF.Exp, [bank[sbk]], [PTb[pi_]], scale=SCALE)
                pts.append(pi_)
            for kc in range(2):
                x = pts[kc]
                k.pe_mm(ps(ob), Vp[:, b * 2 + kc, kvh * 128:(kvh + 1) * 128], PT[x], kc == 0, kc == 1,
                        [Vpb[tb], PTb[x]], [bank[ob]])
                k.pe_mm(ps(sb_), ones1, PT[x], kc == 0, kc == 1, [PTb[x], onesb], [bank[sb_]])
            k.act(rec[r], ps(sb_), AF.Ln, [bank[sb_]], [recb[r]])
            k.act(rec[r], rec[r], AF.Exp, [], [recb[r]], scale=-1.0)
            k.tt(qv, ps(ob).rearrange("p (a t) -> p a t", a=2), rec[r].rearrange("p (a t) -> p a t", a=2),
                 ALU.mult, [bank[ob], recb[r]], qbufs)

    KA1 = k.bfv(k.alloc(RX, 8192), 4096).rearrange("p (r t) -> p r t", r=4)
    VA1 = k.bfv(oKp, 4096).rearrange("p (j c) -> p j c", j=32)
    kab1, vab1 = Buf("KA1"), Buf("VA1")
    kvt = [(KA0, VA0, kab0, vab0, []), (KA1, VA1, kab1, vab1, Kpb + Vpb)]
    kvp_ = [(kap, vap), (k.dprod("KA1"), k.dprod("VA1"))]
    for kvh in range(2):
        KA, VA, kab, vab, extra = kvt[kvh]
        k.dma("sp", KA, ag_out_ap[:, kvh * 1024:(kvh + 1) * 1024].rearrange("(r p) c -> p r c", p=128),
              kvp_[kvh][0], [agob], [kab])
        for r_ in range(4):
            k.dma("sp", VA[:, r_ * 8:(r_ + 1) * 8, :],
                  ag_out_ap[r_ * 128:(r_ + 1) * 128, 2048:4096].rearrange("p (j c) -> p j c", j=8)[:, :, kvh * 128:(kvh + 1) * 128],
                  kvp_[kvh][1], [agob], [vab] + extra)
    for kvh in range(2):
        KA, VA, kab, vab, extra = kvt[kvh]
        chunks = []
        for c in range(2):
            chunks.append((KC[:, kvh, c * 128:(c + 1) * 128], VC[:, c, kvh * 128:(kvh + 1) * 128], [ctxb]))
        for r_ in range(4):
            for j in range(8):
                chunks.append((KA[:, r_, j * 128:(j + 1) * 128], VA[:, r_ * 8 + j, :], [kab, vab]))
        NCH = len(chunks)
        for hd in range(4 * kvh, 4 * kvh + 4):
            for tb in (2, 3):
                tsl = slice(tb * TBS, (tb + 1) * TBS)
                ob, sb_ = 3 + it % 2, 5 + it % 2
                r = it % 2
                it += 1
                qv = Q3[:, hd, tsl]
                sbks = {}

                def qk(c):
                    pi_ = cnts["p"] % 4
                    sbk = SB[pi_]
                    cnts["p"] += 1
                    sbks[c] = c % 8
                    k.pe_mm(ps(sbk), chunks[c][0], qv, True, True, chunks[c][2] + [Qb[hd][tb]], [bank[sbk]])
                    k.act(PT8[c % 8], ps(sbk), AF.Exp, [bank[sbk]], [PT8b[c % 8]], scale=SCALE)

                def sum_mm(cc):
                    j = cc % 4
                    x = sbks[cc]
                    k.op("pe", lambda: nc.tensor.matmul(ps(sb_)[32 * j:32 * j + 32, :], lhsT=ones1[:, 0:32],
                                                        rhs=PT8[x], start=(cc == j), stop=(cc + 4 >= NCH),
                                                        tile_position=(0, 32 * j)),
                         [PT8b[x], onesb], [bank[sb_]])

                qk(0)
                qk(1)
                qk(2)
                for c in range(NCH):
                    x = sbks[c]
                    k.pe_mm(ps(ob), chunks[c][1], PT8[x], c == 0, c == NCH - 1, chunks[c][2] + [PT8b[x]], [bank[ob]])
                    if c % 4 == 3 or c == NCH - 1:
                        for cc in range((c // 4) * 4, c + 1):
                            sum_mm(cc)
                    if c + 3 < NCH:
                        qk(c + 3)
                k.vcopy(xs, ps(sb_), [bank[sb_]], [xsb])
                k.pe_mm(ps(sb_), Esel, xs, True, True, [xsb, eselb], [bank[sb_]])
                k.recip(rec[r], ps(sb_), [bank[sb_]], [recb[r]])
                k.tt(qv, ps(ob), rec[r], ALU.mult, [bank[ob], recb[r]], [Qb[hd][tb]])
                if kvh == 0 and tb == 3:
                    mi = 5 + (hd - 4 * kvh)
                    mod_compute(mi, 7)
                    mod_finish(mi, mi + 1, 7)
                    if mi == 8:
                        derive_late()
    k.barrier()
    cut(4)

    RH.reset(); RX.reset()
    oH2 = k.alloc(RH, NK * T * 2)
    assert oH2 == oH
    GX = RX
    h2tg = [k.bfv(k.alloc(GX, 8192), 4096).rearrange("p (k t) -> p k t", k=NK) for _ in range(2)]
    h2bg = [[Buf() for _ in range(NK)] for _ in range(2)]
    sg = [k.f32v(k.alloc(GX, 2048), 512) for _ in range(2)]
    sgb = [Buf(), Buf()]
    m12 = sg
    m12b = sgb
    hi_ = 0
    mod2(0, h2tg[0], h2bg[0])
    for ocp in range(4):
        si = k.take()
        s = slot[si]
        for tb in range(NTB):
            h2t, h2b = h2tg[hi_ % 2], h2bg[hi_ % 2]
            hi_ += 1
            if not (ocp == 3 and tb == NTB - 1):
                mod2((tb + 1) % NTB, h2tg[hi_ % 2], h2bg[hi_ % 2])
            tsl = slice(tb * TBS, (tb + 1) * TBS)
            for o in range(2):
                oc = 2 * ocp + o
                base = 4 * o
                k.pe_group(ps(base + 0), [(s[:, kk * 256 + o * 128: kk * 256 + (o + 1) * 128], h2t[:, kk, :])
                                          for kk in range(NK)], [slotb[si]] + h2b, [bank[base + 0]])
                k.pe_group(ps(base + 1), [(s[:, 2048 + kk * 256 + o * 128: 2048 + kk * 256 + (o + 1) * 128], h2t[:, kk, :])
                                          for kk in range(NK)], [slotb[si]] + h2b, [bank[base + 1]])
                k.pe_group(ps(base + 2), [(s[:, 4096 + kk * 256 + o * 128: 4096 + kk * 256 + (o + 1) * 128], Q3[:, kk, tsl])
                                          for kk in range(NK)], [slotb[si]] + [Qb[kk][tb] for kk in range(NK)],
                           [bank[base + 2]])
                k.pe_group(ps(base + 3), [(s[:, 6144 + kk * 256 + o * 128: 6144 + kk * 256 + (o + 1) * 128], pooled2[:, kk, tsl])
                                          for kk in range(4)], [slotb[si], p2b[tb]], [bank[base + 3]])
                k.act(sg[0], ps(base + 0), AF.Sigmoid, [bank[base + 0]], [sgb[0]])
                k.act(sg[1], ps(base + 1), AF.Sigmoid, [bank[base + 1]], [sgb[1]])
                k.tt(m12[0], sg[0], ps(base + 2), ALU.mult, [sgb[0], bank[base + 2]], [m12b[0]])
                k.tt(m12[1], sg[1], ps(base + 3), ALU.mult, [sgb[1], bank[base + 3]], [m12b[1]])
                k.tt(H3[:, oc, tsl], m12[0], m12[1], ALU.add, [m12b[0], m12b[1]], [Hb[tb]])
        k.prefetch(si)
    k.barrier()
    cut(5)

    FB2 = ffn_alloc()
    L2 = FB2["Ln"]
    si = k.take()
    s = slot[si]
    cntf = 0
    for tb in range(NTB):
        g = tb // 2
        tsl = slice(tb * TBS, (tb + 1) * TBS)
        for oc in range(NK):
            f = 4 + cntf % 2
            cntf += 1
            k.pe_group(ps(f), [(s[:, kk * 1024 + oc * 128: kk * 1024 + (oc + 1) * 128], H3[:, kk, tsl])
                               for kk in range(NK)], [slotb[si], Hb[tb]], [bank[f]])
            k.stt(R3[:, oc, tsl], ps(f), DERk("G2", g, oc), R3[:, oc, tsl], ALU.mult, ALU.add,
                  [bank[f], derb], [Rbk[tb][oc]])
            ln_stats_chunk(tb, L2, oc)
        sbk6, sbk7 = ((6, 7), (2, 3))[tb % 2]
        ln_pe_stats(tb, L2, sbk6, sbk7)
        if tb >= 1:
            pb6, pb7 = ((6, 7), (2, 3))[(tb - 1) % 2]
            ln_finish(tb - 1, L2, "RS2", "RB2", hs="H3S", hb="H3B", b6=pb6, b7=pb7)
    pb6, pb7 = ((6, 7), (2, 3))[(NTB - 1) % 2]
    ln_finish(NTB - 1, L2, "RS2", "RB2", hs="H3S", hb="H3B", b6=pb6, b7=pb7)
    k.prefetch(si)
    cut(6)

    ffn_phase("HG3", FB2, (("RS3", "RB3"), {"final": True}))

    finalize()


_PROGRAM = None


def _host_consts():
    n_freq = 32
    inv_freq = (10000.0 ** (-np.arange(n_freq, dtype=np.float32) / n_freq)).astype(np.float32)
    pm = np.zeros((128, 128), np.float32)
    for d2 in range(64):
        pm[d2 + 64, d2] = -1.0
        pm[d2, d2 + 64] = 1.0

    def corr_tab(n, left, right):
        out = np.ones((4, 16), np.float32)
        for g, w in enumerate(WINS):
            for i in range(8):
                if left:
                    t = i
                    lo = min(max(t - w // 2, 0), n); hi = min(max(t - w // 2 + w, 0), n)
                    out[g, i] = w / float(hi - lo)
                if right:
                    t = n - 8 + i
                    lo = min(max(t - w // 2, 0), n); hi = min(max(t - w // 2 + w, 0), n)
                    out[g, 8 + i] = w / float(hi - lo)
        return out.reshape(64)
    return inv_freq, pm, corr_tab


def kernel(x_prompt, x_sample, cache_k, cache_v, c, c_ctx, w_mod, b_mod, ln_g, ln_b,
           ffn1_w1, ffn1_w2, w_in, q_norm_g, k_norm_g, pool_w, pool_scale,
           w_up_attn, w_up_pool, w_out, ffn2_w1, ffn2_w2):
    global _PROGRAM
    f32 = np.float32
    A = lambda a: np.ascontiguousarray(np.asarray(a, dtype=f32))
    x_prompt, x_sample, cache_k, cache_v = A(x_prompt), A(x_sample), A(cache_k), A(cache_v)
    c, c_ctx = A(c), A(c_ctx)
    inv_freq, pm, corr_tab = _host_consts()

    def fm(v):
        return np.ascontiguousarray(v.reshape(8, 128).T)

    def w1_pm(w):
        out = np.empty((128, 2, 8 * DFF), f32)
        pieces = [(0, 2)] + [(c0, 3) for c0 in range(2, 20, 3)] + [(20, 2)]
        for half in range(2):
            wh = w[:, half * DFF:(half + 1) * DFF].reshape(8, 128, DFF)
            for (c0, n) in pieces:
                blk = wh[:, :, c0 * 128:(c0 + n) * 128].transpose(1, 0, 2)
                out[:, half, 8 * 128 * c0: 8 * 128 * (c0 + n)] = blk.reshape(128, 8 * n * 128)
        return out

    shared = {
        "w_mod": A(w_mod)[0], "ffn1_w1": w1_pm(A(ffn1_w1)[0]), "ffn1_w2": A(ffn1_w2)[0],
        "ffn2_w1": w1_pm(A(ffn2_w1)[0]), "ffn2_w2": A(ffn2_w2)[0], "w_in": A(w_in)[0],
        "w_up_attn": A(w_up_attn)[0], "w_up_pool": A(w_up_pool)[0], "w_out": A(w_out)[0],
        "bmodT": np.ascontiguousarray(A(b_mod)[0].reshape(72, 128).T),
        "lngT": np.ascontiguousarray(A(ln_g)[0].reshape(24, 128).T),
        "lnbT": np.ascontiguousarray(A(ln_b)[0].reshape(24, 128).T),
        "qkg": np.ascontiguousarray(np.stack([A(q_norm_g)[0], A(k_norm_g)[0]], axis=1)),
        "pool_w": np.ascontiguousarray(A(pool_w)[0].transpose(1, 0, 2).reshape(128, 512)),
        "pscaleT": np.ascontiguousarray(A(pool_scale)[0].reshape(4, 128).T),
        "pmat": pm,
        "esel": np.ascontiguousarray(np.broadcast_to((np.arange(128) % 32 == 0).astype(f32)[:, None], (128, 128))),
    }
    in_maps = []
    for core in range(NCORES):
        b, q = core // 4, core % 4
        xp = x_prompt[4 * core:4 * core + 4].reshape(1024, D)
        xs = x_sample[b, 1024 * q:1024 * (q + 1)]
        xall = np.concatenate([xp, xs], axis=0)
        xT = np.ascontiguousarray(xall.reshape(T, 8, 128).transpose(2, 1, 0))
        condT = np.stack([fm(c_ctx), fm(c[b])], axis=2).reshape(128, 16)
        kctx = np.ascontiguousarray(cache_k[b, 0].transpose(2, 1, 0).reshape(128, 512))
        vctx = np.ascontiguousarray(cache_v[b, 0].reshape(2, 128, 256).transpose(1, 0, 2).reshape(128, 512))
        pos = np.arange(1024 * q, 1024 * (q + 1))
        row = (pos // 64).astype(f32)
        col = (pos % 64).astype(f32)
        ang = np.concatenate([row[:, None] * inv_freq, col[:, None] * inv_freq], axis=-1).astype(f32)
        cs, sn = np.cos(ang).astype(f32).T, np.sin(ang).astype(f32).T
        ropeCS = np.concatenate([np.concatenate([cs, cs], 0), np.concatenate([sn, sn], 0)], axis=1)
        m = np.zeros(8, f32)
        if q - 1 >= 0:
            m[q - 1] = 1.0
        if q + 1 <= 3:
            m[4 + q + 1] = 1.0
        corr = np.concatenate([corr_tab(4096, q == 0, q == 3), corr_tab(256, True, True)])
        im = dict(shared)
        im.update({
            "xT": xT, "condT": np.ascontiguousarray(condT), "kctx": kctx, "vctx": vctx,
            "ropeCS": np.ascontiguousarray(ropeCS.astype(f32)),
            "masks": np.ascontiguousarray(np.broadcast_to(m, (128, 8))),
            "corr": np.ascontiguousarray(np.broadcast_to(corr.astype(f32), (128, 128))),
        })
        in_maps.append(im)

    if _PROGRAM is None:
        _st = os.environ.get("KSTOP")
        _PROGRAM = build_program(None if _st is None else int(_st))
    res = run_bass_kernel_spmd(_PROGRAM, in_maps, core_ids=list(range(NCORES)))

    y_prompt = np.empty((32, 256, D), f32)
    y_sample = np.empty((2, 4096, D), f32)
    new_k = np.empty((32, 1, 256, 2, 128), f32)
    new_v = np.empty((32, 1, 256, 2, 128), f32)
    for core in range(NCORES):
        r = res.results[core]
        b, q = core // 4, core % 4
        yT = np.asarray(r["yT"], dtype=f32)
        yall = yT.transpose(2, 1, 0).reshape(T, D)
        y_prompt[4 * core:4 * core + 4] = yall[:1024].reshape(4, 256, D)
        y_sample[b, 1024 * q:1024 * (q + 1)] = yall[1024:]
        nk_ = np.asarray(r["nk"], dtype=f32)
        new_k[4 * core:4 * core + 4, 0] = nk_.transpose(2, 1, 0).reshape(4, 256, 2, 128)
        nv_ = np.asarray(r["nv"], dtype=f32)
        new_v[4 * core:4 * core + 4, 0] = nv_.transpose(1, 0, 2).reshape(4, 256, 2, 128)
    return (y_prompt, y_sample, new_k, new_v)
```

```python
import contextlib
import os
import numpy as np
import concourse.bass as bass
import concourse.mybir as mybir
from concourse.bass_utils import run_bass_kernel_spmd

F32 = mybir.dt.float32
BF16 = mybir.dt.bfloat16
AF = mybir.ActivationFunctionType
ALU = mybir.AluOpType

D = 1024
DFF = 2816
NK = 8
T = 2048
TBS = 512
NTB = 4
ALPHA = 2.0 ** 0.25
LN_EPS = 1e-6
RMS_EPS = 1e-6
SCALE = 128.0 ** -0.5
WINS = (2, 4, 8, 16)
NCORES = 8
AGW = 4096
ARENA_WORDS = 53200
SLOT_ELEMS = 9216


class Prod:
    def __init__(self, sem, step):
        self.sem = sem
        self.step = step
        self.count = 0


class Buf:
    __slots__ = ("w", "r", "name")

    def __init__(self, name=""):
        self.w = None
        self.r = {}
        self.name = name


class Eng:
    def __init__(self, eng, prod):
        self.eng = eng
        self.prod = prod
        self.seen = {}
        self.self_sync = False

    def wait(self, t):
        prod, val = t
        if prod is self.prod and not self.self_sync:
            return
        if self.seen.get(prod, 0) >= val:
            return
        self.eng.wait_ge(prod.sem, val)
        self.seen[prod] = val


class Region:
    def __init__(self, start, end):
        self.start, self.end, self.top = start, end, start

    def reset(self):
        self.top = self.start


class K:
    def __init__(self, nc, st):
        self.nc = nc
        self.st = st
        self.E = {}
        for name, eng in (("pe", nc.tensor), ("act", nc.scalar), ("dve", nc.vector),
                          ("pool", nc.gpsimd), ("sp", nc.sync)):
            self.E[name] = Eng(eng, self.new_prod("e_" + name, 1))
            self.E[name].self_sync = name in ("act", "dve", "pool")
        self.dprods = []
        self.arena_t = st.enter_context(nc.sbuf_tensor("arena", [128, ARENA_WORDS], F32))
        self.arena = self.arena_t[:, :]
        self.ps_t = st.enter_context(nc.psum_tensor("ps", [128, 8, 512], F32))
        self.bank = [Buf("bank%d" % i) for i in range(8)]
        self.jobs = []
        self.job_next = 0
        self.job_cur = 0
        self.job_slot = {}

    def new_prod(self, name, step):
        sem = self.st.enter_context(self.nc.semaphore(name))
        return Prod(sem, step)

    def dprod(self, name):
        p = self.new_prod("d_" + name, 16)
        self.dprods.append(p)
        return p

    def _deps(self, E, reads, writes):
        for b in reads:
            if b.w is not None:
                E.wait(b.w)
        for b in writes:
            if b.w is not None:
                E.wait(b.w)
            for p, v in b.r.items():
                E.wait((p, v))

    def _commit(self, t, reads, writes):
        p, v = t
        for b in reads:
            if b.r.get(p, 0) < v:
                b.r[p] = v
        for b in writes:
            b.w = t
            b.r = {}

    def op(self, en, fn, reads=(), writes=(), pre=()):
        E = self.E[en]
        self._deps(E, reads, writes)
        for f in pre:
            f()
        ins = fn()
        E.prod.count += 1
        ins.then_inc(E.prod.sem, 1)
        t = (E.prod, E.prod.count)
        self._commit(t, reads, writes)
        return t

    def dma(self, qn, out, in_, prod, reads=(), writes=(), nodep=False):
        E = self.E[qn]
        if not nodep:
            self._deps(E, reads, writes)
        ins = E.eng.dma_start(out=out, in_=in_)
        prod.count += 16
        ins.then_inc(prod.sem, 16)
        t = (prod, prod.count)
        self._commit(t, reads, writes)
        return t

    def barrier(self):
        ticks = [(e.prod, e.prod.count) for e in self.E.values() if e.prod.count > 0]
        ticks += [(p, p.count) for p in self.dprods if p.count > 0]
        for e in self.E.values():
            for t in ticks:
                e.wait(t)

    def psb(self, i):
        return self.ps_t[:, i, :]

    def f32v(self, off_bytes, n):
        o = off_bytes // 4
        return self.arena[:, o:o + n]

    def bfv(self, off_bytes, n):
        o = off_bytes // 4
        return self.arena[:, o:o + n // 2].bitcast(BF16)

    def alloc(self, reg, nbytes):
        nbytes = (nbytes + 63) // 64 * 64
        off = reg.top
        reg.top += nbytes
        assert reg.top <= reg.end, ("region overflow", reg.start, reg.end, reg.top)
        return off

    def pe_group(self, out_ap, pairs, reads, writes):
        n = len(pairs)
        nc = self.nc
        pre = [(lambda l=l, r=r, i=i: nc.tensor.matmul(out_ap, lhsT=l, rhs=r, start=(i == 0), stop=False))
               for i, (l, r) in enumerate(pairs[:-1])]
        l, r = pairs[-1]
        return self.op("pe", lambda: nc.tensor.matmul(out_ap, lhsT=l, rhs=r, start=(n == 1), stop=True),
                       reads, writes, pre=pre)

    def pe_mm(self, out_ap, l, r, start, stop, reads, writes):
        nc = self.nc
        return self.op("pe", lambda: nc.tensor.matmul(out_ap, lhsT=l, rhs=r, start=start, stop=stop),
                       reads, writes)

    def act(self, out, in_, func, reads, writes, bias=None, scale=None):
        nc = self.nc
        kw = {}
        if bias is not None:
            kw["bias"] = bias
        if scale is not None:
            kw["scale"] = scale
        return self.op("act", lambda: nc.scalar.activation(out=out, in_=in_, func=func, **kw), reads, writes)

    def tt(self, out, in0, in1, op, reads, writes):
        nc = self.nc
        return self.op("dve", lambda: nc.vector.tensor_tensor(out=out, in0=in0, in1=in1, op=op), reads, writes)

    def ts(self, out, in0, s1, op0, reads, writes, s2=None, op1=None):
        nc = self.nc
        if op1 is None:
            return self.op("dve", lambda: nc.vector.tensor_scalar(out=out, in0=in0, scalar1=s1, scalar2=None,
                                                                  op0=op0), reads, writes)
        return self.op("dve", lambda: nc.vector.tensor_scalar(out=out, in0=in0, scalar1=s1, scalar2=s2,
                                                              op0=op0, op1=op1), reads, writes)

    def stt(self, out, in0, scalar, in1, op0, op1, reads, writes):
        nc = self.nc
        return self.op("dve", lambda: nc.vector.scalar_tensor_tensor(out=out, in0=in0, scalar=scalar, in1=in1,
                                                                     op0=op0, op1=op1), reads, writes)

    def recip(self, out, in_, reads, writes):
        nc = self.nc
        return self.op("dve", lambda: nc.vector.reciprocal(out=out, in_=in_), reads, writes)

    def vcopy(self, out, in_, reads, writes):
        nc = self.nc
        return self.op("dve", lambda: nc.vector.tensor_copy(out=out, in_=in_), reads, writes)

    def memset(self, out, val, writes):
        nc = self.nc
        return self.op("dve", lambda: nc.vector.memset(out, val), (), writes)

    def add_job(self, fn):
        self.jobs.append(fn)

    def prefetch(self, si):
        if self.job_next < len(self.jobs):
            self.jobs[self.job_next](si)
            self.job_slot[self.job_next] = si
            self.job_next += 1

    def take(self):
        assert self.job_cur < self.job_next, "job not prefetched"
        si = self.job_slot[self.job_cur]
        self.job_cur += 1
        return si


class _Stop(Exception):
    pass


def build_program(stop=None):
    nc = bass.Bass("TRN2", target_bir_lowering=False)
    dr = {}

    def din(name, shape, dt=F32):
        dr[name] = nc.dram_tensor(name, list(shape), dt, kind="ExternalInput").ap()
        return dr[name]

    def dout(name, shape, dt=F32):
        dr[name] = nc.dram_tensor(name, list(shape), dt, kind="ExternalOutput").ap()
        return dr[name]

    xT = din("xT", [128, NK, T])
    condT = din("condT", [128, 16])
    w_mod = din("w_mod", [D, 9 * D])
    bmodT = din("bmodT", [128, 72])
    lngT = din("lngT", [128, 24])
    lnbT = din("lnbT", [128, 24])
    ffn1_w1 = din("ffn1_w1", [128, 2, 8 * DFF])
    ffn1_w2 = din("ffn1_w2", [DFF, D])
    ffn2_w1 = din("ffn2_w1", [128, 2, 8 * DFF])
    ffn2_w2 = din("ffn2_w2", [DFF, D])
    w_in = din("w_in", [D, 4096])
    qkg = din("qkg", [128, 2])
    pool_w = din("pool_w", [128, 512])
    pscaleT = din("pscaleT", [128, 4])
    w_up_attn = din("w_up_attn", [D, D])
    w_up_pool = din("w_up_pool", [512, D])
    w_out = din("w_out", [D, D])
    kctx = din("kctx", [128, 512])
    vctx = din("vctx", [128, 512])
    ropeCS = din("ropeCS", [128, 2048])
    pmat = din("pmat", [128, 128])
    masks = din("masks", [128, 8])
    corr = din("corr", [128, 128])
    esel = din("esel", [128, 128])
    yT = dout("yT", [128, NK, T])
    nk = dout("nk", [128, 2, 1024])
    nv = dout("nv", [128, 8, 256])
    ag_in = nc.dram_tensor("ag_in", [128, AGW], BF16)
    ag_out = nc.dram_tensor("ag_out", [512, AGW], BF16)
    ag_in_ap = ag_in.ap()
    ag_out_ap = ag_out.ap()
    dr["ag2_in"] = nc.dram_tensor("ag2_in", [128, 128], BF16).ap()
    dr["ag2_out"] = nc.dram_tensor("ag2_out", [512, 128], BF16).ap()

    with contextlib.ExitStack() as st:
        k = K(nc, st)
        k.stop = stop
        try:
            _emit(k, nc, dr, ag_in, ag_out, ag_in_ap, ag_out_ap)
        except _Stop:
            pass
    return nc


def _emit(k, nc, dr, ag_in, ag_out, ag_in_ap, ag_out_ap):
    ps = k.psb
    bank = k.bank
    SKIP = set(os.environ.get("KSKIP", "").split(","))
    G = Region(0, ARENA_WORDS * 4)
    oR = k.alloc(G, NK * T * 4)
    oSlot = [k.alloc(G, SLOT_ELEMS * 2), k.alloc(G, SLOT_ELEMS * 2)]
    oOnesLn = k.alloc(G, 256)
    oOnesH = k.alloc(G, 256)
    oOnes1 = k.alloc(G, 256)
    oPm = k.alloc(G, 256)
    oModp = k.alloc(G, 144 * 4)
    oBmod = k.alloc(G, 72 * 4)
    oLng = k.alloc(G, 24 * 4)
    oLnb = k.alloc(G, 24 * 4)
    oCond = k.alloc(G, 16 * 4)
    oScT = k.alloc(G, 16 * 2)
    oQkg = k.alloc(G, 2 * 4)
    oPsc = k.alloc(G, 4 * 4)
    oMask = k.alloc(G, 8 * 4)
    oCorr = k.alloc(G, 128 * 4)
    oPoolW = k.alloc(G, 512 * 2)
    oDer = k.alloc(G, 32 * 8 * 4)
    gend = G.top
    RH = Region(gend, gend + 32768)
    RQ = Region(RH.end, RH.end + 32768)
    RP2 = Region(RQ.end, RQ.end + 16384)
    RX = Region(RP2.end, ARENA_WORDS * 4)
    RHQ = Region(RH.start, RQ.end)
    RQX = Region(RQ.start, RX.end)
    assert RX.end - RX.start >= 20000, (RX.start, RX.end)

    R3 = k.f32v(oR, NK * T).rearrange("p (k t) -> p k t", k=NK)
    Rbk = [[Buf("R%d_%d" % (i, j)) for j in range(NK)] for i in range(NTB)]
    Rprod = [k.dprod("R%d" % i) for i in range(NTB)]
    slot = [k.bfv(o, SLOT_ELEMS) for o in oSlot]
    slotb = [Buf("slot0"), Buf("slot1")]
    slotb2 = [Buf("slot0w2"), Buf("slot1w2")]
    k.stage = None
    slotp = [k.dprod("slot0"), k.dprod("slot1")]
    slotp2 = [k.dprod("slot0w2"), k.dprod("slot1w2")]
    ones_ln = k.bfv(oOnesLn, 128)
    ones_h = k.bfv(oOnesH, 128)
    ones1 = k.bfv(oOnes1, 128)
    Pm = k.bfv(oPm, 128)
    modp = k.f32v(oModp, 144)
    modp3 = modp.rearrange("p (j g) -> p j g", g=2)
    modp4 = modp.rearrange("p (i k g) -> p i k g", i=9, k=8)
    bmod = k.f32v(oBmod, 72)
    lng = k.f32v(oLng, 24)
    lnb = k.f32v(oLnb, 24)
    cond = k.f32v(oCond, 16)
    scT = k.bfv(oScT, 16)
    qkg = k.f32v(oQkg, 2)
    psc = k.f32v(oPsc, 4)
    mask = k.f32v(oMask, 8)
    corr = k.f32v(oCorr, 128)
    poolw = k.bfv(oPoolW, 512)
    der = k.f32v(oDer, 256)
    constb = Buf("const")
    cprod = k.dprod("const")
    cprod2 = k.dprod("const2")
    derb = Buf("der")

    dcol = {}

    def DER(name, g=None):
        key = (name, g)
        if key not in dcol:
            dcol[key] = len(dcol)
            assert len(dcol) <= 32
        c = dcol[key] * 8
        return der[:, c:c + 8]

    def DERk(name, g, kk):
        c = dcol[(name, g)] * 8 + kk
        return der[:, c:c + 1]

    def finalize():
        sp = k.E["sp"]
        for p in k.dprods:
            if p.count > 0:
                sp.wait((p, p.count))
        for e in k.E.values():
            if e.prod.count > 0:
                sp.wait((e.prod, e.prod.count))

    dbgp = k.dprod("dbg")

    def cut(stage):
        if k.stop is not None and k.stop == stage:
            k.barrier()
            for tb in range(NTB):
                k.dma("sp", dr["yT"][:, :, tb * TBS:(tb + 1) * TBS], R3[:, :, tb * TBS:(tb + 1) * TBS], dbgp, Rbk[tb], [])
            finalize()
            raise _Stop()

    def job_wmod(i):
        def f(si):
            k.dma("pool", slot[si][:, 0:8192].rearrange("p (k c) -> p k c", k=8),
                  dr["w_mod"][:, i * 1024:(i + 1) * 1024].rearrange("(k p) c -> p k c", p=128),
                  slotp[si], (), [slotb[si], slotb2[si]])
        return f

    PIECES = [(0, 2)] + [(c0, 3) for c0 in range(2, 20, 3)] + [(20, 2)]

    def job_ffn(w1, w2, c0, n):
        def f(si):
            s = slot[si]
            o0 = 8 * 128 * c0
            k.dma("pool", s[:, 0:3072].rearrange("p (k c) -> p k c", k=8)[:, :, 0:n * 128],
                  w1[:, 0, o0:o0 + 8 * n * 128].rearrange("p (k c) -> p k c", k=8),
                  slotp[si], (), [slotb[si]])
            k.dma("pool", s[:, 3072:6144].rearrange("p (k c) -> p k c", k=8)[:, :, 0:n * 128],
                  w1[:, 1, o0:o0 + 8 * n * 128].rearrange("p (k c) -> p k c", k=8),
                  slotp[si], (), [slotb[si]], nodep=True)
            if k.stage is None:
                k.dma("pool", s[:, 6144:6144 + n * 1024].rearrange("p (j c) -> p j c", j=n),
                      w2[c0 * 128:(c0 + n) * 128, :].rearrange("(j p) c -> p j c", p=128),
                      slotp2[si], (), [slotb2[si]])
            else:
                stg, stgb, stgp = k.stage[si]
                k.dma("sp", stg[:, 0:n * 1024].rearrange("p (j c) -> p j c", j=n),
                      w2[c0 * 128:(c0 + n) * 128, :].rearrange("(j p) c -> p j c", p=128),
                      stgp, (), [stgb])
                for j in range(n):
                    k.act(s[:, 6144 + j * 1024:6144 + (j + 1) * 1024], stg[:, j * 1024:(j + 1) * 1024], AF.Copy,
                          [stgb], [slotb2[si]])
        return f

    def job_win(col0):
        def f(si):
            k.dma("pool", slot[si][:, 0:8192].rearrange("p (k c) -> p k c", k=8),
                  dr["w_in"][:, col0:col0 + 1024].rearrange("(k p) c -> p k c", p=128),
                  slotp[si], (), [slotb[si], slotb2[si]])
        return f

    def job_g1(ocp):
        def f(si):
            s = slot[si]
            c0 = ocp * 256
            k.dma("pool", s[:, 0:2048].rearrange("p (k c) -> p k c", k=8),
                  dr["w_in"][:, 2048 + c0:2048 + c0 + 256].rearrange("(k p) c -> p k c", p=128),
                  slotp[si], (), [slotb[si], slotb2[si]])
            k.dma("pool", s[:, 2048:4096].rearrange("p (k c) -> p k c", k=8),
                  dr["w_in"][:, 3072 + c0:3072 + c0 + 256].rearrange("(k p) c -> p k c", p=128),
                  slotp[si], (), [slotb[si]], nodep=True)
            k.dma("pool", s[:, 4096:6144].rearrange("p (k c) -> p k c", k=8),
                  dr["w_up_attn"][:, c0:c0 + 256].rearrange("(k p) c -> p k c", p=128),
                  slotp[si], (), [slotb[si]], nodep=True)
            k.dma("pool", s[:, 6144:7168].rearrange("p (k c) -> p k c", k=4),
                  dr["w_up_pool"][:, c0:c0 + 256].rearrange("(k p) c -> p k c", p=128),
                  slotp[si], (), [slotb[si]], nodep=True)
        return f

    def job_wout():
        def f(si):
            k.dma("pool", slot[si][:, 0:8192].rearrange("p (k c) -> p k c", k=8),
                  dr["w_out"][:, :].rearrange("(k p) c -> p k c", p=128),
                  slotp[si], (), [slotb[si], slotb2[si]])
        return f

    for i in range(2):
        k.add_job(job_wmod(i))
    for pi, (c0, n) in enumerate(PIECES):
        k.add_job(job_ffn(dr["ffn1_w1"], dr["ffn1_w2"], c0, n))
        if pi == 0:
            k.add_job(job_wmod(2))
        if pi == 3:
            k.add_job(job_wmod(3))
        if pi == 5:
            k.add_job(job_wmod(4))
    k.add_job(job_win(1024))
    k.add_job(job_win(0))
    for i in range(5, 9):
        k.add_job(job_wmod(i))
    for ocp in range(4):
        k.add_job(job_g1(ocp))
    k.add_job(job_wout())
    for (c0, n) in PIECES:
        k.add_job(job_ffn(dr["ffn2_w1"], dr["ffn2_w2"], c0, n))

    for (dst, src) in ((cond, "condT"), (bmod, "bmodT"), (lng, "lngT"), (lnb, "lnbT"), (qkg, "qkg"),
                       (psc, "pscaleT"), (mask, "masks"), (corr, "corr")):
        k.dma("sp", dst, dr[src][:, :], cprod, (), [constb])
    k.dma("pool", Pm, dr["pmat"][:, :], cprod2, (), [constb])
    k.dma("pool", poolw, dr["pool_w"][:, :], cprod2, (), [constb])
    k.prefetch(0)
    k.prefetch(1)
    for tb in range(NTB):
        k.dma("sp", R3[:, :, tb * TBS:(tb + 1) * TBS], dr["xT"][:, :, tb * TBS:(tb + 1) * TBS],
              Rprod[tb], (), Rbk[tb])
    onesb = Buf("ones")
    k.memset(ones_ln, 1.0 / 1024.0, [onesb])
    k.memset(ones_h, 1.0 / 128.0, [onesb])
    k.memset(ones1, 1.0, [onesb])
    scb = Buf("scT")
    k.act(scT, cond, AF.Silu, [constb], [scb])
    def mod_compute(i, bk):
        si = k.take()
        for j in range(8):
            col = (i * 8 + j) * 2
            for kk in range(8):
                nc_l = slot[si][:, kk * 1024 + j * 128: kk * 1024 + (j + 1) * 128]
                if kk == 0 and j == 0:
                    k._deps(k.E["pe"], [slotb[si], scb], [bank[bk]])
                ins = nc.tensor.matmul(ps(bk)[:, col:col + 2], lhsT=nc_l, rhs=scT[:, kk * 2:(kk + 1) * 2],
                                       start=(kk == 0), stop=(kk == 7))
        E = k.E["pe"]
        E.prod.count += 1
        ins.then_inc(E.prod.sem, 1)
        tk = (E.prod, E.prod.count)
        k._commit(tk, [slotb[si], scb], [bank[bk]])
        k.prefetch(si)

    def mod_finish(i0, i1, bk):
        psm3 = ps(bk)[:, 0:144].rearrange("p (j g) -> p j g", g=2)
        for g in range(2):
            k.tt(modp3[:, i0 * 8:i1 * 8, g], psm3[:, i0 * 8:i1 * 8, g], bmod[:, i0 * 8:i1 * 8], ALU.add,
                 [bank[bk], constb], [derb])

    for i in range(2):
        mod_compute(i, 0)
    mod_finish(0, 2, 0)

    def M(i, g):
        return modp4[:, i, :, g]

    for g in range(2):
        k.ts(DER("A1", g), M(1, g), 1.0, ALU.add, [], [derb])
        k.vcopy(DER("SH1", g), M(0, g), [], [derb])
        DER("HG1", g)
    k.ts(DER("RS1"), lng[:, 0:8], ALPHA, ALU.mult, [constb], [derb])
    k.ts(DER("RB1"), lnb[:, 0:8], ALPHA, ALU.mult, [constb], [derb])
    k.ts(DER("RS2"), lng[:, 8:16], ALPHA, ALU.mult, [constb], [derb])
    k.ts(DER("RB2"), lnb[:, 8:16], ALPHA, ALU.mult, [constb], [derb])
    k.vcopy(DER("RS3"), lng[:, 16:24], [constb], [derb])
    k.vcopy(DER("RB3"), lnb[:, 16:24], [constb], [derb])
    for g in range(2):
        for nm in ("H2S", "H2B", "G2", "T3", "H3S", "H3B", "HG3"):
            DER(nm, g)

    def derive_mix():
      for g in range(2):
        k.ts(DER("H2S", g), M(4, g), 1.0, ALU.add, [], [derb], s2=1.0 / ALPHA, op1=ALU.mult)
        k.vcopy(DER("H2B", g), M(3, g), [], [derb])

    def derive_late():
      for g in range(2):
        k.vcopy(DER("G2", g), M(5, g), [], [derb])
        k.ts(DER("T3", g), M(7, g), 1.0, ALU.add, [], [derb])
        k.tt(DER("H3S", g), DER("T3", g), lng[:, 8:16], ALU.mult, [constb], [derb])
        k.tt(DER("H3B", g), DER("T3", g), lnb[:, 8:16], ALU.mult, [constb], [derb])
        k.tt(DER("H3B", g), DER("H3B", g), M(6, g), ALU.add, [], [derb])
        k.ts(DER("HG3", g), M(8, g), 0.5, ALU.mult, [], [derb])

    oH = k.alloc(RH, NK * T * 2)
    H3 = k.bfv(oH, NK * T).rearrange("p (k t) -> p k t", k=NK)
    Hb = [Buf("H%d" % i) for i in range(NTB)]
    yprod = [k.dprod("y%d" % i) for i in range(NTB)]

    def ln_alloc(reg):
        o = {}
        o["yb"] = [k.bfv(k.alloc(reg, 1024), 512) for _ in range(NK)]
        o["ysq"] = [k.bfv(k.alloc(reg, 1024), 512) for _ in range(NK)]
        o["ybb"] = [Buf() for _ in range(NK)]
        o["ysqb"] = [Buf() for _ in range(NK)]
        o["mean"] = [k.f32v(k.alloc(reg, 2048), 512) for _ in range(2)]
        o["tmp"] = k.f32v(k.alloc(reg, 2048), 512)
        o["rstd"] = [k.f32v(k.alloc(reg, 2048), 512) for _ in range(2)]
        o["meanb"], o["tmpb"], o["rstdb"] = [Buf(), Buf()], Buf(), [Buf(), Buf()]
        return o

    def ln_stats_chunk(tb, L, kk):
        tsl = slice(tb * TBS, (tb + 1) * TBS)
        k.act(L["yb"][kk], R3[:, kk, tsl], AF.Copy, [Rbk[tb][kk]], [L["ybb"][kk]])
        k.act(L["ysq"][kk], R3[:, kk, tsl], AF.Square, [Rbk[tb][kk]], [L["ysqb"][kk]])

    def ln_pe_stats(tb, L, b6=6, b7=7):
        for kk in range(NK):
            k.pe_mm(ps(b6), ones_ln, L["yb"][kk], kk == 0, kk == NK - 1, [L["ybb"][kk], onesb], [bank[b6]])
        for kk in range(NK):
            k.pe_mm(ps(b7), ones_ln, L["ysq"][kk], kk == 0, kk == NK - 1, [L["ysqb"][kk], onesb], [bank[b7]])

    def ln_fin_a(tb, L, b6=6, b7=7):
        par = tb % 2
        k.vcopy(L["mean"][par], ps(b6), [bank[b6]], [L["meanb"][par]])
        k.tt(L["tmp"], L["mean"][par], L["mean"][par], ALU.mult, [L["meanb"][par]], [L["tmpb"]])
        k.tt(L["tmp"], ps(b7), L["tmp"], ALU.subtract, [bank[b7]], [L["tmpb"]])
        k.act(L["rstd"][par], L["tmp"], AF.Ln, [L["tmpb"]], [L["rstdb"][par]], bias=LN_EPS, scale=1.0)
        k.act(L["rstd"][par], L["rstd"][par], AF.Exp, [], [L["rstdb"][par]], scale=-0.5)

    def ln_fin_b1(tb, L):
        par = tb % 2
        tsl = slice(tb * TBS, (tb + 1) * TBS)
        Rt = R3[:, :, tsl]
        k.tt(Rt, Rt, L["mean"][par].unsqueeze(1).to_broadcast([128, NK, TBS]), ALU.subtract,
             [L["meanb"][par]], Rbk[tb])

    def ln_fin_b2(tb, L, rs, rb, hs=None, hb=None, final=False):
        par = tb % 2
        g = tb // 2
        tsl = slice(tb * TBS, (tb + 1) * TBS)
        Rt = R3[:, :, tsl]
        k.tt(Rt, Rt, L["rstd"][par].unsqueeze(1).to_broadcast([128, NK, TBS]), ALU.mult,
             [L["rstdb"][par]], Rbk[tb])
        for kk in range(NK):
            if hs is not None:
                k.ts(H3[:, kk, tsl], R3[:, kk, tsl], DERk(hs, g, kk), ALU.mult, [Rbk[tb][kk], derb], [Hb[tb]],
                     s2=DERk(hb, g, kk), op1=ALU.add)
            if hs is None and kk % 2 == 1:
                k.ts(R3[:, kk, tsl], R3[:, kk, tsl], DERk(rs, None, kk), ALU.mult, [derb], [Rbk[tb][kk]],
                     s2=DERk(rb, None, kk), op1=ALU.add)
            else:
                k.act(R3[:, kk, tsl], R3[:, kk, tsl], AF.Identity, [derb], [Rbk[tb][kk]],
                      bias=DERk(rb, None, kk), scale=DERk(rs, None, kk))
            if final:
                k.dma("sp", dr["yT"][:, kk, tsl], R3[:, kk, tsl], yprod[tb], [Rbk[tb][kk]], [])

    def ln_finish(tb, L, rs, rb, hs=None, hb=None, final=False, b6=6, b7=7):
        ln_fin_a(tb, L, b6, b7)
        ln_fin_b1(tb, L)
        ln_fin_b2(tb, L, rs, rb, hs=hs, hb=hb, final=final)

    stgprods = [yprod[0], yprod[1]]

    def ffn_alloc():
        reg = RQX
        reg.reset()
        o = {}
        o["gbuf"] = [[k.bfv(k.alloc(reg, 1024), 512) for _ in range(3)] for _ in range(2)]
        o["gb"] = [[Buf() for _ in range(3)] for _ in range(2)]
        o["sa"] = [k.f32v(k.alloc(reg, 2048), 512) for _ in range(2)]
        o["sab"] = [Buf(), Buf()]
        o["Ln"] = ln_alloc(reg)
        o["stage"] = [(k.f32v(k.alloc(reg, 12288), 3072), Buf("stg%d" % i), stgprods[i]) for i in range(2)]
        return o

    def ffn_phase(hgname, L, ln_args, between=None, pre_stt=None):
        fb = ffn_alloc() if L is None else L
        gbuf, gb, sa, sab, Ln = fb["gbuf"], fb["gb"], fb["sa"], fb["sab"], fb["Ln"]
        k.stage = fb["stage"]
        cnt = 0
        cntf = 0
        gi = 0
        for pi, (c0, n) in enumerate(PIECES):
            si = k.take()
            s = slot[si]
            last = (pi == len(PIECES) - 1)
            for tb in range(NTB):
                g = tb // 2
                tsl = slice(tb * TBS, (tb + 1) * TBS)
                gs = gi % 2
                gi += 1
                for j in range(n):
                    a, b2 = cnt % 2, 2 + cnt % 2
                    cnt += 1
                    k.pe_group(ps(a), [(s[:, kk * 384 + j * 128: kk * 384 + (j + 1) * 128], H3[:, kk, tsl])
                                       for kk in range(NK)], [slotb[si], Hb[tb]], [bank[a]])
                    k.pe_group(ps(b2), [(s[:, 3072 + kk * 384 + j * 128: 3072 + kk * 384 + (j + 1) * 128],
                                         H3[:, kk, tsl]) for kk in range(NK)], [slotb[si], Hb[tb]], [bank[b2]])
                    k.act(sa[a], ps(a), AF.Silu, [bank[a]], [sab[a]])
                    k.tt(gbuf[gs][j], sa[a], ps(b2), ALU.mult, [sab[a], bank[b2]], [gb[gs][j]])
                if last and tb >= 1:
                    ln_fin_b1(tb - 1, Ln)
                if pre_stt is not None and pi == 0 and tb == 0:
                    pre_stt()
                for oc in range(NK):
                    f = 4 + cntf % 2
                    cntf += 1
                    k.pe_group(ps(f), [(s[:, 6144 + j * 1024 + oc * 128: 6144 + j * 1024 + (oc + 1) * 128],
                                        gbuf[gs][j]) for j in range(n)],
                               [slotb2[si]] + [gb[gs][j] for j in range(n)], [bank[f]])
                    k.stt(R3[:, oc, tsl], ps(f), DERk(hgname, g, oc), R3[:, oc, tsl], ALU.mult, ALU.add,
                          [bank[f], derb], [Rbk[tb][oc]])
                    if last:
                        ln_stats_chunk(tb, Ln, oc)
                if last:
                    if tb >= 1:
                        ln_fin_b2(tb - 1, Ln, *ln_args[0], **ln_args[1])
                    ln_pe_stats(tb, Ln)
                    ln_fin_a(tb, Ln)
                    if tb == NTB - 1:
                        ln_fin_b1(tb, Ln)
                        ln_fin_b2(tb, Ln, *ln_args[0], **ln_args[1])
            if pi + 2 >= len(PIECES):
                k.stage = None
            k.prefetch(si)
            if between is not None:
                between(pi)

    for tb in range(NTB):
        g = tb // 2
        tsl = slice(tb * TBS, (tb + 1) * TBS)
        for kk in range(NK):
            if kk % 2 == 1:
                k.ts(H3[:, kk, tsl], R3[:, kk, tsl], DERk("A1", g, kk), ALU.mult, [Rbk[tb][kk], derb], [Hb[tb]],
                     s2=DERk("SH1", g, kk), op1=ALU.add)
            else:
                k.act(H3[:, kk, tsl], R3[:, kk, tsl], AF.Identity, [Rbk[tb][kk], derb], [Hb[tb]],
                      bias=DERk("SH1", g, kk), scale=DERk("A1", g, kk))
        k.ts(R3[:, :, tsl], R3[:, :, tsl], ALPHA, ALU.mult, [], Rbk[tb])

    cut(0)
    def ffn1_between(pi):
        if pi == 3:
            mod_compute(3, 6)
        if pi == 5:
            mod_compute(4, 6)
            mod_finish(3, 5, 6)
            derive_mix()

    def ffn1_pre_stt():
        mod_compute(2, 6)
        mod_finish(2, 3, 6)
        for g in range(2):
            k.ts(DER("HG1", g), M(2, g), 0.5, ALU.mult, [], [derb])

    ffn_phase("HG1", None, (("RS1", "RB1"), {}), between=ffn1_between, pre_stt=ffn1_pre_stt)
    k.barrier()
    cut(1)

    RH.reset(); RQ.reset(); RP2.reset(); RX.reset(); RHQ.reset()
    pooled2 = k.bfv(k.alloc(RP2, 4 * T * 2), 4 * T).rearrange("p (g t) -> p g t", g=4)
    p2b = [Buf("p2_%d" % i) for i in range(NTB)]
    oKp = k.alloc(RX, 2 * 1024 * 2)
    oVp = k.alloc(RX, 8 * 256 * 2)
    Kp = k.bfv(oKp, 2048).rearrange("p (h t) -> p h t", h=2)
    Vp = k.bfv(oVp, 2048).rearrange("p (j c) -> p j c", j=8)
    Kpb = [Buf(), Buf()]
    Vpb = [Buf(), Buf()]
    xmark = RX.top

    A = RHQ
    h2t = k.bfv(k.alloc(A, 8192), 4096).rearrange("p (k t) -> p k t", k=NK)
    h2b = [Buf("h2t%d" % i) for i in range(NK)]
    sq = [k.bfv(k.alloc(A, 1024), 512) for _ in range(2)]
    sqb = [Buf(), Buf()]
    sd = [k.f32v(k.alloc(A, 2048), 512) for _ in range(2)]
    sdb = [Buf(), Buf()]
    kn = [k.f32v(k.alloc(A, 2048), 512) for _ in range(2)]
    knb_ = [Buf(), Buf()]
    knbf = k.bfv(k.alloc(A, 1024), 512)
    knbfb = Buf()
    t1 = k.f32v(k.alloc(A, 2048), 512)
    t2 = k.f32v(k.alloc(A, 2048), 512)
    t1b, t2b = Buf(), Buf()
    kst = [k.bfv(k.alloc(A, 1024), 512) for _ in range(2)]
    kstb = [Buf(), Buf()]
    kstp = [k.dprod("kst0"), k.dprod("kst1")]
    knp = [k.dprod("kn0"), k.dprod("kn1")]
    vst = [k.f32v(k.alloc(A, 1024), 256) for _ in range(2)]
    vstb = [Buf(), Buf()]
    vstp = [k.dprod("vst0"), k.dprod("vst1")]
    vsb = [k.bfv(k.alloc(A, 512), 256) for _ in range(2)]
    vsbb = [Buf(), Buf()]
    vsbp = [k.dprod("vsb0"), k.dprod("vsb1")]
    ropeC = k.f32v(k.alloc(RX, 4096), 1024)
    ropeS = k.f32v(k.alloc(RX, 4096), 1024)
    ropeb = Buf("rope")
    ropep = k.dprod("rope")
    pp = [k.f32v(k.alloc(A, 2176), 544).rearrange("p (b t) -> p b t", b=2) for _ in range(2)]
    ppb = [Buf(), Buf()]
    pin_s = k.f32v(k.alloc(A, 4 * 1040 * 4), 4 * 1040).rearrange("p (g t) -> p g t", g=4)
    pinb = [Buf("pin_s%d" % i) for i in range(4)]
    ta = k.f32v(k.alloc(A, 4160), 1040)
    tb_ = k.f32v(k.alloc(A, 4160), 1040)
    tab, tbb = Buf(), Buf()
    plb = [k.bfv(k.alloc(A, 1024), 512) for _ in range(2)]
    plbb = [Buf(), Buf()]
    e32 = k.f32v(k.alloc(A, 256), 64).rearrange("p (g i) -> p g i", g=4)
    est = k.bfv(k.alloc(A, 256), 128)
    estb = Buf()
    estp = k.dprod("est")
    eg = k.bfv(k.alloc(A, 1024), 512).rearrange("p (r c) -> p r c", r=4)
    egb = Buf()
    egp = k.dprod("eg")
    ef = k.f32v(k.alloc(A, 1024), 256).rearrange("p (r c) -> p r c", r=4)
    efb = Buf()
    agb = Buf("ag_in")
    agob = Buf("ag_out")
    ag2b = Buf("ag2_in")
    ag2ob = Buf("ag2_out")
    ccprod = k.new_prod("cc", 1)
    k.dprods.append(ccprod)

    k.dma("sp", ropeC, dr["ropeCS"][:, 0:1024], ropep, (), [ropeb])
    k.dma("sp", ropeS, dr["ropeCS"][:, 1024:2048], ropep, (), [ropeb])
    for i in range(2):
        k.memset(pp[i], 0.0, [ppb[i]])
    k.memset(pin_s, 0.0, pinb)

    def mod2(tb, h2t_=None, h2b_=None, on_dve=False):
        g = tb // 2
        tsl = slice(tb * TBS, (tb + 1) * TBS)
        h2t_ = h2t if h2t_ is None else h2t_
        h2b_ = h2b if h2b_ is None else h2b_
        for kk in range(NK):
            if on_dve:
                k.ts(h2t_[:, kk, :], R3[:, kk, tsl], DERk("H2S", g, kk), ALU.mult, [Rbk[tb][kk], derb],
                     [h2b_[kk]], s2=DERk("H2B", g, kk), op1=ALU.add)
                continue
            k.act(h2t_[:, kk, :], R3[:, kk, tsl], AF.Identity, [Rbk[tb][kk], derb], [h2b_[kk]],
                  bias=DERk("H2B", g, kk), scale=DERk("H2S", g, kk))

    cnts = {"p": 0, "m": 0, "r": 0, "x": 0}

    def rms_head(si, col0, gcol, tb, sample, out_bf, out_bufs, kout=None):
        s = slot[si]
        pb = cnts["p"] % 3
        cnts["p"] += 1
        mb = 3 + cnts["m"] % 2
        cnts["m"] += 1
        x = cnts["x"] % 2
        cnts["x"] += 1
        k.pe_group(ps(pb), [(s[:, kk * 1024 + col0: kk * 1024 + col0 + 128], h2t[:, kk, :]) for kk in range(NK)],
                   [slotb[si]] + h2b, [bank[pb]])
        k.act(sq[x], ps(pb), AF.Square, [bank[pb]], [sqb[x]])
        k.pe_mm(ps(mb), ones_h, sq[x], True, True, [sqb[x], onesb], [bank[mb]])
        k.act(sd[x], ps(mb), AF.Ln, [bank[mb]], [sdb[x]], bias=RMS_EPS, scale=1.0)
        k.act(sd[x], sd[x], AF.Exp, [], [sdb[x]], scale=-0.5)
        if not sample:
            if kout is None:
                k.stt(out_bf, ps(pb), qkg[:, gcol:gcol + 1], sd[x], ALU.mult, ALU.mult,
                      [bank[pb], sdb[x], constb], out_bufs)
            else:
                k.stt(kn[x], ps(pb), qkg[:, gcol:gcol + 1], sd[x], ALU.mult, ALU.mult,
                      [bank[pb], sdb[x], constb], [knb_[x]])
                if "kvout" not in SKIP:
                    k.dma("sp", kout, kn[x], knp[x], [knb_[x]], [])
                k.act(out_bf, kn[x], AF.Copy, [knb_[x]], out_bufs)
        else:
            rbk = 5 + cnts["r"] % 2
            cnts["r"] += 1
            tq = slice((tb - 2) * TBS, (tb - 1) * TBS)
            k.stt(kn[x], ps(pb), qkg[:, gcol:gcol + 1], sd[x], ALU.mult, ALU.mult,
                  [bank[pb], sdb[x], constb], [knb_[x]])
            k.act(knbf, kn[x], AF.Copy, [knb_[x]], [knbfb])
            k.pe_mm(ps(rbk), Pm, knbf, True, True, [knbfb, constb], [bank[rbk]])
            k.tt(t1, kn[x], ropeC[:, tq], ALU.mult, [knb_[x], ropeb], [t1b])
            k.tt(t2, ps(rbk), ropeS[:, tq], ALU.mult, [bank[rbk], ropeb], [t2b])
            k.tt(out_bf, t1, t2, ALU.add, [t1b, t2b], out_bufs)

    def pool_segment(src3, nseg, n, gq, corr_off, tbo, outcol0):
        w = WINS[gq]
        half = w // 2
        L = n + 16
        cur = src3
        curb = None
        bufs = [(ta, tab), (tb_, tbb)]
        bi = 0
        d = 1
        Lc = L
        while d < w:
            dst, dstb = bufs[bi]
            bi ^= 1
            dv = dst[:, 0:nseg * L].rearrange("p (b t) -> p b t", b=nseg)
            Ln_ = Lc - d
            rd = [] if curb is None else [curb]
            k.tt(dv[:, :, 0:Ln_], cur[:, :, 0:Ln_], cur[:, :, d:d + Ln_], ALU.add, rd + list(tbo["src"]), [dstb])
            cur, curb, Lc = dv, dstb, Ln_
            d *= 2
        o0 = 8 - half
        dst, dstb = bufs[bi]
        dv = dst[:, 0:nseg * n].rearrange("p (b t) -> p b t", b=nseg)
        k.ts(dv, cur[:, :, o0:o0 + n], 1.0 / w, ALU.mult, [curb], [dstb])
        cl = corr[:, corr_off + gq * 16: corr_off + gq * 16 + 8].unsqueeze(1).to_broadcast([128, nseg, 8])
        cr = corr[:, corr_off + gq * 16 + 8: corr_off + gq * 16 + 16].unsqueeze(1).to_broadcast([128, nseg, 8])
        k.tt(dv[:, :, 0:8], dv[:, :, 0:8], cl, ALU.mult, [constb], [dstb])
        k.tt(dv[:, :, n - 8:n], dv[:, :, n - 8:n], cr, ALU.mult, [constb], [dstb])
        tot = nseg * n
        segs_per = 512 // n if n < 512 else 1
        for c in range(tot // 512):
            x = cnts["x"] % 2
            cnts["x"] += 1
            if n < 512:
                k.tt(plb[x].rearrange("p (b t) -> p b t", b=nseg), dv, src3[:, :, 8:8 + n], ALU.subtract,
                     [dstb] + list(tbo["src"]), [plbb[x]])
            else:
                k.tt(plb[x], dst[:, c * 512:(c + 1) * 512], src3[:, 0, 8 + c * 512: 8 + (c + 1) * 512],
                     ALU.subtract, [dstb] + list(tbo["src"]), [plbb[x]])
            k.pe_mm(ps(7), poolw[:, gq * 128:(gq + 1) * 128], plb[x], True, True, [plbb[x], constb], [bank[7]])
            k.act(pooled2[:, gq, outcol0 + c * 512: outcol0 + (c + 1) * 512], ps(7), AF.Identity,
                  [bank[7], constb], tbo["dst"][c], scale=psc[:, gq:gq + 1])

    si1 = k.take()
    s1 = slot[si1]
    vcnt = [0]
    OVERLAP_CC = os.environ.get("KNOOVERLAP") is None

    def part_kv(tb):
        sample = tb >= 2
        tsl = slice(tb * TBS, (tb + 1) * TBS)
        for hd in range(2):
            if sample:
                x = cnts["x"] % 2
                rms_head(si1, hd * 128, 1, tb, True, kst[x], [kstb[x]])
                c0 = hd * 1024 + (tb - 2) * TBS
                k.dma("sp", ag_in_ap[:, c0:c0 + TBS], kst[x], kstp[x], [kstb[x], agb], [])
            else:
                rms_head(si1, hd * 128, 1, tb, False, Kp[:, hd, tsl], [Kpb[tb]],
                         kout=dr["nk"][:, hd, tsl])
        for tt_ in range(4):
            pb = cnts["p"] % 3
            cnts["p"] += 1
            tile = (tb % 2) * 4 + tt_
            k.pe_group(ps(pb)[:, 0:256], [(h2t[:, kk, tt_ * 128:(tt_ + 1) * 128], s1[:, kk * 1024 + 256: kk * 1024 + 512])
                                          for kk in range(NK)], [slotb[si1]] + h2b, [bank[pb]])
            v = vcnt[0] % 2
            vcnt[0] += 1
            if sample:
                k.act(vsb[v], ps(pb)[:, 0:256], AF.Copy, [bank[pb]], [vsbb[v]])
                c0 = 2048 + tile * 256
                k.dma("sp", ag_in_ap[:, c0:c0 + 256], vsb[v], vsbp[v], [vsbb[v], agb], [])
            else:
                k.vcopy(vst[v], ps(pb)[:, 0:256], [bank[pb]], [vstb[v]])
                k.dma("sp", dr["nv"][:, tile, :], vst[v], vstp[v], [vstb[v]], [])
                k.act(Vp[:, tile, :], vst[v], AF.Copy, [vstb[v]], [Vpb[tb]])

    def part_pin(tb):
        sample = tb >= 2
        for gq in range(4):
            pb = cnts["p"] % 3
            cnts["p"] += 1
            k.pe_group(ps(pb), [(s1[:, kk * 1024 + 512 + gq * 128: kk * 1024 + 512 + (gq + 1) * 128], h2t[:, kk, :])
                                for kk in range(NK)], [slotb[si1]] + h2b, [bank[pb]])
            if sample:
                c0 = 8 + (tb - 2) * TBS
                k.act(pin_s[:, gq, c0:c0 + TBS], ps(pb), AF.Copy, [bank[pb]], [pinb[gq]])
            else:
                x = gq % 2
                k.act(pp[x][:, :, 8:264], ps(pb).rearrange("p (b t) -> p b t", b=2), AF.Copy, [bank[pb]], [ppb[x]])
                pool_segment(pp[x], 2, 256, gq, 64, {"src": [ppb[x]], "dst": [[p2b[tb]]]}, tb * TBS)

    def collectives():
        k.vcopy(e32[:, :, 0:8], pin_s[:, :, 8:16], pinb, [efb])
        k.vcopy(e32[:, :, 8:16], pin_s[:, :, 1024:1032], pinb, [efb])
        e32f = e32.rearrange("p g i -> p (g i)")
        k.vcopy(est[:, 0:64], e32f, [efb], [estb])
        k.tt(est[:, 64:128], e32f, est[:, 0:64], ALU.subtract, [efb], [estb])
        k.dma("sp", dr["ag2_in"][:, :], est, estp, [estb, ag2b], [])
        k.barrier()
        E = k.E["pool"]
        for (src, dst, sb, db) in ((dr["ag2_in"], dr["ag2_out"], ag2b, ag2ob), (ag_in_ap, ag_out_ap, agb, agob)):
            k._deps(E, [], [sb, db])
            ins = nc.gpsimd.collective_compute("AllGather", ALU.bypass,
                                               replica_groups=[[0, 1, 2, 3], [4, 5, 6, 7]],
                                               ins=[src.opt()], outs=[dst.opt()])
            ccprod.count += 1
            ins.then_inc(ccprod.sem, 1)
            k._commit((ccprod, ccprod.count), [], [sb, db])
        if OVERLAP_CC:
            for qn in ("sp", "pool"):
                k.E[qn].wait((ccprod, ccprod.count))
        else:
            k.barrier()

    if OVERLAP_CC:
        for tb in (2, 3):
            mod2(tb, on_dve=True)
            part_kv(tb)
            part_pin(tb)
        for tb in (0, 1):
            mod2(tb, on_dve=True)
            part_kv(tb)
        collectives()
        for tb in (0, 1):
            mod2(tb)
            part_pin(tb)
        k.barrier()
    else:
        for tb in (2, 3, 0, 1):
            mod2(tb)
            part_kv(tb)
            part_pin(tb)
            if tb == 3:
                collectives()
    if "cc" in SKIP or "cutpost" in SKIP:
        cut(2)
    k.dma("sp", eg, dr["ag2_out"][:, :].rearrange("(r p) c -> p r c", p=128), egp, [ag2ob], [egb])
    k.tt(ef, eg[:, :, 0:64], eg[:, :, 64:128], ALU.add, [egb], [efb])
    ef4 = ef.rearrange("p r (g i) -> p r g i", g=4)
    for side, (dst, lo) in enumerate(((pin_s[:, :, 0:8], 8), (pin_s[:, :, 1032:1040], 0))):
        for r in range(4):
            m = mask[:, side * 4 + r: side * 4 + r + 1]
            src = ef4[:, r, :, lo:lo + 8]
            if r == 0:
                k.ts(dst, src, m, ALU.mult, [efb, constb], pinb)
            else:
                k.stt(dst, src, m, dst, ALU.mult, ALU.add, [efb, constb], pinb)
    for gq in range(4):
        pool_segment(pin_s[:, gq:gq + 1, :], 1, 1024, gq, 0,
                     {"src": [pinb[gq]], "dst": [[p2b[2]], [p2b[3]]]}, 1024)
    k.prefetch(si1)
    k.barrier()
    cut(2)

    RH.reset(); RQ.reset()
    RX.top = xmark
    Q3 = k.bfv(k.alloc(RQ, NK * T * 2), NK * T).rearrange("p (h t) -> p h t", h=NK)
    Qb = [[Buf("Q%d_%d" % (h, i)) for i in range(NTB)] for h in range(NK)]
    B1 = RH
    h2t2 = [k.bfv(k.alloc(B1, 8192), 4096).rearrange("p (k t) -> p k t", k=NK) for _ in range(2)]
    h2b2 = [[Buf() for _ in range(NK)] for _ in range(2)]
    sq = [k.bfv(k.alloc(B1, 1024), 512) for _ in range(2)]
    sqb = [Buf(), Buf()]
    sd = [k.f32v(k.alloc(B1, 2048), 512) for _ in range(3)]
    sdb = [Buf() for _ in range(3)]
    kn = [k.f32v(k.alloc(B1, 2048), 512) for _ in range(3)]
    knb_ = [Buf() for _ in range(3)]
    knbf2 = [k.bfv(k.alloc(B1, 1024), 512) for _ in range(2)]
    knbfb2 = [Buf(), Buf()]
    ropeC = k.f32v(k.alloc(RX, 4096), 1024)
    ropeS = k.f32v(k.alloc(RX, 4096), 1024)
    t1 = k.f32v(k.alloc(RX, 2048), 512)
    t2 = k.f32v(k.alloc(RX, 2048), 512)
    t1b, t2b = Buf(), Buf()
    ropeb = Buf("rope2")
    k.dma("sp", ropeC, dr["ropeCS"][:, 0:1024], ropep, (), [ropeb])
    k.dma("sp", ropeS, dr["ropeCS"][:, 1024:2048], ropep, (), [ropeb])
    si2 = k.take()
    s2 = slot[si2]
    items = [(tb, hd) for tb in range(NTB) for hd in range(NK)]
    NI = len(items)

    def mod2c(tb, kk):
        g = tb // 2
        tsl = slice(tb * TBS, (tb + 1) * TBS)
        k.ts(h2t2[tb % 2][:, kk, :], R3[:, kk, tsl], DERk("H2S", g, kk), ALU.mult, [Rbk[tb][kk], derb],
             [h2b2[tb % 2][kk]], s2=DERk("H2B", g, kk), op1=ALU.add)

    def P1(i):
        tb, hd = items[i]
        pb, x = i % 3, i % 2
        k.pe_group(ps(pb), [(s2[:, kk * 1024 + hd * 128: kk * 1024 + (hd + 1) * 128], h2t2[tb % 2][:, kk, :])
                            for kk in range(NK)], [slotb[si2]] + h2b2[tb % 2], [bank[pb]])
        k.act(sq[x], ps(pb), AF.Square, [bank[pb]], [sqb[x]])
        if tb + 1 < NTB:
            mod2c(tb + 1, hd)

    def P2(i):
        tb, hd = items[i]
        pb, x, mb, y = i % 3, i % 2, 3 + i % 2, i % 3
        tsl = slice(tb * TBS, (tb + 1) * TBS)
        k.pe_mm(ps(mb), ones_h, sq[x], True, True, [sqb[x], onesb], [bank[mb]])
        k.act(sd[y], ps(mb), AF.Ln, [bank[mb]], [sdb[y]], bias=RMS_EPS, scale=1.0)
        k.act(sd[y], sd[y], AF.Exp, [], [sdb[y]], scale=-0.5)
        if tb < 2:
            k.stt(Q3[:, hd, tsl], ps(pb), qkg[:, 0:1], sd[y], ALU.mult, ALU.mult,
                  [bank[pb], sdb[y], constb], [Qb[hd][tb]])
        else:
            k.stt(kn[y], ps(pb), qkg[:, 0:1], sd[y], ALU.mult, ALU.mult,
                  [bank[pb], sdb[y], constb], [knb_[y]])
            k.act(knbf2[x], kn[y], AF.Copy, [knb_[y]], [knbfb2[x]])

    def P3(i):
        tb, hd = items[i]
        if tb < 2:
            return
        x, y, rbk = i % 2, i % 3, 5 + i % 2
        tsl = slice(tb * TBS, (tb + 1) * TBS)
        tq = slice((tb - 2) * TBS, (tb - 1) * TBS)
        k.pe_mm(ps(rbk), Pm, knbf2[x], True, True, [knbfb2[x], constb], [bank[rbk]])
        k.tt(t1, kn[y], ropeC[:, tq], ALU.mult, [knb_[y], ropeb], [t1b])
        k.tt(t2, ps(rbk), ropeS[:, tq], ALU.mult, [bank[rbk], ropeb], [t2b])
        k.tt(Q3[:, hd, tsl], t1, t2, ALU.add, [t1b, t2b], [Qb[hd][tb]])

    for kk in range(NK):
        mod2c(0, kk)
    for step in range(NI + 2):
        if step < NI:
            P1(step)
        if 0 <= step - 1 < NI:
            P2(step - 1)
        if 0 <= step - 2 < NI:
            P3(step - 2)
    k.prefetch(si2)
    k.barrier()
    cut(3)

    RH.reset()
    B2 = RH
    KC = k.bfv(k.alloc(B2, 1024), 512).rearrange("p (h t) -> p h t", h=2)
    VC = k.bfv(k.alloc(B2, 1024), 512).rearrange("p (j c) -> p j c", j=2)
    ctxb = Buf("ctx")
    ctxp = k.dprod("ctx")
    KA0 = k.bfv(k.alloc(B2, 8192), 4096).rearrange("p (r t) -> p r t", r=4)
    VA0 = k.bfv(k.alloc(B2, 8192), 4096).rearrange("p (j c) -> p j c", j=32)
    kab0, vab0 = Buf("KA"), Buf("VA")
    kap, vap = k.dprod("KA"), k.dprod("VA")
    SB = [0, 1, 2, 7]
    PT8 = [k.bfv(k.alloc(B2, 1024), 512) for _ in range(8)]
    PT8b = [Buf() for _ in range(8)]
    PT, PTb = PT8[:4], PT8b[:4]
    RX.top = xmark
    xs = k.f32v(k.alloc(RX, 2048), 512)
    xsb = Buf("xs")
    Esel = k.f32v(k.alloc(RX, 512), 128)
    eselb = Buf("esel")
    eselp = k.dprod("esel")
    k.dma("sp", Esel, dr["esel"][:, :], eselp, (), [eselb])
    rec = [k.f32v(k.alloc(B2, 2048), 512) for _ in range(2)]
    recb = [Buf(), Buf()]
    k.dma("pool", KC, dr["kctx"][:, :].rearrange("p (h t) -> p h t", h=2), ctxp, (), [ctxb])
    k.dma("pool", VC, dr["vctx"][:, :].rearrange("p (j c) -> p j c", j=2), ctxp, (), [ctxb])

    it = 0
    for b in range(4):
        tb = b // 2
        bsl = slice(b * 256, (b + 1) * 256)
        for pr in range(4):
            kvh = pr // 2
            ob, sb_ = 3 + it % 2, 5 + it % 2
            r = it % 2
            it += 1
            qv = Q3[:, 2 * pr:2 * pr + 2, bsl]
            qbufs = [Qb[2 * pr][tb], Qb[2 * pr + 1][tb]]
            pts = []
            for kc in range(2):
                pi_ = cnts["p"] % 4
                sbk = SB[pi_]
                cnts["p"] += 1
                k.pe_mm(ps(sbk).rearrange("p (a t) -> p a t", a=2),
                        Kp[:, kvh, b * 256 + kc * 128: b * 256 + (kc + 1) * 128], qv, True, True,
                        [Kpb[tb]] + qbufs, [bank[sbk]])
                k.act(PT[pi_], ps(sbk), AF.Exp, [bank[sbk]], [PTb[pi_]], scale=SCALE)
                pts.append(pi_)
            for kc in range(2):
                x = pts[kc]
                k.pe_mm(ps(ob), Vp[:, b * 2 + kc, kvh * 128:(kvh + 1) * 128], PT[x], kc == 0, kc == 1,
                        [Vpb[tb], PTb[x]], [bank[ob]])
                k.pe_mm(ps(sb_), ones1, PT[x], kc == 0, kc == 1, [PTb[x], onesb], [bank[sb_]])
            k.act(rec[r], ps(sb_), AF.Ln, [bank[sb_]], [recb[r]])
            k.act(rec[r], rec[r], AF.Exp, [], [recb[r]], scale=-1.0)
            k.tt(qv, ps(ob).rearrange("p (a t) -> p a t", a=2), rec[r].rearrange("p (a t) -> p a t", a=2),
                 ALU.mult, [bank[ob], recb[r]], qbufs)

    KA1 = k.bfv(k.alloc(RX, 8192), 4096).rearrange("p (r t) -> p r t", r=4)
    VA1 = k.bfv(oKp, 4096).rearrange("p (j c) -> p j c", j=32)
    kab1, vab1 = Buf("KA1"), Buf("VA1")
    kvt = [(KA0, VA0, kab0, vab0, []), (KA1, VA1, kab1, vab1, Kpb + Vpb)]
    kvp_ = [(kap, vap), (k.dprod("KA1"), k.dprod("VA1"))]
    for kvh in range(2):
        KA, VA, kab, vab, extra = kvt[kvh]
        k.dma("sp", KA, ag_out_ap[:, kvh * 1024:(kvh + 1) * 1024].rearrange("(r p) c -> p r c", p=128),
              kvp_[kvh][0], [agob], [kab])
        for r_ in range(4):
            k.dma("sp", VA[:, r_ * 8:(r_ + 1) * 8, :],
                  ag_out_ap[r_ * 128:(r_ + 1) * 128, 2048:4096].rearrange("p (j c) -> p j c", j=8)[:, :, kvh * 128:(kvh + 1) * 128],
                  kvp_[kvh][1], [agob], [vab] + extra)
    for kvh in range(2):
        KA, VA, kab, vab, extra = kvt[kvh]
        chunks = []
        for c in range(2):
            chunks.append((KC[:, kvh, c * 128:(c + 1) * 128], VC[:, c, kvh * 128:(kvh + 1) * 128], [ctxb]))
        for r_ in range(4):
            for j in range(8):
                chunks.append((KA[:, r_, j * 128:(j + 1) * 128], VA[:, r_ * 8 + j, :], [kab, vab]))
        NCH = len(chunks)
        for hd in range(4 * kvh, 4 * kvh + 4):
            for tb in (2, 3):
                tsl = slice(tb * TBS, (tb + 1) * TBS)
                ob, sb_ = 3 + it % 2, 5 + it % 2
                r = it % 2
                it += 1
                qv = Q3[:, hd, tsl]
                sbks = {}

                def qk(c):
                    pi_ = cnts["p"] % 4
                    sbk = SB[pi_]
                    cnts["p"] += 1
                    sbks[c] = c % 8
                    k.pe_mm(ps(sbk), chunks[c][0], qv, True, True, chunks[c][2] + [Qb[hd][tb]], [bank[sbk]])
                    k.act(PT8[c % 8], ps(sbk), AF.Exp, [bank[sbk]], [PT8b[c % 8]], scale=SCALE)

                def sum_mm(cc):
                    j = cc % 4
                    x = sbks[cc]
                    k.op("pe", lambda: nc.tensor.matmul(ps(sb_)[32 * j:32 * j + 32, :], lhsT=ones1[:, 0:32],
                                                        rhs=PT8[x], start=(cc == j), stop=(cc + 4 >= NCH),
                                                        tile_position=(0, 32 * j)),
                         [PT8b[x], onesb], [bank[sb_]])

                qk(0)
                qk(1)
                qk(2)
                for c in range(NCH):
                    x = sbks[c]
                    k.pe_mm(ps(ob), chunks[c][1], PT8[x], c == 0, c == NCH - 1, chunks[c][2] + [PT8b[x]], [bank[ob]])
                    if c % 4 == 3 or c == NCH - 1:
                        for cc in range((c // 4) * 4, c + 1):
                            sum_mm(cc)
                    if c + 3 < NCH:
                        qk(c + 3)
                k.vcopy(xs, ps(sb_), [bank[sb_]], [xsb])
                k.pe_mm(ps(sb_), Esel, xs, True, True, [xsb, eselb], [bank[sb_]])
                k.recip(rec[r], ps(sb_), [bank[sb_]], [recb[r]])
                k.tt(qv, ps(ob), rec[r], ALU.mult, [bank[ob], recb[r]], [Qb[hd][tb]])
                if kvh == 0 and tb == 3:
                    mi = 5 + (hd - 4 * kvh)
                    mod_compute(mi, 7)
                    mod_finish(mi, mi + 1, 7)
                    if mi == 8:
                        derive_late()
    k.barrier()
    cut(4)

    RH.reset(); RX.reset()
    oH2 = k.alloc(RH, NK * T * 2)
    assert oH2 == oH
    GX = RX
    h2tg = [k.bfv(k.alloc(GX, 8192), 4096).rearrange("p (k t) -> p k t", k=NK) for _ in range(2)]
    h2bg = [[Buf() for _ in range(NK)] for _ in range(2)]
    sg = [k.f32v(k.alloc(GX, 2048), 512) for _ in range(2)]
    sgb = [Buf(), Buf()]
    m12 = sg
    m12b = sgb
    hi_ = 0
    mod2(0, h2tg[0], h2bg[0])
    for ocp in range(4):
        si = k.take()
        s = slot[si]
        for tb in range(NTB):
            h2t, h2b = h2tg[hi_ % 2], h2bg[hi_ % 2]
            hi_ += 1
            if not (ocp == 3 and tb == NTB - 1):
                mod2((tb + 1) % NTB, h2tg[hi_ % 2], h2bg[hi_ % 2])
            tsl = slice(tb * TBS, (tb + 1) * TBS)
            for o in range(2):
                oc = 2 * ocp + o
                base = 4 * o
                k.pe_group(ps(base + 0), [(s[:, kk * 256 + o * 128: kk * 256 + (o + 1) * 128], h2t[:, kk, :])
                                          for kk in range(NK)], [slotb[si]] + h2b, [bank[base + 0]])
                k.pe_group(ps(base + 1), [(s[:, 2048 + kk * 256 + o * 128: 2048 + kk * 256 + (o + 1) * 128], h2t[:, kk, :])
                                          for kk in range(NK)], [slotb[si]] + h2b, [bank[base + 1]])
                k.pe_group(ps(base + 2), [(s[:, 4096 + kk * 256 + o * 128: 4096 + kk * 256 + (o + 1) * 128], Q3[:, kk, tsl])
                                          for kk in range(NK)], [slotb[si]] + [Qb[kk][tb] for kk in range(NK)],
                           [bank[base + 2]])
                k.pe_group(ps(base + 3), [(s[:, 6144 + kk * 256 + o * 128: 6144 + kk * 256 + (o + 1) * 128], pooled2[:, kk, tsl])
                                          for kk in range(4)], [slotb[si], p2b[tb]], [bank[base + 3]])
                k.act(sg[0], ps(base + 0), AF.Sigmoid, [bank[base + 0]], [sgb[0]])
                k.act(sg[1], ps(base + 1), AF.Sigmoid, [bank[base + 1]], [sgb[1]])
                k.tt(m12[0], sg[0], ps(base + 2), ALU.mult, [sgb[0], bank[base + 2]], [m12b[0]])
                k.tt(m12[1], sg[1], ps(base + 3), ALU.mult, [sgb[1], bank[base + 3]], [m12b[1]])
                k.tt(H3[:, oc, tsl], m12[0], m12[1], ALU.add, [m12b[0], m12b[1]], [Hb[tb]])
        k.prefetch(si)
    k.barrier()
    cut(5)

    FB2 = ffn_alloc()
    L2 = FB2["Ln"]
    si = k.take()
    s = slot[si]
    cntf = 0
    for tb in range(NTB):
        g = tb // 2
        tsl = slice(tb * TBS, (tb + 1) * TBS)
        for oc in range(NK):
            f = 4 + cntf % 2
            cntf += 1
            k.pe_group(ps(f), [(s[:, kk * 1024 + oc * 128: kk * 1024 + (oc + 1) * 128], H3[:, kk, tsl])
                               for kk in range(NK)], [slotb[si], Hb[tb]], [bank[f]])
            k.stt(R3[:, oc, tsl], ps(f), DERk("G2", g, oc), R3[:, oc, tsl], ALU.mult, ALU.add,
                  [bank[f], derb], [Rbk[tb][oc]])
            ln_stats_chunk(tb, L2, oc)
        sbk6, sbk7 = ((6, 7), (2, 3))[tb % 2]
        ln_pe_stats(tb, L2, sbk6, sbk7)
        if tb >= 1:
            pb6, pb7 = ((6, 7), (2, 3))[(tb - 1) % 2]
            ln_finish(tb - 1, L2, "RS2", "RB2", hs="H3S", hb="H3B", b6=pb6, b7=pb7)
    pb6, pb7 = ((6, 7), (2, 3))[(NTB - 1) % 2]
    ln_finish(NTB - 1, L2, "RS2", "RB2", hs="H3S", hb="H3B", b6=pb6, b7=pb7)
    k.prefetch(si)
    cut(6)

    ffn_phase("HG3", FB2, (("RS3", "RB3"), {"final": True}))

    finalize()


_PROGRAM = None


def _host_consts():
    n_freq = 32
    inv_freq = (10000.0 ** (-np.arange(n_freq, dtype=np.float32) / n_freq)).astype(np.float32)
    pm = np.zeros((128, 128), np.float32)
    for d2 in range(64):
        pm[d2 + 64, d2] = -1.0
        pm[d2, d2 + 64] = 1.0

    def corr_tab(n, left, right):
        out = np.ones((4, 16), np.float32)
        for g, w in enumerate(WINS):
            for i in range(8):
                if left:
                    t = i
                    lo = min(max(t - w // 2, 0), n); hi = min(max(t - w // 2 + w, 0), n)
                    out[g, i] = w / float(hi - lo)
                if right:
                    t = n - 8 + i
                    lo = min(max(t - w // 2, 0), n); hi = min(max(t - w // 2 + w, 0), n)
                    out[g, 8 + i] = w / float(hi - lo)
        return out.reshape(64)
    return inv_freq, pm, corr_tab


def kernel(x_prompt, x_sample, cache_k, cache_v, c, c_ctx, w_mod, b_mod, ln_g, ln_b,
           ffn1_w1, ffn1_w2, w_in, q_norm_g, k_norm_g, pool_w, pool_scale,
           w_up_attn, w_up_pool, w_out, ffn2_w1, ffn2_w2):
    global _PROGRAM
    f32 = np.float32
    A = lambda a: np.ascontiguousarray(np.asarray(a, dtype=f32))
    x_prompt, x_sample, cache_k, cache_v = A(x_prompt), A(x_sample), A(cache_k), A(cache_v)
    c, c_ctx = A(c), A(c_ctx)
    inv_freq, pm, corr_tab = _host_consts()

    def fm(v):
        return np.ascontiguousarray(v.reshape(8, 128).T)

    def w1_pm(w):
        out = np.empty((128, 2, 8 * DFF), f32)
        pieces = [(0, 2)] + [(c0, 3) for c0 in range(2, 20, 3)] + [(20, 2)]
        for half in range(2):
            wh = w[:, half * DFF:(half + 1) * DFF].reshape(8, 128, DFF)
            for (c0, n) in pieces:
                blk = wh[:, :, c0 * 128:(c0 + n) * 128].transpose(1, 0, 2)
                out[:, half, 8 * 128 * c0: 8 * 128 * (c0 + n)] = blk.reshape(128, 8 * n * 128)
        return out

    shared = {
        "w_mod": A(w_mod)[0], "ffn1_w1": w1_pm(A(ffn1_w1)[0]), "ffn1_w2": A(ffn1_w2)[0],
        "ffn2_w1": w1_pm(A(ffn2_w1)[0]), "ffn2_w2": A(ffn2_w2)[0], "w_in": A(w_in)[0],
        "w_up_attn": A(w_up_attn)[0], "w_up_pool": A(w_up_pool)[0], "w_out": A(w_out)[0],
        "bmodT": np.ascontiguousarray(A(b_mod)[0].reshape(72, 128).T),
        "lngT": np.ascontiguousarray(A(ln_g)[0].reshape(24, 128).T),
        "lnbT": np.ascontiguousarray(A(ln_b)[0].reshape(24, 128).T),
        "qkg": np.ascontiguousarray(np.stack([A(q_norm_g)[0], A(k_norm_g)[0]], axis=1)),
        "pool_w": np.ascontiguousarray(A(pool_w)[0].transpose(1, 0, 2).reshape(128, 512)),
        "pscaleT": np.ascontiguousarray(A(pool_scale)[0].reshape(4, 128).T),
        "pmat": pm,
        "esel": np.ascontiguousarray(np.broadcast_to((np.arange(128) % 32 == 0).astype(f32)[:, None], (128, 128))),
    }
    in_maps = []
    for core in range(NCORES):
        b, q = core // 4, core % 4
        xp = x_prompt[4 * core:4 * core + 4].reshape(1024, D)
        xs = x_sample[b, 1024 * q:1024 * (q + 1)]
        xall = np.concatenate([xp, xs], axis=0)
        xT = np.ascontiguousarray(xall.reshape(T, 8, 128).transpose(2, 1, 0))
        condT = np.stack([fm(c_ctx), fm(c[b])], axis=2).reshape(128, 16)
        kctx = np.ascontiguousarray(cache_k[b, 0].transpose(2, 1, 0).reshape(128, 512))
        vctx = np.ascontiguousarray(cache_v[b, 0].reshape(2, 128, 256).transpose(1, 0, 2).reshape(128, 512))
        pos = np.arange(1024 * q, 1024 * (q + 1))
        row = (pos // 64).astype(f32)
        col = (pos % 64).astype(f32)
        ang = np.concatenate([row[:, None] * inv_freq, col[:, None] * inv_freq], axis=-1).astype(f32)
        cs, sn = np.cos(ang).astype(f32).T, np.sin(ang).astype(f32).T
        ropeCS = np.concatenate([np.concatenate([cs, cs], 0), np.concatenate([sn, sn], 0)], axis=1)
        m = np.zeros(8, f32)
        if q - 1 >= 0:
            m[q - 1] = 1.0
        if q + 1 <= 3:
            m[4 + q + 1] = 1.0
        corr = np.concatenate([corr_tab(4096, q == 0, q == 3), corr_tab(256, True, True)])
        im = dict(shared)
        im.update({
            "xT": xT, "condT": np.ascontiguousarray(condT), "kctx": kctx, "vctx": vctx,
            "ropeCS": np.ascontiguousarray(ropeCS.astype(f32)),
            "masks": np.ascontiguousarray(np.broadcast_to(m, (128, 8))),
            "corr": np.ascontiguousarray(np.broadcast_to(corr.astype(f32), (128, 128))),
        })
        in_maps.append(im)

    if _PROGRAM is None:
        _st = os.environ.get("KSTOP")
        _PROGRAM = build_program(None if _st is None else int(_st))
    res = run_bass_kernel_spmd(_PROGRAM, in_maps, core_ids=list(range(NCORES)))

    y_prompt = np.empty((32, 256, D), f32)
    y_sample = np.empty((2, 4096, D), f32)
    new_k = np.empty((32, 1, 256, 2, 128), f32)
    new_v = np.empty((32, 1, 256, 2, 128), f32)
    for core in range(NCORES):
        r = res.results[core]
        b, q = core // 4, core % 4
        yT = np.asarray(r["yT"], dtype=f32)
        yall = yT.transpose(2, 1, 0).reshape(T, D)
        y_prompt[4 * core:4 * core + 4] = yall[:1024].reshape(4, 256, D)
        y_sample[b, 1024 * q:1024 * (q + 1)] = yall[1024:]
        nk_ = np.asarray(r["nk"], dtype=f32)
        new_k[4 * core:4 * core + 4, 0] = nk_.transpose(2, 1, 0).reshape(4, 256, 2, 128)
        nv_ = np.asarray(r["nv"], dtype=f32)
        new_v[4 * core:4 * core + 4, 0] = nv_.transpose(1, 0, 2).reshape(4, 256, 2, 128)
    return (y_prompt, y_sample, new_k, new_v)
```

```python
import contextlib
import os
import numpy as np
import concourse.bass as bass
import concourse.mybir as mybir
from concourse.bass_utils import run_bass_kernel_spmd

F32 = mybir.dt.float32
BF16 = mybir.dt.bfloat16
AF = mybir.ActivationFunctionType
ALU = mybir.AluOpType

D = 1024
DFF = 2816
NK = 8
T = 2048
TBS = 512
NTB = 4
ALPHA = 2.0 ** 0.25
LN_EPS = 1e-6
RMS_EPS = 1e-6
SCALE = 128.0 ** -0.5
WINS = (2, 4, 8, 16)
NCORES = 8
AGW = 4096
ARENA_WORDS = 53200
SLOT_ELEMS = 9216


class Prod:
    def __init__(self, sem, step):
        self.sem = sem
        self.step = step
        self.count = 0


class Buf:
    __slots__ = ("w", "r", "name")

    def __init__(self, name=""):
        self.w = None
        self.r = {}
        self.name = name


class Eng:
    def __init__(self, eng, prod):
        self.eng = eng
        self.prod = prod
        self.seen = {}
        self.self_sync = False

    def wait(self, t):
        prod, val = t
        if prod is self.prod and not self.self_sync:
            return
        if self.seen.get(prod, 0) >= val:
            return
        self.eng.wait_ge(prod.sem, val)
        self.seen[prod] = val


class Region:
    def __init__(self, start, end):
        self.start, self.end, self.top = start, end, start

    def reset(self):
        self.top = self.start


class K:
    def __init__(self, nc, st):
        self.nc = nc
        self.st = st
        self.E = {}
        for name, eng in (("pe", nc.tensor), ("act", nc.scalar), ("dve", nc.vector),
                          ("pool", nc.gpsimd), ("sp", nc.sync)):
            self.E[name] = Eng(eng, self.new_prod("e_" + name, 1))
            self.E[name].self_sync = name in ("act", "dve", "pool")
        self.dprods = []
        self.arena_t = st.enter_context(nc.sbuf_tensor("arena", [128, ARENA_WORDS], F32))
        self.arena = self.arena_t[:, :]
        self.ps_t = st.enter_context(nc.psum_tensor("ps", [128, 8, 512], F32))
        self.bank = [Buf("bank%d" % i) for i in range(8)]
        self.jobs = []
        self.job_next = 0
        self.job_cur = 0
        self.job_slot = {}

    def new_prod(self, name, step):
        sem = self.st.enter_context(self.nc.semaphore(name))
        return Prod(sem, step)

    def dprod(self, name):
        p = self.new_prod("d_" + name, 16)
        self.dprods.append(p)
        return p

    def _deps(self, E, reads, writes):
        for b in reads:
            if b.w is not None:
                E.wait(b.w)
        for b in writes:
            if b.w is not None:
                E.wait(b.w)
            for p, v in b.r.items():
                E.wait((p, v))

    def _commit(self, t, reads, writes):
        p, v = t
        for b in reads:
            if b.r.get(p, 0) < v:
                b.r[p] = v
        for b in writes:
            b.w = t
            b.r = {}

    def op(self, en, fn, reads=(), writes=(), pre=()):
        E = self.E[en]
        self._deps(E, reads, writes)
        for f in pre:
            f()
        ins = fn()
        E.prod.count += 1
        ins.then_inc(E.prod.sem, 1)
        t = (E.prod, E.prod.count)
        self._commit(t, reads, writes)
        return t

    def dma(self, qn, out, in_, prod, reads=(), writes=(), nodep=False):
        E = self.E[qn]
        if not nodep:
            self._deps(E, reads, writes)
        ins = E.eng.dma_start(out=out, in_=in_)
        prod.count += 16
        ins.then_inc(prod.sem, 16)
        t = (prod, prod.count)
        self._commit(t, reads, writes)
        return t

    def barrier(self):
        ticks = [(e.prod, e.prod.count) for e in self.E.values() if e.prod.count > 0]
        ticks += [(p, p.count) for p in self.dprods if p.count > 0]
        for e in self.E.values():
            for t in ticks:
                e.wait(t)

    def psb(self, i):
        return self.ps_t[:, i, :]

    def f32v(self, off_bytes, n):
        o = off_bytes // 4
        return self.arena[:, o:o + n]

    def bfv(self, off_bytes, n):
        o = off_bytes // 4
        return self.arena[:, o:o + n // 2].bitcast(BF16)

    def alloc(self, reg, nbytes):
        nbytes = (nbytes + 63) // 64 * 64
        off = reg.top
        reg.top += nbytes
        assert reg.top <= reg.end, ("region overflow", reg.start, reg.end, reg.top)
        return off

    def pe_group(self, out_ap, pairs, reads, writes):
        n = len(pairs)
        nc = self.nc
        pre = [(lambda l=l, r=r, i=i: nc.tensor.matmul(out_ap, lhsT=l, rhs=r, start=(i == 0), stop=False))
               for i, (l, r) in enumerate(pairs[:-1])]
        l, r = pairs[-1]
        return self.op("pe", lambda: nc.tensor.matmul(out_ap, lhsT=l, rhs=r, start=(n == 1), stop=True),
                       reads, writes, pre=pre)

    def pe_mm(self, out_ap, l, r, start, stop, reads, writes):
        nc = self.nc
        return self.op("pe", lambda: nc.tensor.matmul(out_ap, lhsT=l, rhs=r, start=start, stop=stop),
                       reads, writes)

    def act(self, out, in_, func, reads, writes, bias=None, scale=None):
        nc = self.nc
        kw = {}
        if bias is not None:
            kw["bias"] = bias
        if scale is not None:
            kw["scale"] = scale
        return self.op("act", lambda: nc.scalar.activation(out=out, in_=in_, func=func, **kw), reads, writes)

    def tt(self, out, in0, in1, op, reads, writes):
        nc = self.nc
        return self.op("dve", lambda: nc.vector.tensor_tensor(out=out, in0=in0, in1=in1, op=op), reads, writes)

    def ts(self, out, in0, s1, op0, reads, writes, s2=None, op1=None):
        nc = self.nc
        if op1 is None:
            return self.op("dve", lambda: nc.vector.tensor_scalar(out=out, in0=in0, scalar1=s1, scalar2=None,
                                                                  op0=op0), reads, writes)
        return self.op("dve", lambda: nc.vector.tensor_scalar(out=out, in0=in0, scalar1=s1, scalar2=s2,
                                                              op0=op0, op1=op1), reads, writes)

    def stt(self, out, in0, scalar, in1, op0, op1, reads, writes):
        nc = self.nc
        return self.op("dve", lambda: nc.vector.scalar_tensor_tensor(out=out, in0=in0, scalar=scalar, in1=in1,
                                                                     op0=op0, op1=op1), reads, writes)

    def recip(self, out, in_, reads, writes):
        nc = self.nc
        return self.op("dve", lambda: nc.vector.reciprocal(out=out, in_=in_), reads, writes)

    def vcopy(self, out, in_, reads, writes):
        nc = self.nc
        return self.op("dve", lambda: nc.vector.tensor_copy(out=out, in_=in_), reads, writes)

    def memset(self, out, val, writes):
        nc = self.nc
        return self.op("dve", lambda: nc.vector.memset(out, val), (), writes)

    def add_job(self, fn):
        self.jobs.append(fn)

    def prefetch(self, si):
        if self.job_next < len(self.jobs):
            self.jobs[self.job_next](si)
            self.job_slot[self.job_next] = si
            self.job_next += 1

    def take(self):
        assert self.job_cur < self.job_next, "job not prefetched"
        si = self.job_slot[self.job_cur]
        self.job_cur += 1
        return si


class _Stop(Exception):
    pass


def build_program(stop=None):
    nc = bass.Bass("TRN2", target_bir_lowering=False)
    dr = {}

    def din(name, shape, dt=F32):
        dr[name] = nc.dram_tensor(name, list(shape), dt, kind="ExternalInput").ap()
        return dr[name]

    def dout(name, shape, dt=F32):
        dr[name] = nc.dram_tensor(name, list(shape), dt, kind="ExternalOutput").ap()
        return dr[name]

    xT = din("xT", [128, NK, T])
    condT = din("condT", [128, 16])
    w_mod = din("w_mod", [D, 9 * D])
    bmodT = din("bmodT", [128, 72])
    lngT = din("lngT", [128, 24])
    lnbT = din("lnbT", [128, 24])
    ffn1_w1 = din("ffn1_w1", [128, 2, 8 * DFF])
    ffn1_w2 = din("ffn1_w2", [DFF, D])
    ffn2_w1 = din("ffn2_w1", [128, 2, 8 * DFF])
    ffn2_w2 = din("ffn2_w2", [DFF, D])
    w_in = din("w_in", [D, 4096])
    qkg = din("qkg", [128, 2])
    pool_w = din("pool_w", [128, 512])
    pscaleT = din("pscaleT", [128, 4])
    w_up_attn = din("w_up_attn", [D, D])
    w_up_pool = din("w_up_pool", [512, D])
    w_out = din("w_out", [D, D])
    kctx = din("kctx", [128, 512])
    vctx = din("vctx", [128, 512])
    ropeCS = din("ropeCS", [128, 2048])
    pmat = din("pmat", [128, 128])
    masks = din("masks", [128, 8])
    corr = din("corr", [128, 128])
    esel = din("esel", [128, 128])
    yT = dout("yT", [128, NK, T])
    nk = dout("nk", [128, 2, 1024])
    nv = dout("nv", [128, 8, 256])
    ag_in = nc.dram_tensor("ag_in", [128, AGW], BF16)
    ag_out = nc.dram_tensor("ag_out", [512, AGW], BF16)
    ag_in_ap = ag_in.ap()
    ag_out_ap = ag_out.ap()
    dr["ag2_in"] = nc.dram_tensor("ag2_in", [128, 128], BF16).ap()
    dr["ag2_out"] = nc.dram_tensor("ag2_out", [512, 128], BF16).ap()

    with contextlib.ExitStack() as st:
        k = K(nc, st)
        k.stop = stop
        try:
            _emit(k, nc, dr, ag_in, ag_out, ag_in_ap, ag_out_ap)
        except _Stop:
            pass
    return nc


def _emit(k, nc, dr, ag_in, ag_out, ag_in_ap, ag_out_ap):
    ps = k.psb
    bank = k.bank
    SKIP = set(os.environ.get("KSKIP", "").split(","))
    G = Region(0, ARENA_WORDS * 4)
    oR = k.alloc(G, NK * T * 4)
    oSlot = [k.alloc(G, SLOT_ELEMS * 2), k.alloc(G, SLOT_ELEMS * 2)]
    oOnesLn = k.alloc(G, 256)
    oOnesH = k.alloc(G, 256)
    oOnes1 = k.alloc(G, 256)
    oPm = k.alloc(G, 256)
    oModp = k.alloc(G, 144 * 4)
    oBmod = k.alloc(G, 72 * 4)
    oLng = k.alloc(G, 24 * 4)
    oLnb = k.alloc(G, 24 * 4)
    oCond = k.alloc(G, 16 * 4)
    oScT = k.alloc(G, 16 * 2)
    oQkg = k.alloc(G, 2 * 4)
    oPsc = k.alloc(G, 4 * 4)
    oMask = k.alloc(G, 8 * 4)
    oCorr = k.alloc(G, 128 * 4)
    oPoolW = k.alloc(G, 512 * 2)
    oDer = k.alloc(G, 32 * 8 * 4)
    gend = G.top
    RH = Region(gend, gend + 32768)
    RQ = Region(RH.end, RH.end + 32768)
    RP2 = Region(RQ.end, RQ.end + 16384)
    RX = Region(RP2.end, ARENA_WORDS * 4)
    RHQ = Region(RH.start, RQ.end)
    RQX = Region(RQ.start, RX.end)
    assert RX.end - RX.start >= 20000, (RX.start, RX.end)

    R3 = k.f32v(oR, NK * T).rearrange("p (k t) -> p k t", k=NK)
    Rbk = [[Buf("R%d_%d" % (i, j)) for j in range(NK)] for i in range(NTB)]
    Rprod = [k.dprod("R%d" % i) for i in range(NTB)]
    slot = [k.bfv(o, SLOT_ELEMS) for o in oSlot]
    slotb = [Buf("slot0"), Buf("slot1")]
    slotb2 = [Buf("slot0w2"), Buf("slot1w2")]
    k.stage = None
    slotp = [k.dprod("slot0"), k.dprod("slot1")]
    slotp2 = [k.dprod("slot0w2"), k.dprod("slot1w2")]
    ones_ln = k.bfv(oOnesLn, 128)
    ones_h = k.bfv(oOnesH, 128)
    ones1 = k.bfv(oOnes1, 128)
    Pm = k.bfv(oPm, 128)
    modp = k.f32v(oModp, 144)
    modp3 = modp.rearrange("p (j g) -> p j g", g=2)
    modp4 = modp.rearrange("p (i k g) -> p i k g", i=9, k=8)
    bmod = k.f32v(oBmod, 72)
    lng = k.f32v(oLng, 24)
    lnb = k.f32v(oLnb, 24)
    cond = k.f32v(oCond, 16)
    scT = k.bfv(oScT, 16)
    qkg = k.f32v(oQkg, 2)
    psc = k.f32v(oPsc, 4)
    mask = k.f32v(oMask, 8)
    corr = k.f32v(oCorr, 128)
    poolw = k.bfv(oPoolW, 512)
    der = k.f32v(oDer, 256)
    constb = Buf("const")
    cprod = k.dprod("const")
    cprod2 = k.dprod("const2")
    derb = Buf("der")

    dcol = {}

    def DER(name, g=None):
        key = (name, g)
        if key not in dcol:
            dcol[key] = len(dcol)
            assert len(dcol) <= 32
        c = dcol[key] * 8
        return der[:, c:c + 8]

    def DERk(name, g, kk):
        c = dcol[(name, g)] * 8 + kk
        return der[:, c:c + 1]

    def finalize():
        sp = k.E["sp"]
        for p in k.dprods:
            if p.count > 0:
                sp.wait((p, p.count))
        for e in k.E.values():
            if e.prod.count > 0:
                sp.wait((e.prod, e.prod.count))

    dbgp = k.dprod("dbg")

    def cut(stage):
        if k.stop is not None and k.stop == stage:
            k.barrier()
            for tb in range(NTB):
                k.dma("sp", dr["yT"][:, :, tb * TBS:(tb + 1) * TBS], R3[:, :, tb * TBS:(tb + 1) * TBS], dbgp, Rbk[tb], [])
            finalize()
            raise _Stop()

    def job_wmod(i):
        def f(si):
            k.dma("pool", slot[si][:, 0:8192].rearrange("p (k c) -> p k c", k=8),
                  dr["w_mod"][:, i * 1024:(i + 1) * 1024].rearrange("(k p) c -> p k c", p=128),
                  slotp[si], (), [slotb[si], slotb2[si]])
        return f

    PIECES = [(0, 2)] + [(c0, 3) for c0 in range(2, 20, 3)] + [(20, 2)]

    def job_ffn(w1, w2, c0, n):
        def f(si):
            s = slot[si]
            o0 = 8 * 128 * c0
            k.dma("pool", s[:, 0:3072].rearrange("p (k c) -> p k c", k=8)[:, :, 0:n * 128],
                  w1[:, 0, o0:o0 + 8 * n * 128].rearrange("p (k c) -> p k c", k=8),
                  slotp[si], (), [slotb[si]])
            k.dma("pool", s[:, 3072:6144].rearrange("p (k c) -> p k c", k=8)[:, :, 0:n * 128],
                  w1[:, 1, o0:o0 + 8 * n * 128].rearrange("p (k c) -> p k c", k=8),
                  slotp[si], (), [slotb[si]], nodep=True)
            if k.stage is None:
                k.dma("pool", s[:, 6144:6144 + n * 1024].rearrange("p (j c) -> p j c", j=n),
                      w2[c0 * 128:(c0 + n) * 128, :].rearrange("(j p) c -> p j c", p=128),
                      slotp2[si], (), [slotb2[si]])
            else:
                stg, stgb, stgp = k.stage[si]
                k.dma("sp", stg[:, 0:n * 1024].rearrange("p (j c) -> p j c", j=n),
                      w2[c0 * 128:(c0 + n) * 128, :].rearrange("(j p) c -> p j c", p=128),
                      stgp, (), [stgb])
                for j in range(n):
                    k.act(s[:, 6144 + j * 1024:6144 + (j + 1) * 1024], stg[:, j * 1024:(j + 1) * 1024], AF.Copy,
                          [stgb], [slotb2[si]])
        return f

    def job_win(col0):
        def f(si):
            k.dma("pool", slot[si][:, 0:8192].rearrange("p (k c) -> p k c", k=8),
                  dr["w_in"][:, col0:col0 + 1024].rearrange("(k p) c -> p k c", p=128),
                  slotp[si], (), [slotb[si], slotb2[si]])
        return f

    def job_g1(ocp):
        def f(si):
            s = slot[si]
            c0 = ocp * 256
            k.dma("pool", s[:, 0:2048].rearrange("p (k c) -> p k c", k=8),
                  dr["w_in"][:, 2048 + c0:2048 + c0 + 256].rearrange("(k p) c -> p k c", p=128),
                  slotp[si], (), [slotb[si], slotb2[si]])
            k.dma("pool", s[:, 2048:4096].rearrange("p (k c) -> p k c", k=8),
                  dr["w_in"][:, 3072 + c0:3072 + c0 + 256].rearrange("(k p) c -> p k c", p=128),
                  slotp[si], (), [slotb[si]], nodep=True)
            k.dma("pool", s[:, 4096:6144].rearrange("p (k c) -> p k c", k=8),
                  dr["w_up_attn"][:, c0:c0 + 256].rearrange("(k p) c -> p k c", p=128),
                  slotp[si], (), [slotb[si]], nodep=True)
            k.dma("pool", s[:, 6144:7168].rearrange("p (k c) -> p k c", k=4),
                  dr["w_up_pool"][:, c0:c0 + 256].rearrange("(k p) c -> p k c", p=128),
                  slotp[si], (), [slotb[si]], nodep=True)
        return f

    def job_wout():
        def f(si):
            k.dma("pool", slot[si][:, 0:8192].rearrange("p (k c) -> p k c", k=8),
                  dr["w_out"][:, :].rearrange("(k p) c -> p k c", p=128),
                  slotp[si], (), [slotb[si], slotb2[si]])
        return f

    for i in range(2):
        k.add_job(job_wmod(i))
    for pi, (c0, n) in enumerate(PIECES):
        k.add_job(job_ffn(dr["ffn1_w1"], dr["ffn1_w2"], c0, n))
        if pi == 0:
            k.add_job(job_wmod(2))
        if pi == 3:
            k.add_job(job_wmod(3))
        if pi == 5:
            k.add_job(job_wmod(4))
    k.add_job(job_win(1024))
    k.add_job(job_win(0))
    for i in range(5, 9):
        k.add_job(job_wmod(i))
    for ocp in range(4):
        k.add_job(job_g1(ocp))
    k.add_job(job_wout())
    for (c0, n) in PIECES:
        k.add_job(job_ffn(dr["ffn2_w1"], dr["ffn2_w2"], c0, n))

    for (dst, src) in ((cond, "condT"), (bmod, "bmodT"), (lng, "lngT"), (lnb, "lnbT"), (qkg, "qkg"),
                       (psc, "pscaleT"), (mask, "masks"), (corr, "corr")):
        k.dma("sp", dst, dr[src][:, :], cprod, (), [constb])
    k.dma("pool", Pm, dr["pmat"][:, :], cprod2, (), [constb])
    k.dma("pool", poolw, dr["pool_w"][:, :], cprod2, (), [constb])
    k.prefetch(0)
    k.prefetch(1)
    for tb in range(NTB):
        k.dma("sp", R3[:, :, tb * TBS:(tb + 1) * TBS], dr["xT"][:, :, tb * TBS:(tb + 1) * TBS],
              Rprod[tb], (), Rbk[tb])
    onesb = Buf("ones")
    k.memset(ones_ln, 1.0 / 1024.0, [onesb])
    k.memset(ones_h, 1.0 / 128.0, [onesb])
    k.memset(ones1, 1.0, [onesb])
    scb = Buf("scT")
    k.act(scT, cond, AF.Silu, [constb], [scb])
    def mod_compute(i, bk):
        si = k.take()
        for j in range(8):
            col = (i * 8 + j) * 2
            for kk in range(8):
                nc_l = slot[si][:, kk * 1024 + j * 128: kk * 1024 + (j + 1) * 128]
                if kk == 0 and j == 0:
                    k._deps(k.E["pe"], [slotb[si], scb], [bank[bk]])
                ins = nc.tensor.matmul(ps(bk)[:, col:col + 2], lhsT=nc_l, rhs=scT[:, kk * 2:(kk + 1) * 2],
                                       start=(kk == 0), stop=(kk == 7))
        E = k.E["pe"]
        E.prod.count += 1
        ins.then_inc(E.prod.sem, 1)
        tk = (E.prod, E.prod.count)
        k._commit(tk, [slotb[si], scb], [bank[bk]])
        k.prefetch(si)

    def mod_finish(i0, i1, bk):
        psm3 = ps(bk)[:, 0:144].rearrange("p (j g) -> p j g", g=2)
        for g in range(2):
            k.tt(modp3[:, i0 * 8:i1 * 8, g], psm3[:, i0 * 8:i1 * 8, g], bmod[:, i0 * 8:i1 * 8], ALU.add,
                 [bank[bk], constb], [derb])

    for i in range(2):
        mod_compute(i, 0)
    mod_finish(0, 2, 0)

    def M(i, g):
        return modp4[:, i, :, g]

    for g in range(2):
        k.ts(DER("A1", g), M(1, g), 1.0, ALU.add, [], [derb])
        k.vcopy(DER("SH1", g), M(0, g), [], [derb])
        DER("HG1", g)
    k.ts(DER("RS1"), lng[:, 0:8], ALPHA, ALU.mult, [constb], [derb])
    k.ts(DER("RB1"), lnb[:, 0:8], ALPHA, ALU.mult, [constb], [derb])
    k.ts(DER("RS2"), lng[:, 8:16], ALPHA, ALU.mult, [constb], [derb])
    k.ts(DER("RB2"), lnb[:, 8:16], ALPHA, ALU.mult, [constb], [derb])
    k.vcopy(DER("RS3"), lng[:, 16:24], [constb], [derb])
    k.vcopy(DER("RB3"), lnb[:, 16:24], [constb], [derb])
    for g in range(2):
        for nm in ("H2S", "H2B", "G2", "T3", "H3S", "H3B", "HG3"):
            DER(nm, g)

    def derive_mix():
      for g in range(2):
        k.ts(DER("H2S", g), M(4, g), 1.0, ALU.add, [], [derb], s2=1.0 / ALPHA, op1=ALU.mult)
        k.vcopy(DER("H2B", g), M(3, g), [], [derb])

    def derive_late():
      for g in range(2):
        k.vcopy(DER("G2", g), M(5, g), [], [derb])
        k.ts(DER("T3", g), M(7, g), 1.0, ALU.add, [], [derb])
        k.tt(DER("H3S", g), DER("T3", g), lng[:, 8:16], ALU.mult, [constb], [derb])
        k.tt(DER("H3B", g), DER("T3", g), lnb[:, 8:16], ALU.mult, [constb], [derb])
        k.tt(DER("H3B", g), DER("H3B", g), M(6, g), ALU.add, [], [derb])
        k.ts(DER("HG3", g), M(8, g), 0.5, ALU.mult, [], [derb])

    oH = k.alloc(RH, NK * T * 2)
    H3 = k.bfv(oH, NK * T).rearrange("p (k t) -> p k t", k=NK)
    Hb = [Buf("H%d" % i) for i in range(NTB)]
    yprod = [k.dprod("y%d" % i) for i in range(NTB)]

    def ln_alloc(reg):
        o = {}
        o["yb"] = [k.bfv(k.alloc(reg, 1024), 512) for _ in range(NK)]
        o["ysq"] = [k.bfv(k.alloc(reg, 1024), 512) for _ in range(NK)]
        o["ybb"] = [Buf() for _ in range(NK)]
        o["ysqb"] = [Buf() for _ in range(NK)]
        o["mean"] = [k.f32v(k.alloc(reg, 2048), 512) for _ in range(2)]
        o["tmp"] = k.f32v(k.alloc(reg, 2048), 512)
        o["rstd"] = [k.f32v(k.alloc(reg, 2048), 512) for _ in range(2)]
        o["meanb"], o["tmpb"], o["rstdb"] = [Buf(), Buf()], Buf(), [Buf(), Buf()]
        return o

    def ln_stats_chunk(tb, L, kk):
        tsl = slice(tb * TBS, (tb + 1) * TBS)
        k.act(L["yb"][kk], R3[:, kk, tsl], AF.Copy, [Rbk[tb][kk]], [L["ybb"][kk]])
        k.act(L["ysq"][kk], R3[:, kk, tsl], AF.Square, [Rbk[tb][kk]], [L["ysqb"][kk]])

    def ln_pe_stats(tb, L, b6=6, b7=7):
        for kk in range(NK):
            k.pe_mm(ps(b6), ones_ln, L["yb"][kk], kk == 0, kk == NK - 1, [L["ybb"][kk], onesb], [bank[b6]])
        for kk in range(NK):
            k.pe_mm(ps(b7), ones_ln, L["ysq"][kk], kk == 0, kk == NK - 1, [L["ysqb"][kk], onesb], [bank[b7]])

    def ln_fin_a(tb, L, b6=6, b7=7):
        par = tb % 2
        k.vcopy(L["mean"][par], ps(b6), [bank[b6]], [L["meanb"][par]])
        k.tt(L["tmp"], L["mean"][par], L["mean"][par], ALU.mult, [L["meanb"][par]], [L["tmpb"]])
        k.tt(L["tmp"], ps(b7), L["tmp"], ALU.subtract, [bank[b7]], [L["tmpb"]])
        k.act(L["rstd"][par], L["tmp"], AF.Ln, [L["tmpb"]], [L["rstdb"][par]], bias=LN_EPS, scale=1.0)
        k.act(L["rstd"][par], L["rstd"][par], AF.Exp, [], [L["rstdb"][par]], scale=-0.5)

    def ln_fin_b1(tb, L):
        par = tb % 2
        tsl = slice(tb * TBS, (tb + 1) * TBS)
        Rt = R3[:, :, tsl]
        k.tt(Rt, Rt, L["mean"][par].unsqueeze(1).to_broadcast([128, NK, TBS]), ALU.subtract,
             [L["meanb"][par]], Rbk[tb])

    def ln_fin_b2(tb, L, rs, rb, hs=None, hb=None, final=False, split=False):
        par = tb % 2
        g = tb // 2
        tsl = slice(tb * TBS, (tb + 1) * TBS)
        Rt = R3[:, :, tsl]
        if not split:
            k.tt(Rt, Rt, L["rstd"][par].unsqueeze(1).to_broadcast([128, NK, TBS]), ALU.mult,
                 [L["rstdb"][par]], Rbk[tb])
        for kk in range(NK):
            if split:
                k.tt(R3[:, kk, tsl], R3[:, kk, tsl], L["mean"][par], ALU.subtract, [L["meanb"][par]], [Rbk[tb][kk]])
                k.tt(R3[:, kk, tsl], R3[:, kk, tsl], L["rstd"][par], ALU.mult, [L["rstdb"][par]], [Rbk[tb][kk]])
            if hs is not None:
                k.ts(H3[:, kk, tsl], R3[:, kk, tsl], DERk(hs, g, kk), ALU.mult, [Rbk[tb][kk], derb], [Hb[tb]],
                     s2=DERk(hb, g, kk), op1=ALU.add)
            if hs is None and kk % 2 == 1:
                k.ts(R3[:, kk, tsl], R3[:, kk, tsl], DERk(rs, None, kk), ALU.mult, [derb], [Rbk[tb][kk]],
                     s2=DERk(rb, None, kk), op1=ALU.add)
            else:
                k.act(R3[:, kk, tsl], R3[:, kk, tsl], AF.Identity, [derb], [Rbk[tb][kk]],
                      bias=DERk(rb, None, kk), scale=DERk(rs, None, kk))
            if final:
                k.dma("sp", dr["yT"][:, kk, tsl], R3[:, kk, tsl], yprod[tb], [Rbk[tb][kk]], [])

    def ln_finish(tb, L, rs, rb, hs=None, hb=None, final=False, b6=6, b7=7):
        ln_fin_a(tb, L, b6, b7)
        ln_fin_b1(tb, L)
        ln_fin_b2(tb, L, rs, rb, hs=hs, hb=hb, final=final)

    stgprods = [yprod[0], yprod[1]]

    def ffn_alloc():
        reg = RQX
        reg.reset()
        o = {}
        o["gbuf"] = [[k.bfv(k.alloc(reg, 1024), 512) for _ in range(3)] for _ in range(2)]
        o["gb"] = [[Buf() for _ in range(3)] for _ in range(2)]
        o["sa"] = [k.f32v(k.alloc(reg, 2048), 512) for _ in range(2)]
        o["sab"] = [Buf(), Buf()]
        o["Ln"] = ln_alloc(reg)
        o["stage"] = [(k.f32v(k.alloc(reg, 12288), 3072), Buf("stg%d" % i), stgprods[i]) for i in range(2)]
        return o

    def ffn_phase(hgname, L, ln_args, between=None, pre_stt=None):
        fb = ffn_alloc() if L is None else L
        gbuf, gb, sa, sab, Ln = fb["gbuf"], fb["gb"], fb["sa"], fb["sab"], fb["Ln"]
        k.stage = fb["stage"]
        cnt = 0
        cntf = 0
        gi = 0
        for pi, (c0, n) in enumerate(PIECES):
            si = k.take()
            s = slot[si]
            last = (pi == len(PIECES) - 1)
            for tb in range(NTB):
                g = tb // 2
                tsl = slice(tb * TBS, (tb + 1) * TBS)
                gs = gi % 2
                gi += 1
                for j in range(n):
                    a, b2 = cnt % 2, 2 + cnt % 2
                    cnt += 1
                    k.pe_group(ps(a), [(s[:, kk * 384 + j * 128: kk * 384 + (j + 1) * 128], H3[:, kk, tsl])
                                       for kk in range(NK)], [slotb[si], Hb[tb]], [bank[a]])
                    k.pe_group(ps(b2), [(s[:, 3072 + kk * 384 + j * 128: 3072 + kk * 384 + (j + 1) * 128],
                                         H3[:, kk, tsl]) for kk in range(NK)], [slotb[si], Hb[tb]], [bank[b2]])
                    k.act(sa[a], ps(a), AF.Silu, [bank[a]], [sab[a]])
                    k.tt(gbuf[gs][j], sa[a], ps(b2), ALU.mult, [sab[a], bank[b2]], [gb[gs][j]])
                if last and tb >= 1:
                    ln_fin_b1(tb - 1, Ln)
                if pre_stt is not None and pi == 0 and tb == 0:
                    pre_stt()
                for oc in range(NK):
                    f = 4 + cntf % 2
                    cntf += 1
                    k.pe_group(ps(f), [(s[:, 6144 + j * 1024 + oc * 128: 6144 + j * 1024 + (oc + 1) * 128],
                                        gbuf[gs][j]) for j in range(n)],
                               [slotb2[si]] + [gb[gs][j] for j in range(n)], [bank[f]])
                    k.stt(R3[:, oc, tsl], ps(f), DERk(hgname, g, oc), R3[:, oc, tsl], ALU.mult, ALU.add,
                          [bank[f], derb], [Rbk[tb][oc]])
                    if last:
                        ln_stats_chunk(tb, Ln, oc)
                if last:
                    if tb >= 1:
                        ln_fin_b2(tb - 1, Ln, *ln_args[0], **ln_args[1])
                    ln_pe_stats(tb, Ln)
                    ln_fin_a(tb, Ln)
                    if tb == NTB - 1:
                        ln_fin_b2(tb, Ln, *ln_args[0], split=True, **ln_args[1])
            if pi + 2 >= len(PIECES):
                k.stage = None
            k.prefetch(si)
            if between is not None:
                between(pi)

    for tb in range(NTB):
        g = tb // 2
        tsl = slice(tb * TBS, (tb + 1) * TBS)
        for kk in range(NK):
            if kk % 2 == 1:
                k.ts(H3[:, kk, tsl], R3[:, kk, tsl], DERk("A1", g, kk), ALU.mult, [Rbk[tb][kk], derb], [Hb[tb]],
                     s2=DERk("SH1", g, kk), op1=ALU.add)
            else:
                k.act(H3[:, kk, tsl], R3[:, kk, tsl], AF.Identity, [Rbk[tb][kk], derb], [Hb[tb]],
                      bias=DERk("SH1", g, kk), scale=DERk("A1", g, kk))
        k.ts(R3[:, :, tsl], R3[:, :, tsl], ALPHA, ALU.mult, [], Rbk[tb])

    cut(0)
    def ffn1_between(pi):
        if pi == 3:
            mod_compute(3, 6)
        if pi == 5:
            mod_compute(4, 6)
            mod_finish(3, 5, 6)
            derive_mix()

    def ffn1_pre_stt():
        mod_compute(2, 6)
        mod_finish(2, 3, 6)
        for g in range(2):
            k.ts(DER("HG1", g), M(2, g), 0.5, ALU.mult, [], [derb])

    ffn_phase("HG1", None, (("RS1", "RB1"), {}), between=ffn1_between, pre_stt=ffn1_pre_stt)
    k.barrier()
    cut(1)

    RH.reset(); RQ.reset(); RP2.reset(); RX.reset(); RHQ.reset()
    pooled2 = k.bfv(k.alloc(RP2, 4 * T * 2), 4 * T).rearrange("p (g t) -> p g t", g=4)
    p2b = [Buf("p2_%d" % i) for i in range(NTB)]
    oKp = k.alloc(RX, 2 * 1024 * 2)
    oVp = k.alloc(RX, 8 * 256 * 2)
    Kp = k.bfv(oKp, 2048).rearrange("p (h t) -> p h t", h=2)
    Vp = k.bfv(oVp, 2048).rearrange("p (j c) -> p j c", j=8)
    Kpb = [Buf(), Buf()]
    Vpb = [Buf(), Buf()]
    xmark = RX.top

    A = RHQ
    h2t = k.bfv(k.alloc(A, 8192), 4096).rearrange("p (k t) -> p k t", k=NK)
    h2b = [Buf("h2t%d" % i) for i in range(NK)]
    sq = [k.bfv(k.alloc(A, 1024), 512) for _ in range(2)]
    sqb = [Buf(), Buf()]
    sd = [k.f32v(k.alloc(A, 2048), 512) for _ in range(2)]
    sdb = [Buf(), Buf()]
    kn = [k.f32v(k.alloc(A, 2048), 512) for _ in range(2)]
    knb_ = [Buf(), Buf()]
    knbf = k.bfv(k.alloc(A, 1024), 512)
    knbfb = Buf()
    t1 = k.f32v(k.alloc(A, 2048), 512)
    t2 = k.f32v(k.alloc(A, 2048), 512)
    t1b, t2b = Buf(), Buf()
    kst = [k.bfv(k.alloc(A, 1024), 512) for _ in range(2)]
    kstb = [Buf(), Buf()]
    kstp = [k.dprod("kst0"), k.dprod("kst1")]
    knp = [k.dprod("kn0"), k.dprod("kn1")]
    vst = [k.f32v(k.alloc(A, 1024), 256) for _ in range(2)]
    vstb = [Buf(), Buf()]
    vstp = [k.dprod("vst0"), k.dprod("vst1")]
    vsb = [k.bfv(k.alloc(A, 512), 256) for _ in range(2)]
    vsbb = [Buf(), Buf()]
    vsbp = [k.dprod("vsb0"), k.dprod("vsb1")]
    ropeC = k.f32v(k.alloc(RX, 4096), 1024)
    ropeS = k.f32v(k.alloc(RX, 4096), 1024)
    ropeb = Buf("rope")
    ropep = k.dprod("rope")
    pp = [k.f32v(k.alloc(A, 2176), 544).rearrange("p (b t) -> p b t", b=2) for _ in range(2)]
    ppb = [Buf(), Buf()]
    pin_s = k.f32v(k.alloc(A, 4 * 1040 * 4), 4 * 1040).rearrange("p (g t) -> p g t", g=4)
    pinb = [Buf("pin_s%d" % i) for i in range(4)]
    ta = k.f32v(k.alloc(A, 4160), 1040)
    tb_ = k.f32v(k.alloc(A, 4160), 1040)
    tab, tbb = Buf(), Buf()
    plb = [k.bfv(k.alloc(A, 1024), 512) for _ in range(2)]
    plbb = [Buf(), Buf()]
    e32 = k.f32v(k.alloc(A, 256), 64).rearrange("p (g i) -> p g i", g=4)
    est = k.bfv(k.alloc(A, 256), 128)
    estb = Buf()
    estp = k.dprod("est")
    eg = k.bfv(k.alloc(A, 1024), 512).rearrange("p (r c) -> p r c", r=4)
    egb = Buf()
    egp = k.dprod("eg")
    ef = k.f32v(k.alloc(A, 1024), 256).rearrange("p (r c) -> p r c", r=4)
    efb = Buf()
    agb = Buf("ag_in")
    agob = Buf("ag_out")
    ag2b = Buf("ag2_in")
    ag2ob = Buf("ag2_out")
    ccprod = k.new_prod("cc", 1)
    k.dprods.append(ccprod)

    k.dma("sp", ropeC, dr["ropeCS"][:, 0:1024], ropep, (), [ropeb])
    k.dma("sp", ropeS, dr["ropeCS"][:, 1024:2048], ropep, (), [ropeb])
    for i in range(2):
        k.memset(pp[i], 0.0, [ppb[i]])
    k.memset(pin_s, 0.0, pinb)

    def mod2(tb, h2t_=None, h2b_=None, on_dve=False):
        g = tb // 2
        tsl = slice(tb * TBS, (tb + 1) * TBS)
        h2t_ = h2t if h2t_ is None else h2t_
        h2b_ = h2b if h2b_ is None else h2b_
        for kk in range(NK):
            if on_dve:
                k.ts(h2t_[:, kk, :], R3[:, kk, tsl], DERk("H2S", g, kk), ALU.mult, [Rbk[tb][kk], derb],
                     [h2b_[kk]], s2=DERk("H2B", g, kk), op1=ALU.add)
                continue
            k.act(h2t_[:, kk, :], R3[:, kk, tsl], AF.Identity, [Rbk[tb][kk], derb], [h2b_[kk]],
                  bias=DERk("H2B", g, kk), scale=DERk("H2S", g, kk))

    cnts = {"p": 0, "m": 0, "r": 0, "x": 0}

    def rms_head(si, col0, gcol, tb, sample, out_bf, out_bufs, kout=None):
        s = slot[si]
        pb = cnts["p"] % 3
        cnts["p"] += 1
        mb = 3 + cnts["m"] % 2
        cnts["m"] += 1
        x = cnts["x"] % 2
        cnts["x"] += 1
        k.pe_group(ps(pb), [(s[:, kk * 1024 + col0: kk * 1024 + col0 + 128], h2t[:, kk, :]) for kk in range(NK)],
                   [slotb[si]] + h2b, [bank[pb]])
        k.act(sq[x], ps(pb), AF.Square, [bank[pb]], [sqb[x]])
        k.pe_mm(ps(mb), ones_h, sq[x], True, True, [sqb[x], onesb], [bank[mb]])
        k.act(sd[x], ps(mb), AF.Ln, [bank[mb]], [sdb[x]], bias=RMS_EPS, scale=1.0)
        k.act(sd[x], sd[x], AF.Exp, [], [sdb[x]], scale=-0.5)
        if not sample:
            if kout is None:
                k.stt(out_bf, ps(pb), qkg[:, gcol:gcol + 1], sd[x], ALU.mult, ALU.mult,
                      [bank[pb], sdb[x], constb], out_bufs)
            else:
                k.stt(kn[x], ps(pb), qkg[:, gcol:gcol + 1], sd[x], ALU.mult, ALU.mult,
                      [bank[pb], sdb[x], constb], [knb_[x]])
                if "kvout" not in SKIP:
                    k.dma("sp", kout, kn[x], knp[x], [knb_[x]], [])
                k.act(out_bf, kn[x], AF.Copy, [knb_[x]], out_bufs)
        else:
            rbk = 5 + cnts["r"] % 2
            cnts["r"] += 1
            tq = slice((tb - 2) * TBS, (tb - 1) * TBS)
            k.stt(kn[x], ps(pb), qkg[:, gcol:gcol + 1], sd[x], ALU.mult, ALU.mult,
                  [bank[pb], sdb[x], constb], [knb_[x]])
            k.act(knbf, kn[x], AF.Copy, [knb_[x]], [knbfb])
            k.pe_mm(ps(rbk), Pm, knbf, True, True, [knbfb, constb], [bank[rbk]])
            k.tt(t1, kn[x], ropeC[:, tq], ALU.mult, [knb_[x], ropeb], [t1b])
            k.tt(t2, ps(rbk), ropeS[:, tq], ALU.mult, [bank[rbk], ropeb], [t2b])
            k.tt(out_bf, t1, t2, ALU.add, [t1b, t2b], out_bufs)

    def pool_segment(src3, nseg, n, gq, corr_off, tbo, outcol0):
        w = WINS[gq]
        half = w // 2
        L = n + 16
        cur = src3
        curb = None
        bufs = [(ta, tab), (tb_, tbb)]
        bi = 0
        d = 1
        Lc = L
        while d < w:
            dst, dstb = bufs[bi]
            bi ^= 1
            dv = dst[:, 0:nseg * L].rearrange("p (b t) -> p b t", b=nseg)
            Ln_ = Lc - d
            rd = [] if curb is None else [curb]
            k.tt(dv[:, :, 0:Ln_], cur[:, :, 0:Ln_], cur[:, :, d:d + Ln_], ALU.add, rd + list(tbo["src"]), [dstb])
            cur, curb, Lc = dv, dstb, Ln_
            d *= 2
        o0 = 8 - half
        dst, dstb = bufs[bi]
        dv = dst[:, 0:nseg * n].rearrange("p (b t) -> p b t", b=nseg)
        k.ts(dv, cur[:, :, o0:o0 + n], 1.0 / w, ALU.mult, [curb], [dstb])
        cl = corr[:, corr_off + gq * 16: corr_off + gq * 16 + 8].unsqueeze(1).to_broadcast([128, nseg, 8])
        cr = corr[:, corr_off + gq * 16 + 8: corr_off + gq * 16 + 16].unsqueeze(1).to_broadcast([128, nseg, 8])
        k.tt(dv[:, :, 0:8], dv[:, :, 0:8], cl, ALU.mult, [constb], [dstb])
        k.tt(dv[:, :, n - 8:n], dv[:, :, n - 8:n], cr, ALU.mult, [constb], [dstb])
        tot = nseg * n
        segs_per = 512 // n if n < 512 else 1
        for c in range(tot // 512):
            x = cnts["x"] % 2
            cnts["x"] += 1
            if n < 512:
                k.tt(plb[x].rearrange("p (b t) -> p b t", b=nseg), dv, src3[:, :, 8:8 + n], ALU.subtract,
                     [dstb] + list(tbo["src"]), [plbb[x]])
            else:
                k.tt(plb[x], dst[:, c * 512:(c + 1) * 512], src3[:, 0, 8 + c * 512: 8 + (c + 1) * 512],
                     ALU.subtract, [dstb] + list(tbo["src"]), [plbb[x]])
            k.pe_mm(ps(7), poolw[:, gq * 128:(gq + 1) * 128], plb[x], True, True, [plbb[x], constb], [bank[7]])
            k.act(pooled2[:, gq, outcol0 + c * 512: outcol0 + (c + 1) * 512], ps(7), AF.Identity,
                  [bank[7], constb], tbo["dst"][c], scale=psc[:, gq:gq + 1])

    si1 = k.take()
    s1 = slot[si1]
    vcnt = [0]
    OVERLAP_CC = os.environ.get("KNOOVERLAP") is None

    def part_kv(tb):
        sample = tb >= 2
        tsl = slice(tb * TBS, (tb + 1) * TBS)
        for hd in range(2):
            if sample:
                x = cnts["x"] % 2
                rms_head(si1, hd * 128, 1, tb, True, kst[x], [kstb[x]])
                c0 = hd * 1024 + (tb - 2) * TBS
                k.dma("sp", ag_in_ap[:, c0:c0 + TBS], kst[x], kstp[x], [kstb[x], agb], [])
            else:
                rms_head(si1, hd * 128, 1, tb, False, Kp[:, hd, tsl], [Kpb[tb]],
                         kout=dr["nk"][:, hd, tsl])
        for tt_ in range(4):
            pb = cnts["p"] % 3
            cnts["p"] += 1
            tile = (tb % 2) * 4 + tt_
            k.pe_group(ps(pb)[:, 0:256], [(h2t[:, kk, tt_ * 128:(tt_ + 1) * 128], s1[:, kk * 1024 + 256: kk * 1024 + 512])
                                          for kk in range(NK)], [slotb[si1]] + h2b, [bank[pb]])
            v = vcnt[0] % 2
            vcnt[0] += 1
            if sample:
                k.act(vsb[v], ps(pb)[:, 0:256], AF.Copy, [bank[pb]], [vsbb[v]])
                c0 = 2048 + tile * 256
                k.dma("sp", ag_in_ap[:, c0:c0 + 256], vsb[v], vsbp[v], [vsbb[v], agb], [])
            else:
                k.vcopy(vst[v], ps(pb)[:, 0:256], [bank[pb]], [vstb[v]])
                k.dma("sp", dr["nv"][:, tile, :], vst[v], vstp[v], [vstb[v]], [])
                k.act(Vp[:, tile, :], vst[v], AF.Copy, [vstb[v]], [Vpb[tb]])

    def part_pin(tb):
        sample = tb >= 2
        for gq in range(4):
            pb = cnts["p"] % 3
            cnts["p"] += 1
            k.pe_group(ps(pb), [(s1[:, kk * 1024 + 512 + gq * 128: kk * 1024 + 512 + (gq + 1) * 128], h2t[:, kk, :])
                                for kk in range(NK)], [slotb[si1]] + h2b, [bank[pb]])
            if sample:
                c0 = 8 + (tb - 2) * TBS
                k.act(pin_s[:, gq, c0:c0 + TBS], ps(pb), AF.Copy, [bank[pb]], [pinb[gq]])
            else:
                x = gq % 2
                k.act(pp[x][:, :, 8:264], ps(pb).rearrange("p (b t) -> p b t", b=2), AF.Copy, [bank[pb]], [ppb[x]])
                pool_segment(pp[x], 2, 256, gq, 64, {"src": [ppb[x]], "dst": [[p2b[tb]]]}, tb * TBS)

    def collectives():
        k.vcopy(e32[:, :, 0:8], pin_s[:, :, 8:16], pinb, [efb])
        k.vcopy(e32[:, :, 8:16], pin_s[:, :, 1024:1032], pinb, [efb])
        e32f = e32.rearrange("p g i -> p (g i)")
        k.vcopy(est[:, 0:64], e32f, [efb], [estb])
        k.tt(est[:, 64:128], e32f, est[:, 0:64], ALU.subtract, [efb], [estb])
        k.dma("sp", dr["ag2_in"][:, :], est, estp, [estb, ag2b], [])
        k.barrier()
        E = k.E["pool"]
        for (src, dst, sb, db) in ((dr["ag2_in"], dr["ag2_out"], ag2b, ag2ob), (ag_in_ap, ag_out_ap, agb, agob)):
            k._deps(E, [], [sb, db])
            ins = nc.gpsimd.collective_compute("AllGather", ALU.bypass,
                                               replica_groups=[[0, 1, 2, 3], [4, 5, 6, 7]],
                                               ins=[src.opt()], outs=[dst.opt()])
            ccprod.count += 1
            ins.then_inc(ccprod.sem, 1)
            k._commit((ccprod, ccprod.count), [], [sb, db])
        if OVERLAP_CC:
            for qn in ("sp", "pool"):
                k.E[qn].wait((ccprod, ccprod.count))
        else:
            k.barrier()

    if OVERLAP_CC:
        for tb in (2, 3):
            mod2(tb, on_dve=True)
            part_kv(tb)
            part_pin(tb)
        for tb in (0, 1):
            mod2(tb, on_dve=True)
            part_kv(tb)
        collectives()
        for tb in (0, 1):
            mod2(tb)
            part_pin(tb)
        k.barrier()
    else:
        for tb in (2, 3, 0, 1):
            mod2(tb)
            part_kv(tb)
            part_pin(tb)
            if tb == 3:
                collectives()
    if "cc" in SKIP or "cutpost" in SKIP:
        cut(2)
    k.dma("sp", eg, dr["ag2_out"][:, :].rearrange("(r p) c -> p r c", p=128), egp, [ag2ob], [egb])
    k.tt(ef, eg[:, :, 0:64], eg[:, :, 64:128], ALU.add, [egb], [efb])
    ef4 = ef.rearrange("p r (g i) -> p r g i", g=4)
    for side, (dst, lo) in enumerate(((pin_s[:, :, 0:8], 8), (pin_s[:, :, 1032:1040], 0))):
        for r in range(4):
            m = mask[:, side * 4 + r: side * 4 + r + 1]
            src = ef4[:, r, :, lo:lo + 8]
            if r == 0:
                k.ts(dst, src, m, ALU.mult, [efb, constb], pinb)
            else:
                k.stt(dst, src, m, dst, ALU.mult, ALU.add, [efb, constb], pinb)
    for gq in range(4):
        pool_segment(pin_s[:, gq:gq + 1, :], 1, 1024, gq, 0,
                     {"src": [pinb[gq]], "dst": [[p2b[2]], [p2b[3]]]}, 1024)
    k.prefetch(si1)
    k.barrier()
    cut(2)

    RH.reset(); RQ.reset()
    RX.top = xmark
    Q3 = k.bfv(k.alloc(RQ, NK * T * 2), NK * T).rearrange("p (h t) -> p h t", h=NK)
    Qb = [[Buf("Q%d_%d" % (h, i)) for i in range(NTB)] for h in range(NK)]
    B1 = RH
    h2t2 = [k.bfv(k.alloc(B1, 8192), 4096).rearrange("p (k t) -> p k t", k=NK) for _ in range(2)]
    h2b2 = [[Buf() for _ in range(NK)] for _ in range(2)]
    sq = [k.bfv(k.alloc(B1, 1024), 512) for _ in range(2)]
    sqb = [Buf(), Buf()]
    sd = [k.f32v(k.alloc(B1, 2048), 512) for _ in range(3)]
    sdb = [Buf() for _ in range(3)]
    kn = [k.f32v(k.alloc(B1, 2048), 512) for _ in range(3)]
    knb_ = [Buf() for _ in range(3)]
    knbf2 = [k.bfv(k.alloc(B1, 1024), 512) for _ in range(2)]
    knbfb2 = [Buf(), Buf()]
    ropeC = k.f32v(k.alloc(RX, 4096), 1024)
    ropeS = k.f32v(k.alloc(RX, 4096), 1024)
    t1 = k.f32v(k.alloc(RX, 2048), 512)
    t2 = k.f32v(k.alloc(RX, 2048), 512)
    t1b, t2b = Buf(), Buf()
    ropeb = Buf("rope2")
    k.dma("sp", ropeC, dr["ropeCS"][:, 0:1024], ropep, (), [ropeb])
    k.dma("sp", ropeS, dr["ropeCS"][:, 1024:2048], ropep, (), [ropeb])
    si2 = k.take()
    s2 = slot[si2]
    items = [(tb, hd) for tb in range(NTB) for hd in range(NK)]
    NI = len(items)

    def mod2c(tb, kk):
        g = tb // 2
        tsl = slice(tb * TBS, (tb + 1) * TBS)
        k.ts(h2t2[tb % 2][:, kk, :], R3[:, kk, tsl], DERk("H2S", g, kk), ALU.mult, [Rbk[tb][kk], derb],
             [h2b2[tb % 2][kk]], s2=DERk("H2B", g, kk), op1=ALU.add)

    def P1(i):
        tb, hd = items[i]
        pb, x = i % 3, i % 2
        k.pe_group(ps(pb), [(s2[:, kk * 1024 + hd * 128: kk * 1024 + (hd + 1) * 128], h2t2[tb % 2][:, kk, :])
                            for kk in range(NK)], [slotb[si2]] + h2b2[tb % 2], [bank[pb]])
        k.act(sq[x], ps(pb), AF.Square, [bank[pb]], [sqb[x]])
        if tb + 1 < NTB:
            mod2c(tb + 1, hd)

    def P2(i):
        tb, hd = items[i]
        pb, x, mb, y = i % 3, i % 2, 3 + i % 2, i % 3
        tsl = slice(tb * TBS, (tb + 1) * TBS)
        k.pe_mm(ps(mb), ones_h, sq[x], True, True, [sqb[x], onesb], [bank[mb]])
        k.act(sd[y], ps(mb), AF.Ln, [bank[mb]], [sdb[y]], bias=RMS_EPS, scale=1.0)
        k.act(sd[y], sd[y], AF.Exp, [], [sdb[y]], scale=-0.5)
        if tb < 2:
            k.stt(Q3[:, hd, tsl], ps(pb), qkg[:, 0:1], sd[y], ALU.mult, ALU.mult,
                  [bank[pb], sdb[y], constb], [Qb[hd][tb]])
        else:
            k.stt(kn[y], ps(pb), qkg[:, 0:1], sd[y], ALU.mult, ALU.mult,
                  [bank[pb], sdb[y], constb], [knb_[y]])
            k.act(knbf2[x], kn[y], AF.Copy, [knb_[y]], [knbfb2[x]])

    def P3(i):
        tb, hd = items[i]
        if tb < 2:
            return
        x, y, rbk = i % 2, i % 3, 5 + i % 2
        tsl = slice(tb * TBS, (tb + 1) * TBS)
        tq = slice((tb - 2) * TBS, (tb - 1) * TBS)
        k.pe_mm(ps(rbk), Pm, knbf2[x], True, True, [knbfb2[x], constb], [bank[rbk]])
        k.tt(t1, kn[y], ropeC[:, tq], ALU.mult, [knb_[y], ropeb], [t1b])
        k.tt(t2, ps(rbk), ropeS[:, tq], ALU.mult, [bank[rbk], ropeb], [t2b])
        k.tt(Q3[:, hd, tsl], t1, t2, ALU.add, [t1b, t2b], [Qb[hd][tb]])

    for kk in range(NK):
        mod2c(0, kk)
    for step in range(NI + 2):
        if step < NI:
            P1(step)
        if 0 <= step - 1 < NI:
            P2(step - 1)
        if 0 <= step - 2 < NI:
            P3(step - 2)
    k.prefetch(si2)
    k.barrier()
    cut(3)

    RH.reset()
    B2 = RH
    KC = k.bfv(k.alloc(B2, 1024), 512).rearrange("p (h t) -> p h t", h=2)
    VC = k.bfv(k.alloc(B2, 1024), 512).rearrange("p (j c) -> p j c", j=2)
    ctxb = Buf("ctx")
    ctxp = k.dprod("ctx")
    KA0 = k.bfv(k.alloc(B2, 8192), 4096).rearrange("p (r t) -> p r t", r=4)
    VA0 = k.bfv(k.alloc(B2, 8192), 4096).rearrange("p (j c) -> p j c", j=32)
    kab0, vab0 = Buf("KA"), Buf("VA")
    kap, vap = k.dprod("KA"), k.dprod("VA")
    SB = [0, 1, 2, 7]
    PT8 = [k.bfv(k.alloc(B2, 1024), 512) for _ in range(8)]
    PT8b = [Buf() for _ in range(8)]
    PT, PTb = PT8[:4], PT8b[:4]
    RX.top = xmark
    xs = k.f32v(k.alloc(RX, 2048), 512)
    xsb = Buf("xs")
    Esel = k.f32v(k.alloc(RX, 512), 128)
    eselb = Buf("esel")
    eselp = k.dprod("esel")
    k.dma("sp", Esel, dr["esel"][:, :], eselp, (), [eselb])
    rec = [k.f32v(k.alloc(B2, 2048), 512) for _ in range(2)]
    recb = [Buf(), Buf()]
    k.dma("pool", KC, dr["kctx"][:, :].rearrange("p (h t) -> p h t", h=2), ctxp, (), [ctxb])
    k.dma("pool", VC, dr["vctx"][:, :].rearrange("p (j c) -> p j c", j=2), ctxp, (), [ctxb])

    it = 0
    for b in range(4):
        tb = b // 2
        bsl = slice(b * 256, (b + 1) * 256)
        for pr in range(4):
            kvh = pr // 2
            ob, sb_ = 3 + it % 2, 5 + it % 2
            r = it % 2
            it += 1
            qv = Q3[:, 2 * pr:2 * pr + 2, bsl]
            qbufs = [Qb[2 * pr][tb], Qb[2 * pr + 1][tb]]
            pts = []
            for kc in range(2):
                pi_ = cnts["p"] % 4
                sbk = SB[pi_]
                cnts["p"] += 1
                k.pe_mm(ps(sbk).rearrange("p (a t) -> p a t", a=2),
                        Kp[:, kvh, b * 256 + kc * 128: b * 256 + (kc + 1) * 128], qv, True, True,
                        [Kpb[tb]] + qbufs, [bank[sbk]])
                k.act(PT[pi_], ps(sbk), AF.Exp, [bank[sbk]], [PTb[pi_]], scale=SCALE)
                pts.append(pi_)
            for kc in range(2):
                x = pts[kc]
                k.pe_mm(ps(ob), Vp[:, b * 2 + kc, kvh * 128:(kvh + 1) * 128], PT[x], kc == 0, kc == 1,
                        [Vpb[tb], PTb[x]], [bank[ob]])
                k.pe_mm(ps(sb_), ones1, PT[x], kc == 0, kc == 1, [PTb[x], onesb], [bank[sb_]])
            k.act(rec[r], ps(sb_), AF.Ln, [bank[sb_]], [recb[r]])
            k.act(rec[r], rec[r], AF.Exp, [], [recb[r]], scale=-1.0)
            k.tt(qv, ps(ob).rearrange("p (a t) -> p a t", a=2), rec[r].rearrange("p (a t) -> p a t", a=2),
                 ALU.mult, [bank[ob], recb[r]], qbufs)

    KA1 = k.bfv(k.alloc(RX, 8192), 4096).rearrange("p (r t) -> p r t", r=4)
    VA1 = k.bfv(oKp, 4096).rearrange("p (j c) -> p j c", j=32)
    kab1, vab1 = Buf("KA1"), Buf("VA1")
    kvt = [(KA0, VA0, kab0, vab0, []), (KA1, VA1, kab1, vab1, Kpb + Vpb)]
    kvp_ = [(kap, vap), (k.dprod("KA1"), k.dprod("VA1"))]
    for kvh in range(2):
        KA, VA, kab, vab, extra = kvt[kvh]
        k.dma("sp", KA, ag_out_ap[:, kvh * 1024:(kvh + 1) * 1024].rearrange("(r p) c -> p r c", p=128),
              kvp_[kvh][0], [agob], [kab])
        for r_ in range(4):
            k.dma("sp", VA[:, r_ * 8:(r_ + 1) * 8, :],
                  ag_out_ap[r_ * 128:(r_ + 1) * 128, 2048:4096].rearrange("p (j c) -> p j c", j=8)[:, :, kvh * 128:(kvh + 1) * 128],
                  kvp_[kvh][1], [agob], [vab] + extra)
    for kvh in range(2):
        KA, VA, kab, vab, extra = kvt[kvh]
        chunks = []
        for c in range(2):
            chunks.append((KC[:, kvh, c * 128:(c + 1) * 128], VC[:, c, kvh * 128:(kvh + 1) * 128], [ctxb]))
        for r_ in range(4):
            for j in range(8):
                chunks.append((KA[:, r_, j * 128:(j + 1) * 128], VA[:, r_ * 8 + j, :], [kab, vab]))
        NCH = len(chunks)
        for hd in range(4 * kvh, 4 * kvh + 4):
            for tb in (2, 3):
                tsl = slice(tb * TBS, (tb + 1) * TBS)
                ob, sb_ = 3 + it % 2, 5 + it % 2
                r = it % 2
                it += 1
                qv = Q3[:, hd, tsl]
                sbks = {}

                def qk(c):
                    pi_ = cnts["p"] % 4
                    sbk = SB[pi_]
                    cnts["p"] += 1
                    sbks[c] = c % 8
                    k.pe_mm(ps(sbk), chunks[c][0], qv, True, True, chunks[c][2] + [Qb[hd][tb]], [bank[sbk]])
                    k.act(PT8[c % 8], ps(sbk), AF.Exp, [bank[sbk]], [PT8b[c % 8]], scale=SCALE)

                def sum_mm(cc):
                    j = cc % 4
                    x = sbks[cc]
                    k.op("pe", lambda: nc.tensor.matmul(ps(sb_)[32 * j:32 * j + 32, :], lhsT=ones1[:, 0:32],
                                                        rhs=PT8[x], start=(cc == j), stop=(cc + 4 >= NCH),
                                                        tile_position=(0, 32 * j)),
                         [PT8b[x], onesb], [bank[sb_]])

                qk(0)
                qk(1)
                qk(2)
                for c in range(NCH):
                    x = sbks[c]
                    k.pe_mm(ps(ob), chunks[c][1], PT8[x], c == 0, c == NCH - 1, chunks[c][2] + [PT8b[x]], [bank[ob]])
                    if c % 4 == 3 or c == NCH - 1:
                        for cc in range((c // 4) * 4, c + 1):
                            sum_mm(cc)
                    if c + 3 < NCH:
                        qk(c + 3)
                k.vcopy(xs, ps(sb_), [bank[sb_]], [xsb])
                k.pe_mm(ps(sb_), Esel, xs, True, True, [xsb, eselb], [bank[sb_]])
                k.recip(rec[r], ps(sb_), [bank[sb_]], [recb[r]])
                k.tt(qv, ps(ob), rec[r], ALU.mult, [bank[ob], recb[r]], [Qb[hd][tb]])
                if kvh == 0 and tb == 3:
                    mi = 5 + (hd - 4 * kvh)
                    mod_compute(mi, 7)
                    mod_finish(mi, mi + 1, 7)
                    if mi == 8:
                        derive_late()
    k.barrier()
    cut(4)

    RH.reset(); RX.reset()
    oH2 = k.alloc(RH, NK * T * 2)
    assert oH2 == oH
    GX = RX
    h2tg = [k.bfv(k.alloc(GX, 8192), 4096).rearrange("p (k t) -> p k t", k=NK) for _ in range(2)]
    h2bg = [[Buf() for _ in range(NK)] for _ in range(2)]
    sg = [k.f32v(k.alloc(GX, 2048), 512) for _ in range(2)]
    sgb = [Buf(), Buf()]
    m12 = sg
    m12b = sgb
    hi_ = 0
    mod2(0, h2tg[0], h2bg[0])
    for ocp in range(4):
        si = k.take()
        s = slot[si]
        for tb in range(NTB):
            h2t, h2b = h2tg[hi_ % 2], h2bg[hi_ % 2]
            hi_ += 1
            if not (ocp == 3 and tb == NTB - 1):
                mod2((tb + 1) % NTB, h2tg[hi_ % 2], h2bg[hi_ % 2])
            tsl = slice(tb * TBS, (tb + 1) * TBS)
            for o in range(2):
                oc = 2 * ocp + o
                base = 4 * o
                k.pe_group(ps(base + 0), [(s[:, kk * 256 + o * 128: kk * 256 + (o + 1) * 128], h2t[:, kk, :])
                                          for kk in range(NK)], [slotb[si]] + h2b, [bank[base + 0]])
                k.pe_group(ps(base + 1), [(s[:, 2048 + kk * 256 + o * 128: 2048 + kk * 256 + (o + 1) * 128], h2t[:, kk, :])
                                          for kk in range(NK)], [slotb[si]] + h2b, [bank[base + 1]])
                k.pe_group(ps(base + 2), [(s[:, 4096 + kk * 256 + o * 128: 4096 + kk * 256 + (o + 1) * 128], Q3[:, kk, tsl])
                                          for kk in range(NK)], [slotb[si]] + [Qb[kk][tb] for kk in range(NK)],
                           [bank[base + 2]])
                k.pe_group(ps(base + 3), [(s[:, 6144 + kk * 256 + o * 128: 6144 + kk * 256 + (o + 1) * 128], pooled2[:, kk, tsl])
                                          for kk in range(4)], [slotb[si], p2b[tb]], [bank[base + 3]])
                k.act(sg[0], ps(base + 0), AF.Sigmoid, [bank[base + 0]], [sgb[0]])
                k.act(sg[1], ps(base + 1), AF.Sigmoid, [bank[base + 1]], [sgb[1]])
                k.tt(m12[0], sg[0], ps(base + 2), ALU.mult, [sgb[0], bank[base + 2]], [m12b[0]])
                k.tt(m12[1], sg[1], ps(base + 3), ALU.mult, [sgb[1], bank[base + 3]], [m12b[1]])
                k.tt(H3[:, oc, tsl], m12[0], m12[1], ALU.add, [m12b[0], m12b[1]], [Hb[tb]])
        k.prefetch(si)
    k.barrier()
    cut(5)

    FB2 = ffn_alloc()
    L2 = FB2["Ln"]
    si = k.take()
    s = slot[si]
    cntf = 0
    for tb in range(NTB):
        g = tb // 2
        tsl = slice(tb * TBS, (tb + 1) * TBS)
        for oc in range(NK):
            f = 4 + cntf % 2
            cntf += 1
            k.pe_group(ps(f), [(s[:, kk * 1024 + oc * 128: kk * 1024 + (oc + 1) * 128], H3[:, kk, tsl])
                               for kk in range(NK)], [slotb[si], Hb[tb]], [bank[f]])
            k.stt(R3[:, oc, tsl], ps(f), DERk("G2", g, oc), R3[:, oc, tsl], ALU.mult, ALU.add,
                  [bank[f], derb], [Rbk[tb][oc]])
            ln_stats_chunk(tb, L2, oc)
        sbk6, sbk7 = ((6, 7), (2, 3))[tb % 2]
        ln_pe_stats(tb, L2, sbk6, sbk7)
        if tb >= 1:
            pb6, pb7 = ((6, 7), (2, 3))[(tb - 1) % 2]
            ln_finish(tb - 1, L2, "RS2", "RB2", hs="H3S", hb="H3B", b6=pb6, b7=pb7)
    pb6, pb7 = ((6, 7), (2, 3))[(NTB - 1) % 2]
    ln_finish(NTB - 1, L2, "RS2", "RB2", hs="H3S", hb="H3B", b6=pb6, b7=pb7)
    k.prefetch(si)
    cut(6)

    ffn_phase("HG3", FB2, (("RS3", "RB3"), {"final": True}))

    finalize()


_PROGRAM = None


def _host_consts():
    n_freq = 32
    inv_freq = (10000.0 ** (-np.arange(n_freq, dtype=np.float32) / n_freq)).astype(np.float32)
    pm = np.zeros((128, 128), np.float32)
    for d2 in range(64):
        pm[d2 + 64, d2] = -1.0
        pm[d2, d2 + 64] = 1.0

    def corr_tab(n, left, right):
        out = np.ones((4, 16), np.float32)
        for g, w in enumerate(WINS):
            for i in range(8):
                if left:
                    t = i
                    lo = min(max(t - w // 2, 0), n); hi = min(max(t - w // 2 + w, 0), n)
                    out[g, i] = w / float(hi - lo)
                if right:
                    t = n - 8 + i
                    lo = min(max(t - w // 2, 0), n); hi = min(max(t - w // 2 + w, 0), n)
                    out[g, 8 + i] = w / float(hi - lo)
        return out.reshape(64)
    return inv_freq, pm, corr_tab


def kernel(x_prompt, x_sample, cache_k, cache_v, c, c_ctx, w_mod, b_mod, ln_g, ln_b,
           ffn1_w1, ffn1_w2, w_in, q_norm_g, k_norm_g, pool_w, pool_scale,
           w_up_attn, w_up_pool, w_out, ffn2_w1, ffn2_w2):
    global _PROGRAM
    f32 = np.float32
    A = lambda a: np.ascontiguousarray(np.asarray(a, dtype=f32))
    x_prompt, x_sample, cache_k, cache_v = A(x_prompt), A(x_sample), A(cache_k), A(cache_v)
    c, c_ctx = A(c), A(c_ctx)
    inv_freq, pm, corr_tab = _host_consts()

    def fm(v):
        return np.ascontiguousarray(v.reshape(8, 128).T)

    def w1_pm(w):
        out = np.empty((128, 2, 8 * DFF), f32)
        pieces = [(0, 2)] + [(c0, 3) for c0 in range(2, 20, 3)] + [(20, 2)]
        for half in range(2):
            wh = w[:, half * DFF:(half + 1) * DFF].reshape(8, 128, DFF)
            for (c0, n) in pieces:
                blk = wh[:, :, c0 * 128:(c0 + n) * 128].transpose(1, 0, 2)
                out[:, half, 8 * 128 * c0: 8 * 128 * (c0 + n)] = blk.reshape(128, 8 * n * 128)
        return out

    shared = {
        "w_mod": A(w_mod)[0], "ffn1_w1": w1_pm(A(ffn1_w1)[0]), "ffn1_w2": A(ffn1_w2)[0],
        "ffn2_w1": w1_pm(A(ffn2_w1)[0]), "ffn2_w2": A(ffn2_w2)[0], "w_in": A(w_in)[0],
        "w_up_attn": A(w_up_attn)[0], "w_up_pool": A(w_up_pool)[0], "w_out": A(w_out)[0],
        "bmodT": np.ascontiguousarray(A(b_mod)[0].reshape(72, 128).T),
        "lngT": np.ascontiguousarray(A(ln_g)[0].reshape(24, 128).T),
        "lnbT": np.ascontiguousarray(A(ln_b)[0].reshape(24, 128).T),
        "qkg": np.ascontiguousarray(np.stack([A(q_norm_g)[0], A(k_norm_g)[0]], axis=1)),
        "pool_w": np.ascontiguousarray(A(pool_w)[0].transpose(1, 0, 2).reshape(128, 512)),
        "pscaleT": np.ascontiguousarray(A(pool_scale)[0].reshape(4, 128).T),
        "pmat": pm,
        "esel": np.ascontiguousarray(np.broadcast_to((np.arange(128) % 32 == 0).astype(f32)[:, None], (128, 128))),
    }
    in_maps = []
    for core in range(NCORES):
        b, q = core // 4, core % 4
        xp = x_prompt[4 * core:4 * core + 4].reshape(1024, D)
        xs = x_sample[b, 1024 * q:1024 * (q + 1)]
        xall = np.concatenate([xp, xs], axis=0)
        xT = np.ascontiguousarray(xall.reshape(T, 8, 128).transpose(2, 1, 0))
        condT = np.stack([fm(c_ctx), fm(c[b])], axis=2).reshape(128, 16)
        kctx = np.ascontiguousarray(cache_k[b, 0].transpose(2, 1, 0).reshape(128, 512))
        vctx = np.ascontiguousarray(cache_v[b, 0].reshape(2, 128, 256).transpose(1, 0, 2).reshape(128, 512))
        pos = np.arange(1024 * q, 1024 * (q + 1))
        row = (pos // 64).astype(f32)
        col = (pos % 64).astype(f32)
        ang = np.concatenate([row[:, None] * inv_freq, col[:, None] * inv_freq], axis=-1).astype(f32)
        cs, sn = np.cos(ang).astype(f32).T, np.sin(ang).astype(f32).T
        ropeCS = np.concatenate([np.concatenate([cs, cs], 0), np.concatenate([sn, sn], 0)], axis=1)
        m = np.zeros(8, f32)
        if q - 1 >= 0:
            m[q - 1] = 1.0
        if q + 1 <= 3:
            m[4 + q + 1] = 1.0
        corr = np.concatenate([corr_tab(4096, q == 0, q == 3), corr_tab(256, True, True)])
        im = dict(shared)
        im.update({
            "xT": xT, "condT": np.ascontiguousarray(condT), "kctx": kctx, "vctx": vctx,
            "ropeCS": np.ascontiguousarray(ropeCS.astype(f32)),
            "masks": np.ascontiguousarray(np.broadcast_to(m, (128, 8))),
            "corr": np.ascontiguousarray(np.broadcast_to(corr.astype(f32), (128, 128))),
        })
        in_maps.append(im)

    if _PROGRAM is None:
        _st = os.environ.get("KSTOP")
        _PROGRAM = build_program(None if _st is None else int(_st))
    res = run_bass_kernel_spmd(_PROGRAM, in_maps, core_ids=list(range(NCORES)))

    y_prompt = np.empty((32, 256, D), f32)
    y_sample = np.empty((2, 4096, D), f32)
    new_k = np.empty((32, 1, 256, 2, 128), f32)
    new_v = np.empty((32, 1, 256, 2, 128), f32)
    for core in range(NCORES):
        r = res.results[core]
        b, q = core // 4, core % 4
        yT = np.asarray(r["yT"], dtype=f32)
        yall = yT.transpose(2, 1, 0).reshape(T, D)
        y_prompt[4 * core:4 * core + 4] = yall[:1024].reshape(4, 256, D)
        y_sample[b, 1024 * q:1024 * (q + 1)] = yall[1024:]
        nk_ = np.asarray(r["nk"], dtype=f32)
        new_k[4 * core:4 * core + 4, 0] = nk_.transpose(2, 1, 0).reshape(4, 256, 2, 128)
        nv_ = np.asarray(r["nv"], dtype=f32)
        new_v[4 * core:4 * core + 4, 0] = nv_.transpose(1, 0, 2).reshape(4, 256, 2, 128)
    return (y_prompt, y_sample, new_k, new_v)
```

```python
import contextlib
import os
import numpy as np
import concourse.bass as bass
import concourse.mybir as mybir
from concourse.bass_utils import run_bass_kernel_spmd

F32 = mybir.dt.float32
BF16 = mybir.dt.bfloat16
AF = mybir.ActivationFunctionType
ALU = mybir.AluOpType

D = 1024
DFF = 2816
NK = 8
T = 2048
TBS = 512
NTB = 4
ALPHA = 2.0 ** 0.25
LN_EPS = 1e-6
RMS_EPS = 1e-6
SCALE = 128.0 ** -0.5
WINS = (2, 4, 8, 16)
NCORES = 8
AGW = 4096
ARENA_WORDS = 53200
SLOT_ELEMS = 9216


class Prod:
    def __init__(self, sem, step):
        self.sem = sem
        self.step = step
        self.count = 0


class Buf:
    __slots__ = ("w", "r", "name")

    def __init__(self, name=""):
        self.w = None
        self.r = {}
        self.name = name


class Eng:
    def __init__(self, eng, prod):
        self.eng = eng
        self.prod = prod
        self.seen = {}
        self.self_sync = False

    def wait(self, t):
        prod, val = t
        if prod is self.prod and not self.self_sync:
            return
        if self.seen.get(prod, 0) >= val:
            return
        self.eng.wait_ge(prod.sem, val)
        self.seen[prod] = val


class Region:
    def __init__(self, start, end):
        self.start, self.end, self.top = start, end, start

    def reset(self):
        self.top = self.start


class K:
    def __init__(self, nc, st):
        self.nc = nc
        self.st = st
        self.E = {}
        for name, eng in (("pe", nc.tensor), ("act", nc.scalar), ("dve", nc.vector),
                          ("pool", nc.gpsimd), ("sp", nc.sync)):
            self.E[name] = Eng(eng, self.new_prod("e_" + name, 1))
            self.E[name].self_sync = name in ("act", "dve", "pool")
        self.dprods = []
        self.arena_t = st.enter_context(nc.sbuf_tensor("arena", [128, ARENA_WORDS], F32))
        self.arena = self.arena_t[:, :]
        self.ps_t = st.enter_context(nc.psum_tensor("ps", [128, 8, 512], F32))
        self.bank = [Buf("bank%d" % i) for i in range(8)]
        self.jobs = []
        self.job_next = 0
        self.job_cur = 0
        self.job_slot = {}

    def new_prod(self, name, step):
        sem = self.st.enter_context(self.nc.semaphore(name))
        return Prod(sem, step)

    def dprod(self, name):
        p = self.new_prod("d_" + name, 16)
        self.dprods.append(p)
        return p

    def _deps(self, E, reads, writes):
        for b in reads:
            if b.w is not None:
                E.wait(b.w)
        for b in writes:
            if b.w is not None:
                E.wait(b.w)
            for p, v in b.r.items():
                E.wait((p, v))

    def _commit(self, t, reads, writes):
        p, v = t
        for b in reads:
            if b.r.get(p, 0) < v:
                b.r[p] = v
        for b in writes:
            b.w = t
            b.r = {}

    def op(self, en, fn, reads=(), writes=(), pre=()):
        E = self.E[en]
        self._deps(E, reads, writes)
        for f in pre:
            f()
        ins = fn()
        E.prod.count += 1
        ins.then_inc(E.prod.sem, 1)
        t = (E.prod, E.prod.count)
        self._commit(t, reads, writes)
        return t

    def dma(self, qn, out, in_, prod, reads=(), writes=(), nodep=False):
        E = self.E[qn]
        if not nodep:
            self._deps(E, reads, writes)
        ins = E.eng.dma_start(out=out, in_=in_)
        prod.count += 16
        ins.then_inc(prod.sem, 16)
        t = (prod, prod.count)
        self._commit(t, reads, writes)
        return t

    def barrier(self):
        ticks = [(e.prod, e.prod.count) for e in self.E.values() if e.prod.count > 0]
        ticks += [(p, p.count) for p in self.dprods if p.count > 0]
        for name, e in self.E.items():
            if name == "pe":
                continue
            for t in ticks:
                e.wait(t)

    def psb(self, i):
        return self.ps_t[:, i, :]

    def f32v(self, off_bytes, n):
        o = off_bytes // 4
        return self.arena[:, o:o + n]

    def bfv(self, off_bytes, n):
        o = off_bytes // 4
        return self.arena[:, o:o + n // 2].bitcast(BF16)

    def alloc(self, reg, nbytes):
        nbytes = (nbytes + 63) // 64 * 64
        off = reg.top
        reg.top += nbytes
        assert reg.top <= reg.end, ("region overflow", reg.start, reg.end, reg.top)
        return off

    def pe_group(self, out_ap, pairs, reads, writes):
        n = len(pairs)
        nc = self.nc
        pre = [(lambda l=l, r=r, i=i: nc.tensor.matmul(out_ap, lhsT=l, rhs=r, start=(i == 0), stop=False))
               for i, (l, r) in enumerate(pairs[:-1])]
        l, r = pairs[-1]
        return self.op("pe", lambda: nc.tensor.matmul(out_ap, lhsT=l, rhs=r, start=(n == 1), stop=True),
                       reads, writes, pre=pre)

    def pe_mm(self, out_ap, l, r, start, stop, reads, writes):
        nc = self.nc
        return self.op("pe", lambda: nc.tensor.matmul(out_ap, lhsT=l, rhs=r, start=start, stop=stop),
                       reads, writes)

    def act(self, out, in_, func, reads, writes, bias=None, scale=None):
        nc = self.nc
        kw = {}
        if bias is not None:
            kw["bias"] = bias
        if scale is not None:
            kw["scale"] = scale
        return self.op("act", lambda: nc.scalar.activation(out=out, in_=in_, func=func, **kw), reads, writes)

    def tt(self, out, in0, in1, op, reads, writes):
        nc = self.nc
        return self.op("dve", lambda: nc.vector.tensor_tensor(out=out, in0=in0, in1=in1, op=op), reads, writes)

    def ts(self, out, in0, s1, op0, reads, writes, s2=None, op1=None):
        nc = self.nc
        if op1 is None:
            return self.op("dve", lambda: nc.vector.tensor_scalar(out=out, in0=in0, scalar1=s1, scalar2=None,
                                                                  op0=op0), reads, writes)
        return self.op("dve", lambda: nc.vector.tensor_scalar(out=out, in0=in0, scalar1=s1, scalar2=s2,
                                                              op0=op0, op1=op1), reads, writes)

    def stt(self, out, in0, scalar, in1, op0, op1, reads, writes):
        nc = self.nc
        return self.op("dve", lambda: nc.vector.scalar_tensor_tensor(out=out, in0=in0, scalar=scalar, in1=in1,
                                                                     op0=op0, op1=op1), reads, writes)

    def recip(self, out, in_, reads, writes):
        nc = self.nc
        return self.op("dve", lambda: nc.vector.reciprocal(out=out, in_=in_), reads, writes)

    def vcopy(self, out, in_, reads, writes):
        nc = self.nc
        return self.op("dve", lambda: nc.vector.tensor_copy(out=out, in_=in_), reads, writes)

    def memset(self, out, val, writes):
        nc = self.nc
        return self.op("dve", lambda: nc.vector.memset(out, val), (), writes)

    def add_job(self, fn):
        self.jobs.append(fn)

    def prefetch(self, si):
        if self.job_next < len(self.jobs):
            self.jobs[self.job_next](si)
            self.job_slot[self.job_next] = si
            self.job_next += 1

    def take(self):
        assert self.job_cur < self.job_next, "job not prefetched"
        si = self.job_slot[self.job_cur]
        self.job_cur += 1
        return si


class _Stop(Exception):
    pass


def build_program(stop=None):
    nc = bass.Bass("TRN2", target_bir_lowering=False)
    dr = {}

    def din(name, shape, dt=F32):
        dr[name] = nc.dram_tensor(name, list(shape), dt, kind="ExternalInput").ap()
        return dr[name]

    def dout(name, shape, dt=F32):
        dr[name] = nc.dram_tensor(name, list(shape), dt, kind="ExternalOutput").ap()
        return dr[name]

    xT = din("xT", [128, NK, T])
    condT = din("condT", [128, 16])
    w_mod = din("w_mod", [D, 9 * D])
    bmodT = din("bmodT", [128, 72])
    lngT = din("lngT", [128, 24])
    lnbT = din("lnbT", [128, 24])
    ffn1_w1 = din("ffn1_w1", [128, 2, 8 * DFF])
    ffn1_w2 = din("ffn1_w2", [DFF, D])
    ffn2_w1 = din("ffn2_w1", [128, 2, 8 * DFF])
    ffn2_w2 = din("ffn2_w2", [DFF, D])
    w_in = din("w_in", [D, 4096])
    qkg = din("qkg", [128, 2])
    pool_w = din("pool_w", [128, 512])
    pscaleT = din("pscaleT", [128, 4])
    w_up_attn = din("w_up_attn", [D, D])
    w_up_pool = din("w_up_pool", [512, D])
    w_out = din("w_out", [D, D])
    kctx = din("kctx", [128, 512])
    vctx = din("vctx", [128, 512])
    ropeCS = din("ropeCS", [128, 2048])
    pmat = din("pmat", [128, 128])
    masks = din("masks", [128, 8])
    corr = din("corr", [128, 128])
    esel = din("esel", [128, 128])
    yT = dout("yT", [128, NK, T])
    nk = dout("nk", [128, 2, 1024])
    nv = dout("nv", [128, 8, 256])
    ag_in = nc.dram_tensor("ag_in", [128, AGW], BF16)
    ag_out = nc.dram_tensor("ag_out", [512, AGW], BF16)
    ag_in_ap = ag_in.ap()
    ag_out_ap = ag_out.ap()
    dr["ag2_in"] = nc.dram_tensor("ag2_in", [128, 128], BF16).ap()
    dr["ag2_out"] = nc.dram_tensor("ag2_out", [512, 128], BF16).ap()

    with contextlib.ExitStack() as st:
        k = K(nc, st)
        k.stop = stop
        try:
            _emit(k, nc, dr, ag_in, ag_out, ag_in_ap, ag_out_ap)
        except _Stop:
            pass
    return nc


def _emit(k, nc, dr, ag_in, ag_out, ag_in_ap, ag_out_ap):
    ps = k.psb
    bank = k.bank
    SKIP = set(os.environ.get("KSKIP", "").split(","))
    G = Region(0, ARENA_WORDS * 4)
    oR = k.alloc(G, NK * T * 4)
    oSlot = [k.alloc(G, SLOT_ELEMS * 2), k.alloc(G, SLOT_ELEMS * 2)]
    oOnesLn = k.alloc(G, 256)
    oOnesH = k.alloc(G, 256)
    oOnes1 = k.alloc(G, 256)
    oPm = k.alloc(G, 256)
    oModp = k.alloc(G, 144 * 4)
    oBmod = k.alloc(G, 72 * 4)
    oLng = k.alloc(G, 24 * 4)
    oLnb = k.alloc(G, 24 * 4)
    oCond = k.alloc(G, 16 * 4)
    oScT = k.alloc(G, 16 * 2)
    oQkg = k.alloc(G, 2 * 4)
    oPsc = k.alloc(G, 4 * 4)
    oMask = k.alloc(G, 8 * 4)
    oCorr = k.alloc(G, 128 * 4)
    oPoolW = k.alloc(G, 512 * 2)
    oDer = k.alloc(G, 32 * 8 * 4)
    gend = G.top
    RH = Region(gend, gend + 32768)
    RQ = Region(RH.end, RH.end + 32768)
    RP2 = Region(RQ.end, RQ.end + 16384)
    RX = Region(RP2.end, ARENA_WORDS * 4)
    RHQ = Region(RH.start, RQ.end)
    RQX = Region(RQ.start, RX.end)
    assert RX.end - RX.start >= 20000, (RX.start, RX.end)

    R3 = k.f32v(oR, NK * T).rearrange("p (k t) -> p k t", k=NK)
    Rbk = [[Buf("R%d_%d" % (i, j)) for j in range(NK)] for i in range(NTB)]
    Rprod = [k.dprod("R%d" % i) for i in range(NTB)]
    slot = [k.bfv(o, SLOT_ELEMS) for o in oSlot]
    slotb = [Buf("slot0"), Buf("slot1")]
    slotb2 = [Buf("slot0w2"), Buf("slot1w2")]
    k.stage = None
    slotp = [k.dprod("slot0"), k.dprod("slot1")]
    slotp2 = [k.dprod("slot0w2"), k.dprod("slot1w2")]
    ones_ln = k.bfv(oOnesLn, 128)
    ones_h = k.bfv(oOnesH, 128)
    ones1 = k.bfv(oOnes1, 128)
    Pm = k.bfv(oPm, 128)
    modp = k.f32v(oModp, 144)
    modp3 = modp.rearrange("p (j g) -> p j g", g=2)
    modp4 = modp.rearrange("p (i k g) -> p i k g", i=9, k=8)
    bmod = k.f32v(oBmod, 72)
    lng = k.f32v(oLng, 24)
    lnb = k.f32v(oLnb, 24)
    cond = k.f32v(oCond, 16)
    scT = k.bfv(oScT, 16)
    qkg = k.f32v(oQkg, 2)
    psc = k.f32v(oPsc, 4)
    mask = k.f32v(oMask, 8)
    corr = k.f32v(oCorr, 128)
    poolw = k.bfv(oPoolW, 512)
    der = k.f32v(oDer, 256)
    constb = Buf("const")
    cprod = k.dprod("const")
    cprod2 = k.dprod("const2")
    derb = Buf("der")

    dcol = {}

    def DER(name, g=None):
        key = (name, g)
        if key not in dcol:
            dcol[key] = len(dcol)
            assert len(dcol) <= 32
        c = dcol[key] * 8
        return der[:, c:c + 8]

    def DERk(name, g, kk):
        c = dcol[(name, g)] * 8 + kk
        return der[:, c:c + 1]

    def finalize():
        sp = k.E["sp"]
        for p in k.dprods:
            if p.count > 0:
                sp.wait((p, p.count))
        for e in k.E.values():
            if e.prod.count > 0:
                sp.wait((e.prod, e.prod.count))

    dbgp = k.dprod("dbg")

    def cut(stage):
        if k.stop is not None and k.stop == stage:
            k.barrier()
            for tb in range(NTB):
                k.dma("sp", dr["yT"][:, :, tb * TBS:(tb + 1) * TBS], R3[:, :, tb * TBS:(tb + 1) * TBS], dbgp, Rbk[tb], [])
            finalize()
            raise _Stop()

    def job_wmod(i):
        def f(si):
            k.dma("pool", slot[si][:, 0:8192].rearrange("p (k c) -> p k c", k=8),
                  dr["w_mod"][:, i * 1024:(i + 1) * 1024].rearrange("(k p) c -> p k c", p=128),
                  slotp[si], (), [slotb[si], slotb2[si]])
        return f

    PIECES = [(0, 2)] + [(c0, 3) for c0 in range(2, 20, 3)] + [(20, 2)]

    def job_ffn(w1, w2, c0, n):
        def f(si):
            s = slot[si]
            o0 = 8 * 128 * c0
            k.dma("pool", s[:, 0:3072].rearrange("p (k c) -> p k c", k=8)[:, :, 0:n * 128],
                  w1[:, 0, o0:o0 + 8 * n * 128].rearrange("p (k c) -> p k c", k=8),
                  slotp[si], (), [slotb[si]])
            k.dma("pool", s[:, 3072:6144].rearrange("p (k c) -> p k c", k=8)[:, :, 0:n * 128],
                  w1[:, 1, o0:o0 + 8 * n * 128].rearrange("p (k c) -> p k c", k=8),
                  slotp[si], (), [slotb[si]], nodep=True)
            if k.stage is None:
                k.dma("pool", s[:, 6144:6144 + n * 1024].rearrange("p (j c) -> p j c", j=n),
                      w2[c0 * 128:(c0 + n) * 128, :].rearrange("(j p) c -> p j c", p=128),
                      slotp2[si], (), [slotb2[si]])
            else:
                stg, stgb, stgp = k.stage[si]
                k.dma("sp", stg[:, 0:n * 1024].rearrange("p (j c) -> p j c", j=n),
                      w2[c0 * 128:(c0 + n) * 128, :].rearrange("(j p) c -> p j c", p=128),
                      stgp, (), [stgb])
                for j in range(n):
                    k.act(s[:, 6144 + j * 1024:6144 + (j + 1) * 1024], stg[:, j * 1024:(j + 1) * 1024], AF.Copy,
                          [stgb], [slotb2[si]])
        return f

    def job_win(col0):
        def f(si):
            k.dma("pool", slot[si][:, 0:8192].rearrange("p (k c) -> p k c", k=8),
                  dr["w_in"][:, col0:col0 + 1024].rearrange("(k p) c -> p k c", p=128),
                  slotp[si], (), [slotb[si], slotb2[si]])
        return f

    def job_g1(ocp):
        def f(si):
            s = slot[si]
            c0 = ocp * 256
            k.dma("pool", s[:, 0:2048].rearrange("p (k c) -> p k c", k=8),
                  dr["w_in"][:, 2048 + c0:2048 + c0 + 256].rearrange("(k p) c -> p k c", p=128),
                  slotp[si], (), [slotb[si], slotb2[si]])
            k.dma("pool", s[:, 2048:4096].rearrange("p (k c) -> p k c", k=8),
                  dr["w_in"][:, 3072 + c0:3072 + c0 + 256].rearrange("(k p) c -> p k c", p=128),
                  slotp[si], (), [slotb[si]], nodep=True)
            k.dma("pool", s[:, 4096:6144].rearrange("p (k c) -> p k c", k=8),
                  dr["w_up_attn"][:, c0:c0 + 256].rearrange("(k p) c -> p k c", p=128),
                  slotp[si], (), [slotb[si]], nodep=True)
            k.dma("pool", s[:, 6144:7168].rearrange("p (k c) -> p k c", k=4),
                  dr["w_up_pool"][:, c0:c0 + 256].rearrange("(k p) c -> p k c", p=128),
                  slotp[si], (), [slotb[si]], nodep=True)
        return f

    def job_wout():
        def f(si):
            k.dma("pool", slot[si][:, 0:8192].rearrange("p (k c) -> p k c", k=8),
                  dr["w_out"][:, :].rearrange("(k p) c -> p k c", p=128),
                  slotp[si], (), [slotb[si], slotb2[si]])
        return f

    for i in range(2):
        k.add_job(job_wmod(i))
    for pi, (c0, n) in enumerate(PIECES):
        k.add_job(job_ffn(dr["ffn1_w1"], dr["ffn1_w2"], c0, n))
        if pi == 0:
            k.add_job(job_wmod(2))
        if pi == 3:
            k.add_job(job_wmod(3))
        if pi == 5:
            k.add_job(job_wmod(4))
    k.add_job(job_win(1024))
    k.add_job(job_win(0))
    for i in range(5, 9):
        k.add_job(job_wmod(i))
    for ocp in range(4):
        k.add_job(job_g1(ocp))
    k.add_job(job_wout())
    for (c0, n) in PIECES:
        k.add_job(job_ffn(dr["ffn2_w1"], dr["ffn2_w2"], c0, n))

    for (dst, src) in ((cond, "condT"), (bmod, "bmodT"), (lng, "lngT"), (lnb, "lnbT"), (qkg, "qkg"),
                       (psc, "pscaleT"), (mask, "masks"), (corr, "corr")):
        k.dma("sp", dst, dr[src][:, :], cprod, (), [constb])
    k.dma("pool", Pm, dr["pmat"][:, :], cprod2, (), [constb])
    k.dma("pool", poolw, dr["pool_w"][:, :], cprod2, (), [constb])
    k.prefetch(0)
    k.prefetch(1)
    for tb in range(NTB):
        k.dma("sp", R3[:, :, tb * TBS:(tb + 1) * TBS], dr["xT"][:, :, tb * TBS:(tb + 1) * TBS],
              Rprod[tb], (), Rbk[tb])
    onesb = Buf("ones")
    k.memset(ones_ln, 1.0 / 1024.0, [onesb])
    k.memset(ones_h, 1.0 / 128.0, [onesb])
    k.memset(ones1, 1.0, [onesb])
    scb = Buf("scT")
    k.act(scT, cond, AF.Silu, [constb], [scb])
    def mod_compute(i, bk):
        si = k.take()
        for j in range(8):
            col = (i * 8 + j) * 2
            for kk in range(8):
                nc_l = slot[si][:, kk * 1024 + j * 128: kk * 1024 + (j + 1) * 128]
                if kk == 0 and j == 0:
                    k._deps(k.E["pe"], [slotb[si], scb], [bank[bk]])
                ins = nc.tensor.matmul(ps(bk)[:, col:col + 2], lhsT=nc_l, rhs=scT[:, kk * 2:(kk + 1) * 2],
                                       start=(kk == 0), stop=(kk == 7))
        E = k.E["pe"]
        E.prod.count += 1
        ins.then_inc(E.prod.sem, 1)
        tk = (E.prod, E.prod.count)
        k._commit(tk, [slotb[si], scb], [bank[bk]])
        k.prefetch(si)

    def mod_finish(i0, i1, bk):
        psm3 = ps(bk)[:, 0:144].rearrange("p (j g) -> p j g", g=2)
        for g in range(2):
            k.tt(modp3[:, i0 * 8:i1 * 8, g], psm3[:, i0 * 8:i1 * 8, g], bmod[:, i0 * 8:i1 * 8], ALU.add,
                 [bank[bk], constb], [derb])

    for i in range(2):
        mod_compute(i, 0)
    mod_finish(0, 2, 0)

    def M(i, g):
        return modp4[:, i, :, g]

    for g in range(2):
        k.ts(DER("A1", g), M(1, g), 1.0, ALU.add, [], [derb])
        k.vcopy(DER("SH1", g), M(0, g), [], [derb])
        DER("HG1", g)
    k.ts(DER("RS1"), lng[:, 0:8], ALPHA, ALU.mult, [constb], [derb])
    k.ts(DER("RB1"), lnb[:, 0:8], ALPHA, ALU.mult, [constb], [derb])
    k.ts(DER("RS2"), lng[:, 8:16], ALPHA, ALU.mult, [constb], [derb])
    k.ts(DER("RB2"), lnb[:, 8:16], ALPHA, ALU.mult, [constb], [derb])
    k.vcopy(DER("RS3"), lng[:, 16:24], [constb], [derb])
    k.vcopy(DER("RB3"), lnb[:, 16:24], [constb], [derb])
    for g in range(2):
        for nm in ("H2S", "H2B", "G2", "T3", "H3S", "H3B", "HG3"):
            DER(nm, g)

    def derive_mix():
      for g in range(2):
        k.ts(DER("H2S", g), M(4, g), 1.0, ALU.add, [], [derb], s2=1.0 / ALPHA, op1=ALU.mult)
        k.vcopy(DER("H2B", g), M(3, g), [], [derb])

    def derive_late():
      for g in range(2):
        k.vcopy(DER("G2", g), M(5, g), [], [derb])
        k.ts(DER("T3", g), M(7, g), 1.0, ALU.add, [], [derb])
        k.tt(DER("H3S", g), DER("T3", g), lng[:, 8:16], ALU.mult, [constb], [derb])
        k.tt(DER("H3B", g), DER("T3", g), lnb[:, 8:16], ALU.mult, [constb], [derb])
        k.tt(DER("H3B", g), DER("H3B", g), M(6, g), ALU.add, [], [derb])
        k.ts(DER("HG3", g), M(8, g), 0.5, ALU.mult, [], [derb])

    oH = k.alloc(RH, NK * T * 2)
    H3 = k.bfv(oH, NK * T).rearrange("p (k t) -> p k t", k=NK)
    Hb = [Buf("H%d" % i) for i in range(NTB)]
    yprod = [k.dprod("y%d" % i) for i in range(NTB)]

    def ln_alloc(reg):
        o = {}
        o["yb"] = [k.bfv(k.alloc(reg, 1024), 512) for _ in range(NK)]
        o["ysq"] = [k.bfv(k.alloc(reg, 1024), 512) for _ in range(NK)]
        o["ybb"] = [Buf() for _ in range(NK)]
        o["ysqb"] = [Buf() for _ in range(NK)]
        o["mean"] = [k.f32v(k.alloc(reg, 2048), 512) for _ in range(2)]
        o["tmp"] = k.f32v(k.alloc(reg, 2048), 512)
        o["rstd"] = [k.f32v(k.alloc(reg, 2048), 512) for _ in range(2)]
        o["meanb"], o["tmpb"], o["rstdb"] = [Buf(), Buf()], Buf(), [Buf(), Buf()]
        return o

    def ln_stats_chunk(tb, L, kk):
        tsl = slice(tb * TBS, (tb + 1) * TBS)
        k.act(L["yb"][kk], R3[:, kk, tsl], AF.Copy, [Rbk[tb][kk]], [L["ybb"][kk]])
        k.act(L["ysq"][kk], R3[:, kk, tsl], AF.Square, [Rbk[tb][kk]], [L["ysqb"][kk]])

    def ln_pe_stats(tb, L, b6=6, b7=7):
        for kk in range(NK):
            k.pe_mm(ps(b6), ones_ln, L["yb"][kk], kk == 0, kk == NK - 1, [L["ybb"][kk], onesb], [bank[b6]])
        for kk in range(NK):
            k.pe_mm(ps(b7), ones_ln, L["ysq"][kk], kk == 0, kk == NK - 1, [L["ysqb"][kk], onesb], [bank[b7]])

    def ln_fin_a(tb, L, b6=6, b7=7):
        par = tb % 2
        k.vcopy(L["mean"][par], ps(b6), [bank[b6]], [L["meanb"][par]])
        k.tt(L["tmp"], L["mean"][par], L["mean"][par], ALU.mult, [L["meanb"][par]], [L["tmpb"]])
        k.tt(L["tmp"], ps(b7), L["tmp"], ALU.subtract, [bank[b7]], [L["tmpb"]])
        k.act(L["rstd"][par], L["tmp"], AF.Ln, [L["tmpb"]], [L["rstdb"][par]], bias=LN_EPS, scale=1.0)
        k.act(L["rstd"][par], L["rstd"][par], AF.Exp, [], [L["rstdb"][par]], scale=-0.5)

    def ln_fin_b1(tb, L):
        par = tb % 2
        tsl = slice(tb * TBS, (tb + 1) * TBS)
        Rt = R3[:, :, tsl]
        k.tt(Rt, Rt, L["mean"][par].unsqueeze(1).to_broadcast([128, NK, TBS]), ALU.subtract,
             [L["meanb"][par]], Rbk[tb])

    def ln_fin_b2(tb, L, rs, rb, hs=None, hb=None, final=False):
        par = tb % 2
        g = tb // 2
        tsl = slice(tb * TBS, (tb + 1) * TBS)
        Rt = R3[:, :, tsl]
        k.tt(Rt, Rt, L["rstd"][par].unsqueeze(1).to_broadcast([128, NK, TBS]), ALU.mult,
             [L["rstdb"][par]], Rbk[tb])
        for kk in range(NK):
            if hs is not None:
                k.ts(H3[:, kk, tsl], R3[:, kk, tsl], DERk(hs, g, kk), ALU.mult, [Rbk[tb][kk], derb], [Hb[tb]],
                     s2=DERk(hb, g, kk), op1=ALU.add)
            if hs is None and kk % 2 == 1:
                k.ts(R3[:, kk, tsl], R3[:, kk, tsl], DERk(rs, None, kk), ALU.mult, [derb], [Rbk[tb][kk]],
                     s2=DERk(rb, None, kk), op1=ALU.add)
            else:
                k.act(R3[:, kk, tsl], R3[:, kk, tsl], AF.Identity, [derb], [Rbk[tb][kk]],
                      bias=DERk(rb, None, kk), scale=DERk(rs, None, kk))
            if final:
                k.dma("sp", dr["yT"][:, kk, tsl], R3[:, kk, tsl], yprod[tb], [Rbk[tb][kk]], [])

    def ln_finish(tb, L, rs, rb, hs=None, hb=None, final=False, b6=6, b7=7):
        ln_fin_a(tb, L, b6, b7)
        ln_fin_b1(tb, L)
        ln_fin_b2(tb, L, rs, rb, hs=hs, hb=hb, final=final)

    stgprods = [yprod[0], yprod[1]]

    def ffn_alloc():
        reg = RQX
        reg.reset()
        o = {}
        o["gbuf"] = [[k.bfv(k.alloc(reg, 1024), 512) for _ in range(3)] for _ in range(2)]
        o["gb"] = [[Buf() for _ in range(3)] for _ in range(2)]
        o["sa"] = [k.f32v(k.alloc(reg, 2048), 512) for _ in range(2)]
        o["sab"] = [Buf(), Buf()]
        o["Ln"] = ln_alloc(reg)
        o["stage"] = [(k.f32v(k.alloc(reg, 12288), 3072), Buf("stg%d" % i), stgprods[i]) for i in range(2)]
        return o

    def ffn_phase(hgname, L, ln_args, between=None, pre_stt=None):
        fb = ffn_alloc() if L is None else L
        gbuf, gb, sa, sab, Ln = fb["gbuf"], fb["gb"], fb["sa"], fb["sab"], fb["Ln"]
        k.stage = fb["stage"]
        cnt = 0
        cntf = 0
        gi = 0
        for pi, (c0, n) in enumerate(PIECES):
            si = k.take()
            s = slot[si]
            last = (pi == len(PIECES) - 1)
            for tb in range(NTB):
                g = tb // 2
                tsl = slice(tb * TBS, (tb + 1) * TBS)
                gs = gi % 2
                gi += 1
                for j in range(n):
                    a, b2 = cnt % 2, 2 + cnt % 2
                    cnt += 1
                    k.pe_group(ps(a), [(s[:, kk * 384 + j * 128: kk * 384 + (j + 1) * 128], H3[:, kk, tsl])
                                       for kk in range(NK)], [slotb[si], Hb[tb]], [bank[a]])
                    k.pe_group(ps(b2), [(s[:, 3072 + kk * 384 + j * 128: 3072 + kk * 384 + (j + 1) * 128],
                                         H3[:, kk, tsl]) for kk in range(NK)], [slotb[si], Hb[tb]], [bank[b2]])
                    k.act(sa[a], ps(a), AF.Silu, [bank[a]], [sab[a]])
                    k.tt(gbuf[gs][j], sa[a], ps(b2), ALU.mult, [sab[a], bank[b2]], [gb[gs][j]])
                if last and tb >= 1:
                    ln_fin_b1(tb - 1, Ln)
                if pre_stt is not None and pi == 0 and tb == 0:
                    pre_stt()
                for oc in range(NK):
                    f = 4 + cntf % 2
                    cntf += 1
                    k.pe_group(ps(f), [(s[:, 6144 + j * 1024 + oc * 128: 6144 + j * 1024 + (oc + 1) * 128],
                                        gbuf[gs][j]) for j in range(n)],
                               [slotb2[si]] + [gb[gs][j] for j in range(n)], [bank[f]])
                    k.stt(R3[:, oc, tsl], ps(f), DERk(hgname, g, oc), R3[:, oc, tsl], ALU.mult, ALU.add,
                          [bank[f], derb], [Rbk[tb][oc]])
                    if last:
                        ln_stats_chunk(tb, Ln, oc)
                if last:
                    if tb >= 1:
                        ln_fin_b2(tb - 1, Ln, *ln_args[0], **ln_args[1])
                    ln_pe_stats(tb, Ln)
                    ln_fin_a(tb, Ln)
                    if tb == NTB - 1:
                        ln_fin_b1(tb, Ln)
                        ln_fin_b2(tb, Ln, *ln_args[0], **ln_args[1])
            if pi + 2 >= len(PIECES):
                k.stage = None
            k.prefetch(si)
            if between is not None:
                between(pi)

    for tb in range(NTB):
        g = tb // 2
        tsl = slice(tb * TBS, (tb + 1) * TBS)
        for kk in range(NK):
            if kk % 2 == 1:
                k.ts(H3[:, kk, tsl], R3[:, kk, tsl], DERk("A1", g, kk), ALU.mult, [Rbk[tb][kk], derb], [Hb[tb]],
                     s2=DERk("SH1", g, kk), op1=ALU.add)
            else:
                k.act(H3[:, kk, tsl], R3[:, kk, tsl], AF.Identity, [Rbk[tb][kk], derb], [Hb[tb]],
                      bias=DERk("SH1", g, kk), scale=DERk("A1", g, kk))
        k.ts(R3[:, :, tsl], R3[:, :, tsl], ALPHA, ALU.mult, [], Rbk[tb])

    cut(0)
    def ffn1_between(pi):
        if pi == 3:
            mod_compute(3, 6)
        if pi == 5:
            mod_compute(4, 6)
            mod_finish(3, 5, 6)
            derive_mix()

    def ffn1_pre_stt():
        mod_compute(2, 6)
        mod_finish(2, 3, 6)
        for g in range(2):
            k.ts(DER("HG1", g), M(2, g), 0.5, ALU.mult, [], [derb])

    ffn_phase("HG1", None, (("RS1", "RB1"), {}), between=ffn1_between, pre_stt=ffn1_pre_stt)
    k.barrier()
    cut(1)

    RH.reset(); RQ.reset(); RP2.reset(); RX.reset(); RHQ.reset()
    pooled2 = k.bfv(k.alloc(RP2, 4 * T * 2), 4 * T).rearrange("p (g t) -> p g t", g=4)
    p2b = [Buf("p2_%d" % i) for i in range(NTB)]
    oKp = k.alloc(RX, 2 * 1024 * 2)
    oVp = k.alloc(RX, 8 * 256 * 2)
    Kp = k.bfv(oKp, 2048).rearrange("p (h t) -> p h t", h=2)
    Vp = k.bfv(oVp, 2048).rearrange("p (j c) -> p j c", j=8)
    Kpb = [Buf(), Buf()]
    Vpb = [Buf(), Buf()]
    xmark = RX.top

    A = RHQ
    h2t = k.bfv(k.alloc(A, 8192), 4096).rearrange("p (k t) -> p k t", k=NK)
    h2b = [Buf("h2t%d" % i) for i in range(NK)]
    sq = [k.bfv(k.alloc(A, 1024), 512) for _ in range(2)]
    sqb = [Buf(), Buf()]
    sd = [k.f32v(k.alloc(A, 2048), 512) for _ in range(2)]
    sdb = [Buf(), Buf()]
    kn = [k.f32v(k.alloc(A, 2048), 512) for _ in range(2)]
    knb_ = [Buf(), Buf()]
    knbf = k.bfv(k.alloc(A, 1024), 512)
    knbfb = Buf()
    t1 = k.f32v(k.alloc(A, 2048), 512)
    t2 = k.f32v(k.alloc(A, 2048), 512)
    t1b, t2b = Buf(), Buf()
    kst = [k.bfv(k.alloc(A, 1024), 512) for _ in range(2)]
    kstb = [Buf(), Buf()]
    kstp = [k.dprod("kst0"), k.dprod("kst1")]
    knp = [k.dprod("kn0"), k.dprod("kn1")]
    vst = [k.f32v(k.alloc(A, 1024), 256) for _ in range(2)]
    vstb = [Buf(), Buf()]
    vstp = [k.dprod("vst0"), k.dprod("vst1")]
    vsb = [k.bfv(k.alloc(A, 512), 256) for _ in range(2)]
    vsbb = [Buf(), Buf()]
    vsbp = [k.dprod("vsb0"), k.dprod("vsb1")]
    ropeC = k.f32v(k.alloc(RX, 4096), 1024)
    ropeS = k.f32v(k.alloc(RX, 4096), 1024)
    ropeb = Buf("rope")
    ropep = k.dprod("rope")
    pp = [k.f32v(k.alloc(A, 2176), 544).rearrange("p (b t) -> p b t", b=2) for _ in range(2)]
    ppb = [Buf(), Buf()]
    pin_s = k.f32v(k.alloc(A, 4 * 1040 * 4), 4 * 1040).rearrange("p (g t) -> p g t", g=4)
    pinb = [Buf("pin_s%d" % i) for i in range(4)]
    ta = k.f32v(k.alloc(A, 4160), 1040)
    tb_ = k.f32v(k.alloc(A, 4160), 1040)
    tab, tbb = Buf(), Buf()
    plb = [k.bfv(k.alloc(A, 1024), 512) for _ in range(2)]
    plbb = [Buf(), Buf()]
    e32 = k.f32v(k.alloc(A, 256), 64).rearrange("p (g i) -> p g i", g=4)
    est = k.bfv(k.alloc(A, 256), 128)
    estb = Buf()
    estp = k.dprod("est")
    eg = k.bfv(k.alloc(A, 1024), 512).rearrange("p (r c) -> p r c", r=4)
    egb = Buf()
    egp = k.dprod("eg")
    ef = k.f32v(k.alloc(A, 1024), 256).rearrange("p (r c) -> p r c", r=4)
    efb = Buf()
    agb = Buf("ag_in")
    agob = Buf("ag_out")
    ag2b = Buf("ag2_in")
    ag2ob = Buf("ag2_out")
    ccprod = k.new_prod("cc", 1)
    k.dprods.append(ccprod)

    k.dma("sp", ropeC, dr["ropeCS"][:, 0:1024], ropep, (), [ropeb])
    k.dma("sp", ropeS, dr["ropeCS"][:, 1024:2048], ropep, (), [ropeb])
    for i in range(2):
        k.memset(pp[i], 0.0, [ppb[i]])
    k.memset(pin_s, 0.0, pinb)

    def mod2(tb, h2t_=None, h2b_=None, on_dve=False):
        g = tb // 2
        tsl = slice(tb * TBS, (tb + 1) * TBS)
        h2t_ = h2t if h2t_ is None else h2t_
        h2b_ = h2b if h2b_ is None else h2b_
        for kk in range(NK):
            if on_dve:
                k.ts(h2t_[:, kk, :], R3[:, kk, tsl], DERk("H2S", g, kk), ALU.mult, [Rbk[tb][kk], derb],
                     [h2b_[kk]], s2=DERk("H2B", g, kk), op1=ALU.add)
                continue
            k.act(h2t_[:, kk, :], R3[:, kk, tsl], AF.Identity, [Rbk[tb][kk], derb], [h2b_[kk]],
                  bias=DERk("H2B", g, kk), scale=DERk("H2S", g, kk))

    cnts = {"p": 0, "m": 0, "r": 0, "x": 0}

    def rms_head(si, col0, gcol, tb, sample, out_bf, out_bufs, kout=None):
        s = slot[si]
        pb = cnts["p"] % 3
        cnts["p"] += 1
        mb = 3 + cnts["m"] % 2
        cnts["m"] += 1
        x = cnts["x"] % 2
        cnts["x"] += 1
        k.pe_group(ps(pb), [(s[:, kk * 1024 + col0: kk * 1024 + col0 + 128], h2t[:, kk, :]) for kk in range(NK)],
                   [slotb[si]] + h2b, [bank[pb]])
        k.act(sq[x], ps(pb), AF.Square, [bank[pb]], [sqb[x]])
        k.pe_mm(ps(mb), ones_h, sq[x], True, True, [sqb[x], onesb], [bank[mb]])
        k.act(sd[x], ps(mb), AF.Ln, [bank[mb]], [sdb[x]], bias=RMS_EPS, scale=1.0)
        k.act(sd[x], sd[x], AF.Exp, [], [sdb[x]], scale=-0.5)
        if not sample:
            if kout is None:
                k.stt(out_bf, ps(pb), qkg[:, gcol:gcol + 1], sd[x], ALU.mult, ALU.mult,
                      [bank[pb], sdb[x], constb], out_bufs)
            else:
                k.stt(kn[x], ps(pb), qkg[:, gcol:gcol + 1], sd[x], ALU.mult, ALU.mult,
                      [bank[pb], sdb[x], constb], [knb_[x]])
                if "kvout" not in SKIP:
                    k.dma("sp", kout, kn[x], knp[x], [knb_[x]], [])
                k.act(out_bf, kn[x], AF.Copy, [knb_[x]], out_bufs)
        else:
            rbk = 5 + cnts["r"] % 2
            cnts["r"] += 1
            tq = slice((tb - 2) * TBS, (tb - 1) * TBS)
            k.stt(kn[x], ps(pb), qkg[:, gcol:gcol + 1], sd[x], ALU.mult, ALU.mult,
                  [bank[pb], sdb[x], constb], [knb_[x]])
            k.act(knbf, kn[x], AF.Copy, [knb_[x]], [knbfb])
            k.pe_mm(ps(rbk), Pm, knbf, True, True, [knbfb, constb], [bank[rbk]])
            k.tt(t1, kn[x], ropeC[:, tq], ALU.mult, [knb_[x], ropeb], [t1b])
            k.tt(t2, ps(rbk), ropeS[:, tq], ALU.mult, [bank[rbk], ropeb], [t2b])
            k.tt(out_bf, t1, t2, ALU.add, [t1b, t2b], out_bufs)

    def pool_segment(src3, nseg, n, gq, corr_off, tbo, outcol0):
        w = WINS[gq]
        half = w // 2
        L = n + 16
        cur = src3
        curb = None
        bufs = [(ta, tab), (tb_, tbb)]
        bi = 0
        d = 1
        Lc = L
        while d < w:
            dst, dstb = bufs[bi]
            bi ^= 1
            dv = dst[:, 0:nseg * L].rearrange("p (b t) -> p b t", b=nseg)
            Ln_ = Lc - d
            rd = [] if curb is None else [curb]
            k.tt(dv[:, :, 0:Ln_], cur[:, :, 0:Ln_], cur[:, :, d:d + Ln_], ALU.add, rd + list(tbo["src"]), [dstb])
            cur, curb, Lc = dv, dstb, Ln_
            d *= 2
        o0 = 8 - half
        dst, dstb = bufs[bi]
        dv = dst[:, 0:nseg * n].rearrange("p (b t) -> p b t", b=nseg)
        k.ts(dv, cur[:, :, o0:o0 + n], 1.0 / w, ALU.mult, [curb], [dstb])
        cl = corr[:, corr_off + gq * 16: corr_off + gq * 16 + 8].unsqueeze(1).to_broadcast([128, nseg, 8])
        cr = corr[:, corr_off + gq * 16 + 8: corr_off + gq * 16 + 16].unsqueeze(1).to_broadcast([128, nseg, 8])
        k.tt(dv[:, :, 0:8], dv[:, :, 0:8], cl, ALU.mult, [constb], [dstb])
        k.tt(dv[:, :, n - 8:n], dv[:, :, n - 8:n], cr, ALU.mult, [constb], [dstb])
        tot = nseg * n
        segs_per = 512 // n if n < 512 else 1
        for c in range(tot // 512):
            x = cnts["x"] % 2
            cnts["x"] += 1
            if n < 512:
                k.tt(plb[x].rearrange("p (b t) -> p b t", b=nseg), dv, src3[:, :, 8:8 + n], ALU.subtract,
                     [dstb] + list(tbo["src"]), [plbb[x]])
            else:
                k.tt(plb[x], dst[:, c * 512:(c + 1) * 512], src3[:, 0, 8 + c * 512: 8 + (c + 1) * 512],
                     ALU.subtract, [dstb] + list(tbo["src"]), [plbb[x]])
            k.pe_mm(ps(7), poolw[:, gq * 128:(gq + 1) * 128], plb[x], True, True, [plbb[x], constb], [bank[7]])
            k.act(pooled2[:, gq, outcol0 + c * 512: outcol0 + (c + 1) * 512], ps(7), AF.Identity,
                  [bank[7], constb], tbo["dst"][c], scale=psc[:, gq:gq + 1])

    si1 = k.take()
    s1 = slot[si1]
    vcnt = [0]
    OVERLAP_CC = os.environ.get("KNOOVERLAP") is None

    def part_kv(tb):
        sample = tb >= 2
        tsl = slice(tb * TBS, (tb + 1) * TBS)
        for hd in range(2):
            if sample:
                x = cnts["x"] % 2
                rms_head(si1, hd * 128, 1, tb, True, kst[x], [kstb[x]])
                c0 = hd * 1024 + (tb - 2) * TBS
                k.dma("sp", ag_in_ap[:, c0:c0 + TBS], kst[x], kstp[x], [kstb[x], agb], [])
            else:
                rms_head(si1, hd * 128, 1, tb, False, Kp[:, hd, tsl], [Kpb[tb]],
                         kout=dr["nk"][:, hd, tsl])
        for tt_ in range(4):
            pb = cnts["p"] % 3
            cnts["p"] += 1
            tile = (tb % 2) * 4 + tt_
            k.pe_group(ps(pb)[:, 0:256], [(h2t[:, kk, tt_ * 128:(tt_ + 1) * 128], s1[:, kk * 1024 + 256: kk * 1024 + 512])
                                          for kk in range(NK)], [slotb[si1]] + h2b, [bank[pb]])
            v = vcnt[0] % 2
            vcnt[0] += 1
            if sample:
                k.act(vsb[v], ps(pb)[:, 0:256], AF.Copy, [bank[pb]], [vsbb[v]])
                c0 = 2048 + tile * 256
                k.dma("sp", ag_in_ap[:, c0:c0 + 256], vsb[v], vsbp[v], [vsbb[v], agb], [])
            else:
                k.vcopy(vst[v], ps(pb)[:, 0:256], [bank[pb]], [vstb[v]])
                k.dma("sp", dr["nv"][:, tile, :], vst[v], vstp[v], [vstb[v]], [])
                k.act(Vp[:, tile, :], vst[v], AF.Copy, [vstb[v]], [Vpb[tb]])

    def part_pin(tb):
        sample = tb >= 2
        for gq in range(4):
            pb = cnts["p"] % 3
            cnts["p"] += 1
            k.pe_group(ps(pb), [(s1[:, kk * 1024 + 512 + gq * 128: kk * 1024 + 512 + (gq + 1) * 128], h2t[:, kk, :])
                                for kk in range(NK)], [slotb[si1]] + h2b, [bank[pb]])
            if sample:
                c0 = 8 + (tb - 2) * TBS
                k.act(pin_s[:, gq, c0:c0 + TBS], ps(pb), AF.Copy, [bank[pb]], [pinb[gq]])
            else:
                x = gq % 2
                k.act(pp[x][:, :, 8:264], ps(pb).rearrange("p (b t) -> p b t", b=2), AF.Copy, [bank[pb]], [ppb[x]])
                pool_segment(pp[x], 2, 256, gq, 64, {"src": [ppb[x]], "dst": [[p2b[tb]]]}, tb * TBS)

    def collectives():
        k.vcopy(e32[:, :, 0:8], pin_s[:, :, 8:16], pinb, [efb])
        k.vcopy(e32[:, :, 8:16], pin_s[:, :, 1024:1032], pinb, [efb])
        e32f = e32.rearrange("p g i -> p (g i)")
        k.vcopy(est[:, 0:64], e32f, [efb], [estb])
        k.tt(est[:, 64:128], e32f, est[:, 0:64], ALU.subtract, [efb], [estb])
        k.dma("sp", dr["ag2_in"][:, :], est, estp, [estb, ag2b], [])
        k.barrier()
        E = k.E["pool"]
        for (src, dst, sb, db) in ((dr["ag2_in"], dr["ag2_out"], ag2b, ag2ob), (ag_in_ap, ag_out_ap, agb, agob)):
            k._deps(E, [], [sb, db])
            ins = nc.gpsimd.collective_compute("AllGather", ALU.bypass,
                                               replica_groups=[[0, 1, 2, 3], [4, 5, 6, 7]],
                                               ins=[src.opt()], outs=[dst.opt()])
            ccprod.count += 1
            ins.then_inc(ccprod.sem, 1)
            k._commit((ccprod, ccprod.count), [], [sb, db])
        if OVERLAP_CC:
            for qn in ("sp", "pool"):
                k.E[qn].wait((ccprod, ccprod.count))
        else:
            k.barrier()

    if OVERLAP_CC:
        for tb in (2, 3):
            mod2(tb, on_dve=True)
            part_kv(tb)
            part_pin(tb)
        for tb in (0, 1):
            mod2(tb, on_dve=True)
            part_kv(tb)
        collectives()
        for tb in (0, 1):
            mod2(tb)
            part_pin(tb)
        k.barrier()
    else:
        for tb in (2, 3, 0, 1):
            mod2(tb)
            part_kv(tb)
            part_pin(tb)
            if tb == 3:
                collectives()
    if "cc" in SKIP or "cutpost" in SKIP:
        cut(2)
    k.dma("sp", eg, dr["ag2_out"][:, :].rearrange("(r p) c -> p r c", p=128), egp, [ag2ob], [egb])
    k.tt(ef, eg[:, :, 0:64], eg[:, :, 64:128], ALU.add, [egb], [efb])
    ef4 = ef.rearrange("p r (g i) -> p r g i", g=4)
    for side, (dst, lo) in enumerate(((pin_s[:, :, 0:8], 8), (pin_s[:, :, 1032:1040], 0))):
        for r in range(4):
            m = mask[:, side * 4 + r: side * 4 + r + 1]
            src = ef4[:, r, :, lo:lo + 8]
            if r == 0:
                k.ts(dst, src, m, ALU.mult, [efb, constb], pinb)
            else:
                k.stt(dst, src, m, dst, ALU.mult, ALU.add, [efb, constb], pinb)
    for gq in range(4):
        pool_segment(pin_s[:, gq:gq + 1, :], 1, 1024, gq, 0,
                     {"src": [pinb[gq]], "dst": [[p2b[2]], [p2b[3]]]}, 1024)
    k.prefetch(si1)
    k.barrier()
    cut(2)

    RH.reset(); RQ.reset()
    RX.top = xmark
    Q3 = k.bfv(k.alloc(RQ, NK * T * 2), NK * T).rearrange("p (h t) -> p h t", h=NK)
    Qb = [[Buf("Q%d_%d" % (h, i)) for i in range(NTB)] for h in range(NK)]
    B1 = RH
    h2t2 = [k.bfv(k.alloc(B1, 8192), 4096).rearrange("p (k t) -> p k t", k=NK) for _ in range(2)]
    h2b2 = [[Buf() for _ in range(NK)] for _ in range(2)]
    sq = [k.bfv(k.alloc(B1, 1024), 512) for _ in range(2)]
    sqb = [Buf(), Buf()]
    sd = [k.f32v(k.alloc(B1, 2048), 512) for _ in range(3)]
    sdb = [Buf() for _ in range(3)]
    kn = [k.f32v(k.alloc(B1, 2048), 512) for _ in range(3)]
    knb_ = [Buf() for _ in range(3)]
    knbf2 = [k.bfv(k.alloc(B1, 1024), 512) for _ in range(2)]
    knbfb2 = [Buf(), Buf()]
    ropeC = k.f32v(k.alloc(RX, 4096), 1024)
    ropeS = k.f32v(k.alloc(RX, 4096), 1024)
    t1 = k.f32v(k.alloc(RX, 2048), 512)
    t2 = k.f32v(k.alloc(RX, 2048), 512)
    t1b, t2b = Buf(), Buf()
    ropeb = Buf("rope2")
    k.dma("sp", ropeC, dr["ropeCS"][:, 0:1024], ropep, (), [ropeb])
    k.dma("sp", ropeS, dr["ropeCS"][:, 1024:2048], ropep, (), [ropeb])
    si2 = k.take()
    s2 = slot[si2]
    items = [(tb, hd) for tb in range(NTB) for hd in range(NK)]
    NI = len(items)

    def mod2c(tb, kk):
        g = tb // 2
        tsl = slice(tb * TBS, (tb + 1) * TBS)
        k.ts(h2t2[tb % 2][:, kk, :], R3[:, kk, tsl], DERk("H2S", g, kk), ALU.mult, [Rbk[tb][kk], derb],
             [h2b2[tb % 2][kk]], s2=DERk("H2B", g, kk), op1=ALU.add)

    def P1(i):
        tb, hd = items[i]
        pb, x = i % 3, i % 2
        k.pe_group(ps(pb), [(s2[:, kk * 1024 + hd * 128: kk * 1024 + (hd + 1) * 128], h2t2[tb % 2][:, kk, :])
                            for kk in range(NK)], [slotb[si2]] + h2b2[tb % 2], [bank[pb]])
        k.act(sq[x], ps(pb), AF.Square, [bank[pb]], [sqb[x]])
        if tb + 1 < NTB:
            mod2c(tb + 1, hd)

    def P2(i):
        tb, hd = items[i]
        pb, x, mb, y = i % 3, i % 2, 3 + i % 2, i % 3
        tsl = slice(tb * TBS, (tb + 1) * TBS)
        k.pe_mm(ps(mb), ones_h, sq[x], True, True, [sqb[x], onesb], [bank[mb]])
        k.act(sd[y], ps(mb), AF.Ln, [bank[mb]], [sdb[y]], bias=RMS_EPS, scale=1.0)
        k.act(sd[y], sd[y], AF.Exp, [], [sdb[y]], scale=-0.5)
        if tb < 2:
            k.stt(Q3[:, hd, tsl], ps(pb), qkg[:, 0:1], sd[y], ALU.mult, ALU.mult,
                  [bank[pb], sdb[y], constb], [Qb[hd][tb]])
        else:
            k.stt(kn[y], ps(pb), qkg[:, 0:1], sd[y], ALU.mult, ALU.mult,
                  [bank[pb], sdb[y], constb], [knb_[y]])
            k.act(knbf2[x], kn[y], AF.Copy, [knb_[y]], [knbfb2[x]])

    def P3(i):
        tb, hd = items[i]
        if tb < 2:
            return
        x, y, rbk = i % 2, i % 3, 5 + i % 2
        tsl = slice(tb * TBS, (tb + 1) * TBS)
        tq = slice((tb - 2) * TBS, (tb - 1) * TBS)
        k.pe_mm(ps(rbk), Pm, knbf2[x], True, True, [knbfb2[x], constb], [bank[rbk]])
        k.tt(t1, kn[y], ropeC[:, tq], ALU.mult, [knb_[y], ropeb], [t1b])
        k.tt(t2, ps(rbk), ropeS[:, tq], ALU.mult, [bank[rbk], ropeb], [t2b])
        k.tt(Q3[:, hd, tsl], t1, t2, ALU.add, [t1b, t2b], [Qb[hd][tb]])

    for kk in range(NK):
        mod2c(0, kk)
    for step in range(NI + 2):
        if step < NI:
            P1(step)
        if 0 <= step - 1 < NI:
            P2(step - 1)
        if 0 <= step - 2 < NI:
            P3(step - 2)
    k.prefetch(si2)
    k.barrier()
    cut(3)

    RH.reset()
    B2 = RH
    KC = k.bfv(k.alloc(B2, 1024), 512).rearrange("p (h t) -> p h t", h=2)
    VC = k.bfv(k.alloc(B2, 1024), 512).rearrange("p (j c) -> p j c", j=2)
    ctxb = Buf("ctx")
    ctxp = k.dprod("ctx")
    KA0 = k.bfv(k.alloc(B2, 8192), 4096).rearrange("p (r t) -> p r t", r=4)
    VA0 = k.bfv(k.alloc(B2, 8192), 4096).rearrange("p (j c) -> p j c", j=32)
    kab0, vab0 = Buf("KA"), Buf("VA")
    kap, vap = k.dprod("KA"), k.dprod("VA")
    SB = [0, 1, 2, 7]
    PT8 = [k.bfv(k.alloc(B2, 1024), 512) for _ in range(8)]
    PT8b = [Buf() for _ in range(8)]
    PT, PTb = PT8[:4], PT8b[:4]
    RX.top = xmark
    xs = k.f32v(k.alloc(RX, 2048), 512)
    xsb = Buf("xs")
    Esel = k.f32v(k.alloc(RX, 512), 128)
    eselb = Buf("esel")
    eselp = k.dprod("esel")
    k.dma("sp", Esel, dr["esel"][:, :], eselp, (), [eselb])
    rec = [k.f32v(k.alloc(B2, 2048), 512) for _ in range(2)]
    recb = [Buf(), Buf()]
    k.dma("pool", KC, dr["kctx"][:, :].rearrange("p (h t) -> p h t", h=2), ctxp, (), [ctxb])
    k.dma("pool", VC, dr["vctx"][:, :].rearrange("p (j c) -> p j c", j=2), ctxp, (), [ctxb])

    it = 0
    for b in range(4):
        tb = b // 2
        bsl = slice(b * 256, (b + 1) * 256)
        for pr in range(4):
            kvh = pr // 2
            ob, sb_ = 3 + it % 2, 5 + it % 2
            r = it % 2
            it += 1
            qv = Q3[:, 2 * pr:2 * pr + 2, bsl]
            qbufs = [Qb[2 * pr][tb], Qb[2 * pr + 1][tb]]
            pts = []
            for kc in range(2):
                pi_ = cnts["p"] % 4
                sbk = SB[pi_]
                cnts["p"] += 1
                k.pe_mm(ps(sbk).rearrange("p (a t) -> p a t", a=2),
                        Kp[:, kvh, b * 256 + kc * 128: b * 256 + (kc + 1) * 128], qv, True, True,
                        [Kpb[tb]] + qbufs, [bank[sbk]])
                k.act(PT[pi_], ps(sbk), AF.Exp, [bank[sbk]], [PTb[pi_]], scale=SCALE)
                pts.append(pi_)
            for kc in range(2):
                x = pts[kc]
                k.pe_mm(ps(ob), Vp[:, b * 2 + kc, kvh * 128:(kvh + 1) * 128], PT[x], kc == 0, kc == 1,
                        [Vpb[tb], PTb[x]], [bank[ob]])
                k.pe_mm(ps(sb_), ones1, PT[x], kc == 0, kc == 1, [PTb[x], onesb], [bank[sb_]])
            k.act(rec[r], ps(sb_), AF.Ln, [bank[sb_]], [recb[r]])
            k.act(rec[r], rec[r], AF.Exp, [], [recb[r]], scale=-1.0)
            k.tt(qv, ps(ob).rearrange("p (a t) -> p a t", a=2), rec[r].rearrange("p (a t) -> p a t", a=2),
                 ALU.mult, [bank[ob], recb[r]], qbufs)

    KA1 = k.bfv(k.alloc(RX, 8192), 4096).rearrange("p (r t) -> p r t", r=4)
    VA1 = k.bfv(oKp, 4096).rearrange("p (j c) -> p j c", j=32)
    kab1, vab1 = Buf("KA1"), Buf("VA1")
    kvt = [(KA0, VA0, kab0, vab0, []), (KA1, VA1, kab1, vab1, Kpb + Vpb)]
    kvp_ = [(kap, vap), (k.dprod("KA1"), k.dprod("VA1"))]
    for kvh in range(2):
        KA, VA, kab, vab, extra = kvt[kvh]
        k.dma("sp", KA, ag_out_ap[:, kvh * 1024:(kvh + 1) * 1024].rearrange("(r p) c -> p r c", p=128),
              kvp_[kvh][0], [agob], [kab])
        for r_ in range(4):
            k.dma("sp", VA[:, r_ * 8:(r_ + 1) * 8, :],
                  ag_out_ap[r_ * 128:(r_ + 1) * 128, 2048:4096].rearrange("p (j c) -> p j c", j=8)[:, :, kvh * 128:(kvh + 1) * 128],
                  kvp_[kvh][1], [agob], [vab] + extra)
    for kvh in range(2):
        KA, VA, kab, vab, extra = kvt[kvh]
        chunks = []
        for c in range(2):
            chunks.append((KC[:, kvh, c * 128:(c + 1) * 128], VC[:, c, kvh * 128:(kvh + 1) * 128], [ctxb]))
        for r_ in range(4):
            for j in range(8):
                chunks.append((KA[:, r_, j * 128:(j + 1) * 128], VA[:, r_ * 8 + j, :], [kab, vab]))
        NCH = len(chunks)
        for hd in range(4 * kvh, 4 * kvh + 4):
            for tb in (2, 3):
                tsl = slice(tb * TBS, (tb + 1) * TBS)
                ob, sb_ = 3 + it % 2, 5 + it % 2
                r = it % 2
                it += 1
                qv = Q3[:, hd, tsl]
                sbks = {}

                def qk(c):
                    pi_ = cnts["p"] % 4
                    sbk = SB[pi_]
                    cnts["p"] += 1
                    sbks[c] = c % 8
                    k.pe_mm(ps(sbk), chunks[c][0], qv, True, True, chunks[c][2] + [Qb[hd][tb]], [bank[sbk]])
                    k.act(PT8[c % 8], ps(sbk), AF.Exp, [bank[sbk]], [PT8b[c % 8]], scale=SCALE)

                def sum_mm(cc):
                    j = cc % 4
                    x = sbks[cc]
                    k.op("pe", lambda: nc.tensor.matmul(ps(sb_)[32 * j:32 * j + 32, :], lhsT=ones1[:, 0:32],
                                                        rhs=PT8[x], start=(cc == j), stop=(cc + 4 >= NCH),
                                                        tile_position=(0, 32 * j)),
                         [PT8b[x], onesb], [bank[sb_]])

                qk(0)
                qk(1)
                qk(2)
                for c in range(NCH):
                    x = sbks[c]
                    k.pe_mm(ps(ob), chunks[c][1], PT8[x], c == 0, c == NCH - 1, chunks[c][2] + [PT8b[x]], [bank[ob]])
                    if c % 4 == 3 or c == NCH - 1:
                        for cc in range((c // 4) * 4, c + 1):
                            sum_mm(cc)
                    if c + 3 < NCH:
                        qk(c + 3)
                k.vcopy(xs, ps(sb_), [bank[sb_]], [xsb])
                k.pe_mm(ps(sb_), Esel, xs, True, True, [xsb, eselb], [bank[sb_]])
                k.recip(rec[r], ps(sb_), [bank[sb_]], [recb[r]])
                k.tt(qv, ps(ob), rec[r], ALU.mult, [bank[ob], recb[r]], [Qb[hd][tb]])
                if kvh == 0 and tb == 3:
                    mi = 5 + (hd - 4 * kvh)
                    mod_compute(mi, 7)
                    mod_finish(mi, mi + 1, 7)
                    if mi == 8:
                        derive_late()
    k.barrier()
    cut(4)

    RH.reset(); RX.reset()
    oH2 = k.alloc(RH, NK * T * 2)
    assert oH2 == oH
    GX = RX
    h2tg = [k.bfv(k.alloc(GX, 8192), 4096).rearrange("p (k t) -> p k t", k=NK) for _ in range(2)]
    h2bg = [[Buf() for _ in range(NK)] for _ in range(2)]
    sg = [k.f32v(k.alloc(GX, 2048), 512) for _ in range(2)]
    sgb = [Buf(), Buf()]
    m12 = sg
    m12b = sgb
    hi_ = 0
    mod2(0, h2tg[0], h2bg[0])
    for ocp in range(4):
        si = k.take()
        s = slot[si]
        for tb in range(NTB):
            h2t, h2b = h2tg[hi_ % 2], h2bg[hi_ % 2]
            hi_ += 1
            if not (ocp == 3 and tb == NTB - 1):
                mod2((tb + 1) % NTB, h2tg[hi_ % 2], h2bg[hi_ % 2])
            tsl = slice(tb * TBS, (tb + 1) * TBS)
            for o in range(2):
                oc = 2 * ocp + o
                base = 4 * o
                k.pe_group(ps(base + 0), [(s[:, kk * 256 + o * 128: kk * 256 + (o + 1) * 128], h2t[:, kk, :])
                                          for kk in range(NK)], [slotb[si]] + h2b, [bank[base + 0]])
                k.pe_group(ps(base + 1), [(s[:, 2048 + kk * 256 + o * 128: 2048 + kk * 256 + (o + 1) * 128], h2t[:, kk, :])
                                          for kk in range(NK)], [slotb[si]] + h2b, [bank[base + 1]])
                k.pe_group(ps(base + 2), [(s[:, 4096 + kk * 256 + o * 128: 4096 + kk * 256 + (o + 1) * 128], Q3[:, kk, tsl])
                                          for kk in range(NK)], [slotb[si]] + [Qb[kk][tb] for kk in range(NK)],
                           [bank[base + 2]])
                k.pe_group(ps(base + 3), [(s[:, 6144 + kk * 256 + o * 128: 6144 + kk * 256 + (o + 1) * 128], pooled2[:, kk, tsl])
                                          for kk in range(4)], [slotb[si], p2b[tb]], [bank[base + 3]])
                k.act(sg[0], ps(base + 0), AF.Sigmoid, [bank[base + 0]], [sgb[0]])
                k.act(sg[1], ps(base + 1), AF.Sigmoid, [bank[base + 1]], [sgb[1]])
                k.tt(m12[0], sg[0], ps(base + 2), ALU.mult, [sgb[0], bank[base + 2]], [m12b[0]])
                k.tt(m12[1], sg[1], ps(base + 3), ALU.mult, [sgb[1], bank[base + 3]], [m12b[1]])
                k.tt(H3[:, oc, tsl], m12[0], m12[1], ALU.add, [m12b[0], m12b[1]], [Hb[tb]])
        k.prefetch(si)
    k.barrier()
    cut(5)

    FB2 = ffn_alloc()
    L2 = FB2["Ln"]
    si = k.take()
    s = slot[si]
    cntf = 0
    for tb in range(NTB):
        g = tb // 2
        tsl = slice(tb * TBS, (tb + 1) * TBS)
        for oc in range(NK):
            f = 4 + cntf % 2
            cntf += 1
            k.pe_group(ps(f), [(s[:, kk * 1024 + oc * 128: kk * 1024 + (oc + 1) * 128], H3[:, kk, tsl])
                               for kk in range(NK)], [slotb[si], Hb[tb]], [bank[f]])
            k.stt(R3[:, oc, tsl], ps(f), DERk("G2", g, oc), R3[:, oc, tsl], ALU.mult, ALU.add,
                  [bank[f], derb], [Rbk[tb][oc]])
            ln_stats_chunk(tb, L2, oc)
        sbk6, sbk7 = ((6, 7), (2, 3))[tb % 2]
        ln_pe_stats(tb, L2, sbk6, sbk7)
        if tb >= 1:
            pb6, pb7 = ((6, 7), (2, 3))[(tb - 1) % 2]
            ln_finish(tb - 1, L2, "RS2", "RB2", hs="H3S", hb="H3B", b6=pb6, b7=pb7)
    pb6, pb7 = ((6, 7), (2, 3))[(NTB - 1) % 2]
    ln_finish(NTB - 1, L2, "RS2", "RB2", hs="H3S", hb="H3B", b6=pb6, b7=pb7)
    k.prefetch(si)
    cut(6)

    ffn_phase("HG3", FB2, (("RS3", "RB3"), {"final": True}))

    finalize()


_PROGRAM = None


def _host_consts():
    n_freq = 32
    inv_freq = (10000.0 ** (-np.arange(n_freq, dtype=np.float32) / n_freq)).astype(np.float32)
    pm = np.zeros((128, 128), np.float32)
    for d2 in range(64):
        pm[d2 + 64, d2] = -1.0
        pm[d2, d2 + 64] = 1.0

    def corr_tab(n, left, right):
        out = np.ones((4, 16), np.float32)
        for g, w in enumerate(WINS):
            for i in range(8):
                if left:
                    t = i
                    lo = min(max(t - w // 2, 0), n); hi = min(max(t - w // 2 + w, 0), n)
                    out[g, i] = w / float(hi - lo)
                if right:
                    t = n - 8 + i
                    lo = min(max(t - w // 2, 0), n); hi = min(max(t - w // 2 + w, 0), n)
                    out[g, 8 + i] = w / float(hi - lo)
        return out.reshape(64)
    return inv_freq, pm, corr_tab


def kernel(x_prompt, x_sample, cache_k, cache_v, c, c_ctx, w_mod, b_mod, ln_g, ln_b,
           ffn1_w1, ffn1_w2, w_in, q_norm_g, k_norm_g, pool_w, pool_scale,
           w_up_attn, w_up_pool, w_out, ffn2_w1, ffn2_w2):
    global _PROGRAM
    f32 = np.float32
    A = lambda a: np.ascontiguousarray(np.asarray(a, dtype=f32))
    x_prompt, x_sample, cache_k, cache_v = A(x_prompt), A(x_sample), A(cache_k), A(cache_v)
    c, c_ctx = A(c), A(c_ctx)
    inv_freq, pm, corr_tab = _host_consts()

    def fm(v):
        return np.ascontiguousarray(v.reshape(8, 128).T)

    def w1_pm(w):
        out = np.empty((128, 2, 8 * DFF), f32)
        pieces = [(0, 2)] + [(c0, 3) for c0 in range(2, 20, 3)] + [(20, 2)]
        for half in range(2):
            wh = w[:, half * DFF:(half + 1) * DFF].reshape(8, 128, DFF)
            for (c0, n) in pieces:
                blk = wh[:, :, c0 * 128:(c0 + n) * 128].transpose(1, 0, 2)
                out[:, half, 8 * 128 * c0: 8 * 128 * (c0 + n)] = blk.reshape(128, 8 * n * 128)
        return out

    shared = {
        "w_mod": A(w_mod)[0], "ffn1_w1": w1_pm(A(ffn1_w1)[0]), "ffn1_w2": A(ffn1_w2)[0],
        "ffn2_w1": w1_pm(A(ffn2_w1)[0]), "ffn2_w2": A(ffn2_w2)[0], "w_in": A(w_in)[0],
        "w_up_attn": A(w_up_attn)[0], "w_up_pool": A(w_up_pool)[0], "w_out": A(w_out)[0],
        "bmodT": np.ascontiguousarray(A(b_mod)[0].reshape(72, 128).T),
        "lngT": np.ascontiguousarray(A(ln_g)[0].reshape(24, 128).T),
        "lnbT": np.ascontiguousarray(A(ln_b)[0].reshape(24, 128).T),
        "qkg": np.ascontiguousarray(np.stack([A(q_norm_g)[0], A(k_norm_g)[0]], axis=1)),
        "pool_w": np.ascontiguousarray(A(pool_w)[0].transpose(1, 0, 2).reshape(128, 512)),
        "pscaleT": np.ascontiguousarray(A(pool_scale)[0].reshape(4, 128).T),
        "pmat": pm,
        "esel": np.ascontiguousarray(np.broadcast_to((np.arange(128) % 32 == 0).astype(f32)[:, None], (128, 128))),
    }
    in_maps = []
    for core in range(NCORES):
        b, q = core // 4, core % 4
        xp = x_prompt[4 * core:4 * core + 4].reshape(1024, D)
        xs = x_sample[b, 1024 * q:1024 * (q + 1)]
        xall = np.concatenate([xp, xs], axis=0)
        xT = np.ascontiguousarray(xall.reshape(T, 8, 128).transpose(2, 1, 0))
        condT = np.stack([fm(c_ctx), fm(c[b])], axis=2).reshape(128, 16)
        kctx = np.ascontiguousarray(cache_k[b, 0].transpose(2, 1, 0).reshape(128, 512))
        vctx = np.ascontiguousarray(cache_v[b, 0].reshape(2, 128, 256).transpose(1, 0, 2).reshape(128, 512))
        pos = np.arange(1024 * q, 1024 * (q + 1))
        row = (pos // 64).astype(f32)
        col = (pos % 64).astype(f32)
        ang = np.concatenate([row[:, None] * inv_freq, col[:, None] * inv_freq], axis=-1).astype(f32)
        cs, sn = np.cos(ang).astype(f32).T, np.sin(ang).astype(f32).T
        ropeCS = np.concatenate([np.concatenate([cs, cs], 0), np.concatenate([sn, sn], 0)], axis=1)
        m = np.zeros(8, f32)
        if q - 1 >= 0:
            m[q - 1] = 1.0
        if q + 1 <= 3:
            m[4 + q + 1] = 1.0
        corr = np.concatenate([corr_tab(4096, q == 0, q == 3), corr_tab(256, True, True)])
        im = dict(shared)
        im.update({
            "xT": xT, "condT": np.ascontiguousarray(condT), "kctx": kctx, "vctx": vctx,
            "ropeCS": np.ascontiguousarray(ropeCS.astype(f32)),
            "masks": np.ascontiguousarray(np.broadcast_to(m, (128, 8))),
            "corr": np.ascontiguousarray(np.broadcast_to(corr.astype(f32), (128, 128))),
        })
        in_maps.append(im)

    if _PROGRAM is None:
        _st = os.environ.get("KSTOP")
        _PROGRAM = build_program(None if _st is None else int(_st))
    res = run_bass_kernel_spmd(_PROGRAM, in_maps, core_ids=list(range(NCORES)))

    y_prompt = np.empty((32, 256, D), f32)
    y_sample = np.empty((2, 4096, D), f32)
    new_k = np.empty((32, 1, 256, 2, 128), f32)
    new_v = np.empty((32, 1, 256, 2, 128), f32)
    for core in range(NCORES):
        r = res.results[core]
        b, q = core // 4, core % 4
        yT = np.asarray(r["yT"], dtype=f32)
        yall = yT.transpose(2, 1, 0).reshape(T, D)
        y_prompt[4 * core:4 * core + 4] = yall[:1024].reshape(4, 256, D)
        y_sample[b, 1024 * q:1024 * (q + 1)] = yall[1024:]
        nk_ = np.asarray(r["nk"], dtype=f32)
        new_k[4 * core:4 * core + 4, 0] = nk_.transpose(2, 1, 0).reshape(4, 256, 2, 128)
        nv_ = np.asarray(r["nv"], dtype=f32)
        new_v[4 * core:4 * core + 4, 0] = nv_.transpose(1, 0, 2).reshape(4, 256, 2, 128)
    return (y_prompt, y_sample, new_k, new_v)
```
